# Optimizing a Trainium2 kernel written in Bass

```python
import math
import jax, jax.numpy as jnp
from jax import lax
import numpy as np


D_MODEL = 2048
BATCH = 4
SEQ = 4096
DEPTH = 2

GRID_W = 64
CTX_LEN = 256
EPS = 1e-6

MLA_HEADS = 8
MLA_NOPE = 128
MLA_ROPE = 64
MLA_QK = MLA_NOPE + MLA_ROPE
MLA_V = 128
MLA_Q_LORA = 512
MLA_KV_LORA = 512
MLA_WIDTH = MLA_HEADS * MLA_V
ROPE_BASE = 10000.0
ROPE_PAIRS_PER_AXIS = MLA_ROPE // 4
Q_BLOCK = 128

HY_WIDTH = 512
HY_ORDER = 2
HY_BANDS = 16
HY_EMB = 1 + 2 * HY_BANDS
HY_HIDDEN = 64
HY_FILTER_OUT = HY_ORDER * 2 * HY_WIDTH
HY_SHIFT = 0.05
HY_DECAY_MIN = 3.0
HY_DECAY_MAX = 15.0
SHORT_CONV = 3

ML_HEADS = 4
ML_HEAD_DIM = 128
ML_WIDTH = ML_HEADS * ML_HEAD_DIM
ML_CHUNK = 128
ML_GATES = 4 * ML_HEADS

MIX_WIDTH = MLA_WIDTH + HY_WIDTH + ML_WIDTH
FFN_HIDDEN = ((8 * D_MODEL // 3 + 255) // 256) * 256

IN_SIZES = (MLA_Q_LORA, MLA_KV_LORA, MLA_ROPE, 3 * HY_WIDTH, 2 * ML_WIDTH, ML_WIDTH, ML_WIDTH, ML_GATES)
IN_SPLITS = tuple(sum(IN_SIZES[:i + 1]) for i in range(len(IN_SIZES) - 1))
N_IN = sum(IN_SIZES)

kernel_name = 'hybrid_mla_hyena_mlstm_dit_block'


def rmsnorm(x, g):
    xf = x.astype(jnp.float32)
    y = xf * lax.rsqrt(jnp.mean(xf * xf, axis=-1, keepdims=True) + EPS)
    return (y * g.astype(jnp.float32)).astype(x.dtype)


def modulate(h, shift, scale):
    return h * (1 + scale) + shift


def axial_rope(n_tokens):
    rows = n_tokens // GRID_W
    row = jnp.repeat(jnp.arange(rows, dtype=jnp.float32), GRID_W)
    col = jnp.tile(jnp.arange(GRID_W, dtype=jnp.float32), rows)
    freqs = ROPE_BASE ** (-jnp.arange(ROPE_PAIRS_PER_AXIS, dtype=jnp.float32) / ROPE_PAIRS_PER_AXIS)
    ang = jnp.concatenate([row[:, None] * freqs, col[:, None] * freqs], axis=-1)
    return jnp.cos(ang), jnp.sin(ang)


def apply_rope(x, cos, sin):
    half = x.shape[-1] // 2
    x1, x2 = x[..., :half], x[..., half:]
    cos = cos[None, :, None, :].astype(x.dtype)
    sin = sin[None, :, None, :].astype(x.dtype)
    return jnp.concatenate([x1 * cos - x2 * sin, x2 * cos + x1 * sin], axis=-1)


def short_conv(u, w, b):
    L = u.shape[1]
    pad = SHORT_CONV // 2
    up = jnp.pad(u, ((0, 0), (pad, pad), (0, 0)))
    out = b
    for j in range(SHORT_CONV):
        out = out + up[:, j:j + L] * w[j]
    return out


def mla_heads(cq, ckv, krope, p, rope):
    B, L, _ = cq.shape
    q = (rmsnorm(cq, p['mla_qa_norm']) @ p['mla_w_uq']).reshape(B, L, MLA_HEADS, MLA_QK)
    kv = (rmsnorm(ckv, p['mla_kva_norm']) @ p['mla_w_ukv']).reshape(B, L, MLA_HEADS, MLA_NOPE + MLA_V)
    k_rope = jnp.broadcast_to(krope[:, :, None, :], (B, L, MLA_HEADS, MLA_ROPE))
    k = jnp.concatenate([kv[..., :MLA_NOPE], k_rope], axis=-1)
    v = kv[..., MLA_NOPE:]
    q = rmsnorm(q, p['mla_q_norm'])
    k = rmsnorm(k, p['mla_k_norm'])
    if rope is not None:
        cos, sin = rope
        q = jnp.concatenate([q[..., :MLA_NOPE], apply_rope(q[..., MLA_NOPE:], cos, sin)], axis=-1)
        k = jnp.concatenate([k[..., :MLA_NOPE], apply_rope(k[..., MLA_NOPE:], cos, sin)], axis=-1)
    return q.transpose(0, 2, 1, 3), k.transpose(0, 2, 1, 3), v.transpose(0, 2, 1, 3)


def attend(q, k, v):
    s = jnp.einsum('bhqd,bhkd->bhqk', q, k).astype(jnp.float32) * (MLA_QK ** -0.5)
    pr = jax.nn.softmax(s, axis=-1).astype(v.dtype)
    return jnp.einsum('bhqk,bhkd->bhqd', pr, v)


def block_attention(q, k, v):
    B, H, L, d = q.shape
    nb = L // Q_BLOCK
    qb = jnp.moveaxis(q.reshape(B, H, nb, Q_BLOCK, d), 2, 0)
    ob = lax.map(lambda qi: attend(qi, k, v), qb)
    return jnp.moveaxis(ob, 0, 2).reshape(B, H, L, -1)


def hyena_filter_fft(L, p):
    f32 = lambda a: a.astype(jnp.float32)
    t = jnp.arange(L, dtype=jnp.float32) / L
    bands = jnp.arange(1, HY_BANDS + 1, dtype=jnp.float32)
    ang = 2.0 * math.pi * t[:, None] * bands
    feats = jnp.concatenate([t[:, None], jnp.cos(ang), jnp.sin(ang)], axis=-1)
    h = jnp.sin(feats @ f32(p['hy_w1']) + f32(p['hy_b1']))
    h = jnp.sin(h @ f32(p['hy_w2']) + f32(p['hy_b2']))
    h = (h @ f32(p['hy_w3'])) * (jnp.exp(-t[:, None] * f32(p['hy_decay'])) + HY_SHIFT)
    h = h.reshape(L, HY_ORDER, 2, HY_WIDTH)
    h_fwd, h_bwd = h[:, :, 0], h[:, :, 1]
    buf = jnp.concatenate([h_fwd, jnp.zeros((1, HY_ORDER, HY_WIDTH), jnp.float32), h_bwd[:0:-1]], axis=0)
    return jnp.fft.rfft(buf, axis=0)


def fft_conv(z, kf):
    L = z.shape[1]
    zf = jnp.fft.rfft(z.astype(jnp.float32), n=2 * L, axis=1)
    y = jnp.fft.irfft(zf * kf[None], n=2 * L, axis=1)[:, :L]
    return y.astype(z.dtype)


def hyena(u, p):
    L = u.shape[1]
    v, x1, x2 = jnp.split(short_conv(u, p['hy_conv_w'], p['hy_conv_b']), 3, axis=-1)
    kf = hyena_filter_fft(L, p)
    z = v
    for n, gate in enumerate((x1, x2)):
        z = gate * (fft_conv(z, kf[:, n]) + p['hy_skip'][n] * z)
    return z


def mlstm_inputs(qk, v, gates, p):
    B, L, _ = qk.shape
    qk = jax.nn.silu(short_conv(qk, p['ml_conv_w'], p['ml_conv_b']))
    q, k = jnp.split(qk, 2, axis=-1)
    heads = lambda t: t.reshape(B, L, ML_HEADS, ML_HEAD_DIM).transpose(0, 2, 1, 3).astype(jnp.float32)
    q = heads(q) * (ML_HEAD_DIM ** -0.5)
    k = heads(k)
    v = heads(v)
    g = (gates + p['ml_gate_b']).astype(jnp.float32).reshape(B, L, 4, ML_HEADS).transpose(2, 0, 3, 1)
    log_i = g[:2]
    log_f = jax.nn.log_sigmoid(g[2:])
    return q, k, v, log_i, log_f


def mlstm_scan(q, k, v, log_i, log_f, state):
    B, H, L, _ = q.shape
    nc = L // ML_CHUNK
    chunks = lambda t: jnp.moveaxis(t.reshape((B, H, nc, ML_CHUNK) + t.shape[3:]), 2, 0)
    causal = jnp.tril(jnp.ones((ML_CHUNK, ML_CHUNK), dtype=bool))

    def step(carry, inp):
        C, n, m = carry
        qc, kc, vc, lic, lfc = inp
        b = jnp.cumsum(lfc, axis=-1)
        d = jnp.where(causal, b[..., :, None] - b[..., None, :] + lic[..., None, :], -jnp.inf)
        inter = b + m[..., None]
        m_t = jnp.maximum(inter, jnp.max(d, axis=-1))
        dexp = jnp.exp(d - m_t[..., None])
        inter_w = jnp.exp(inter - m_t)
        s = jnp.einsum('bhtd,bhsd->bhts', qc, kc) * dexp
        num = inter_w[..., None] * jnp.einsum('bhtd,bhde->bhte', qc, C) + jnp.einsum('bhts,bhse->bhte', s, vc)
        den = inter_w * jnp.einsum('bhtd,bhd->bht', qc, n) + jnp.sum(s, axis=-1)
        h = num / jnp.maximum(jnp.abs(den), jnp.exp(-m_t))[..., None]
        b_last = b[..., -1]
        g_s = b_last[..., None] - b + lic
        m_new = jnp.maximum(b_last + m, jnp.max(g_s, axis=-1))
        a = jnp.exp(b_last + m - m_new)
        w = jnp.exp(g_s - m_new[..., None])
        C_new = a[..., None, None] * C + jnp.einsum('bhsd,bhse->bhde', kc * w[..., None], vc)
        n_new = a[..., None] * n + jnp.einsum('bhs,bhsd->bhd', w, kc)
        return (C_new, n_new, m_new), h

    state, h = lax.scan(step, state, (chunks(q), chunks(k), chunks(v), chunks(log_i), chunks(log_f)))
    return jnp.moveaxis(h, 0, 2).reshape(B, H, L, -1), state


def mlstm_bidir(ctx_in, lat_in):
    qc, kc, vc, lic, lfc = ctx_in
    ql, kl, vl, lil, lfl = lat_in
    B = qc.shape[0]
    zero = (jnp.zeros((B, ML_HEADS, ML_HEAD_DIM, ML_HEAD_DIM), jnp.float32),
            jnp.zeros((B, ML_HEADS, ML_HEAD_DIM), jnp.float32),
            jnp.zeros((B, ML_HEADS), jnp.float32))
    flip = lambda t: jnp.flip(t, axis=2)
    h_cf, s_f = mlstm_scan(qc, kc, vc, lic[0], lfc[0], zero)
    h_lf, _ = mlstm_scan(ql, kl, vl, lil[0], lfl[0], s_f)
    h_cb, s_b = mlstm_scan(flip(qc), flip(kc), flip(vc), flip(lic[1]), flip(lfc[1]), zero)
    h_lb, _ = mlstm_scan(flip(ql), flip(kl), flip(vl), flip(lil[1]), flip(lfl[1]), s_b)
    return h_cf + flip(h_cb), h_lf + flip(h_lb)


def combine_groups(a, y, h, o, p):
    g = p['mix_norm_g']
    ml = rmsnorm(h.transpose(0, 2, 1, 3).astype(o.dtype), g[MLA_WIDTH + HY_WIDTH:].reshape(ML_HEADS, ML_HEAD_DIM))
    ml = ml.reshape(o.shape) * jax.nn.sigmoid(o)
    cat = jnp.concatenate([rmsnorm(a, g[:MLA_WIDTH]), rmsnorm(y, g[MLA_WIDTH:MLA_WIDTH + HY_WIDTH]), ml], axis=-1)
    return cat @ p['w_out']


def mixer(pl, pc, need_ctx, p, rope):
    lat = jnp.split(pl, IN_SPLITS, axis=-1)
    ctx = jnp.split(pc, IN_SPLITS, axis=-1)
    merge = lambda t: t.transpose(0, 2, 1, 3).reshape(t.shape[0], t.shape[2], -1)
    q_l, k_l, v_l = mla_heads(lat[0], lat[1], lat[2], p, rope)
    q_c, k_c, v_c = mla_heads(ctx[0], ctx[1], ctx[2], p, None)
    k_all = jnp.concatenate([k_c, k_l], axis=2)
    v_all = jnp.concatenate([v_c, v_l], axis=2)
    a_l = merge(block_attention(q_l, k_all, v_all))
    y_l = hyena(lat[3], p)
    h_c, h_l = mlstm_bidir(mlstm_inputs(ctx[4], ctx[5], ctx[7], p), mlstm_inputs(lat[4], lat[5], lat[7], p))
    out_l = combine_groups(a_l, y_l, h_l, lat[6], p)
    out_c = None
    if need_ctx:
        a_c = merge(attend(q_c, k_c, v_c))
        y_c = hyena(ctx[3], p)
        out_c = combine_groups(a_c, y_c, h_c, ctx[6], p)
    return out_l, out_c


def swiglu(h, w1, w2):
    gate, up = jnp.split(h @ w1, 2, axis=-1)
    return (jax.nn.silu(gate) * up) @ w2


def setup_inputs(seed: int = 0) -> dict:
    key = jax.random.key(seed)
    ks = iter(jax.random.split(key, 40))
    f32 = jnp.float32

    def nrm(shape, std):
        return std * jax.random.normal(next(ks), shape, f32)

    def gain(shape):
        return 1.0 + 0.02 * jax.random.normal(next(ks), shape, f32)

    x = nrm((BATCH, SEQ, D_MODEL), 1.0)
    c = nrm((BATCH, D_MODEL), 1.0)
    ctx = nrm((BATCH, CTX_LEN, D_MODEL), 1.0)
    c_ctx = nrm((D_MODEL,), 1.0)
    ada_w = nrm((DEPTH, D_MODEL, 6 * D_MODEL), 0.5 * D_MODEL ** -0.5)
    ada_b = nrm((DEPTH, 6 * D_MODEL), 0.01)
    norm1_g = gain((DEPTH, D_MODEL))
    norm2_g = gain((DEPTH, D_MODEL))
    w_in = nrm((DEPTH, D_MODEL, N_IN), D_MODEL ** -0.5)
    mla_qa_norm = gain((DEPTH, MLA_Q_LORA))
    mla_kva_norm = gain((DEPTH, MLA_KV_LORA))
    mla_w_uq = nrm((DEPTH, MLA_Q_LORA, MLA_HEADS * MLA_QK), MLA_Q_LORA ** -0.5)
    mla_w_ukv = nrm((DEPTH, MLA_KV_LORA, MLA_HEADS * (MLA_NOPE + MLA_V)), MLA_KV_LORA ** -0.5)
    mla_q_norm = gain((DEPTH, MLA_QK))
    mla_k_norm = gain((DEPTH, MLA_QK))
    hy_conv_w = nrm((DEPTH, SHORT_CONV, 3 * HY_WIDTH), SHORT_CONV ** -0.5)
    hy_conv_b = nrm((DEPTH, 3 * HY_WIDTH), 0.01)
    hy_w1 = nrm((DEPTH, HY_EMB, HY_HIDDEN), 1.0)
    hy_b1 = nrm((DEPTH, HY_HIDDEN), 0.1)
    hy_w2 = nrm((DEPTH, HY_HIDDEN, HY_HIDDEN), HY_HIDDEN ** -0.5)
    hy_b2 = nrm((DEPTH, HY_HIDDEN), 0.1)
    hy_w3 = nrm((DEPTH, HY_HIDDEN, HY_FILTER_OUT), 0.01)
    hy_decay = jax.random.uniform(next(ks), (DEPTH, HY_FILTER_OUT), f32, HY_DECAY_MIN, HY_DECAY_MAX)
    hy_skip = nrm((DEPTH, HY_ORDER, HY_WIDTH), 0.5)
    ml_conv_w = nrm((DEPTH, SHORT_CONV, 2 * ML_WIDTH), SHORT_CONV ** -0.5)
    ml_conv_b = nrm((DEPTH, 2 * ML_WIDTH), 0.01)
    ml_gate_b = jnp.concatenate([nrm((DEPTH, 2 * ML_HEADS), 0.1),
                                 jax.random.uniform(next(ks), (DEPTH, 2 * ML_HEADS), f32, 3.0, 6.0)], axis=-1)
    mix_norm_g = gain((DEPTH, MIX_WIDTH))
    w_out = nrm((DEPTH, MIX_WIDTH, D_MODEL), MIX_WIDTH ** -0.5)
    ffn_w1 = nrm((DEPTH, D_MODEL, 2 * FFN_HIDDEN), D_MODEL ** -0.5)
    ffn_w2 = nrm((DEPTH, FFN_HIDDEN, D_MODEL), FFN_HIDDEN ** -0.5)
    return {'x': x, 'c': c, 'ctx': ctx, 'c_ctx': c_ctx, 'ada_w': ada_w, 'ada_b': ada_b,
            'norm1_g': norm1_g, 'norm2_g': norm2_g, 'w_in': w_in,
            'mla_qa_norm': mla_qa_norm, 'mla_kva_norm': mla_kva_norm, 'mla_w_uq': mla_w_uq,
            'mla_w_ukv': mla_w_ukv, 'mla_q_norm': mla_q_norm, 'mla_k_norm': mla_k_norm,
            'hy_conv_w': hy_conv_w, 'hy_conv_b': hy_conv_b, 'hy_w1': hy_w1, 'hy_b1': hy_b1,
            'hy_w2': hy_w2, 'hy_b2': hy_b2, 'hy_w3': hy_w3, 'hy_decay': hy_decay, 'hy_skip': hy_skip,
            'ml_conv_w': ml_conv_w, 'ml_conv_b': ml_conv_b, 'ml_gate_b': ml_gate_b,
            'mix_norm_g': mix_norm_g, 'w_out': w_out, 'ffn_w1': ffn_w1, 'ffn_w2': ffn_w2}


def reference(x, c, ctx, c_ctx, ada_w, ada_b, norm1_g, norm2_g, w_in,
              mla_qa_norm, mla_kva_norm, mla_w_uq, mla_w_ukv, mla_q_norm, mla_k_norm,
              hy_conv_w, hy_conv_b, hy_w1, hy_b1, hy_w2, hy_b2, hy_w3, hy_decay, hy_skip,
              ml_conv_w, ml_conv_b, ml_gate_b, mix_norm_g, w_out, ffn_w1, ffn_w2):
    rope = axial_rope(x.shape[1])
    silu_c = jax.nn.silu(c)
    silu_cc = jax.nn.silu(c_ctx)
    for l in range(DEPTH):
        need_ctx = l < DEPTH - 1
        p = {'mla_qa_norm': mla_qa_norm[l], 'mla_kva_norm': mla_kva_norm[l], 'mla_w_uq': mla_w_uq[l],
             'mla_w_ukv': mla_w_ukv[l], 'mla_q_norm': mla_q_norm[l], 'mla_k_norm': mla_k_norm[l],
             'hy_conv_w': hy_conv_w[l], 'hy_conv_b': hy_conv_b[l], 'hy_w1': hy_w1[l], 'hy_b1': hy_b1[l],
             'hy_w2': hy_w2[l], 'hy_b2': hy_b2[l], 'hy_w3': hy_w3[l], 'hy_decay': hy_decay[l],
             'hy_skip': hy_skip[l], 'ml_conv_w': ml_conv_w[l], 'ml_conv_b': ml_conv_b[l],
             'ml_gate_b': ml_gate_b[l], 'mix_norm_g': mix_norm_g[l], 'w_out': w_out[l]}
        mod_l = jnp.split((silu_c @ ada_w[l] + ada_b[l])[:, None, :], 6, axis=-1)
        mod_c = jnp.split(silu_cc @ ada_w[l] + ada_b[l], 6, axis=-1)
        xn = modulate(rmsnorm(x, norm1_g[l]), mod_l[0], mod_l[1])
        cn = modulate(rmsnorm(ctx, norm1_g[l]), mod_c[0], mod_c[1])
        y_l, y_c = mixer(xn @ w_in[l], cn @ w_in[l], need_ctx, p, rope)
        x = x + mod_l[2] * y_l
        x = x + mod_l[5] * swiglu(modulate(rmsnorm(x, norm2_g[l]), mod_l[3], mod_l[4]), ffn_w1[l], ffn_w2[l])
        if need_ctx:
            ctx = ctx + mod_c[2] * y_c
            ctx = ctx + mod_c[5] * swiglu(modulate(rmsnorm(ctx, norm2_g[l]), mod_c[3], mod_c[4]), ffn_w1[l], ffn_w2[l])
    return x
```

```python
import math
from contextlib import ExitStack

import numpy as np
import concourse.bass as bass
import concourse.mybir as mybir
from concourse.bass_utils import run_bass_kernel_spmd

F32 = mybir.dt.float32
BF16 = mybir.dt.bfloat16
AF = mybir.ActivationFunctionType
ALU = mybir.AluOpType
AX = mybir.AxisListType

ENGS = ["sync", "scalar", "vector", "gpsimd", "tensor"]

D = 2048
KC = 16
NCTX = 256
NLAT = 4096
T = NCTX + NLAT
DEPTH = 2
EPS = 1e-6
N_IN = 4688
FFN_H = 5632
ARENA = 204800
TILES = [(0, 256)] + [(256 + 512 * i, 512) for i in range(8)]
NTILES = [(256 * i, 256) for i in range(17)]


class Prog:
    def __init__(self, nc):
        self.nc = nc
        self.ops = {e: [] for e in ENGS}
        self.last_w = {}
        self.readers = {}
        self.dma_cnt = {}
        self.fence = set()
        self.sealed = {}
        self.strict_same = {"scalar": True, "vector": True, "gpsimd": True,
                            "tensor": False, "sync": False}

    def _deps(self, r, w):
        deps = set()
        for k in r:
            lw = self.last_w.get(k)
            if lw is not None:
                deps.add(lw)
        for k in w:
            lw = self.last_w.get(k)
            if lw is not None:
                deps.add(lw)
            for rd in self.readers.get(k, ()):
                deps.add(rd)
        return deps

    def _commit(self, ref, r, w):
        for k in r:
            self.readers.setdefault(k, []).append(ref)
        for k in w:
            self.last_w[k] = ref
            self.readers[k] = []

    def barrier(self):
        f = set()
        for e in ENGS:
            for i in range(len(self.ops[e]) - 1, -1, -1):
                if self.ops[e][i]["dma"] is None:
                    f.add(("c", e, i))
                    break
        for k, c in self.dma_cnt.items():
            f.add(("d", k, c))
        self.fence = f

    def op(self, eng, fn, r=(), w=()):
        deps = self._deps(r, w) | self.fence
        ref = ("c", eng, len(self.ops[eng]))
        self.ops[eng].append(dict(fn=fn, deps=deps, ref=ref, dma=None))
        self._commit(ref, r, w)
        return ref

    def seal(self, key="pc"):
        c = self.dma_cnt.get(key, 0)
        for k, ref in list(self.last_w.items()):
            if ref[0] == "d" and ref[1] == key:
                self.last_w[k] = ("d", key, c)
        self.sealed[key] = c

    def dma(self, eng, out, in_, key, r=(), w=(), **kw):
        deps = self._deps(r, w) | self.fence
        if self.sealed.get(key):
            deps = deps | {("d", key, self.sealed[key])}
        cnt = self.dma_cnt.get(key, 0) + 1
        self.dma_cnt[key] = cnt
        ref = ("d", key, cnt)
        self.ops[eng].append(dict(fn=lambda e: e.dma_start(out=out, in_=in_, **kw),
                                  deps=deps, ref=ref, dma=key))
        self._commit(ref, r, w)
        return ref

    def emit(self):
        nc = self.nc
        signaled = {e: set() for e in ENGS}
        for e in ENGS:
            for o in self.ops[e]:
                for d in o["deps"]:
                    if d[0] == "c":
                        if d[1] == e and not self.strict_same[e]:
                            continue
                        signaled[d[1]].add(d[2])
        sigcount = {}
        for e in ENGS:
            c = 0
            m = {}
            for i, o in enumerate(self.ops[e]):
                if o["dma"] is None and i in signaled[e]:
                    c += 1
                    m[i] = c
            sigcount[e] = m
        with ExitStack() as st:
            csem = {e: st.enter_context(nc.semaphore("s_" + e)) for e in ENGS}
            dsem = {k: st.enter_context(nc.semaphore("d_%d" % i))
                    for i, k in enumerate(self.dma_cnt)}
            block = st.enter_context(nc.Block())

            def make(eng):
                def body(e):
                    waited = {}
                    for i, o in enumerate(self.ops[eng]):
                        waits = {}
                        for d in o["deps"]:
                            if d[0] == "c":
                                if d[1] == eng and not self.strict_same[eng]:
                                    continue
                                s, v, kk = csem[d[1]], sigcount[d[1]][d[2]], ("c", d[1])
                            else:
                                s, v, kk = dsem[d[1]], 16 * d[2], ("d", d[1])
                            if waited.get(kk, 0) >= v:
                                continue
                            if kk not in waits or waits[kk][1] < v:
                                waits[kk] = (s, v)
                        for kk, (s, v) in waits.items():
                            e.wait_ge(s, v)
                            waited[kk] = v
                        ins = o["fn"](e)
                        if o["dma"] is not None:
                            ins.then_inc(dsem[o["dma"]], 16)
                        elif i in sigcount[eng]:
                            ins.then_inc(csem[eng], 1)
                    if eng == "sync":
                        for k, c in self.dma_cnt.items():
                            e.wait_ge(dsem[k], 16 * c)
                return body

            block.sync(make("sync"))
            block.scalar(make("scalar"))
            block.vector(make("vector"))
            block.gpsimd(make("gpsimd"))
            block.tensor(make("tensor"))


class Ctx:
    pass


def build(stop_after="all", dbg=()):
    nc = bass.Bass("TRN2", target_bir_lowering=False)
    P = Prog(nc)
    g = Ctx()
    g.nc, g.P = nc, P
    g.din = lambda name, shape, dt=F32: nc.dram_tensor(name, list(shape), dt, kind="ExternalInput").ap()
    g.dout = lambda name, shape, dt=F32: nc.dram_tensor(name, list(shape), dt, kind="ExternalOutput").ap()
    dbgset = set(dbg)

    def dscr(name, shape, dt=F32):
        kind = "ExternalOutput" if name in dbgset else "Internal"
        return nc.dram_tensor(name, list(shape), dt, kind=kind).ap()
    g.dscr = dscr

    with ExitStack() as st:
        g.st = st
        g.sb = lambda name, shape, dt=F32: st.enter_context(nc.sbuf_tensor(name, list(shape), dt))
        g.ps = [st.enter_context(nc.psum_tensor("ps%d" % i, [128, 512], F32)) for i in range(8)]
        g.ps_i = 0

        g.ps_rot = list(range(8))

        def next_ps():
            g.ps_i = (g.ps_i + 1) % len(g.ps_rot)
            i = g.ps_rot[g.ps_i]
            return g.ps[i], "ps%d" % i
        g.next_ps = next_ps
        g.psb = lambda i: (g.ps[i], "ps%d" % i)

        g.arena = st.enter_context(nc.sbuf_tensor("arena", [128, ARENA // 4], F32))
        g.arena_off = 0

        def carve(shape, dt=F32):
            esz = 2 if dt == BF16 else 4
            free = int(np.prod(shape[1:]))
            nbytes = (free * esz + 63) // 64 * 64
            off = g.arena_off
            assert off + nbytes <= ARENA, ("arena overflow", off, nbytes)
            g.arena_off = off + nbytes
            words = (free * esz + 3) // 4
            v = g.arena[:, off // 4:off // 4 + words]
            if dt == BF16:
                v = v.bitcast(BF16)
            elif dt != F32:
                v = v.bitcast(dt)
            v = v[:shape[0], :free]
            if len(shape) == 3:
                v = v.rearrange("p (a b) -> p a b", b=shape[2])
            return v
        g.carve = carve

        def new_phase():
            g.arena_off = 0
            P.barrier()
        g.new_phase = new_phase

        declare_io(g)
        consts(g)
        for l in range(DEPTH):
            phase_mod(g, l)
        for l in range(DEPTH):
            phase_a(g, l)
            if stop_after == "a%d" % l:
                break
            phase_mla(g, l)
            if stop_after == "mla%d" % l:
                break
            if "noattn" not in dbgset:
                phase_attn(g, l)
            if stop_after == "attn%d" % l:
                break
            if "nomlstm" not in dbgset:
                phase_mlstm(g, l)
            if stop_after == "mlstm%d" % l:
                break
            hyena_seq(g, l, 4096, NCTX)
            if l == 0 and "nohyctx" not in dbgset:
                hyena_seq(g, l, 256, 0)
            if stop_after == "hy%d" % l:
                break
            phase_c(g, l)
            if stop_after == "c%d" % l:
                break
        P.emit()
    return nc


def declare_io(g):
    g.xT = g.din("xT", [D, T])
    g.cT = g.din("cT", [128, KC, 2])
    g.ada_w = g.din("ada_w", [DEPTH, D, 6 * D])
    g.ada_bT = g.din("ada_bT", [DEPTH, 128, 96])
    g.norm1_gT = g.din("norm1_gT", [DEPTH, 128, KC])
    g.norm2_gT = g.din("norm2_gT", [DEPTH, 128, KC])
    g.w_in = g.din("w_in", [DEPTH, D, N_IN])
    g.w_in_rot = g.din("w_in_rot", [DEPTH, D, 64])
    g.PT = [g.dscr("PT%d" % l, [N_IN + 64, T]) for l in range(DEPTH)]
    g.PTM = [g.dscr("PTM%d" % l, [T, 1040]) for l in range(DEPTH)]
    g.modd = g.dscr("modd", [DEPTH, 128, 96 * 2])
    g.mla_gT = g.din("mla_gT", [DEPTH, 128, 8])
    g.mla_hg = g.din("mla_hg", [DEPTH, 128, 6])
    g.rope_cs = g.din("rope_cs", [2, 64, T])
    g.w_uq = g.din("w_uq", [DEPTH, 512, 1536])
    g.w_uq_rot = g.din("w_uq_rot", [DEPTH, 512, 512])
    g.w_ukv = g.din("w_ukv", [DEPTH, 512, 2048])
    g.QT = g.dscr("QT", [8, 192, T], BF16)
    g.tri = g.din("tri", [2, 128, 128])
    g.ident = g.din("ident", [128, 128])
    g.sel4 = g.din("sel4", [4, 4, 128])
    g.ml_cw = g.din("ml_cw", [DEPTH, 128, 8, 4])
    g.ml_gate_b = g.din("ml_gate_b", [DEPTH, 16])
    g.mix_g = g.din("mix_g", [DEPTH, D])
    g.mix_gT = g.din("mix_gT", [DEPTH, 128, KC])
    g.HD = [g.dscr("HD%d" % i, [T, 512]) for i in range(2)]
    g.hy_feats = {4096: g.din("hy_feats4096", [33, 4096]), 256: g.din("hy_feats256", [33, 256])}
    g.hy_negt = {4096: g.din("hy_negt4096", [128, 32]), 256: g.din("hy_negt256", [128, 2])}
    g.hy_cs = {4096: g.din("hy_cs4096", [2, 4096, 4096], BF16), 256: g.din("hy_cs256", [2, 256, 256], BF16)}
    g.hy_alt = g.din("hy_alt", [128, 512])
    g.hy_cw = g.din("hy_cw", [DEPTH, 128, 12, 4])
    g.hy_w1 = g.din("hy_w1", [DEPTH, 33, 64])
    g.hy_b12 = g.din("hy_b12", [DEPTH, 64, 2])
    g.hy_w2 = g.din("hy_w2", [DEPTH, 64, 64])
    g.hy_w3 = g.din("hy_w3", [DEPTH, 64, 2048])
    g.hy_decay = g.din("hy_decay", [DEPTH, 2048])
    g.hy_skipT = g.din("hy_skipT", [DEPTH, 128, 2, 4])
    g.HS = g.dscr("HS", [2, 4096, 1024], BF16)
    g.KF = g.dscr("KF", [2, 4097, 1024])
    g.HV = g.dscr("HV", [1536, 4096])
    g.HZ2 = g.dscr("HZ2", [512, 4096])
    g.w_out = g.din("w_out", [DEPTH, D, D])
    g.ffn_w1 = g.din("ffn_w1", [DEPTH, D, 2 * FFN_H])
    g.ffn_w2 = g.din("ffn_w2", [DEPTH, FFN_H, D])
    g.XM = g.dscr("XM", [D, T])
    g.xs = g.dscr("xs", [D, T])
    g.outT = g.dout("outT", [D, NLAT])
    g.KT = g.dscr("KT", [8, 192, T], BF16)
    g.VTM = g.dscr("VTM", [T, 1024], BF16)
    g.CATT = g.dscr("CATT", [D, T])


def consts(g):
    P, nc = g.P, g.nc
    g.ones_bf = g.sb("ones_bf", [128, 128], BF16)
    P.op("vector", lambda e: e.memset(g.ones_bf[:], 1.0), w=["ones_bf"])
    g.eps_t = g.sb("eps_t", [128, 1], F32)
    P.op("vector", lambda e: e.memset(g.eps_t[:], EPS), w=["eps_t"])
    g.mod = [g.sb("mod%d" % l, [128, 96, 2]) for l in range(DEPTH)]
    g.gs1 = [g.sb("gs1_%d" % l, [128, KC, 2]) for l in range(DEPTH)]
    g.gs2 = [g.sb("gs2_%d" % l, [128, KC, 2]) for l in range(DEPTH)]
    g.n1g = g.sb("n1g", [128, DEPTH, KC])
    g.n2g = g.sb("n2g", [128, DEPTH, KC])
    P.dma("sync", g.n1g[:], g.norm1_gT.rearrange("l p k -> p l k"), "pc", w=["n1g"])
    P.dma("sync", g.n2g[:], g.norm2_gT.rearrange("l p k -> p l k"), "pc", w=["n2g"])
    g.scT = g.sb("scT", [128, KC, 2])
    g.cTs = g.sb("cTs", [128, KC, 2])
    P.dma("sync", g.cTs[:], g.cT[:, :, :], "pc", w=["cTs"])
    P.seal()
    P.op("scalar", lambda e: e.activation(out=g.scT[:], in_=g.cTs[:], func=AF.Silu), r=["cTs"], w=["scT"])
    g.scb = g.sb("scb", [128, KC, 2], BF16)
    P.op("vector", lambda e: e.tensor_copy(out=g.scb[:], in_=g.scT[:]), r=["scT"], w=["scb"])


def next_stage(g):
    i = g.stage_i
    g.stage_i = (i + 1) % len(g.stage)
    return g.stage[i], "stage%d" % i


def carve_common(g, wcols):
    g.wbuf = [g.carve([128, KC * wcols], BF16) for i in range(2)]
    g.w_i = 0
    g.stage = [g.carve([128, 512], F32) for i in range(4)]
    g.stage_i = 0


def phase_mod(g, l):
    P, nc = g.P, g.nc
    g.new_phase()
    carve_common(g, 512)
    adab = g.carve([128, 96], F32)
    ak = "adab"
    P.dma("sync", adab, g.ada_bT[l], "pc", w=[ak])
    P.seal()
    wv = g.ada_w[l].rearrange("(kc p) n -> p kc n", p=128)
    ps, psk = g.next_ps()
    psv = ps[:, 0:192].rearrange("p (c j) -> p c j", j=2)
    for sbk in range(24):
        wb = g.wbuf[sbk % 2]
        wk = "wbuf%d" % (sbk % 2)
        wt = wb.rearrange("p (kc n) -> p kc n", n=512)
        P.dma("gpsimd", wt, wv[:, :, sbk * 512:(sbk + 1) * 512], wk, w=[wk])
        for j in range(4):
            cb = sbk * 4 + j
            for kc in range(KC):
                P.op("tensor", lambda e, wt=wt, j=j, kc=kc, cb=cb: e.matmul(
                    psv[:, cb, :], wt[:, kc, j * 128:(j + 1) * 128], g.scb[:, kc, :],
                    start=(kc == 0), stop=(kc == KC - 1)),
                    r=[wk, "scb"], w=[psk])
    mk = "mod%d" % l
    for j in range(2):
        P.op("vector", lambda e, j=j: e.tensor_tensor(out=g.mod[l][:, :, j], in0=psv[:, :, j], in1=adab,
                                                     op=ALU.add), r=[psk, ak], w=[mk])
    for j in range(2):
        P.op("vector", lambda e, j=j: e.scalar_tensor_tensor(
            out=g.gs1[l][:, :, j], in0=g.mod[l][:, 16:32, j], scalar=1.0, in1=g.n1g[:, l, :],
            op0=ALU.add, op1=ALU.mult), r=[mk, "n1g"], w=["gs1_%d" % l])
        P.op("vector", lambda e, j=j: e.scalar_tensor_tensor(
            out=g.gs2[l][:, :, j], in0=g.mod[l][:, 64:80, j], scalar=1.0, in1=g.n2g[:, l, :],
            op0=ALU.add, op1=ALU.mult), r=[mk, "n2g"], w=["gs2_%d" % l])
    P.dma("sync", g.modd[l], g.mod[l][:].rearrange("p c j -> p (c j)"), mk, r=[mk], w=["modd"])


def rms_bufs(g):
    NT = 128
    return (g.carve([128, KC, NT], F32), g.carve([128, KC, NT], BF16), g.carve([128, NT], F32),
            [g.carve([128, NT], F32) for i in range(2)])


def rms_modulate(g, src_T, l, gs, gskey, shift_lo, dstT, dkey, tiles, jfn, bufs=None, srckey=None, dst_t0=0):
    P, nc = g.P, g.nc
    xt, sq, rstd, tmps = bufs if bufs is not None else rms_bufs(g)
    rk = [srckey] if srckey else []
    srcv = src_T.rearrange("(kc p) t -> p kc t", p=128)
    for (t0, n) in tiles:
        j = jfn(t0)
        P.dma("sync", xt[:, :, :n], srcv[:, :, t0:t0 + n], "xt0", r=rk, w=["xt0"])
        P.op("scalar", lambda e, n=n: e.activation(out=sq[:, :, :n], in_=xt[:, :, :n], func=AF.Square),
             r=["xt0"], w=["sq"])
        ps, psk = g.next_ps()
        for kc in range(KC):
            P.op("tensor", lambda e, kc=kc, n=n, ps=ps: e.matmul(ps[:, :n], g.ones_bf[:], sq[:, kc, :n],
                                                              start=(kc == 0), stop=(kc == KC - 1)),
                 r=["sq", "ones_bf"], w=[psk])
        P.op("scalar", lambda e, n=n, ps=ps: e.activation(out=rstd[:, :n], in_=ps[:, :n], func=AF.Sqrt,
                                                       bias=g.eps_t[:, 0:1], scale=1.0 / D),
             r=[psk, "eps_t"], w=["rstd"])
        P.op("vector", lambda e, n=n: e.reciprocal(out=rstd[:, :n], in_=rstd[:, :n]), r=["rstd"], w=["rstd"])
        for kc in range(KC):
            tmp = tmps[kc % 2]
            tk = "tmp%d" % (kc % 2)
            P.op("vector", lambda e, kc=kc, n=n, tmp=tmp, j=j: e.scalar_tensor_tensor(
                out=tmp[:, :n], in0=xt[:, kc, :n], scalar=gs[:, kc, j:j + 1], in1=rstd[:, :n],
                op0=ALU.mult, op1=ALU.mult), r=["xt0", "rstd", gskey], w=[tk])
            P.op("scalar", lambda e, kc=kc, n=n, tmp=tmp, j=j, t0=t0: e.activation(
                out=dstT[:, kc, t0 - dst_t0:t0 - dst_t0 + n], in_=tmp[:, :n], func=AF.Identity,
                bias=g.mod[l][:, shift_lo + kc, j:j + 1], scale=1.0), r=[tk, "mod%d" % l], w=[dkey])


def proj_fm(g, w_view, col0, ncols, actT, akey, tiles, evac):
    P = g.P
    nkc = w_view.shape[1]
    wi = g.w_i
    g.w_i = (wi + 1) % 2
    wk = "wbuf%d" % wi
    wt = g.wbuf[wi][:, :nkc * ncols].rearrange("p (kc n) -> p kc n", n=ncols)
    P.dma("gpsimd", wt, w_view[:, :, col0:col0 + ncols], wk, w=[wk])
    for c0 in range(0, ncols, 128):
        m = min(128, ncols - c0)
        for (t0, n) in tiles:
            ps, psk = g.next_ps()
            for kc in range(nkc):
                P.op("tensor", lambda e, kc=kc, c0=c0, m=m, t0=t0, n=n, ps=ps: e.matmul(
                    ps[:m, :n], wt[:, kc, c0:c0 + m], actT[:, kc, t0:t0 + n],
                    start=(kc == 0), stop=(kc == nkc - 1)), r=[wk, akey], w=[psk])
            evac(col0 + c0, m, t0, n, ps, psk)


def phase_a(g, l):
    P, nc = g.P, g.nc
    g.new_phase()
    xnT = g.carve([128, KC, T], BF16)
    carve_common(g, 256)
    src = g.xT if l == 0 else g.xs
    rms_modulate(g, src, l, g.gs1[l], "gs1_%d" % l, 0, xnT, "xnT", [(128 * i, 128) for i in range(34)],
                 lambda t0: 1 if t0 < NCTX else 0, srckey="xs")
    wv = g.w_in[l].rearrange("(kc p) n -> p kc n", p=128)
    wrv = g.w_in_rot[l].rearrange("(kc p) n -> p kc n", p=128)
    PT = g.PT[l]
    cnt = [0]

    def evac_to(PTrow0):
        def evac(cb, m, t0, n, ps, psk):
            stg, sk = next_stage(g)
            if cnt[0] % 2 == 0:
                P.op("vector", lambda e: e.tensor_copy(out=stg[:m, :n], in_=ps[:m, :n]), r=[psk], w=[sk])
            else:
                P.op("scalar", lambda e: e.copy(out=stg[:m, :n], in_=ps[:m, :n]), r=[psk], w=[sk])
            cnt[0] += 1
            P.dma("sync", PT[PTrow0 + cb:PTrow0 + cb + m, t0:t0 + n], stg[:m, :n], sk, r=[sk], w=["PT%d" % l])
        return evac

    blocks = [(c, 256) for c in range(0, 1024, 256)] + [(1024, 64)] + [(c, 256) for c in range(1088, 3648, 256)]
    for (c0, ncol) in blocks:
        proj_fm(g, wv, c0, ncol, xnT, "xnT", TILES, evac_to(0))
    proj_fm(g, wrv, 0, 64, xnT, "xnT", TILES, evac_to(N_IN))
    for (c0, ncol) in [(3648, 256), (3904, 256), (4160, 256), (4416, 256), (4672, 16)]:
        wi = g.w_i
        g.w_i = (wi + 1) % 2
        wk = "wbuf%d" % wi
        wt = g.wbuf[wi][:, :KC * ncol].rearrange("p (kc n) -> p kc n", n=ncol)
        P.dma("gpsimd", wt, wv[:, :, c0:c0 + ncol], wk, w=[wk])
        for c in range(34):
            ps, psk = g.next_ps()
            for kc in range(KC):
                P.op("tensor", lambda e, kc=kc, c=c, ps=ps, wt=wt, ncol=ncol: e.matmul(
                    ps[:, :ncol], xnT[:, kc, c * 128:(c + 1) * 128], wt[:, kc, :],
                    start=(kc == 0), stop=(kc == KC - 1)), r=[wk, "xnT"], w=[psk])
            stg, sk = next_stage(g)
            if c % 2 == 0:
                P.op("vector", lambda e, ps=ps, stg=stg, ncol=ncol: e.tensor_copy(out=stg[:, :ncol], in_=ps[:, :ncol]),
                     r=[psk], w=[sk])
            else:
                P.op("scalar", lambda e, ps=ps, stg=stg, ncol=ncol: e.copy(out=stg[:, :ncol], in_=ps[:, :ncol]),
                     r=[psk], w=[sk])
            P.dma("sync", g.PTM[l][c * 128:(c + 1) * 128, c0 - 3648:c0 - 3648 + ncol], stg[:, :ncol], sk,
                  r=[sk], w=["PTM%d" % l])


def rstd_op(g, ps, psk, n, nfeat, rstd, rkey, rows=128):
    P = g.P
    P.op("scalar", lambda e: e.activation(out=rstd[:rows, :n], in_=ps[:rows, :n], func=AF.Sqrt,
                                          bias=g.eps_t[:rows, 0:1], scale=1.0 / nfeat), r=[psk, "eps_t"], w=[rkey])
    P.op("vector", lambda e: e.reciprocal(out=rstd[:rows, :n], in_=rstd[:rows, :n]), r=[rkey], w=[rkey])


def phase_mla(g, l):
    P = g.P
    g.new_phase()
    g.ps_rot = list(range(8))
    PT = g.PT[l]
    cqn = g.carve([128, 4, T], BF16)
    ckvn = g.carve([128, 4, T], BF16)
    cos2 = g.carve([64, T], F32)
    sin2 = g.carve([64, T], F32)
    KR = g.carve([64, T], F32)
    sqr = g.carve([64, T], BF16)
    xt = g.carve([128, 4, 512], F32)
    sq = g.carve([128, 4, 512], BF16)
    rstd = g.carve([128, 512], F32)
    gT = g.carve([128, 8], F32)
    hg = g.carve([128, 6], F32)
    wq = g.carve([128, 4, 256], BF16)
    wk = g.carve([128, 4, 128], BF16)
    wvv = g.carve([128, 4, 1024], BF16)
    stb = [g.carve([128, 512], BF16) for i in range(4)]
    tA = g.carve([64, 512], F32)
    tB = g.carve([64, 512], F32)
    P.dma("sync", gT, g.mla_gT[l], "pc", w=["gT"])
    P.dma("sync", hg, g.mla_hg[l], "pc", w=["hg"])
    P.dma("sync", cos2, g.rope_cs[0], "pc", w=["cos2"])
    P.dma("sync", sin2, g.rope_cs[1], "pc", w=["sin2"])
    P.seal()
    sc = 192.0 ** -0.5
    for c in (0, 2, 3):
        P.op("vector", lambda e, c=c: e.tensor_scalar(out=hg[:, c:c + 1], in0=hg[:, c:c + 1], scalar1=sc, scalar2=None,
                                                      op0=ALU.mult), r=["hg"], w=["hg"])
    for (r0, dst, dk, gc) in [(0, cqn, "cqn", 0), (512, ckvn, "ckvn", 4)]:
        srcv = PT[r0:r0 + 512, :].rearrange("(kc p) t -> p kc t", p=128)
        for (t0, n) in TILES:
            P.dma("sync", xt[:, :, :n], srcv[:, :, t0:t0 + n], "xt0", r=["PT%d" % l], w=["xt0"])
            P.op("scalar", lambda e, n=n: e.activation(out=sq[:, :, :n], in_=xt[:, :, :n], func=AF.Square),
                 r=["xt0"], w=["sq"])
            ps, psk = g.next_ps()
            for kc in range(4):
                P.op("tensor", lambda e, kc=kc, n=n, ps=ps: e.matmul(ps[:, :n], g.ones_bf[:], sq[:, kc, :n],
                                                                  start=(kc == 0), stop=(kc == 3)),
                     r=["sq", "ones_bf"], w=[psk])
            rstd_op(g, ps, psk, n, 512, rstd, "rstd")
            for kc in range(4):
                P.op("vector", lambda e, kc=kc, n=n, t0=t0, dst=dst, gc=gc: e.scalar_tensor_tensor(
                    out=dst[:, kc, t0:t0 + n], in0=xt[:, kc, :n], scalar=gT[:, gc + kc:gc + kc + 1], in1=rstd[:, :n],
                    op0=ALU.mult, op1=ALU.mult), r=["xt0", "rstd", "gT"], w=[dk])
    for (t0, n) in TILES:
        P.dma("sync", xt[:64, 0, :n], PT[1024:1088, t0:t0 + n], "xt0", r=["PT%d" % l], w=["xt0"])
        P.dma("sync", xt[:64, 1, :n], PT[N_IN:N_IN + 64, t0:t0 + n], "xt0", r=["PT%d" % l], w=["xt0"])
        P.op("scalar", lambda e, n=n, t0=t0: e.activation(out=sqr[:, t0:t0 + n], in_=xt[:64, 0, :n], func=AF.Square),
             r=["xt0"], w=["sqr"])
        P.op("vector", lambda e, n=n, t0=t0: e.scalar_tensor_tensor(
            out=KR[:, t0:t0 + n], in0=xt[:64, 0, :n], scalar=hg[:64, 4:5], in1=cos2[:, t0:t0 + n],
            op0=ALU.mult, op1=ALU.mult), r=["xt0", "hg", "cos2"], w=["KR"])
        P.op("vector", lambda e, n=n, t0=t0: e.scalar_tensor_tensor(
            out=tA[:, :n], in0=xt[:64, 1, :n], scalar=hg[:64, 5:6], in1=sin2[:, t0:t0 + n],
            op0=ALU.mult, op1=ALU.mult), r=["xt0", "hg", "sin2"], w=["tA"])
        P.op("vector", lambda e, n=n, t0=t0: e.tensor_tensor(out=KR[:, t0:t0 + n], in0=KR[:, t0:t0 + n], in1=tA[:, :n],
                                                           op=ALU.add), r=["tA", "KR"], w=["KR"])
    wuq = g.w_uq[l].rearrange("(kc p) n -> p kc n", p=128)
    wuqr = g.w_uq_rot[l].rearrange("(kc p) n -> p kc n", p=128)
    wukv = g.w_ukv[l].rearrange("(kc p) n -> p kc n", p=128)
    qtiles = TILES if l == 0 else TILES[1:]
    si = [0]

    def nstb():
        si[0] = (si[0] + 1) % 4
        return stb[si[0]], "stb%d" % si[0]

    for h in range(8):
        P.dma("gpsimd", wq[:, :, 0:192], wuq[:, :, h * 192:(h + 1) * 192], "wq", w=["wq"])
        P.dma("gpsimd", wq[:, :, 192:256], wuqr[:, :, h * 64:(h + 1) * 64], "wq", w=["wq"])
        for (t0, n) in qtiles:
            psn, kn_ = g.next_ps()
            psr, kr_ = g.next_ps()
            pst, kt_ = g.next_ps()
            for (ps, pk, c0, m) in [(psn, kn_, 0, 128), (psr, kr_, 128, 64), (pst, kt_, 192, 64)]:
                for kc in range(4):
                    P.op("tensor", lambda e, ps=ps, c0=c0, m=m, kc=kc, t0=t0, n=n: e.matmul(
                        ps[:m, :n], wq[:, kc, c0:c0 + m], cqn[:, kc, t0:t0 + n], start=(kc == 0), stop=(kc == 3)),
                        r=["wq", "cqn"], w=[pk])
            P.op("scalar", lambda e, n=n, psn=psn: e.activation(out=sq[:, 0, :n], in_=psn[:, :n], func=AF.Square),
                 r=[kn_], w=["sq"])
            P.op("scalar", lambda e, n=n, psr=psr: e.activation(out=sq[:64, 1, :n], in_=psr[:64, :n], func=AF.Square),
                 r=[kr_], w=["sq"])
            pss, ks_ = g.next_ps()
            P.op("tensor", lambda e, n=n, pss=pss: e.matmul(pss[:, :n], g.ones_bf[:], sq[:, 0, :n], start=True, stop=False),
                 r=["sq", "ones_bf"], w=[ks_])
            P.op("tensor", lambda e, n=n, pss=pss: e.matmul(pss[:, :n], g.ones_bf[:64, :], sq[:64, 1, :n], start=False, stop=True),
                 r=["sq", "ones_bf"], w=[ks_])
            rstd_op(g, pss, ks_, n, 192, rstd, "rstd")
            o1, o1k = nstb()
            P.op("vector", lambda e, n=n, psn=psn, o1=o1: e.scalar_tensor_tensor(
                out=o1[:, :n], in0=psn[:, :n], scalar=hg[:, 0:1], in1=rstd[:, :n], op0=ALU.mult, op1=ALU.mult),
                r=[kn_, "hg", "rstd"], w=[o1k])
            P.dma("sync", g.QT[h, 0:128, t0:t0 + n], o1[:, :n], o1k, r=[o1k], w=["QT"])
            P.op("vector", lambda e, n=n, psr=psr, t0=t0: e.scalar_tensor_tensor(
                out=tA[:, :n], in0=psr[:64, :n], scalar=hg[:64, 2:3], in1=cos2[:, t0:t0 + n], op0=ALU.mult, op1=ALU.mult),
                r=[kr_, "hg", "cos2"], w=["tA"])
            P.op("vector", lambda e, n=n, pst=pst, t0=t0: e.scalar_tensor_tensor(
                out=tB[:, :n], in0=pst[:64, :n], scalar=hg[:64, 3:4], in1=sin2[:, t0:t0 + n], op0=ALU.mult, op1=ALU.mult),
                r=[kt_, "hg", "sin2"], w=["tB"])
            P.op("vector", lambda e, n=n: e.tensor_tensor(out=tA[:, :n], in0=tA[:, :n], in1=tB[:, :n], op=ALU.add),
                 r=["tA", "tB"], w=["tA"])
            o2, o2k = nstb()
            P.op("vector", lambda e, n=n, o2=o2: e.tensor_tensor(out=o2[:64, :n], in0=tA[:, :n], in1=rstd[:64, :n],
                                                              op=ALU.mult), r=["tA", "rstd"], w=[o2k])
            P.dma("sync", g.QT[h, 128:192, t0:t0 + n], o2[:64, :n], o2k, r=[o2k], w=["QT"])
    for h in range(8):
        P.dma("gpsimd", wk[:, :, :], wukv[:, :, h * 256:h * 256 + 128], "wk", w=["wk"])
        for (t0, n) in TILES:
            psn, kn_ = g.next_ps()
            for kc in range(4):
                P.op("tensor", lambda e, psn=psn, kc=kc, t0=t0, n=n: e.matmul(
                    psn[:, :n], wk[:, kc, :], ckvn[:, kc, t0:t0 + n], start=(kc == 0), stop=(kc == 3)),
                    r=["wk", "ckvn"], w=[kn_])
            P.op("scalar", lambda e, n=n, psn=psn: e.activation(out=sq[:, 0, :n], in_=psn[:, :n], func=AF.Square),
                 r=[kn_], w=["sq"])
            pss, ks_ = g.next_ps()
            P.op("tensor", lambda e, n=n, pss=pss: e.matmul(pss[:, :n], g.ones_bf[:], sq[:, 0, :n], start=True, stop=False),
                 r=["sq", "ones_bf"], w=[ks_])
            P.op("tensor", lambda e, n=n, pss=pss, t0=t0: e.matmul(pss[:, :n], g.ones_bf[:64, :], sqr[:, t0:t0 + n],
                                                                 start=False, stop=True),
                 r=["sqr", "ones_bf"], w=[ks_])
            rstd_op(g, pss, ks_, n, 192, rstd, "rstd")
            o1, o1k = nstb()
            P.op("vector", lambda e, n=n, psn=psn, o1=o1: e.scalar_tensor_tensor(
                out=o1[:, :n], in0=psn[:, :n], scalar=hg[:, 1:2], in1=rstd[:, :n], op0=ALU.mult, op1=ALU.mult),
                r=[kn_, "hg", "rstd"], w=[o1k])
            P.dma("sync", g.KT[h, 0:128, t0:t0 + n], o1[:, :n], o1k, r=[o1k], w=["KT"])
            o2, o2k = nstb()
            P.op("vector", lambda e, n=n, o2=o2, t0=t0: e.tensor_tensor(out=o2[:64, :n], in0=KR[:, t0:t0 + n],
                                                                     in1=rstd[:64, :n], op=ALU.mult),
                 r=["KR", "rstd"], w=[o2k])
            P.dma("sync", g.KT[h, 128:192, t0:t0 + n], o2[:64, :n], o2k, r=[o2k], w=["KT"])
    for kc in range(4):
        P.dma("gpsimd", wvv[:, kc, :].rearrange("p (h x) -> p h x", x=128),
              g.w_ukv[l][kc * 128:(kc + 1) * 128, :].rearrange("p (h x) -> p h x", x=256)[:, :, 128:256], "wvv", w=["wvv"])
    for c in range(34):
        for half in range(2):
            ps, psk = g.next_ps()
            for kc in range(4):
                P.op("tensor", lambda e, ps=ps, kc=kc, c=c, half=half: e.matmul(
                    ps[:, :], ckvn[:, kc, c * 128:(c + 1) * 128], wvv[:, kc, half * 512:(half + 1) * 512],
                    start=(kc == 0), stop=(kc == 3)), r=["wvv", "ckvn"], w=[psk])
            o1, o1k = nstb()
            if half == 0:
                P.op("vector", lambda e, ps=ps, o1=o1: e.tensor_copy(out=o1[:, :], in_=ps[:, :]), r=[psk], w=[o1k])
            else:
                P.op("scalar", lambda e, ps=ps, o1=o1: e.copy(out=o1[:, :], in_=ps[:, :]), r=[psk], w=[o1k])
            P.dma("sync", g.VTM[c * 128:(c + 1) * 128, half * 512:(half + 1) * 512], o1[:, :], o1k, r=[o1k], w=["VTM"])


def phase_attn(g, l):
    P = g.P
    g.new_phase()
    g.ps_rot = [0, 1, 2, 3]
    kn = g.carve([128, T], BF16)
    kr = g.carve([64, T], BF16)
    qn = g.carve([128, T], BF16)
    qr = g.carve([64, T], BF16)
    vh = g.carve([128, 34, 128], BF16)
    ebuf = [g.carve([128, 512], BF16) for i in range(4)]
    rden = g.carve([128, 512], F32)
    ost = [g.carve([128, 512], F32) for i in range(2)]
    qtiles = [(t0, n, 34) for (t0, n) in TILES[1:]]
    if l == 0:
        qtiles = [(0, 256, 2)] + qtiles
    ti = 0
    for h in range(8):
        P.dma("sync", kn, g.KT[h, 0:128, :], "kn", r=["KT"], w=["kn"])
        P.dma("sync", kr, g.KT[h, 128:192, :], "kr", r=["KT"], w=["kr"])
        P.dma("sync", qn, g.QT[h, 0:128, :], "qn", r=["QT"], w=["qn"])
        P.dma("sync", qr, g.QT[h, 128:192, :], "qr", r=["QT"], w=["qr"])
        P.dma("sync", vh, g.VTM[:, h * 128:(h + 1) * 128].rearrange("(c p) x -> p c x", p=128), "vh", r=["VTM"], w=["vh"])
        for (t0, n, nch) in qtiles:
            po, pok = g.psb(4 + 2 * (ti % 2))
            pd, pdk = g.psb(5 + 2 * (ti % 2))
            ti += 1

            def s_mm(c, t0=t0, n=n):
                ps, psk = g.next_ps()
                P.op("tensor", lambda e: e.matmul(ps[:, :n], kn[:, c * 128:(c + 1) * 128], qn[:, t0:t0 + n],
                                                  start=True, stop=False), r=["kn", "qn"], w=[psk])
                P.op("tensor", lambda e: e.matmul(ps[:, :n], kr[:, c * 128:(c + 1) * 128], qr[:, t0:t0 + n],
                                                  start=False, stop=True), r=["kr", "qr"], w=[psk])
                return ps, psk

            cur = s_mm(0)
            for c in range(nch):
                nxt = s_mm(c + 1) if c + 1 < nch else None
                ps, psk = cur
                eb = ebuf[c % 4]
                ek = "ebuf%d" % (c % 4)
                P.op("scalar", lambda e, ps=ps, eb=eb, n=n: e.activation(out=eb[:, :n], in_=ps[:, :n], func=AF.Exp),
                     r=[psk], w=[ek])
                P.op("tensor", lambda e, eb=eb, c=c, n=n, po=po, nch=nch: e.matmul(
                    po[:, :n], vh[:, c, :], eb[:, :n], start=(c == 0), stop=(c == nch - 1)), r=["vh", ek], w=[pok])
                P.op("tensor", lambda e, eb=eb, c=c, n=n, pd=pd, nch=nch: e.matmul(
                    pd[:, :n], g.ones_bf[:], eb[:, :n], start=(c == 0), stop=(c == nch - 1)), r=["ones_bf", ek], w=[pdk])
                cur = nxt
            P.op("vector", lambda e, n=n, pd=pd: e.reciprocal(out=rden[:, :n], in_=pd[:, :n]), r=[pdk], w=["rden"])
            os_ = ost[ti % 2]
            ok = "ost%d" % (ti % 2)
            P.op("vector", lambda e, n=n, po=po, os_=os_: e.tensor_tensor(out=os_[:, :n], in0=po[:, :n], in1=rden[:, :n],
                                                                       op=ALU.mult), r=[pok, "rden"], w=[ok])
            P.dma("sync", g.CATT[h * 128:(h + 1) * 128, t0:t0 + n], os_[:, :n], ok, r=[ok], w=["CATT"])


def phase_mlstm(g, l):
    P = g.P
    g.new_phase()
    g.ps_rot = list(range(8))
    PT, PTM = g.PT[l], g.PTM[l]
    NCH = 34
    order = [list(range(NCH)), [1, 0] + list(range(NCH - 1, 1, -1))]
    tri = g.carve([128, 2, 128], F32)
    ident = g.carve([128, 128], F32)
    identb = g.carve([128, 128], BF16)
    sel4 = g.carve([4, 4, 128], F32)
    ones4 = g.carve([4, 128], F32)
    one_t = g.carve([128, 1], F32)
    onesf = g.carve([128, 128], F32)
    P.op("vector", lambda e: e.memset(onesf, 1.0), w=["onesf"])
    pers = [[g.carve([128, NCH, 4], F32) for i in range(6)] for dr in range(2)]
    mark1 = g.arena_off
    P.dma("sync", tri, g.tri.rearrange("d s t -> s d t"), "pc", w=["tri"])
    P.dma("sync", ident, g.ident[:, :], "pc", w=["ident"])
    P.dma("gpsimd", identb, g.ident[:, :], "pc", w=["identb"])
    P.dma("sync", sel4, g.sel4[:, :, :], "pc", w=["sel4"])
    P.seal()
    P.op("vector", lambda e: e.memset(ones4, 1.0), w=["ones4"])
    P.op("vector", lambda e: e.memset(one_t, 1.0), w=["one_t"])
    Gt = g.carve([128, NCH, 16], F32)
    gb = g.carve([128, 16], F32)
    P.dma("sync", Gt, PTM[:, 1024:1040].rearrange("(c p) x -> p c x", p=128), "pc", r=["PTM%d" % l], w=["Gt"])
    P.dma("sync", gb, g.ml_gate_b[l:l + 1, :].to_broadcast([128, 16]), "pc", w=["gb"])
    P.seal()
    P.op("vector", lambda e: e.tensor_tensor(out=Gt, in0=Gt, in1=gb.unsqueeze(1).to_broadcast([128, NCH, 16]), op=ALU.add),
         r=["Gt", "gb"], w=["Gt"])
    t1 = g.carve([128, NCH, 8], F32)
    t2 = g.carve([128, NCH, 8], F32)
    lf = g.carve([128, NCH, 8], F32)
    xf = Gt[:, :, 8:16]
    P.op("vector", lambda e: e.scalar_tensor_tensor(out=t1, in0=xf, scalar=-1.0, in1=xf, op0=ALU.mult, op1=ALU.max),
         r=["Gt"], w=["t1"])
    P.op("scalar", lambda e: e.activation(out=t1, in_=t1, func=AF.Exp, scale=-1.0), r=["t1"], w=["t1"])
    P.op("scalar", lambda e: e.activation(out=t1, in_=t1, func=AF.Ln, bias=one_t[:, 0:1], scale=1.0), r=["t1", "one_t"], w=["t1"])
    P.op("vector", lambda e: e.tensor_scalar(out=t2, in0=xf, scalar1=-1.0, scalar2=0.0, op0=ALU.mult, op1=ALU.max),
         r=["Gt"], w=["t2"])
    P.op("vector", lambda e: e.scalar_tensor_tensor(out=lf, in0=t2, scalar=-1.0, in1=t1, op0=ALU.mult, op1=ALU.subtract),
         r=["t1", "t2"], w=["lf"])
    lfd, lid, bd, ud, wtok, iwtok, emt, abc, Mtok = [], [], [], [], [], [], [], [], []
    negMR = []
    for dr in range(2):
        k = "d%d" % dr
        lf_ = g.carve([128, NCH, 4], F32)
        li_ = g.carve([128, NCH, 4], F32)
        b_ = g.carve([128, NCH, 4], F32)
        u_ = pers[dr][0]
        P.op("vector", lambda e, lf_=lf_, dr=dr: e.tensor_copy(out=lf_, in_=lf[:, :, dr * 4:(dr + 1) * 4]), r=["lf"], w=["lf" + k])
        P.op("vector", lambda e, li_=li_, dr=dr: e.tensor_copy(out=li_, in_=Gt[:, :, dr * 4:(dr + 1) * 4]), r=["Gt"], w=["li" + k])
        ps, psk = g.next_ps()
        P.op("tensor", lambda e, ps=ps, dr=dr, lf_=lf_: e.matmul(ps[:, :NCH * 4], tri[:, dr, :], lf_.rearrange("p c h -> p (c h)"),
                                                              start=True, stop=True), r=["tri", "lf" + k], w=[psk])
        P.op("vector", lambda e, ps=ps, b_=b_: e.tensor_copy(out=b_.rearrange("p c h -> p (c h)"), in_=ps[:, :NCH * 4]),
             r=[psk], w=["b" + k])
        P.op("vector", lambda e, u_=u_, li_=li_, b_=b_: e.tensor_tensor(out=u_, in0=li_, in1=b_, op=ALU.subtract),
             r=["li" + k, "b" + k], w=["u" + k])
        uR = g.carve([4, NCH, 128], F32)
        lfR = g.carve([4, NCH, 128], F32)
        for (src, sk_, dstR, dk_) in [(u_, "u" + k, uR, "uR" + k), (lf_, "lf" + k, lfR, "lfR" + k)]:
            for c4 in range(0, NCH, 4):
                ps, psk = g.next_ps()
                nn = min(4, NCH - c4)
                for j in range(nn):
                    P.op("tensor", lambda e, ps=ps, src=src, c4=c4, j=j: e.transpose(ps[:4, j * 128:(j + 1) * 128],
                                                                                  src[:, c4 + j, :], ident),
                         r=[sk_, "ident"], w=[psk])
                P.op("vector", lambda e, ps=ps, dstR=dstR, c4=c4, nn=nn: e.tensor_copy(
                    out=dstR[:, c4:c4 + nn, :].rearrange("p c t -> p (c t)"), in_=ps[:4, :nn * 128]), r=[psk], w=[dk_])
        blast = g.carve([4, NCH], F32)
        cml = g.carve([4, NCH], F32)
        P.op("vector", lambda e, blast=blast, lfR=lfR: e.reduce_sum(out=blast, in_=lfR, axis=AX.X), r=["lfR" + k], w=["blast" + k])
        P.op("vector", lambda e, cml=cml, uR=uR: e.reduce_max(out=cml, in_=uR, axis=AX.X), r=["uR" + k], w=["cml" + k])
        cmB = g.carve([4, NCH, 128], F32)
        bufs = [(uR, "uR" + k), (cmB, "cmB" + k)]
        cur = 0
        for sh in (1, 2, 4, 8, 16, 32, 64):
            (sa, sak), (da, dak) = bufs[cur], bufs[1 - cur]
            if dr == 0:
                P.op("vector", lambda e, sa=sa, da=da, sh=sh: e.tensor_tensor(out=da[:, :, sh:], in0=sa[:, :, sh:], in1=sa[:, :, :128 - sh],
                                                                            op=ALU.max), r=[sak], w=[dak])
                P.op("gpsimd", lambda e, sa=sa, da=da, sh=sh: e.tensor_copy(out=da[:, :, :sh], in_=sa[:, :, :sh]), r=[sak], w=[dak])
            else:
                P.op("vector", lambda e, sa=sa, da=da, sh=sh: e.tensor_tensor(out=da[:, :, :128 - sh], in0=sa[:, :, :128 - sh], in1=sa[:, :, sh:],
                                                                            op=ALU.max), r=[sak], w=[dak])
                P.op("gpsimd", lambda e, sa=sa, da=da, sh=sh: e.tensor_copy(out=da[:, :, 128 - sh:], in_=sa[:, :, 128 - sh:]), r=[sak], w=[dak])
            cur = 1 - cur
        cm, cmk = bufs[cur]
        mR = g.carve([4, NCH], F32)
        P.op("vector", lambda e, mR=mR: e.memset(mR, 0.0), w=["mR" + k])
        od = order[dr]
        for i in range(NCH - 1):
            c, c2 = od[i], od[i + 1]
            P.op("vector", lambda e, mR=mR, c=c, c2=c2, cml=cml, blast=blast: e.scalar_tensor_tensor(
                out=mR[:, c2:c2 + 1], in0=mR[:, c:c + 1], scalar=cml[:, c:c + 1], in1=blast[:, c:c + 1],
                op0=ALU.max, op1=ALU.add), r=["mR" + k, "cml" + k, "blast" + k], w=["mR" + k])
        MR = bufs[1 - cur][0]
        MRk = bufs[1 - cur][1]
        P.op("vector", lambda e, MR=MR, cm=cm, mR=mR: e.tensor_tensor(out=MR, in0=cm, in1=mR.unsqueeze(2).to_broadcast([4, NCH, 128]),
                                                                    op=ALU.max), r=[cmk, "mR" + k], w=[MRk])
        MlR = g.carve([4, NCH], F32)
        aR = g.carve([4, NCH], F32)
        P.op("vector", lambda e, MlR=MlR, mR=mR, cml=cml: e.tensor_tensor(out=MlR, in0=mR, in1=cml, op=ALU.max),
             r=["mR" + k, "cml" + k], w=["MlR" + k])
        P.op("vector", lambda e, aR=aR, mR=mR, MlR=MlR: e.tensor_tensor(out=aR, in0=mR, in1=MlR, op=ALU.subtract),
             r=["mR" + k, "MlR" + k], w=["aR" + k])
        P.op("scalar", lambda e, aR=aR: e.activation(out=aR, in_=aR, func=AF.Exp), r=["aR" + k], w=["aR" + k])
        Mt = pers[dr][1]
        for c4 in range(0, NCH, 32):
            ps, psk = g.next_ps()
            nn = min(32, NCH - c4)
            for j in range(nn):
                P.op("tensor", lambda e, ps=ps, MR=MR, c4=c4, j=j: e.transpose(ps[:, j * 4:(j + 1) * 4], MR[:, c4 + j, :], ident[:4, :4]),
                     r=[MRk, "ident"], w=[psk])
            P.op("vector", lambda e, ps=ps, Mt=Mt, c4=c4, nn=nn: e.tensor_copy(
                out=Mt[:, c4:c4 + nn, :].rearrange("p c h -> p (c h)"), in_=ps[:, :nn * 4]), r=[psk], w=["Mt" + k])
        bc = []
        dg = g.carve([4, NCH, 4], F32)
        for (src, sk_) in [(MlR, "MlR" + k), (aR, "aR" + k), (mR, "mR" + k)]:
            P.op("vector", lambda e, src=src: e.tensor_tensor(out=dg, in0=src.unsqueeze(2).to_broadcast([4, NCH, 4]),
                                                             in1=ident[:4, :4].unsqueeze(1).to_broadcast([4, NCH, 4]), op=ALU.mult),
                 r=[sk_, "ident"], w=["dg" + k])
            ps, psk = g.next_ps()
            P.op("tensor", lambda e, ps=ps: e.matmul(ps[:, :NCH * 4], ones4, dg.rearrange("p c h -> p (c h)"), start=True, stop=True),
                 r=["dg" + k, "ones4"], w=[psk])
            o_ = pers[dr][5] if src is aR else g.carve([128, NCH, 4], F32)
            P.op("vector", lambda e, ps=ps, o_=o_: e.tensor_copy(out=o_.rearrange("p c h -> p (c h)"), in_=ps[:, :NCH * 4]),
                 r=[psk], w=["bc" + k + sk_])
            bc.append((o_, "bc" + k + sk_))
        (Mlb, Mlbk), (ab, abk), (mb, mbk) = bc
        w_, iw_, em_ = pers[dr][2], pers[dr][3], pers[dr][4]
        P.op("vector", lambda e, w_=w_, u_=u_, Mlb=Mlb: e.tensor_tensor(out=w_, in0=u_, in1=Mlb, op=ALU.subtract), r=["u" + k, Mlbk], w=["w" + k])
        P.op("scalar", lambda e, w_=w_: e.activation(out=w_, in_=w_, func=AF.Exp), r=["w" + k], w=["w" + k])
        P.op("vector", lambda e, iw_=iw_, mb=mb, Mt=Mt: e.tensor_tensor(out=iw_, in0=mb, in1=Mt, op=ALU.subtract), r=[mbk, "Mt" + k], w=["iw" + k])
        P.op("scalar", lambda e, iw_=iw_: e.activation(out=iw_, in_=iw_, func=AF.Exp), r=["iw" + k], w=["iw" + k])
        P.op("vector", lambda e, em_=em_, b_=b_, Mt=Mt: e.tensor_tensor(out=em_, in0=b_, in1=Mt, op=ALU.add), r=["b" + k, "Mt" + k], w=["em" + k])
        P.op("scalar", lambda e, em_=em_: e.activation(out=em_, in_=em_, func=AF.Exp, scale=-1.0), r=["em" + k], w=["em" + k])
        P.op("vector", lambda e, Mt=Mt: e.tensor_scalar(out=Mt, in0=Mt, scalar1=-1.0, scalar2=None, op0=ALU.mult),
             r=["Mt" + k, "iw" + k, "em" + k], w=["Mt" + k])
        ud.append(u_); wtok.append(w_); iwtok.append(iw_); emt.append(em_); abc.append(ab); Mtok.append(Mt)
    g.arena_off = mark1
    P.barrier()
    qT = g.carve([128, 4, T], BF16)
    kT = g.carve([128, 4, T], BF16)
    ktok = g.carve([128, NCH, 512], BF16)
    v1 = g.carve([128, NCH * 4, 132], BF16)
    mark2 = g.arena_off
    cw = g.carve([128, 8, 4], F32)
    P.dma("sync", cw, g.ml_cw[l], "pc", w=["cw"])
    P.seal()
    W = T + 3
    xb = g.carve([128, W], F32)
    acc = g.carve([128, W], F32)
    P.op("vector", lambda e: e.memset(xb, 0.0), w=["xb"])
    for rc in range(8):
        r0 = 2624 + rc * 128
        P.dma("sync", xb[:, 1:1 + NCTX], PT[r0:r0 + 128, 0:NCTX], "xb", r=["PT%d" % l], w=["xb"])
        P.dma("sync", xb[:, 2 + NCTX:2 + T], PT[r0:r0 + 128, NCTX:T], "xb", r=["PT%d" % l], w=["xb"])
        n = T + 1
        P.op("vector", lambda e, rc=rc: e.tensor_scalar(out=acc[:, 1:1 + n], in0=xb[:, 0:n], scalar1=cw[:, rc, 0:1],
                                                       scalar2=cw[:, rc, 3:4], op0=ALU.mult, op1=ALU.add),
             r=["xb", "cw"], w=["acc"])
        for j in (1, 2):
            P.op("vector", lambda e, rc=rc, j=j: e.scalar_tensor_tensor(
                out=acc[:, 1:1 + n], in0=xb[:, j:j + n], scalar=cw[:, rc, j:j + 1], in1=acc[:, 1:1 + n],
                op0=ALU.mult, op1=ALU.add), r=["xb", "cw", "acc"], w=["acc"])
        dst = qT if rc < 4 else kT
        dk = "qT" if rc < 4 else "kT"
        for (a0, d0, n2) in [(1, 0, NCTX), (2 + NCTX, NCTX, NLAT)]:
            P.op("scalar", lambda e, a0=a0, n2=n2: e.activation(out=acc[:, a0:a0 + n2], in_=acc[:, a0:a0 + n2], func=AF.Silu),
                 r=["acc"], w=["acc"])
            if rc < 4:
                P.op("vector", lambda e, a0=a0, d0=d0, n2=n2, dst=dst, rc=rc: e.tensor_scalar(
                    out=dst[:, rc % 4, d0:d0 + n2], in0=acc[:, a0:a0 + n2], scalar1=128.0 ** -0.5, scalar2=None, op0=ALU.mult),
                    r=["acc"], w=[dk])
            else:
                P.op("vector", lambda e, a0=a0, d0=d0, n2=n2, dst=dst, rc=rc: e.tensor_copy(
                    out=dst[:, rc % 4, d0:d0 + n2], in_=acc[:, a0:a0 + n2]), r=["acc"], w=[dk])
    psb16 = [g.ps[i][:].bitcast(BF16) for i in range(8)]
    for c in range(NCH):
        g.ps_i = (g.ps_i + 1) % 8
        pi = g.ps_i
        pk = "ps%d" % pi
        for hd in range(4):
            P.op("tensor", lambda e, c=c, hd=hd, pi=pi: e.transpose(psb16[pi][:, hd * 128:(hd + 1) * 128],
                                                                   kT[:, hd, c * 128:(c + 1) * 128], identb),
                 r=["kT", "identb"], w=[pk])
        P.op("vector", lambda e, c=c, pi=pi: e.tensor_copy(out=ktok[:, c, :], in_=psb16[pi][:, 0:512]), r=[pk], w=["ktok"])
    v1v = v1.rearrange("p (c h) x -> p c h x", h=4)
    P.op("vector", lambda e: e.memset(v1[:, :, 128:129], 1.0), w=["v1"])
    for hd in range(4):
        P.dma("gpsimd", v1v[:, :, hd, 0:128], PTM[:, hd * 128:(hd + 1) * 128].rearrange("(c p) x -> p c x", p=128), "v1",
              r=["PTM%d" % l], w=["v1"])
    g.arena_off = mark2
    P.barrier()
    chains = [(dr, hd) for dr in range(2) for hd in range(4)]
    st8 = {}
    for (dr, hd) in chains:
        ck = "c%d%d" % (dr, hd)
        st8[ck] = dict(Cf=g.carve([128, 132], F32), Cb=g.carve([128, 132], BF16), dx=g.carve([128, 128], F32),
                       dxm=g.carve([128, 128], F32), sw=g.carve([128, 128], BF16), Asb=g.carve([128, 132], F32),
                       nt=g.carve([128, 132], F32), kw=g.carve([128, 128], BF16), ho=g.carve([128, 128], F32),
                       dd=g.carve([128, 2], F32), dgm=g.carve([128, 128], F32))
        b8 = st8[ck]
        P.op("vector", lambda e, b8=b8: e.memset(b8["Cf"], 0.0), w=[ck + "Cf"])
        P.op("vector", lambda e, b8=b8: e.memset(b8["Cb"], 0.0), w=[ck + "Cb"])
    for step in range(NCH):
        for (dr, hd) in chains:
            ck = "c%d%d" % (dr, hd)
            k = "d%d" % dr
            b8 = st8[ck]
            c = order[dr][step]
            cs = slice(c * 128, (c + 1) * 128)
            ps1, k1 = g.next_ps()
            P.op("tensor", lambda e, ps1=ps1, hd=hd, cs=cs: e.matmul(ps1[:, :128], kT[:, hd, cs], qT[:, hd, cs], start=True, stop=True),
                 r=["kT", "qT"], w=[k1])
            ps2, k2 = g.next_ps()
            P.op("gpsimd", lambda e, b8=b8, dr=dr, c=c, hd=hd: e.tensor_scalar(
                out=b8["dgm"], in0=ident, scalar1=Mtok[dr][:, c, hd:hd + 1], scalar2=None, op0=ALU.mult),
                r=["ident", "Mt" + k], w=[ck + "dgm"])
            P.op("tensor", lambda e, ps2=ps2, b8=b8: e.matmul(ps2[:, :128], onesf, b8["dgm"], start=True, stop=True),
                 r=["onesf", ck + "dgm"], w=[k2])
            P.op("scalar", lambda e, ps2=ps2, b8=b8, dr=dr, c=c, hd=hd: e.activation(
                out=b8["dx"], in_=ps2[:, :128], func=AF.Exp, bias=ud[dr][:, c, hd:hd + 1], scale=1.0), r=[k2, "u" + k], w=[ck + "dx"])
            P.op("gpsimd", lambda e, b8=b8, dr=dr: e.tensor_tensor(out=b8["dxm"], in0=b8["dx"], in1=tri[:, dr, :], op=ALU.mult),
                 r=[ck + "dx", "tri"], w=[ck + "dxm"])
            P.op("vector", lambda e, ps1=ps1, b8=b8: e.tensor_tensor(out=b8["sw"], in0=ps1[:, :128], in1=b8["dxm"], op=ALU.mult),
                 r=[k1, ck + "dxm"], w=[ck + "sw"])
            ps3, k3 = g.next_ps()
            P.op("tensor", lambda e, ps3=ps3, b8=b8, c=c, hd=hd: e.matmul(ps3[:, :129], b8["sw"], v1[:, c * 4 + hd, 0:129], start=True, stop=True),
                 r=[ck + "sw", "v1"], w=[k3])
            ps4, k4 = g.next_ps()
            P.op("tensor", lambda e, ps4=ps4, b8=b8, hd=hd, cs=cs: e.matmul(ps4[:, :129], qT[:, hd, cs], b8["Cb"][:, 0:129], start=True, stop=True),
                 r=["qT", ck + "Cb"], w=[k4])
            P.op("scalar", lambda e, ps3=ps3, b8=b8: e.copy(out=b8["Asb"][:, 0:129], in_=ps3[:, :129]), r=[k3], w=[ck + "Asb"])
            P.op("vector", lambda e, ps4=ps4, b8=b8, dr=dr, c=c, hd=hd: e.scalar_tensor_tensor(
                out=b8["nt"][:, 0:129], in0=ps4[:, :129], scalar=iwtok[dr][:, c, hd:hd + 1], in1=b8["Asb"][:, 0:129],
                op0=ALU.mult, op1=ALU.add), r=[k4, ck + "Asb", "iw" + k], w=[ck + "nt"])
            P.op("vector", lambda e, b8=b8: e.scalar_tensor_tensor(
                out=b8["dd"][:, 0:1], in0=b8["nt"][:, 128:129], scalar=-1.0, in1=b8["nt"][:, 128:129], op0=ALU.mult, op1=ALU.max),
                r=[ck + "nt"], w=[ck + "dd"])
            P.op("vector", lambda e, b8=b8, dr=dr, c=c, hd=hd: e.tensor_tensor(
                out=b8["dd"][:, 0:1], in0=b8["dd"][:, 0:1], in1=emt[dr][:, c, hd:hd + 1], op=ALU.max),
                r=[ck + "dd", "em" + k], w=[ck + "dd"])
            P.op("vector", lambda e, b8=b8: e.reciprocal(out=b8["dd"][:, 1:2], in_=b8["dd"][:, 0:1]), r=[ck + "dd"], w=[ck + "dd"])
            P.op("scalar", lambda e, b8=b8: e.mul(out=b8["ho"], in_=b8["nt"][:, 0:128], mul=b8["dd"][:, 1:2]),
                 r=[ck + "nt", ck + "dd"], w=[ck + "ho"])
            P.dma("sync", g.HD[dr][c * 128:(c + 1) * 128, hd * 128:(hd + 1) * 128], b8["ho"], ck + "ho", r=[ck + "ho"], w=["HD%d" % dr])
            P.op("vector", lambda e, b8=b8, dr=dr, c=c, hd=hd: e.tensor_scalar(
                out=b8["kw"], in0=ktok[:, c, hd * 128:(hd + 1) * 128], scalar1=wtok[dr][:, c, hd:hd + 1], scalar2=None, op0=ALU.mult),
                r=["ktok", "w" + k], w=[ck + "kw"])
            ps5, k5 = g.next_ps()
            P.op("tensor", lambda e, ps5=ps5, b8=b8, c=c, hd=hd: e.matmul(ps5[:, :129], b8["kw"], v1[:, c * 4 + hd, 0:129], start=True, stop=True),
                 r=[ck + "kw", "v1"], w=[k5])
            P.op("vector", lambda e, ps5=ps5, b8=b8, dr=dr, c=c, hd=hd: e.scalar_tensor_tensor(
                out=b8["Cf"][:, 0:129], in0=b8["Cf"][:, 0:129], scalar=abc[dr][:, c, hd:hd + 1], in1=ps5[:, :129],
                op0=ALU.mult, op1=ALU.add), r=[ck + "Cf", k5, "bcd%daRd%d" % (dr, dr)], w=[ck + "Cf"])
            P.op("scalar", lambda e, b8=b8: e.copy(out=b8["Cb"][:, 0:129], in_=b8["Cf"][:, 0:129]), r=[ck + "Cf"], w=[ck + "Cb"])
    g.arena_off = mark2
    P.barrier()
    gbc = g.carve([128, 512], F32)
    P.dma("sync", gbc, g.mix_g[l:l + 1, 1536:2048].to_broadcast([128, 512]), "pc", w=["gbc"])
    P.seal()
    hf = g.carve([128, 512], F32)
    hb = g.carve([128, 512], F32)
    og = g.carve([128, 512], F32)
    junk = g.carve([128, 128], F32)
    ss = g.carve([128, 8], F32)
    outs = [g.carve([128, 512], F32) for i in range(2)]
    for c in range(NCH if l == 0 else NCH):
        P.dma("sync", hf, g.HD[0][c * 128:(c + 1) * 128, :], "hf", r=["HD0"], w=["hf"])
        P.dma("sync", hb, g.HD[1][c * 128:(c + 1) * 128, :], "hb", r=["HD1"], w=["hb"])
        P.dma("sync", og, PTM[c * 128:(c + 1) * 128, 512:1024], "og", r=["PTM%d" % l], w=["og"])
        P.op("vector", lambda e: e.tensor_tensor(out=hf, in0=hf, in1=hb, op=ALU.add), r=["hf", "hb"], w=["hf"])
        for hd in range(4):
            P.op("scalar", lambda e, hd=hd: e.activation(out=junk, in_=hf[:, hd * 128:(hd + 1) * 128], func=AF.Square,
                                                        accum_out=ss[:, hd:hd + 1]), r=["hf"], w=["junk", "ss"])
        P.op("scalar", lambda e: e.activation(out=ss[:, 4:8], in_=ss[:, 0:4], func=AF.Sqrt, bias=g.eps_t[:, 0:1], scale=1.0 / 128),
             r=["ss", "eps_t"], w=["ss"])
        P.op("vector", lambda e: e.reciprocal(out=ss[:, 4:8], in_=ss[:, 4:8]), r=["ss"], w=["ss"])
        P.op("scalar", lambda e: e.activation(out=og, in_=og, func=AF.Sigmoid), r=["og"], w=["og"])
        P.op("vector", lambda e: e.tensor_tensor(out=og, in0=og, in1=gbc, op=ALU.mult), r=["og", "gbc"], w=["og"])
        for hd in range(4):
            P.op("vector", lambda e, hd=hd: e.scalar_tensor_tensor(
                out=hf[:, hd * 128:(hd + 1) * 128], in0=hf[:, hd * 128:(hd + 1) * 128], scalar=ss[:, 4 + hd:5 + hd],
                in1=og[:, hd * 128:(hd + 1) * 128], op0=ALU.mult, op1=ALU.mult), r=["hf", "ss", "og"], w=["hf"])
        ps, psk = g.next_ps()
        for hd in range(4):
            P.op("tensor", lambda e, ps=ps, hd=hd: e.transpose(ps[:, hd * 128:(hd + 1) * 128], hf[:, hd * 128:(hd + 1) * 128], ident),
                 r=["hf", "ident"], w=[psk])
        o_ = outs[c % 2]
        ok = "mlo%d" % (c % 2)
        P.op("vector", lambda e, ps=ps, o_=o_: e.tensor_copy(out=o_, in_=ps[:, :]), r=[psk], w=[ok])
        P.dma("sync", g.CATT[1536:2048, c * 128:(c + 1) * 128].rearrange("(h p) t -> p h t", p=128),
              o_.rearrange("p (h t) -> p h t", t=128), ok, r=[ok], w=["CATT"])


def dft_fwd(g, L, rhsC, rkC, rhsS, rkS, evac, nyq):
    P = g.P
    nch = L // 128
    KG = min(512, L)
    nj = KG // 128
    mats = g.hy_cs[L]
    for kg in range(L // KG):
        banks = [[g.psb(j), g.psb(4 + j)] for j in range(nj)]
        for mi, (rhs, rk) in enumerate([(rhsC, rkC), (rhsS, rkS)]):
            for tc in range(nch):
                bi = g.mt_i
                g.mt_i = (bi + 1) % len(g.mt)
                mt, mk = g.mt[bi], "mt%d" % bi
                P.dma("sync", mt[:, :KG], mats[mi, tc * 128:(tc + 1) * 128, kg * KG:(kg + 1) * KG], mk, w=[mk])
                for j in range(nj):
                    ps, pk = banks[j][mi]
                    P.op("tensor", lambda e, ps=ps, mt=mt, j=j, rhs=rhs, tc=tc: e.matmul(
                        ps[:, :512], mt[:, j * 128:(j + 1) * 128], rhs[:, tc, :], start=(tc == 0), stop=(tc == nch - 1)),
                        r=[mk, rk], w=[pk])
        for j in range(nj):
            (pc, pck), (ps_, psk) = banks[j]
            evac(kg * nj + j, pc, pck, ps_, psk)
    g.ps_i = (g.ps_i + 1) % 8
    pn, pnk = g.psb(g.ps_i)
    for tc in range(nch):
        P.op("tensor", lambda e, tc=tc, pn=pn: e.matmul(pn[:1, :512], g.altc[:, 0:1], rhsC[:, tc, :], start=(tc == 0), stop=(tc == nch - 1)),
             r=["altc", rkC], w=[pnk])
    nyq(pn, pnk)


def hyena_seq(g, l, L, tok0):
    P = g.P
    nch = L // 128
    N2 = 2 * L
    PT = g.PT[l]
    g.new_phase()
    g.ps_rot = list(range(8))
    ident = g.carve([128, 128], F32)
    identb = g.carve([128, 128], BF16)
    P.dma("sync", ident, g.ident[:, :], "pc", w=["ident"])
    P.dma("gpsimd", identb, g.ident[:, :], "pc", w=["identb"])
    g.altc = g.carve([128, 1], BF16)
    altr = g.carve([1, 512], BF16)
    P.dma("gpsimd", g.altc, g.hy_alt[:, 0:1], "pc", w=["altc"], allow_slow_non_contiguous=True)
    P.dma("gpsimd", altr, g.hy_alt[0:1, :], "pc", w=["altr"])
    g.mt = [g.carve([128, 512], BF16) for i in range(4)]
    g.mt_i = 0
    pic = g.carve([128, 2], F32)
    P.op("vector", lambda e: e.memset(pic[:, 0:1], 0.05), w=["pic"])
    mark0 = g.arena_off
    w1 = g.carve([33, 64], F32)
    w2 = g.carve([64, 64], F32)
    w3 = g.carve([64, 2048], F32)
    b12 = g.carve([64, 2], F32)
    dec = g.carve([128, 2048], F32)
    negt = g.carve([128, nch], F32)
    feats = g.carve([33, L], F32)
    h2T = g.carve([64, L], F32)
    P.dma("sync", w1, g.hy_w1[l], "pc", w=["hw1"])
    P.dma("sync", w2, g.hy_w2[l], "pc", w=["hw2"])
    P.dma("sync", w3, g.hy_w3[l], "pc", w=["hw3"])
    P.dma("sync", b12, g.hy_b12[l], "pc", w=["hb12"])
    P.dma("sync", dec, g.hy_decay[l:l + 1, :].to_broadcast([128, 2048]), "pc", w=["hdec"])
    P.dma("sync", negt, g.hy_negt[L][:, :], "pc", w=["negt"])
    P.dma("sync", feats, g.hy_feats[L][:, :], "pc", w=["feats"])
    P.seal()
    zt = g.carve([64, 512], F32)
    ki = g.carve([64, 512], mybir.dt.int32)
    kf = g.carve([64, 512], F32)
    h1 = g.carve([64, 512], F32)
    TW = min(512, L)

    def sin_layer(ps, pk, bcol, dst, dk):
        n = TW
        P.op("vector", lambda e: e.tensor_scalar(out=zt[:, :n], in0=ps[:64, :n], scalar1=b12[:, bcol:bcol + 1], scalar2=None, op0=ALU.add),
             r=[pk, "hb12"], w=["zt"])
        P.op("vector", lambda e: e.tensor_scalar(out=ki[:, :n], in0=zt[:, :n], scalar1=1.0 / (2 * math.pi), scalar2=None, op0=ALU.mult),
             r=["zt"], w=["ki"])
        P.op("vector", lambda e: e.tensor_copy(out=kf[:, :n], in_=ki[:, :n]), r=["ki"], w=["kf"])
        P.op("vector", lambda e: e.scalar_tensor_tensor(out=zt[:, :n], in0=kf[:, :n], scalar=-2 * math.pi, in1=zt[:, :n],
                                                        op0=ALU.mult, op1=ALU.add), r=["kf", "zt"], w=["zt"])
        P.op("vector", lambda e: e.tensor_scalar(out=zt[:, :n], in0=zt[:, :n], scalar1=-3.141592, scalar2=3.141592,
                                                 op0=ALU.max, op1=ALU.min), r=["zt"], w=["zt"])
        P.op("scalar", lambda e: e.activation(out=dst, in_=zt[:, :n], func=AF.Sin), r=["zt"], w=[dk])

    for t0 in range(0, L, TW):
        ps, pk = g.next_ps()
        P.op("tensor", lambda e, ps=ps, t0=t0: e.matmul(ps[:64, :TW], w1, feats[:, t0:t0 + TW], start=True, stop=True),
             r=["hw1", "feats"], w=[pk])
        sin_layer(ps, pk, 0, h1[:, :TW], "h1")
        ps, pk = g.next_ps()
        P.op("tensor", lambda e, ps=ps: e.matmul(ps[:64, :TW], w2, h1[:, :TW], start=True, stop=True), r=["hw2", "h1"], w=[pk])
        sin_layer(ps, pk, 1, h2T[:, t0:t0 + TW], "h2T")
    wt_ = [g.carve([128, 512], F32) for i in range(2)]
    hc = [g.carve([128, 512], F32) for i in range(4)]
    hsb = [g.carve([128, 512], BF16) for i in range(4)]
    q = 0
    for tc in range(nch):
        for cb in range(4):
            ps, pk = g.next_ps()
            P.op("tensor", lambda e, ps=ps, tc=tc, cb=cb: e.matmul(ps[:, :512], h2T[:, tc * 128:(tc + 1) * 128], w3[:, cb * 512:(cb + 1) * 512],
                                                                start=True, stop=True), r=["h2T", "hw3"], w=[pk])
            wt = wt_[cb % 2]
            wk_ = "hwt%d" % (cb % 2)
            P.op("scalar", lambda e, wt=wt, tc=tc, cb=cb: e.activation(out=wt, in_=dec[:, cb * 512:(cb + 1) * 512], func=AF.Exp,
                                                                     scale=negt[:, tc:tc + 1]), r=["hdec", "negt"], w=[wk_])
            P.op("vector", lambda e, wt=wt, ps=ps, cb=cb: e.scalar_tensor_tensor(out=hc[cb], in0=wt, scalar=pic[:, 0:1], in1=ps[:, :512],
                                                                             op0=ALU.add, op1=ALU.mult), r=[wk_, pk, "pic"], w=["hc%d" % cb])
            if tc == 0 and cb % 2 == 1:
                P.op("vector", lambda e, cb=cb: e.memset(hc[cb][0:1, :], 0.0), w=["hc%d" % cb])
        for o in range(2):
            for mi in range(2):
                hb_ = hsb[q % 4]
                hk = "hsb%d" % (q % 4)
                q += 1
                if mi == 0:
                    P.op("gpsimd", lambda e, hb_=hb_, o=o: e.tensor_tensor(out=hb_, in0=hc[2 * o], in1=hc[2 * o + 1], op=ALU.add),
                         r=["hc%d" % (2 * o), "hc%d" % (2 * o + 1)], w=[hk])
                else:
                    P.op("gpsimd", lambda e, hb_=hb_, o=o: e.tensor_tensor(out=hb_, in0=hc[2 * o + 1], in1=hc[2 * o], op=ALU.subtract),
                         r=["hc%d" % (2 * o), "hc%d" % (2 * o + 1)], w=[hk])
                P.dma("sync", g.HS[mi, tc * 128:(tc + 1) * 128, o * 512:(o + 1) * 512], hb_, hk, r=[hk], w=["HS"])
    g.arena_off = mark0
    P.barrier()
    rC = g.carve([128, nch, 512], BF16)
    rS = g.carve([128, nch, 512], BF16)
    stg = [g.carve([128, 512], F32) for i in range(4)]
    sti = [0]

    def nstg():
        sti[0] = (sti[0] + 1) % 4
        return stg[sti[0]], "hstg%d" % sti[0]

    for o in range(2):
        P.dma("sync", rC, g.HS[0, 0:L, o * 512:(o + 1) * 512].rearrange("(c p) x -> p c x", p=128), "rC", r=["HS"], w=["rC"])
        P.dma("sync", rS, g.HS[1, 0:L, o * 512:(o + 1) * 512].rearrange("(c p) x -> p c x", p=128), "rS", r=["HS"], w=["rS"])

        def evac(kc, pc, pck, ps_, psk, o=o):
            for mi, (pp, ppk) in enumerate([(pc, pck), (ps_, psk)]):
                st_, sk_ = nstg()
                if mi == 0:
                    P.op("vector", lambda e, st_=st_, pp=pp: e.tensor_scalar(out=st_, in0=pp[:, :512], scalar1=2.0 / N2, scalar2=None, op0=ALU.mult),
                         r=[ppk], w=[sk_])
                else:
                    P.op("scalar", lambda e, st_=st_, pp=pp: e.mul(out=st_, in_=pp[:, :512], mul=2.0 / N2), r=[ppk], w=[sk_])
                if kc == 0:
                    P.op("vector", lambda e, st_=st_: e.tensor_scalar(out=st_[0:1, :], in0=st_[0:1, :], scalar1=0.5, scalar2=None, op0=ALU.mult),
                         r=[sk_], w=[sk_])
                P.dma("sync", g.KF[mi, kc * 128:(kc + 1) * 128, o * 512:(o + 1) * 512], st_, sk_, r=[sk_], w=["KF"])

        def nyq(pn, pnk, o=o):
            st_, sk_ = nstg()
            P.op("vector", lambda e, st_=st_: e.tensor_scalar(out=st_[0:1, :], in0=pn[0:1, :512], scalar1=1.0 / N2, scalar2=None, op0=ALU.mult),
                 r=[pnk], w=[sk_])
            P.dma("sync", g.KF[0, L:L + 1, o * 512:(o + 1) * 512], st_[0:1, :], sk_, r=[sk_], w=["KF"])

        dft_fwd(g, L, rC, "rC", rS, "rS", evac, nyq)
    g.arena_off = mark0
    P.barrier()
    ztok = g.carve([128, nch, 512], BF16)
    Asb = g.carve([128, nch + 1, 512], BF16)
    Bsb = g.carve([128, nch, 512], BF16)
    skp = g.carve([128, 2, 4], F32)
    P.dma("sync", skp, g.hy_skipT[l], "pc", w=["skp"])
    P.seal()
    mark1 = g.arena_off
    cw = g.carve([128, 12, 4], F32)
    P.dma("sync", cw, g.hy_cw[l], "pc", w=["hcw"])
    P.seal()
    xb = g.carve([128, L + 2], F32)
    acc = g.carve([128, L], F32)
    accb = g.carve([128, L], BF16)
    psb16 = [g.ps[i][:].bitcast(BF16) for i in range(8)]
    P.op("vector", lambda e: e.memset(xb, 0.0), w=["xb"])

    def to_tok(srcb, sk_, cc):
        for t8 in range(0, nch, 8):
            g.ps_i = (g.ps_i + 1) % 8
            pi = g.ps_i
            nn = min(8, nch - t8)
            for j in range(nn):
                P.op("tensor", lambda e, pi=pi, j=j, t8=t8: e.transpose(psb16[pi][:, j * 128:(j + 1) * 128],
                                                                      srcb[:, (t8 + j) * 128:(t8 + j + 1) * 128], identb),
                     r=[sk_, "identb"], w=["ps%d" % pi])
            P.op("vector", lambda e, pi=pi, t8=t8, nn=nn, cc=cc: e.tensor_copy(
                out=ztok[:, t8:t8 + nn, cc * 128:(cc + 1) * 128], in_=psb16[pi][:, 0:nn * 128].rearrange("p (c x) -> p c x", x=128)),
                r=["ps%d" % pi], w=["ztok"])

    for rc in range(12):
        r0 = 1088 + rc * 128
        P.dma("sync", xb[:, 1:1 + L], PT[r0:r0 + 128, tok0:tok0 + L], "xb", r=["PT%d" % l], w=["xb"])
        P.op("vector", lambda e, rc=rc: e.tensor_scalar(out=acc, in0=xb[:, 0:L], scalar1=cw[:, rc, 0:1], scalar2=cw[:, rc, 3:4],
                                                       op0=ALU.mult, op1=ALU.add), r=["xb", "hcw"], w=["acc"])
        for j in (1, 2):
            P.op("vector", lambda e, rc=rc, j=j: e.scalar_tensor_tensor(out=acc, in0=xb[:, j:j + L], scalar=cw[:, rc, j:j + 1], in1=acc,
                                                                     op0=ALU.mult, op1=ALU.add), r=["xb", "hcw", "acc"], w=["acc"])
        P.dma("sync", g.HV[rc * 128:(rc + 1) * 128, 0:L], acc, "acc", r=["acc"], w=["HV"])
        if rc < 4:
            P.op("scalar", lambda e: e.copy(out=accb, in_=acc), r=["acc"], w=["accb"])
            to_tok(accb, "accb", rc)
    g.arena_off = mark1
    P.barrier()
    kr_ = [g.carve([128, 512], F32) for i in range(2)]
    ki_ = [g.carve([128, 512], F32) for i in range(2)]
    t1 = g.carve([128, 512], F32)
    t2 = g.carve([128, 512], F32)
    zin = [g.carve([128, 512], F32) for i in range(2)]
    gat = [g.carve([128, 512], F32) for i in range(2)]
    ot = [g.carve([128, 512], F32) for i in range(2)]
    otb = [g.carve([128, 512], BF16) for i in range(2)]
    stg = [g.carve([128, 512], F32) for i in range(2)]
    for o in range(2):
        def evac(kc, pc, pck, ps_, psk, o=o):
            kr, krk = kr_[kc % 2], "kr%d" % (kc % 2)
            ki2, kik = ki_[kc % 2], "ki%d" % (kc % 2)
            P.dma("sync", kr, g.KF[0, kc * 128:(kc + 1) * 128, o * 512:(o + 1) * 512], krk, r=["KF"], w=[krk])
            P.dma("sync", ki2, g.KF[1, kc * 128:(kc + 1) * 128, o * 512:(o + 1) * 512], kik, r=["KF"], w=[kik])
            P.op("vector", lambda e: e.tensor_tensor(out=t1, in0=pc[:, :512], in1=kr, op=ALU.mult), r=[pck, krk], w=["ht1"])
            P.op("vector", lambda e: e.tensor_tensor(out=t2, in0=ps_[:, :512], in1=ki2, op=ALU.mult), r=[psk, kik], w=["ht2"])
            P.op("gpsimd", lambda e: e.tensor_tensor(out=Asb[:, kc, :], in0=t1, in1=t2, op=ALU.add), r=["ht1", "ht2"], w=["Asb"])
            P.op("vector", lambda e: e.tensor_tensor(out=t1, in0=ps_[:, :512], in1=kr, op=ALU.mult), r=[psk, krk], w=["ht1"])
            P.op("vector", lambda e: e.tensor_tensor(out=t2, in0=pc[:, :512], in1=ki2, op=ALU.mult), r=[pck, kik], w=["ht2"])
            P.op("gpsimd", lambda e: e.tensor_tensor(out=Bsb[:, kc, :], in0=t1, in1=t2, op=ALU.subtract), r=["ht1", "ht2"], w=["Bsb"])

        def nyq(pn, pnk, o=o):
            P.dma("sync", kr_[0][0:1, :], g.KF[0, L:L + 1, o * 512:(o + 1) * 512], "kr0", r=["KF"], w=["kr0"])
            P.op("vector", lambda e: e.tensor_tensor(out=Asb[0:1, nch, :], in0=pn[0:1, :512], in1=kr_[0][0:1, :], op=ALU.mult),
                 r=[pnk, "kr0"], w=["Asb"])

        dft_fwd(g, L, ztok, "ztok", ztok, "ztok", evac, nyq)
        TWI = min(512, L)
        mats = g.hy_cs[L]
        for ti, t0 in enumerate(range(0, L, TWI)):
            banks = [g.psb(4 * (ti % 2) + cc) for cc in range(4)]
            for kc in range(nch):
                tiles = []
                for mi in range(2):
                    bi = g.mt_i
                    g.mt_i = (bi + 1) % len(g.mt)
                    mt, mk = g.mt[bi], "mt%d" % bi
                    P.dma("sync", mt[:, :TWI], mats[mi, kc * 128:(kc + 1) * 128, t0:t0 + TWI], mk, w=[mk])
                    tiles.append((mt, mk))
                for cc in range(4):
                    pb, pbk = banks[cc]
                    P.op("tensor", lambda e, pb=pb, kc=kc, cc=cc, mt=tiles[0][0]: e.matmul(
                        pb[:, :TWI], Asb[:, kc, cc * 128:(cc + 1) * 128], mt[:, :TWI], start=(kc == 0), stop=False),
                        r=["Asb", tiles[0][1]], w=[pbk])
                    P.op("tensor", lambda e, pb=pb, kc=kc, cc=cc, mt=tiles[1][0]: e.matmul(
                        pb[:, :TWI], Bsb[:, kc, cc * 128:(cc + 1) * 128], mt[:, :TWI], start=False, stop=False),
                        r=["Bsb", tiles[1][1]], w=[pbk])
            for cc in range(4):
                pb, pbk = banks[cc]
                P.op("tensor", lambda e, pb=pb, cc=cc: e.matmul(pb[:, :TWI], Asb[0:1, nch, cc * 128:(cc + 1) * 128], altr[0:1, :TWI],
                                                             start=False, stop=True), r=["Asb", "altr"], w=[pbk])
                zi_, zk = zin[cc % 2], "zin%d" % (cc % 2)
                ga_, gk = gat[cc % 2], "gat%d" % (cc % 2)
                o_, ok = ot[cc % 2], "hot%d" % (cc % 2)
                src = g.HV if o == 0 else g.HZ2
                P.dma("sync", zi_[:, :TWI], src[cc * 128:(cc + 1) * 128, t0:t0 + TWI], zk, r=["HV", "HZ2"], w=[zk])
                P.dma("sync", ga_[:, :TWI], g.HV[(o + 1) * 512 + cc * 128:(o + 1) * 512 + (cc + 1) * 128, t0:t0 + TWI], gk, r=["HV"], w=[gk])
                P.op("vector", lambda e, zi_=zi_, pb=pb, cc=cc, o_=o_, o=o: e.scalar_tensor_tensor(
                    out=o_[:, :TWI], in0=zi_[:, :TWI], scalar=skp[:, o, cc:cc + 1], in1=pb[:, :TWI], op0=ALU.mult, op1=ALU.add),
                    r=[zk, "skp", pbk], w=[ok])
                P.op("gpsimd", lambda e, o_=o_, ga_=ga_: e.tensor_tensor(out=o_[:, :TWI], in0=o_[:, :TWI], in1=ga_[:, :TWI], op=ALU.mult),
                     r=[ok, gk], w=[ok])
                if o == 0:
                    P.dma("sync", g.HZ2[cc * 128:(cc + 1) * 128, t0:t0 + TWI], o_[:, :TWI], ok, r=[ok], w=["HZ2"])
                    ob_, obk = otb[cc % 2], "hotb%d" % (cc % 2)
                    P.op("scalar", lambda e, ob_=ob_, o_=o_: e.copy(out=ob_[:, :TWI], in_=o_[:, :TWI]), r=[ok], w=[obk])
                    for j in range(TWI // 128):
                        g.ps_i = (g.ps_i + 1) % 8
                        pi = 4 * (1 - ti % 2) + (j % 4)
                        P.op("tensor", lambda e, pi=pi, j=j, ob_=ob_: e.transpose(psb16[pi][:, 0:128], ob_[:, j * 128:(j + 1) * 128], identb),
                             r=[obk, "identb"], w=["ps%d" % pi])
                        P.op("vector", lambda e, pi=pi, j=j, cc=cc, t0=t0: e.tensor_copy(
                            out=ztok[:, t0 // 128 + j, cc * 128:(cc + 1) * 128], in_=psb16[pi][:, 0:128]), r=["ps%d" % pi], w=["ztok"])
                else:
                    P.dma("sync", g.CATT[1024 + cc * 128:1024 + (cc + 1) * 128, tok0 + t0:tok0 + t0 + TWI], o_[:, :TWI], ok, r=[ok], w=["CATT"])


def phase_c(g, l):
    P = g.P
    g.new_phase()
    g.ps_rot = list(range(8))
    last = (l == DEPTH - 1)
    tiles = TILES[1:] if last else TILES
    jf = lambda t0: 1 if t0 < NCTX else 0
    catn = g.carve([128, KC, T], BF16)
    carve_common(g, 256)
    xt = g.carve([128, 8, 512], F32)
    sq = g.carve([128, 8, 512], BF16)
    rstd = g.carve([128, 512], F32)
    mg = g.carve([128, KC], F32)
    xres = [g.carve([128, 512], F32) for i in range(2)]
    P.dma("sync", mg, g.mix_gT[l], "pc", w=["mg"])
    P.seal()
    for (t0, n) in tiles:
        for (c0, nk, nfeat) in [(0, 8, 1024), (8, 4, 512)]:
            srcv = g.CATT[c0 * 128:(c0 + nk) * 128, :].rearrange("(kc p) t -> p kc t", p=128)
            P.dma("sync", xt[:, :nk, :n], srcv[:, :, t0:t0 + n], "xt0", r=["CATT"], w=["xt0"])
            P.op("scalar", lambda e, n=n, nk=nk: e.activation(out=sq[:, :nk, :n], in_=xt[:, :nk, :n], func=AF.Square), r=["xt0"], w=["sq"])
            ps, psk = g.next_ps()
            for kc in range(nk):
                P.op("tensor", lambda e, kc=kc, n=n, ps=ps, nk=nk: e.matmul(ps[:, :n], g.ones_bf[:], sq[:, kc, :n], start=(kc == 0), stop=(kc == nk - 1)),
                     r=["sq", "ones_bf"], w=[psk])
            rstd_op(g, ps, psk, n, nfeat, rstd, "rstd")
            for kc in range(nk):
                P.op("vector", lambda e, kc=kc, n=n, t0=t0, c0=c0: e.scalar_tensor_tensor(
                    out=catn[:, c0 + kc, t0:t0 + n], in0=xt[:, kc, :n], scalar=mg[:, c0 + kc:c0 + kc + 1], in1=rstd[:, :n],
                    op0=ALU.mult, op1=ALU.mult), r=["xt0", "rstd", "mg"], w=["catn"])
        srcv = g.CATT[1536:2048, :].rearrange("(kc p) t -> p kc t", p=128)
        P.dma("sync", xt[:, :4, :n], srcv[:, :, t0:t0 + n], "xt0", r=["CATT"], w=["xt0"])
        P.op("scalar", lambda e, n=n, t0=t0: e.copy(out=catn[:, 12:16, t0:t0 + n], in_=xt[:, :4, :n]), r=["xt0"], w=["catn"])
    src = g.xT if l == 0 else g.xs
    skey = "xs"
    wv = g.w_out[l].rearrange("(kc p) n -> p kc n", p=128)
    cnt = [0]

    def evac(cb, m, t0, n, ps, psk):
        fc = cb // 128
        j = jf(t0)
        xr = xres[cnt[0] % 2]
        xk = "xres%d" % (cnt[0] % 2)
        cnt[0] += 1
        P.dma("sync", xr[:, :n], src[cb:cb + 128, t0:t0 + n], xk, r=[skey], w=[xk])
        stg, sk = next_stage(g)
        P.op("vector", lambda e: e.scalar_tensor_tensor(out=stg[:, :n], in0=ps[:, :n], scalar=g.mod[l][:, 32 + fc, j:j + 1], in1=xr[:, :n],
                                                        op0=ALU.mult, op1=ALU.add), r=[psk, xk, "mod%d" % l], w=[sk])
        P.dma("sync", g.XM[cb:cb + 128, t0:t0 + n], stg[:, :n], sk, r=[sk], w=["XM"])

    for c0 in range(0, D, 256):
        proj_fm(g, wv, c0, 256, catn, "catn", tiles, evac)
    g.new_phase()
    xn2 = g.carve([128, KC, 1024], BF16)
    hT = g.carve([128, 44, 1024], BF16)
    w1b = [g.carve([128, KC, 256], BF16) for i in range(2)]
    w2b = [g.carve([128, 44, 128], BF16) for i in range(2)]
    g.stage = [g.carve([128, 512], F32) for i in range(2)]
    g.stage_i = 0
    sg = [g.carve([128, 512], F32) for i in range(2)]
    xres = [g.carve([128, 512], F32) for i in range(2)]
    rb = rms_bufs(g)
    w1v = g.ffn_w1[l].rearrange("(kc p) n -> p kc n", p=128)
    w2v = g.ffn_w2[l].rearrange("(kc p) n -> p kc n", p=128)
    groups = [[(256 + 1024 * i, 512), (256 + 1024 * i + 512, 512)] for i in range(4)]
    if not last:
        groups = [[(0, 256)]] + groups
    wi = 0
    w2i = 0
    for grp in groups:
        g0 = grp[0][0]
        gn = sum(n for (_, n) in grp)
        rms_modulate(g, g.XM, l, g.gs2[l], "gs2_%d" % l, 48, xn2, "xn2", [(t0, 128) for t0 in range(g0, g0 + gn, 128)], jf,
                     bufs=rb, srckey="XM", dst_t0=g0)
        for hc in range(44):
            wb, wk = w1b[wi % 2], "w1b%d" % (wi % 2)
            wi += 1
            P.dma("gpsimd", wb[:, :, 0:128], w1v[:, :, hc * 128:(hc + 1) * 128], wk, w=[wk])
            P.dma("gpsimd", wb[:, :, 128:256], w1v[:, :, FFN_H + hc * 128:FFN_H + (hc + 1) * 128], wk, w=[wk])
            for (t0, n) in grp:
                pg, pgk = g.next_ps()
                pu, puk = g.next_ps()
                for (pp, ppk, cofs) in [(pg, pgk, 0), (pu, puk, 128)]:
                    for kc in range(KC):
                        P.op("tensor", lambda e, pp=pp, kc=kc, cofs=cofs, t0=t0, n=n, wb=wb, g0=g0: e.matmul(
                            pp[:, :n], wb[:, kc, cofs:cofs + 128], xn2[:, kc, t0 - g0:t0 - g0 + n], start=(kc == 0), stop=(kc == KC - 1)),
                            r=[wk, "xn2"], w=[ppk])
                s_, sk_ = sg[hc % 2], "sg%d" % (hc % 2)
                P.op("scalar", lambda e, s_=s_, pg=pg, n=n: e.activation(out=s_[:, :n], in_=pg[:, :n], func=AF.Silu), r=[pgk], w=[sk_])
                P.op("vector", lambda e, s_=s_, pu=pu, n=n, hc=hc, t0=t0, g0=g0: e.tensor_tensor(
                    out=hT[:, hc, t0 - g0:t0 - g0 + n], in0=pu[:, :n], in1=s_[:, :n], op=ALU.mult), r=[puk, sk_], w=["hT"])
        for fc in range(KC):
            wb, wk = w2b[w2i % 2], "w2b%d" % (w2i % 2)
            w2i += 1
            P.dma("gpsimd", wb, w2v[:, :, fc * 128:(fc + 1) * 128], wk, w=[wk])
            for (t0, n) in grp:
                ps, psk = g.next_ps()
                for hc in range(44):
                    P.op("tensor", lambda e, ps=ps, hc=hc, t0=t0, n=n, wb=wb, g0=g0: e.matmul(
                        ps[:, :n], wb[:, hc, :], hT[:, hc, t0 - g0:t0 - g0 + n], start=(hc == 0), stop=(hc == 43)), r=[wk, "hT"], w=[psk])
                j = jf(t0)
                xr = xres[fc % 2]
                xk = "xres%d" % (fc % 2)
                P.dma("sync", xr[:, :n], g.XM[fc * 128:(fc + 1) * 128, t0:t0 + n], xk, r=["XM"], w=[xk])
                stg, sk = next_stage(g)
                P.op("vector", lambda e, stg=stg, ps=ps, n=n, fc=fc, j=j, xr=xr: e.scalar_tensor_tensor(
                    out=stg[:, :n], in0=ps[:, :n], scalar=g.mod[l][:, 80 + fc, j:j + 1], in1=xr[:, :n], op0=ALU.mult, op1=ALU.add),
                    r=[psk, xk, "mod%d" % l], w=[sk])
                if last:
                    P.dma("sync", g.outT[fc * 128:(fc + 1) * 128, t0 - NCTX:t0 - NCTX + n], stg[:, :n], sk, r=[sk], w=["outT"])
                else:
                    P.dma("sync", g.xs[fc * 128:(fc + 1) * 128, t0:t0 + n], stg[:, :n], sk, r=[sk], w=["xs"])


def host_inputs(inputs, b):
    f = lambda a: np.ascontiguousarray(a, dtype=np.float32)
    x, ctx, c, c_ctx = inputs["x"], inputs["ctx"], inputs["c"], inputs["c_ctx"]
    m = {}
    m["xT"] = f(np.concatenate([ctx[b], x[b]], axis=0).T)
    cc = np.stack([c[b], c_ctx], axis=-1)
    m["cT"] = f(cc.reshape(KC, 128, 2).transpose(1, 0, 2))
    m["ada_w"] = f(inputs["ada_w"])
    m["ada_bT"] = f(inputs["ada_b"].reshape(DEPTH, 96, 128).transpose(0, 2, 1))
    m["norm1_gT"] = f(inputs["norm1_g"].reshape(DEPTH, KC, 128).transpose(0, 2, 1))
    m["norm2_gT"] = f(inputs["norm2_g"].reshape(DEPTH, KC, 128).transpose(0, 2, 1))
    m["w_in"] = f(inputs["w_in"])
    wi = inputs["w_in"]
    m["w_in_rot"] = f(np.concatenate([wi[:, :, 1056:1088], wi[:, :, 1024:1056]], axis=-1))
    qa, kva = inputs["mla_qa_norm"], inputs["mla_kva_norm"]
    m["mla_gT"] = f(np.concatenate([qa.reshape(DEPTH, 4, 128).transpose(0, 2, 1),
                                    kva.reshape(DEPTH, 4, 128).transpose(0, 2, 1)], axis=-1))
    qn_, kn_ = inputs["mla_q_norm"], inputs["mla_k_norm"]
    hg = np.zeros((DEPTH, 128, 6), np.float32)
    hg[:, :, 0] = qn_[:, :128]
    hg[:, :, 1] = kn_[:, :128]
    hg[:, :64, 2] = qn_[:, 128:192]
    hg[:, :64, 3] = np.concatenate([qn_[:, 160:192], qn_[:, 128:160]], -1)
    hg[:, :64, 4] = kn_[:, 128:192]
    hg[:, :64, 5] = np.concatenate([kn_[:, 160:192], kn_[:, 128:160]], -1)
    m["mla_hg"] = hg
    m["rope_cs"] = rope_tables()
    wq = inputs["mla_w_uq"]
    m["w_uq"] = f(wq)
    wq4 = wq.reshape(DEPTH, 512, 8, 192)
    m["w_uq_rot"] = f(np.concatenate([wq4[..., 160:192], wq4[..., 128:160]], -1).reshape(DEPTH, 512, 512))
    m["w_ukv"] = f(inputs["mla_w_ukv"])
    tri = np.zeros((2, 128, 128), np.float32)
    ii = np.arange(128)
    tri[0] = (ii[:, None] <= ii[None, :])
    tri[1] = (ii[:, None] >= ii[None, :])
    m["tri"] = tri
    m["ident"] = np.eye(128, dtype=np.float32)
    sel4 = np.zeros((4, 4, 128), np.float32)
    for r_ in range(4):
        sel4[r_, r_, :] = 1.0
    m["sel4"] = sel4
    cwm = np.concatenate([inputs["ml_conv_w"], inputs["ml_conv_b"][:, None, :]], axis=1)
    m["ml_cw"] = f(cwm.reshape(DEPTH, 4, 8, 128).transpose(0, 3, 2, 1))
    m["ml_gate_b"] = f(inputs["ml_gate_b"])
    m["mix_g"] = f(inputs["mix_norm_g"])
    m["mix_gT"] = f(inputs["mix_norm_g"].reshape(DEPTH, KC, 128).transpose(0, 2, 1))
    m.update(hyena_tables())
    hcw = np.concatenate([inputs["hy_conv_w"], inputs["hy_conv_b"][:, None, :]], axis=1)
    m["hy_cw"] = f(hcw.reshape(DEPTH, 4, 12, 128).transpose(0, 3, 2, 1))
    m["hy_w1"] = f(inputs["hy_w1"])
    m["hy_b12"] = f(np.stack([inputs["hy_b1"], inputs["hy_b2"]], axis=-1))
    m["hy_w2"] = f(inputs["hy_w2"])
    m["hy_w3"] = f(inputs["hy_w3"])
    m["hy_decay"] = f(inputs["hy_decay"])
    m["hy_skipT"] = f(inputs["hy_skip"].reshape(DEPTH, 2, 4, 128).transpose(0, 3, 1, 2))
    m["w_out"] = f(inputs["w_out"])
    m["ffn_w1"] = f(inputs["ffn_w1"])
    m["ffn_w2"] = f(inputs["ffn_w2"])
    return m


def rope_tables():
    rows = NLAT // 64
    row = np.repeat(np.arange(rows, dtype=np.float32), 64)
    col = np.tile(np.arange(64, dtype=np.float32), rows)
    freqs = (np.float32(10000.0) ** (-np.arange(16, dtype=np.float32) / np.float32(16))).astype(np.float32)
    ang = np.concatenate([row[:, None] * freqs, col[:, None] * freqs], axis=-1).astype(np.float32)
    cos, sin = np.cos(ang).astype(np.float32), np.sin(ang).astype(np.float32)
    out = np.zeros((2, 64, T), np.float32)
    out[0, :, :NCTX] = 1.0
    out[0, :32, NCTX:] = cos.T
    out[0, 32:, NCTX:] = cos.T
    out[1, :32, NCTX:] = -sin.T
    out[1, 32:, NCTX:] = sin.T
    return out


_HY_TABLES = {}


def hyena_tables():
    if _HY_TABLES:
        return _HY_TABLES
    import ml_dtypes
    out = {}
    for L in (4096, 256):
        t = (np.arange(L, dtype=np.float32) / np.float32(L)).astype(np.float32)
        bands = np.arange(1, 17, dtype=np.float32)
        ang = (np.float32(2.0 * math.pi) * t[:, None] * bands).astype(np.float32)
        feats = np.concatenate([t[:, None], np.cos(ang), np.sin(ang)], axis=-1).astype(np.float32)
        out["hy_feats%d" % L] = np.ascontiguousarray(feats.T)
        out["hy_negt%d" % L] = np.ascontiguousarray((-t).reshape(L // 128, 128).T)
        ii = np.arange(L, dtype=np.int64)
        ph = (ii[:, None] * ii[None, :]) % (2 * L)
        th = ph.astype(np.float64) * (2.0 * math.pi / (2 * L))
        cs = np.stack([np.cos(th), np.sin(th)]).astype(np.float32).astype(ml_dtypes.bfloat16)
        out["hy_cs%d" % L] = cs
    alt = np.ones((128, 512), np.float32)
    alt[1::2, :] = -1.0
    alt[0, 1::2] = -1.0
    out["hy_alt"] = alt
    _HY_TABLES.update(out)
    return _HY_TABLES


def kernel(**inputs):
    nc = build()
    inputs = {k: np.asarray(v) for k, v in inputs.items()}
    shared = host_inputs(inputs, 0)
    in_maps = []
    for core in range(8):
        b = core % 4
        m = dict(shared)
        if b != 0:
            mb = host_inputs_batch(inputs, b)
            m.update(mb)
        in_maps.append(m)
    res = run_bass_kernel_spmd(nc, in_maps, core_ids=list(range(8)))
    out = np.stack([np.ascontiguousarray(res.results[b]["outT"].T) for b in range(4)], axis=0)
    return out.astype(np.float32)


def host_inputs_batch(inputs, b):
    f = lambda a: np.ascontiguousarray(a, dtype=np.float32)
    x, ctx, c, c_ctx = inputs["x"], inputs["ctx"], inputs["c"], inputs["c_ctx"]
    m = {}
    m["xT"] = f(np.concatenate([ctx[b], x[b]], axis=0).T)
    cc = np.stack([c[b], c_ctx], axis=-1)
    m["cT"] = f(cc.reshape(KC, 128, 2).transpose(1, 0, 2))
    return m
```

```python
import math
from contextlib import ExitStack

import numpy as np
import concourse.bass as bass
import concourse.mybir as mybir
from concourse.bass_utils import run_bass_kernel_spmd

F32 = mybir.dt.float32
BF16 = mybir.dt.bfloat16
AF = mybir.ActivationFunctionType
ALU = mybir.AluOpType
AX = mybir.AxisListType

ENGS = ["sync", "scalar", "vector", "gpsimd", "tensor"]

D = 2048
KC = 16
NCTX = 256
NLAT = 4096
T = NCTX + NLAT
DEPTH = 2
EPS = 1e-6
N_IN = 4688
FFN_H = 5632
ARENA = 204800
TILES = [(0, 256)] + [(256 + 512 * i, 512) for i in range(8)]
NTILES = [(256 * i, 256) for i in range(17)]


class Prog:
    def __init__(self, nc):
        self.nc = nc
        self.ops = {e: [] for e in ENGS}
        self.last_w = {}
        self.readers = {}
        self.dma_cnt = {}
        self.fence = set()
        self.sealed = {}
        self.strict_same = {"scalar": True, "vector": True, "gpsimd": True,
                            "tensor": False, "sync": False}

    def _deps(self, r, w):
        deps = set()
        for k in r:
            lw = self.last_w.get(k)
            if lw is not None:
                deps.add(lw)
        for k in w:
            lw = self.last_w.get(k)
            if lw is not None:
                deps.add(lw)
            for rd in self.readers.get(k, ()):
                deps.add(rd)
        return deps

    def _commit(self, ref, r, w):
        for k in r:
            self.readers.setdefault(k, []).append(ref)
        for k in w:
            self.last_w[k] = ref
            self.readers[k] = []

    def barrier(self):
        f = set()
        for e in ENGS:
            for i in range(len(self.ops[e]) - 1, -1, -1):
                if self.ops[e][i]["dma"] is None:
                    f.add(("c", e, i))
                    break
        for k, c in self.dma_cnt.items():
            f.add(("d", k, c))
        self.fence = f

    def op(self, eng, fn, r=(), w=()):
        deps = self._deps(r, w) | self.fence
        ref = ("c", eng, len(self.ops[eng]))
        self.ops[eng].append(dict(fn=fn, deps=deps, ref=ref, dma=None))
        self._commit(ref, r, w)
        return ref

    def seal(self, key="pc"):
        c = self.dma_cnt.get(key, 0)
        for k, ref in list(self.last_w.items()):
            if ref[0] == "d" and ref[1] == key:
                self.last_w[k] = ("d", key, c)
        self.sealed[key] = c

    def dma(self, eng, out, in_, key, r=(), w=(), **kw):
        deps = self._deps(r, w) | self.fence
        if self.sealed.get(key):
            deps = deps | {("d", key, self.sealed[key])}
        cnt = self.dma_cnt.get(key, 0) + 1
        self.dma_cnt[key] = cnt
        ref = ("d", key, cnt)
        self.ops[eng].append(dict(fn=lambda e: e.dma_start(out=out, in_=in_, **kw),
                                  deps=deps, ref=ref, dma=key))
        self._commit(ref, r, w)
        return ref

    def emit(self):
        nc = self.nc
        signaled = {e: set() for e in ENGS}
        for e in ENGS:
            for o in self.ops[e]:
                for d in o["deps"]:
                    if d[0] == "c":
                        if d[1] == e and not self.strict_same[e]:
                            continue
                        signaled[d[1]].add(d[2])
        sigcount = {}
        for e in ENGS:
            c = 0
            m = {}
            for i, o in enumerate(self.ops[e]):
                if o["dma"] is None and i in signaled[e]:
                    c += 1
                    m[i] = c
            sigcount[e] = m
        with ExitStack() as st:
            csem = {e: st.enter_context(nc.semaphore("s_" + e)) for e in ENGS}
            dsem = {k: st.enter_context(nc.semaphore("d_%d" % i))
                    for i, k in enumerate(self.dma_cnt)}
            block = st.enter_context(nc.Block())

            def make(eng):
                def body(e):
                    waited = {}
                    for i, o in enumerate(self.ops[eng]):
                        waits = {}
                        for d in o["deps"]:
                            if d[0] == "c":
                                if d[1] == eng and not self.strict_same[eng]:
                                    continue
                                s, v, kk = csem[d[1]], sigcount[d[1]][d[2]], ("c", d[1])
                            else:
                                s, v, kk = dsem[d[1]], 16 * d[2], ("d", d[1])
                            if waited.get(kk, 0) >= v:
                                continue
                            if kk not in waits or waits[kk][1] < v:
                                waits[kk] = (s, v)
                        for kk, (s, v) in waits.items():
                            e.wait_ge(s, v)
                            waited[kk] = v
                        ins = o["fn"](e)
                        if o["dma"] is not None:
                            ins.then_inc(dsem[o["dma"]], 16)
                        elif i in sigcount[eng]:
                            ins.then_inc(csem[eng], 1)
                    if eng == "sync":
                        for k, c in self.dma_cnt.items():
                            e.wait_ge(dsem[k], 16 * c)
                return body

            block.sync(make("sync"))
            block.scalar(make("scalar"))
            block.vector(make("vector"))
            block.gpsimd(make("gpsimd"))
            block.tensor(make("tensor"))


class Ctx:
    pass


def build(stop_after="all", dbg=()):
    nc = bass.Bass("TRN2", target_bir_lowering=False)
    P = Prog(nc)
    g = Ctx()
    g.nc, g.P = nc, P
    g.din = lambda name, shape, dt=F32: nc.dram_tensor(name, list(shape), dt, kind="ExternalInput").ap()
    g.dout = lambda name, shape, dt=F32: nc.dram_tensor(name, list(shape), dt, kind="ExternalOutput").ap()
    dbgset = set(dbg)

    def dscr(name, shape, dt=F32):
        kind = "ExternalOutput" if name in dbgset else "Internal"
        return nc.dram_tensor(name, list(shape), dt, kind=kind).ap()
    g.dscr = dscr

    with ExitStack() as st:
        g.st = st
        g.sb = lambda name, shape, dt=F32: st.enter_context(nc.sbuf_tensor(name, list(shape), dt))
        g.ps = [st.enter_context(nc.psum_tensor("ps%d" % i, [128, 512], F32)) for i in range(8)]
        g.ps_i = 0

        g.ps_rot = list(range(8))

        def next_ps():
            g.ps_i = (g.ps_i + 1) % len(g.ps_rot)
            i = g.ps_rot[g.ps_i]
            return g.ps[i], "ps%d" % i
        g.next_ps = next_ps
        g.psb = lambda i: (g.ps[i], "ps%d" % i)

        g.arena = st.enter_context(nc.sbuf_tensor("arena", [128, ARENA // 4], F32))
        g.arena_off = 0

        def carve(shape, dt=F32):
            esz = 2 if dt == BF16 else 4
            free = int(np.prod(shape[1:]))
            nbytes = (free * esz + 63) // 64 * 64
            off = g.arena_off
            assert off + nbytes <= ARENA, ("arena overflow", off, nbytes)
            g.arena_off = off + nbytes
            words = (free * esz + 3) // 4
            v = g.arena[:, off // 4:off // 4 + words]
            if dt == BF16:
                v = v.bitcast(BF16)
            elif dt != F32:
                v = v.bitcast(dt)
            v = v[:shape[0], :free]
            if len(shape) == 3:
                v = v.rearrange("p (a b) -> p a b", b=shape[2])
            return v
        g.carve = carve

        def new_phase():
            g.arena_off = 0
            P.barrier()
        g.new_phase = new_phase

        declare_io(g)
        consts(g)
        for l in range(DEPTH):
            phase_mod(g, l)
        for l in range(DEPTH):
            if stop_after == "mod":
                break
            phase_a(g, l)
            if stop_after == "a%d" % l:
                break
            phase_mla(g, l)
            if stop_after == "mla%d" % l:
                break
            if "noattn" not in dbgset:
                phase_attn(g, l)
            if stop_after == "attn%d" % l:
                break
            if "nomlstm" not in dbgset:
                phase_mlstm(g, l)
            if stop_after == "mlstm%d" % l:
                break
            hyena_seq(g, l, 4096, NCTX)
            if l == 0 and "nohyctx" not in dbgset:
                hyena_seq(g, l, 256, 0)
            if stop_after == "hy%d" % l:
                break
            phase_c(g, l)
            if stop_after == "c%d" % l:
                break
        P.emit()
    return nc


def declare_io(g):
    g.xT = g.din("xT", [D, T])
    g.cT = g.din("cT", [128, KC, 2])
    g.ada_w = g.din("ada_w", [DEPTH, D, 6 * D])
    g.ada_bT = g.din("ada_bT", [DEPTH, 128, 96])
    g.norm1_gT = g.din("norm1_gT", [DEPTH, 128, KC])
    g.norm2_gT = g.din("norm2_gT", [DEPTH, 128, KC])
    g.w_in = g.din("w_in", [DEPTH, D, N_IN])
    g.w_in_rot = g.din("w_in_rot", [DEPTH, D, 64])
    g.PT = [g.dscr("PT%d" % l, [N_IN + 64, T]) for l in range(DEPTH)]
    g.PTM = [g.dscr("PTM%d" % l, [T, 1040]) for l in range(DEPTH)]
    g.modd = g.dscr("modd", [DEPTH, 128, 96 * 2])
    g.mla_gT = g.din("mla_gT", [DEPTH, 128, 8])
    g.mla_hg = g.din("mla_hg", [DEPTH, 128, 6])
    g.rope_cs = g.din("rope_cs", [2, 64, T])
    g.w_uq = g.din("w_uq", [DEPTH, 512, 1536])
    g.w_uq_rot = g.din("w_uq_rot", [DEPTH, 512, 512])
    g.w_ukv = g.din("w_ukv", [DEPTH, 512, 2048])
    g.QT = g.dscr("QT", [8, 192, T], BF16)
    g.tri = g.din("tri", [2, 128, 128])
    g.ident = g.din("ident", [128, 128])
    g.sel4 = g.din("sel4", [4, 4, 128])
    g.ml_cw = g.din("ml_cw", [DEPTH, 128, 8, 4])
    g.ml_gate_b = g.din("ml_gate_b", [DEPTH, 16])
    g.mix_g = g.din("mix_g", [DEPTH, D])
    g.mix_gT = g.din("mix_gT", [DEPTH, 128, KC])
    g.HD = [g.dscr("HD%d" % i, [T, 512]) for i in range(2)]
    g.hy_feats = {4096: g.din("hy_feats4096", [33, 4096]), 256: g.din("hy_feats256", [33, 256])}
    g.hy_negt = {4096: g.din("hy_negt4096", [128, 32]), 256: g.din("hy_negt256", [128, 2])}
    g.hy_cs = {4096: g.din("hy_cs4096", [2, 4096, 4096], BF16), 256: g.din("hy_cs256", [2, 256, 256], BF16)}
    g.hy_alt = g.din("hy_alt", [128, 512])
    g.hy_cw = g.din("hy_cw", [DEPTH, 128, 12, 4])
    g.hy_w1 = g.din("hy_w1", [DEPTH, 33, 64])
    g.hy_b12 = g.din("hy_b12", [DEPTH, 64, 2])
    g.hy_w2 = g.din("hy_w2", [DEPTH, 64, 64])
    g.hy_w3 = g.din("hy_w3", [DEPTH, 64, 2048])
    g.hy_decay = g.din("hy_decay", [DEPTH, 2048])
    g.hy_skipT = g.din("hy_skipT", [DEPTH, 128, 2, 4])
    g.HS = g.dscr("HS", [2, 4096, 1024], BF16)
    g.KF = g.dscr("KF", [2, 4097, 1024])
    g.HV = g.dscr("HV", [1536, 4096])
    g.HZ2 = g.dscr("HZ2", [512, 4096])
    g.w_out = g.din("w_out", [DEPTH, D, D])
    g.ffn_w1 = g.din("ffn_w1", [DEPTH, D, 2 * FFN_H])
    g.ffn_w2 = g.din("ffn_w2", [DEPTH, FFN_H, D])
    g.XM = g.dscr("XM", [D, T])
    g.W1B = g.dscr("W1B", [44, 128, 4096], BF16)
    g.W2B = g.dscr("W2B", [16, 128, 5632], BF16)
    g.xs = g.dscr("xs", [D, T])
    g.outT = g.dout("outT", [D, NLAT])
    g.KT = g.dscr("KT", [8, 192, T], BF16)
    g.VTM = g.dscr("VTM", [T, 1024], BF16)
    g.CATT = g.dscr("CATT", [D, T])


def consts(g):
    P, nc = g.P, g.nc
    g.ones_bf = g.sb("ones_bf", [128, 128], BF16)
    P.op("vector", lambda e: e.memset(g.ones_bf[:], 1.0), w=["ones_bf"])
    g.eps_t = g.sb("eps_t", [128, 1], F32)
    P.op("vector", lambda e: e.memset(g.eps_t[:], EPS), w=["eps_t"])
    g.mod = [g.sb("mod%d" % l, [128, 96, 2]) for l in range(DEPTH)]
    g.gs1 = [g.sb("gs1_%d" % l, [128, KC, 2]) for l in range(DEPTH)]
    g.gs2 = [g.sb("gs2_%d" % l, [128, KC, 2]) for l in range(DEPTH)]
    g.n1g = g.sb("n1g", [128, DEPTH, KC])
    g.n2g = g.sb("n2g", [128, DEPTH, KC])
    P.dma("sync", g.n1g[:], g.norm1_gT.rearrange("l p k -> p l k"), "pc", w=["n1g"])
    P.dma("sync", g.n2g[:], g.norm2_gT.rearrange("l p k -> p l k"), "pc", w=["n2g"])
    g.scT = g.sb("scT", [128, KC, 2])
    g.cTs = g.sb("cTs", [128, KC, 2])
    P.dma("sync", g.cTs[:], g.cT[:, :, :], "pc", w=["cTs"])
    P.seal()
    P.op("scalar", lambda e: e.activation(out=g.scT[:], in_=g.cTs[:], func=AF.Silu), r=["cTs"], w=["scT"])
    g.scb = g.sb("scb", [128, KC, 2], BF16)
    P.op("vector", lambda e: e.tensor_copy(out=g.scb[:], in_=g.scT[:]), r=["scT"], w=["scb"])


def next_stage(g):
    i = g.stage_i
    g.stage_i = (i + 1) % len(g.stage)
    return g.stage[i], "stage%d" % i


def carve_common(g, wcols):
    g.wbuf = [g.carve([128, KC * wcols], BF16) for i in range(2)]
    g.w_i = 0
    g.stage = [g.carve([128, 512], F32) for i in range(4)]
    g.stage_i = 0


def phase_mod(g, l):
    P, nc = g.P, g.nc
    g.new_phase()
    carve_common(g, 512)
    adab = g.carve([128, 96], F32)
    ak = "adab"
    P.dma("sync", adab, g.ada_bT[l], "pc", w=[ak])
    P.seal()
    wv = g.ada_w[l].rearrange("(kc p) n -> p kc n", p=128)
    ps, psk = g.next_ps()
    psv = ps[:, 0:192].rearrange("p (c j) -> p c j", j=2)
    for sbk in range(24):
        wb = g.wbuf[sbk % 2]
        wk = "wbuf%d" % (sbk % 2)
        wt = wb.rearrange("p (kc n) -> p kc n", n=512)
        P.dma("gpsimd", wt, wv[:, :, sbk * 512:(sbk + 1) * 512], wk, w=[wk])
        for j in range(4):
            cb = sbk * 4 + j
            for kc in range(KC):
                P.op("tensor", lambda e, wt=wt, j=j, kc=kc, cb=cb: e.matmul(
                    psv[:, cb, :], wt[:, kc, j * 128:(j + 1) * 128], g.scb[:, kc, :],
                    start=(kc == 0), stop=(kc == KC - 1)),
                    r=[wk, "scb"], w=[psk])
    mk = "mod%d" % l
    for j in range(2):
        P.op("vector", lambda e, j=j: e.tensor_tensor(out=g.mod[l][:, :, j], in0=psv[:, :, j], in1=adab,
                                                     op=ALU.add), r=[psk, ak], w=[mk])
    for j in range(2):
        P.op("vector", lambda e, j=j: e.scalar_tensor_tensor(
            out=g.gs1[l][:, :, j], in0=g.mod[l][:, 16:32, j], scalar=1.0, in1=g.n1g[:, l, :],
            op0=ALU.add, op1=ALU.mult), r=[mk, "n1g"], w=["gs1_%d" % l])
        P.op("vector", lambda e, j=j: e.scalar_tensor_tensor(
            out=g.gs2[l][:, :, j], in0=g.mod[l][:, 64:80, j], scalar=1.0, in1=g.n2g[:, l, :],
            op0=ALU.add, op1=ALU.mult), r=[mk, "n2g"], w=["gs2_%d" % l])
    P.dma("sync", g.modd[l], g.mod[l][:].rearrange("p c j -> p (c j)"), mk, r=[mk], w=["modd"])


def rms_bufs(g, NT=128):
    return (g.carve([128, KC, NT], F32), g.carve([128, KC, NT], BF16), g.carve([128, NT], F32),
            [g.carve([128, NT], F32) for i in range(2)])


def rms_modulate(g, src_T, l, gs, gskey, shift_lo, dstT, dkey, tiles, jfn, bufs=None, srckey=None, dst_t0=0):
    P, nc = g.P, g.nc
    xt, sq, rstd, tmps = bufs if bufs is not None else rms_bufs(g)
    rk = [srckey] if srckey else []
    srcv = src_T.rearrange("(kc p) t -> p kc t", p=128)
    for (t0, n) in tiles:
        j = jfn(t0)
        P.dma("sync", xt[:, :, :n], srcv[:, :, t0:t0 + n], "xt0", r=rk, w=["xt0"])
        P.op("scalar", lambda e, n=n: e.activation(out=sq[:, :, :n], in_=xt[:, :, :n], func=AF.Square),
             r=["xt0"], w=["sq"])
        ps, psk = g.next_ps()
        for kc in range(KC):
            P.op("tensor", lambda e, kc=kc, n=n, ps=ps: e.matmul(ps[:, :n], g.ones_bf[:], sq[:, kc, :n],
                                                              start=(kc == 0), stop=(kc == KC - 1)),
                 r=["sq", "ones_bf"], w=[psk])
        P.op("scalar", lambda e, n=n, ps=ps: e.activation(out=rstd[:, :n], in_=ps[:, :n], func=AF.Sqrt,
                                                       bias=g.eps_t[:, 0:1], scale=1.0 / D),
             r=[psk, "eps_t"], w=["rstd"])
        P.op("vector", lambda e, n=n: e.reciprocal(out=rstd[:, :n], in_=rstd[:, :n]), r=["rstd"], w=["rstd"])
        for kc in range(KC):
            tmp = tmps[kc % 2]
            tk = "tmp%d" % (kc % 2)
            P.op("vector", lambda e, kc=kc, n=n, tmp=tmp, j=j: e.scalar_tensor_tensor(
                out=tmp[:, :n], in0=xt[:, kc, :n], scalar=gs[:, kc, j:j + 1], in1=rstd[:, :n],
                op0=ALU.mult, op1=ALU.mult), r=["xt0", "rstd", gskey], w=[tk])
            P.op("scalar", lambda e, kc=kc, n=n, tmp=tmp, j=j, t0=t0: e.activation(
                out=dstT[:, kc, t0 - dst_t0:t0 - dst_t0 + n], in_=tmp[:, :n], func=AF.Identity,
                bias=g.mod[l][:, shift_lo + kc, j:j + 1], scale=1.0), r=[tk, "mod%d" % l], w=[dkey])


def proj_fm(g, w_view, col0, ncols, actT, akey, tiles, evac):
    P = g.P
    nkc = w_view.shape[1]
    wi = g.w_i
    g.w_i = (wi + 1) % 2
    wk = "wbuf%d" % wi
    wt = g.wbuf[wi][:, :nkc * ncols].rearrange("p (kc n) -> p kc n", n=ncols)
    P.dma("gpsimd", wt, w_view[:, :, col0:col0 + ncols], wk, w=[wk])
    for c0 in range(0, ncols, 128):
        m = min(128, ncols - c0)
        for (t0, n) in tiles:
            ps, psk = g.next_ps()
            for kc in range(nkc):
                P.op("tensor", lambda e, kc=kc, c0=c0, m=m, t0=t0, n=n, ps=ps: e.matmul(
                    ps[:m, :n], wt[:, kc, c0:c0 + m], actT[:, kc, t0:t0 + n],
                    start=(kc == 0), stop=(kc == nkc - 1)), r=[wk, akey], w=[psk])
            evac(col0 + c0, m, t0, n, ps, psk)


def phase_a(g, l):
    P, nc = g.P, g.nc
    g.new_phase()
    xnT = g.carve([128, KC, T], BF16)
    carve_common(g, 256)
    src = g.xT if l == 0 else g.xs
    rms_modulate(g, src, l, g.gs1[l], "gs1_%d" % l, 0, xnT, "xnT", [(256 * i, 256) for i in range(17)],
                 lambda t0: 1 if t0 < NCTX else 0, bufs=rms_bufs(g, 256), srckey="xs")
    wv = g.w_in[l].rearrange("(kc p) n -> p kc n", p=128)
    wrv = g.w_in_rot[l].rearrange("(kc p) n -> p kc n", p=128)
    PT = g.PT[l]
    cnt = [0]

    def evac_to(PTrow0):
        def evac(cb, m, t0, n, ps, psk):
            stg, sk = next_stage(g)
            if cnt[0] % 2 == 0:
                P.op("vector", lambda e: e.tensor_copy(out=stg[:m, :n], in_=ps[:m, :n]), r=[psk], w=[sk])
            else:
                P.op("scalar", lambda e: e.copy(out=stg[:m, :n], in_=ps[:m, :n]), r=[psk], w=[sk])
            cnt[0] += 1
            P.dma("sync", PT[PTrow0 + cb:PTrow0 + cb + m, t0:t0 + n], stg[:m, :n], sk, r=[sk], w=["PT%d" % l])
        return evac

    blocks = [(c, 256) for c in range(0, 1024, 256)] + [(1024, 64)] + [(c, 256) for c in range(1088, 3648, 256)]
    for (c0, ncol) in blocks:
        proj_fm(g, wv, c0, ncol, xnT, "xnT", TILES, evac_to(0))
    proj_fm(g, wrv, 0, 64, xnT, "xnT", TILES, evac_to(N_IN))
    for (c0, ncol) in [(3648, 256), (3904, 256), (4160, 256), (4416, 256), (4672, 16)]:
        wi = g.w_i
        g.w_i = (wi + 1) % 2
        wk = "wbuf%d" % wi
        wt = g.wbuf[wi][:, :KC * ncol].rearrange("p (kc n) -> p kc n", n=ncol)
        P.dma("gpsimd", wt, wv[:, :, c0:c0 + ncol], wk, w=[wk])
        for c in range(34):
            ps, psk = g.next_ps()
            for kc in range(KC):
                P.op("tensor", lambda e, kc=kc, c=c, ps=ps, wt=wt, ncol=ncol: e.matmul(
                    ps[:, :ncol], xnT[:, kc, c * 128:(c + 1) * 128], wt[:, kc, :],
                    start=(kc == 0), stop=(kc == KC - 1)), r=[wk, "xnT"], w=[psk])
            stg, sk = next_stage(g)
            if c % 2 == 0:
                P.op("vector", lambda e, ps=ps, stg=stg, ncol=ncol: e.tensor_copy(out=stg[:, :ncol], in_=ps[:, :ncol]),
                     r=[psk], w=[sk])
            else:
                P.op("scalar", lambda e, ps=ps, stg=stg, ncol=ncol: e.copy(out=stg[:, :ncol], in_=ps[:, :ncol]),
                     r=[psk], w=[sk])
            P.dma("sync", g.PTM[l][c * 128:(c + 1) * 128, c0 - 3648:c0 - 3648 + ncol], stg[:, :ncol], sk,
                  r=[sk], w=["PTM%d" % l])


def rstd_op(g, ps, psk, n, nfeat, rstd, rkey, rows=128):
    P = g.P
    P.op("scalar", lambda e: e.activation(out=rstd[:rows, :n], in_=ps[:rows, :n], func=AF.Sqrt,
                                          bias=g.eps_t[:rows, 0:1], scale=1.0 / nfeat), r=[psk, "eps_t"], w=[rkey])
    P.op("vector", lambda e: e.reciprocal(out=rstd[:rows, :n], in_=rstd[:rows, :n]), r=[rkey], w=[rkey])


def phase_mla(g, l):
    P = g.P
    g.new_phase()
    g.ps_rot = list(range(8))
    PT = g.PT[l]
    cqn = g.carve([128, 4, T], BF16)
    ckvn = g.carve([128, 4, T], BF16)
    cos2 = g.carve([64, T], F32)
    sin2 = g.carve([64, T], F32)
    KR = g.carve([64, T], F32)
    sqr = g.carve([64, T], BF16)
    xt = g.carve([128, 4, 512], F32)
    sq = g.carve([128, 4, 512], BF16)
    rstd = g.carve([128, 512], F32)
    gT = g.carve([128, 8], F32)
    hg = g.carve([128, 6], F32)
    wq = g.carve([128, 4, 256], BF16)
    wk = g.carve([128, 4, 128], BF16)
    wvv = g.carve([128, 4, 1024], BF16)
    stb = [g.carve([128, 512], BF16) for i in range(4)]
    tA = g.carve([64, 512], F32)
    tB = g.carve([64, 512], F32)
    P.dma("sync", gT, g.mla_gT[l], "pc", w=["gT"])
    P.dma("sync", hg, g.mla_hg[l], "pc", w=["hg"])
    P.dma("sync", cos2, g.rope_cs[0], "pc", w=["cos2"])
    P.dma("sync", sin2, g.rope_cs[1], "pc", w=["sin2"])
    P.seal()
    sc = 192.0 ** -0.5
    for c in (0, 2, 3):
        P.op("vector", lambda e, c=c: e.tensor_scalar(out=hg[:, c:c + 1], in0=hg[:, c:c + 1], scalar1=sc, scalar2=None,
                                                      op0=ALU.mult), r=["hg"], w=["hg"])
    for (r0, dst, dk, gc) in [(0, cqn, "cqn", 0), (512, ckvn, "ckvn", 4)]:
        srcv = PT[r0:r0 + 512, :].rearrange("(kc p) t -> p kc t", p=128)
        for (t0, n) in TILES:
            P.dma("sync", xt[:, :, :n], srcv[:, :, t0:t0 + n], "xt0", r=["PT%d" % l], w=["xt0"])
            P.op("scalar", lambda e, n=n: e.activation(out=sq[:, :, :n], in_=xt[:, :, :n], func=AF.Square),
                 r=["xt0"], w=["sq"])
            ps, psk = g.next_ps()
            for kc in range(4):
                P.op("tensor", lambda e, kc=kc, n=n, ps=ps: e.matmul(ps[:, :n], g.ones_bf[:], sq[:, kc, :n],
                                                                  start=(kc == 0), stop=(kc == 3)),
                     r=["sq", "ones_bf"], w=[psk])
            rstd_op(g, ps, psk, n, 512, rstd, "rstd")
            for kc in range(4):
                P.op("vector", lambda e, kc=kc, n=n, t0=t0, dst=dst, gc=gc: e.scalar_tensor_tensor(
                    out=dst[:, kc, t0:t0 + n], in0=xt[:, kc, :n], scalar=gT[:, gc + kc:gc + kc + 1], in1=rstd[:, :n],
                    op0=ALU.mult, op1=ALU.mult), r=["xt0", "rstd", "gT"], w=[dk])
    for (t0, n) in TILES:
        P.dma("sync", xt[:64, 0, :n], PT[1024:1088, t0:t0 + n], "xt0", r=["PT%d" % l], w=["xt0"])
        P.dma("sync", xt[:64, 1, :n], PT[N_IN:N_IN + 64, t0:t0 + n], "xt0", r=["PT%d" % l], w=["xt0"])
        P.op("scalar", lambda e, n=n, t0=t0: e.activation(out=sqr[:, t0:t0 + n], in_=xt[:64, 0, :n], func=AF.Square),
             r=["xt0"], w=["sqr"])
        P.op("vector", lambda e, n=n, t0=t0: e.scalar_tensor_tensor(
            out=KR[:, t0:t0 + n], in0=xt[:64, 0, :n], scalar=hg[:64, 4:5], in1=cos2[:, t0:t0 + n],
            op0=ALU.mult, op1=ALU.mult), r=["xt0", "hg", "cos2"], w=["KR"])
        P.op("vector", lambda e, n=n, t0=t0: e.scalar_tensor_tensor(
            out=tA[:, :n], in0=xt[:64, 1, :n], scalar=hg[:64, 5:6], in1=sin2[:, t0:t0 + n],
            op0=ALU.mult, op1=ALU.mult), r=["xt0", "hg", "sin2"], w=["tA"])
        P.op("vector", lambda e, n=n, t0=t0: e.tensor_tensor(out=KR[:, t0:t0 + n], in0=KR[:, t0:t0 + n], in1=tA[:, :n],
                                                           op=ALU.add), r=["tA", "KR"], w=["KR"])
    wuq = g.w_uq[l].rearrange("(kc p) n -> p kc n", p=128)
    wuqr = g.w_uq_rot[l].rearrange("(kc p) n -> p kc n", p=128)
    wukv = g.w_ukv[l].rearrange("(kc p) n -> p kc n", p=128)
    qtiles = TILES if l == 0 else TILES[1:]
    si = [0]

    def nstb():
        si[0] = (si[0] + 1) % 4
        return stb[si[0]], "stb%d" % si[0]

    for h in range(8):
        P.dma("gpsimd", wq[:, :, 0:192], wuq[:, :, h * 192:(h + 1) * 192], "wq", w=["wq"])
        P.dma("gpsimd", wq[:, :, 192:256], wuqr[:, :, h * 64:(h + 1) * 64], "wq", w=["wq"])
        for (t0, n) in qtiles:
            psn, kn_ = g.next_ps()
            psr, kr_ = g.next_ps()
            pst, kt_ = g.next_ps()
            for (ps, pk, c0, m) in [(psn, kn_, 0, 128), (psr, kr_, 128, 64), (pst, kt_, 192, 64)]:
                for kc in range(4):
                    P.op("tensor", lambda e, ps=ps, c0=c0, m=m, kc=kc, t0=t0, n=n: e.matmul(
                        ps[:m, :n], wq[:, kc, c0:c0 + m], cqn[:, kc, t0:t0 + n], start=(kc == 0), stop=(kc == 3)),
                        r=["wq", "cqn"], w=[pk])
            P.op("scalar", lambda e, n=n, psn=psn: e.activation(out=sq[:, 0, :n], in_=psn[:, :n], func=AF.Square),
                 r=[kn_], w=["sq"])
            P.op("scalar", lambda e, n=n, psr=psr: e.activation(out=sq[:64, 1, :n], in_=psr[:64, :n], func=AF.Square),
                 r=[kr_], w=["sq"])
            pss, ks_ = g.next_ps()
            P.op("tensor", lambda e, n=n, pss=pss: e.matmul(pss[:, :n], g.ones_bf[:], sq[:, 0, :n], start=True, stop=False),
                 r=["sq", "ones_bf"], w=[ks_])
            P.op("tensor", lambda e, n=n, pss=pss: e.matmul(pss[:, :n], g.ones_bf[:64, :], sq[:64, 1, :n], start=False, stop=True),
                 r=["sq", "ones_bf"], w=[ks_])
            rstd_op(g, pss, ks_, n, 192, rstd, "rstd")
            o1, o1k = nstb()
            P.op("vector", lambda e, n=n, psn=psn, o1=o1: e.scalar_tensor_tensor(
                out=o1[:, :n], in0=psn[:, :n], scalar=hg[:, 0:1], in1=rstd[:, :n], op0=ALU.mult, op1=ALU.mult),
                r=[kn_, "hg", "rstd"], w=[o1k])
            P.dma("sync", g.QT[h, 0:128, t0:t0 + n], o1[:, :n], o1k, r=[o1k], w=["QT"])
            P.op("vector", lambda e, n=n, psr=psr, t0=t0: e.scalar_tensor_tensor(
                out=tA[:, :n], in0=psr[:64, :n], scalar=hg[:64, 2:3], in1=cos2[:, t0:t0 + n], op0=ALU.mult, op1=ALU.mult),
                r=[kr_, "hg", "cos2"], w=["tA"])
            P.op("vector", lambda e, n=n, pst=pst, t0=t0: e.scalar_tensor_tensor(
                out=tB[:, :n], in0=pst[:64, :n], scalar=hg[:64, 3:4], in1=sin2[:, t0:t0 + n], op0=ALU.mult, op1=ALU.mult),
                r=[kt_, "hg", "sin2"], w=["tB"])
            P.op("vector", lambda e, n=n: e.tensor_tensor(out=tA[:, :n], in0=tA[:, :n], in1=tB[:, :n], op=ALU.add),
                 r=["tA", "tB"], w=["tA"])
            o2, o2k = nstb()
            P.op("vector", lambda e, n=n, o2=o2: e.tensor_tensor(out=o2[:64, :n], in0=tA[:, :n], in1=rstd[:64, :n],
                                                              op=ALU.mult), r=["tA", "rstd"], w=[o2k])
            P.dma("sync", g.QT[h, 128:192, t0:t0 + n], o2[:64, :n], o2k, r=[o2k], w=["QT"])
    for h in range(8):
        P.dma("gpsimd", wk[:, :, :], wukv[:, :, h * 256:h * 256 + 128], "wk", w=["wk"])
        for (t0, n) in TILES:
            psn, kn_ = g.next_ps()
            for kc in range(4):
                P.op("tensor", lambda e, psn=psn, kc=kc, t0=t0, n=n: e.matmul(
                    psn[:, :n], wk[:, kc, :], ckvn[:, kc, t0:t0 + n], start=(kc == 0), stop=(kc == 3)),
                    r=["wk", "ckvn"], w=[kn_])
            P.op("scalar", lambda e, n=n, psn=psn: e.activation(out=sq[:, 0, :n], in_=psn[:, :n], func=AF.Square),
                 r=[kn_], w=["sq"])
            pss, ks_ = g.next_ps()
            P.op("tensor", lambda e, n=n, pss=pss: e.matmul(pss[:, :n], g.ones_bf[:], sq[:, 0, :n], start=True, stop=False),
                 r=["sq", "ones_bf"], w=[ks_])
            P.op("tensor", lambda e, n=n, pss=pss, t0=t0: e.matmul(pss[:, :n], g.ones_bf[:64, :], sqr[:, t0:t0 + n],
                                                                 start=False, stop=True),
                 r=["sqr", "ones_bf"], w=[ks_])
            rstd_op(g, pss, ks_, n, 192, rstd, "rstd")
            o1, o1k = nstb()
            P.op("vector", lambda e, n=n, psn=psn, o1=o1: e.scalar_tensor_tensor(
                out=o1[:, :n], in0=psn[:, :n], scalar=hg[:, 1:2], in1=rstd[:, :n], op0=ALU.mult, op1=ALU.mult),
                r=[kn_, "hg", "rstd"], w=[o1k])
            P.dma("sync", g.KT[h, 0:128, t0:t0 + n], o1[:, :n], o1k, r=[o1k], w=["KT"])
            o2, o2k = nstb()
            P.op("vector", lambda e, n=n, o2=o2, t0=t0: e.tensor_tensor(out=o2[:64, :n], in0=KR[:, t0:t0 + n],
                                                                     in1=rstd[:64, :n], op=ALU.mult),
                 r=["KR", "rstd"], w=[o2k])
            P.dma("sync", g.KT[h, 128:192, t0:t0 + n], o2[:64, :n], o2k, r=[o2k], w=["KT"])
    for kc in range(4):
        P.dma("gpsimd", wvv[:, kc, :].rearrange("p (h x) -> p h x", x=128),
              g.w_ukv[l][kc * 128:(kc + 1) * 128, :].rearrange("p (h x) -> p h x", x=256)[:, :, 128:256], "wvv", w=["wvv"])
    for c in range(34):
        for half in range(2):
            ps, psk = g.next_ps()
            for kc in range(4):
                P.op("tensor", lambda e, ps=ps, kc=kc, c=c, half=half: e.matmul(
                    ps[:, :], ckvn[:, kc, c * 128:(c + 1) * 128], wvv[:, kc, half * 512:(half + 1) * 512],
                    start=(kc == 0), stop=(kc == 3)), r=["wvv", "ckvn"], w=[psk])
            o1, o1k = nstb()
            if half == 0:
                P.op("vector", lambda e, ps=ps, o1=o1: e.tensor_copy(out=o1[:, :], in_=ps[:, :]), r=[psk], w=[o1k])
            else:
                P.op("scalar", lambda e, ps=ps, o1=o1: e.copy(out=o1[:, :], in_=ps[:, :]), r=[psk], w=[o1k])
            P.dma("sync", g.VTM[c * 128:(c + 1) * 128, half * 512:(half + 1) * 512], o1[:, :], o1k, r=[o1k], w=["VTM"])


def phase_attn(g, l):
    P = g.P
    g.new_phase()
    g.ps_rot = [0, 1, 2, 3]
    kn = g.carve([128, T], BF16)
    kr = g.carve([64, T], BF16)
    qn = g.carve([128, T], BF16)
    qr = g.carve([64, T], BF16)
    vh = g.carve([128, 34, 128], BF16)
    ebuf = [g.carve([128, 512], BF16) for i in range(4)]
    rden = g.carve([128, 512], F32)
    ost = [g.carve([128, 512], F32) for i in range(2)]
    qtiles = [(t0, n, 34) for (t0, n) in TILES[1:]]
    if l == 0:
        qtiles = [(0, 256, 2)] + qtiles
    ti = 0
    cv = [g.carve([128, 5632], BF16) for i in range(2)]
    w1v = g.ffn_w1[l].rearrange("(kc p) n -> p kc n", p=128)
    w2v = g.ffn_w2[l].rearrange("(kc p) n -> p kc n", p=128)
    units = [("w1", hc) for hc in range(44)] + [("w2", fc) for fc in range(16)]
    pend = [None]
    ui = [0]

    def conv_unit():
        i = ui[0]
        if i < len(units):
            kind, j = units[i]
            buf, bk = cv[i % 2], "cv%d" % (i % 2)
            if kind == "w1":
                v = buf[:, 0:4096].rearrange("p (kc n) -> p kc n", n=256)
                P.dma("gpsimd", v[:, :, 0:128], w1v[:, :, j * 128:(j + 1) * 128], bk, w=[bk])
                P.dma("gpsimd", v[:, :, 128:256], w1v[:, :, FFN_H + j * 128:FFN_H + (j + 1) * 128], bk, w=[bk])
                st_ = (g.W1B[j], buf[:, 0:4096], bk, "W1B")
            else:
                v = buf[:, 0:5632].rearrange("p (hc n) -> p hc n", n=128)
                P.dma("gpsimd", v, w2v[:, :, j * 128:(j + 1) * 128], bk, w=[bk])
                st_ = (g.W2B[j], buf[:, 0:5632], bk, "W2B")
        else:
            st_ = None
        if pend[0] is not None:
            dst, src_, bk_, wkey = pend[0]
            P.dma("gpsimd", dst, src_, bk_ + "s", r=[bk_], w=[wkey])
        pend[0] = st_
        ui[0] = i + 1

    for h in range(8):
        P.dma("sync", kn, g.KT[h, 0:128, :], "kn", r=["KT"], w=["kn"])
        P.dma("sync", kr, g.KT[h, 128:192, :], "kr", r=["KT"], w=["kr"])
        P.dma("sync", qn, g.QT[h, 0:128, :], "qn", r=["QT"], w=["qn"])
        P.dma("sync", qr, g.QT[h, 128:192, :], "qr", r=["QT"], w=["qr"])
        P.dma("sync", vh, g.VTM[:, h * 128:(h + 1) * 128].rearrange("(c p) x -> p c x", p=128), "vh", r=["VTM"], w=["vh"])
        for (t0, n, nch) in qtiles:
            po, pok = g.psb(4 + 2 * (ti % 2))
            pd, pdk = g.psb(5 + 2 * (ti % 2))
            ti += 1
            conv_unit()

            def s_mm(c, t0=t0, n=n):
                ps, psk = g.next_ps()
                P.op("tensor", lambda e: e.matmul(ps[:, :n], kn[:, c * 128:(c + 1) * 128], qn[:, t0:t0 + n],
                                                  start=True, stop=False), r=["kn", "qn"], w=[psk])
                P.op("tensor", lambda e: e.matmul(ps[:, :n], kr[:, c * 128:(c + 1) * 128], qr[:, t0:t0 + n],
                                                  start=False, stop=True), r=["kr", "qr"], w=[psk])
                return ps, psk

            cur = s_mm(0)
            for c in range(nch):
                nxt = s_mm(c + 1) if c + 1 < nch else None
                ps, psk = cur
                eb = ebuf[c % 4]
                ek = "ebuf%d" % (c % 4)
                P.op("scalar", lambda e, ps=ps, eb=eb, n=n: e.activation(out=eb[:, :n], in_=ps[:, :n], func=AF.Exp),
                     r=[psk], w=[ek])
                P.op("tensor", lambda e, eb=eb, c=c, n=n, po=po, nch=nch: e.matmul(
                    po[:, :n], vh[:, c, :], eb[:, :n], start=(c == 0), stop=(c == nch - 1)), r=["vh", ek], w=[pok])
                P.op("tensor", lambda e, eb=eb, c=c, n=n, pd=pd, nch=nch: e.matmul(
                    pd[:, :n], g.ones_bf[:], eb[:, :n], start=(c == 0), stop=(c == nch - 1)), r=["ones_bf", ek], w=[pdk])
                cur = nxt
            P.op("vector", lambda e, n=n, pd=pd: e.reciprocal(out=rden[:, :n], in_=pd[:, :n]), r=[pdk], w=["rden"])
            os_ = ost[ti % 2]
            ok = "ost%d" % (ti % 2)
            P.op("vector", lambda e, n=n, po=po, os_=os_: e.tensor_tensor(out=os_[:, :n], in0=po[:, :n], in1=rden[:, :n],
                                                                       op=ALU.mult), r=[pok, "rden"], w=[ok])
            P.dma("sync", g.CATT[h * 128:(h + 1) * 128, t0:t0 + n], os_[:, :n], ok, r=[ok], w=["CATT"])
    while ui[0] <= len(units):
        conv_unit()


def phase_mlstm(g, l):
    P = g.P
    g.new_phase()
    g.ps_rot = list(range(8))
    PT, PTM = g.PT[l], g.PTM[l]
    NCH = 34
    order = [list(range(NCH)), [1, 0] + list(range(NCH - 1, 1, -1))]
    tri = g.carve([128, 2, 128], F32)
    ident = g.carve([128, 128], F32)
    identb = g.carve([128, 128], BF16)
    sel4 = g.carve([4, 4, 128], F32)
    ones4 = g.carve([4, 128], F32)
    one_t = g.carve([128, 1], F32)
    onesf = g.carve([128, 128], F32)
    P.op("vector", lambda e: e.memset(onesf, 1.0), w=["onesf"])
    pers = [[g.carve([128, NCH, 4], F32) for i in range(6)] for dr in range(2)]
    mark1 = g.arena_off
    P.dma("sync", tri, g.tri.rearrange("d s t -> s d t"), "pc", w=["tri"])
    P.dma("sync", ident, g.ident[:, :], "pc", w=["ident"])
    P.dma("gpsimd", identb, g.ident[:, :], "pc", w=["identb"])
    P.dma("sync", sel4, g.sel4[:, :, :], "pc", w=["sel4"])
    P.seal()
    P.op("vector", lambda e: e.memset(ones4, 1.0), w=["ones4"])
    P.op("vector", lambda e: e.memset(one_t, 1.0), w=["one_t"])
    Gt = g.carve([128, NCH, 16], F32)
    gb = g.carve([128, 16], F32)
    P.dma("sync", Gt, PTM[:, 1024:1040].rearrange("(c p) x -> p c x", p=128), "pc", r=["PTM%d" % l], w=["Gt"])
    P.dma("sync", gb, g.ml_gate_b[l:l + 1, :].to_broadcast([128, 16]), "pc", w=["gb"])
    P.seal()
    P.op("vector", lambda e: e.tensor_tensor(out=Gt, in0=Gt, in1=gb.unsqueeze(1).to_broadcast([128, NCH, 16]), op=ALU.add),
         r=["Gt", "gb"], w=["Gt"])
    t1 = g.carve([128, NCH, 8], F32)
    t2 = g.carve([128, NCH, 8], F32)
    lf = g.carve([128, NCH, 8], F32)
    xf = Gt[:, :, 8:16]
    P.op("vector", lambda e: e.scalar_tensor_tensor(out=t1, in0=xf, scalar=-1.0, in1=xf, op0=ALU.mult, op1=ALU.max),
         r=["Gt"], w=["t1"])
    P.op("scalar", lambda e: e.activation(out=t1, in_=t1, func=AF.Exp, scale=-1.0), r=["t1"], w=["t1"])
    P.op("scalar", lambda e: e.activation(out=t1, in_=t1, func=AF.Ln, bias=one_t[:, 0:1], scale=1.0), r=["t1", "one_t"], w=["t1"])
    P.op("vector", lambda e: e.tensor_scalar(out=t2, in0=xf, scalar1=-1.0, scalar2=0.0, op0=ALU.mult, op1=ALU.max),
         r=["Gt"], w=["t2"])
    P.op("vector", lambda e: e.scalar_tensor_tensor(out=lf, in0=t2, scalar=-1.0, in1=t1, op0=ALU.mult, op1=ALU.subtract),
         r=["t1", "t2"], w=["lf"])
    lfd, lid, bd, ud, wtok, iwtok, emt, abc, Mtok = [], [], [], [], [], [], [], [], []
    negMR = []
    for dr in range(2):
        k = "d%d" % dr
        lf_ = g.carve([128, NCH, 4], F32)
        li_ = g.carve([128, NCH, 4], F32)
        b_ = g.carve([128, NCH, 4], F32)
        u_ = pers[dr][0]
        P.op("vector", lambda e, lf_=lf_, dr=dr: e.tensor_copy(out=lf_, in_=lf[:, :, dr * 4:(dr + 1) * 4]), r=["lf"], w=["lf" + k])
        P.op("vector", lambda e, li_=li_, dr=dr: e.tensor_copy(out=li_, in_=Gt[:, :, dr * 4:(dr + 1) * 4]), r=["Gt"], w=["li" + k])
        ps, psk = g.next_ps()
        P.op("tensor", lambda e, ps=ps, dr=dr, lf_=lf_: e.matmul(ps[:, :NCH * 4], tri[:, dr, :], lf_.rearrange("p c h -> p (c h)"),
                                                              start=True, stop=True), r=["tri", "lf" + k], w=[psk])
        P.op("vector", lambda e, ps=ps, b_=b_: e.tensor_copy(out=b_.rearrange("p c h -> p (c h)"), in_=ps[:, :NCH * 4]),
             r=[psk], w=["b" + k])
        P.op("vector", lambda e, u_=u_, li_=li_, b_=b_: e.tensor_tensor(out=u_, in0=li_, in1=b_, op=ALU.subtract),
             r=["li" + k, "b" + k], w=["u" + k])
        uR = g.carve([4, NCH, 128], F32)
        lfR = g.carve([4, NCH, 128], F32)
        for (src, sk_, dstR, dk_) in [(u_, "u" + k, uR, "uR" + k), (lf_, "lf" + k, lfR, "lfR" + k)]:
            for c4 in range(0, NCH, 4):
                ps, psk = g.next_ps()
                nn = min(4, NCH - c4)
                for j in range(nn):
                    P.op("tensor", lambda e, ps=ps, src=src, c4=c4, j=j: e.transpose(ps[:4, j * 128:(j + 1) * 128],
                                                                                  src[:, c4 + j, :], ident),
                         r=[sk_, "ident"], w=[psk])
                P.op("vector", lambda e, ps=ps, dstR=dstR, c4=c4, nn=nn: e.tensor_copy(
                    out=dstR[:, c4:c4 + nn, :].rearrange("p c t -> p (c t)"), in_=ps[:4, :nn * 128]), r=[psk], w=[dk_])
        blast = g.carve([4, NCH], F32)
        cml = g.carve([4, NCH], F32)
        P.op("vector", lambda e, blast=blast, lfR=lfR: e.reduce_sum(out=blast, in_=lfR, axis=AX.X), r=["lfR" + k], w=["blast" + k])
        P.op("vector", lambda e, cml=cml, uR=uR: e.reduce_max(out=cml, in_=uR, axis=AX.X), r=["uR" + k], w=["cml" + k])
        cmB = g.carve([4, NCH, 128], F32)
        bufs = [(uR, "uR" + k), (cmB, "cmB" + k)]
        cur = 0
        for sh in (1, 2, 4, 8, 16, 32, 64):
            (sa, sak), (da, dak) = bufs[cur], bufs[1 - cur]
            if dr == 0:
                P.op("vector", lambda e, sa=sa, da=da, sh=sh: e.tensor_tensor(out=da[:, :, sh:], in0=sa[:, :, sh:], in1=sa[:, :, :128 - sh],
                                                                            op=ALU.max), r=[sak], w=[dak])
                P.op("gpsimd", lambda e, sa=sa, da=da, sh=sh: e.tensor_copy(out=da[:, :, :sh], in_=sa[:, :, :sh]), r=[sak], w=[dak])
            else:
                P.op("vector", lambda e, sa=sa, da=da, sh=sh: e.tensor_tensor(out=da[:, :, :128 - sh], in0=sa[:, :, :128 - sh], in1=sa[:, :, sh:],
                                                                            op=ALU.max), r=[sak], w=[dak])
                P.op("gpsimd", lambda e, sa=sa, da=da, sh=sh: e.tensor_copy(out=da[:, :, 128 - sh:], in_=sa[:, :, 128 - sh:]), r=[sak], w=[dak])
            cur = 1 - cur
        cm, cmk = bufs[cur]
        mR = g.carve([4, NCH], F32)
        P.op("vector", lambda e, mR=mR: e.memset(mR, 0.0), w=["mR" + k])
        od = order[dr]
        for i in range(NCH - 1):
            c, c2 = od[i], od[i + 1]
            P.op("vector", lambda e, mR=mR, c=c, c2=c2, cml=cml, blast=blast: e.scalar_tensor_tensor(
                out=mR[:, c2:c2 + 1], in0=mR[:, c:c + 1], scalar=cml[:, c:c + 1], in1=blast[:, c:c + 1],
                op0=ALU.max, op1=ALU.add), r=["mR" + k, "cml" + k, "blast" + k], w=["mR" + k])
        MR = bufs[1 - cur][0]
        MRk = bufs[1 - cur][1]
        P.op("vector", lambda e, MR=MR, cm=cm, mR=mR: e.tensor_tensor(out=MR, in0=cm, in1=mR.unsqueeze(2).to_broadcast([4, NCH, 128]),
                                                                    op=ALU.max), r=[cmk, "mR" + k], w=[MRk])
        MlR = g.carve([4, NCH], F32)
        aR = g.carve([4, NCH], F32)
        P.op("vector", lambda e, MlR=MlR, mR=mR, cml=cml: e.tensor_tensor(out=MlR, in0=mR, in1=cml, op=ALU.max),
             r=["mR" + k, "cml" + k], w=["MlR" + k])
        P.op("vector", lambda e, aR=aR, mR=mR, MlR=MlR: e.tensor_tensor(out=aR, in0=mR, in1=MlR, op=ALU.subtract),
             r=["mR" + k, "MlR" + k], w=["aR" + k])
        P.op("scalar", lambda e, aR=aR: e.activation(out=aR, in_=aR, func=AF.Exp), r=["aR" + k], w=["aR" + k])
        Mt = pers[dr][1]
        for c4 in range(0, NCH, 32):
            ps, psk = g.next_ps()
            nn = min(32, NCH - c4)
            for j in range(nn):
                P.op("tensor", lambda e, ps=ps, MR=MR, c4=c4, j=j: e.transpose(ps[:, j * 4:(j + 1) * 4], MR[:, c4 + j, :], ident[:4, :4]),
                     r=[MRk, "ident"], w=[psk])
            P.op("vector", lambda e, ps=ps, Mt=Mt, c4=c4, nn=nn: e.tensor_copy(
                out=Mt[:, c4:c4 + nn, :].rearrange("p c h -> p (c h)"), in_=ps[:, :nn * 4]), r=[psk], w=["Mt" + k])
        bc = []
        dg = g.carve([4, NCH, 4], F32)
        for (src, sk_) in [(MlR, "MlR" + k), (aR, "aR" + k), (mR, "mR" + k)]:
            P.op("vector", lambda e, src=src: e.tensor_tensor(out=dg, in0=src.unsqueeze(2).to_broadcast([4, NCH, 4]),
                                                             in1=ident[:4, :4].unsqueeze(1).to_broadcast([4, NCH, 4]), op=ALU.mult),
                 r=[sk_, "ident"], w=["dg" + k])
            ps, psk = g.next_ps()
            P.op("tensor", lambda e, ps=ps: e.matmul(ps[:, :NCH * 4], ones4, dg.rearrange("p c h -> p (c h)"), start=True, stop=True),
                 r=["dg" + k, "ones4"], w=[psk])
            o_ = pers[dr][5] if src is aR else g.carve([128, NCH, 4], F32)
            P.op("vector", lambda e, ps=ps, o_=o_: e.tensor_copy(out=o_.rearrange("p c h -> p (c h)"), in_=ps[:, :NCH * 4]),
                 r=[psk], w=["bc" + k + sk_])
            bc.append((o_, "bc" + k + sk_))
        (Mlb, Mlbk), (ab, abk), (mb, mbk) = bc
        w_, iw_, em_ = pers[dr][2], pers[dr][3], pers[dr][4]
        P.op("vector", lambda e, w_=w_, u_=u_, Mlb=Mlb: e.tensor_tensor(out=w_, in0=u_, in1=Mlb, op=ALU.subtract), r=["u" + k, Mlbk], w=["w" + k])
        P.op("scalar", lambda e, w_=w_: e.activation(out=w_, in_=w_, func=AF.Exp), r=["w" + k], w=["w" + k])
        P.op("vector", lambda e, iw_=iw_, mb=mb, Mt=Mt: e.tensor_tensor(out=iw_, in0=mb, in1=Mt, op=ALU.subtract), r=[mbk, "Mt" + k], w=["iw" + k])
        P.op("scalar", lambda e, iw_=iw_: e.activation(out=iw_, in_=iw_, func=AF.Exp), r=["iw" + k], w=["iw" + k])
        P.op("vector", lambda e, em_=em_, b_=b_, Mt=Mt: e.tensor_tensor(out=em_, in0=b_, in1=Mt, op=ALU.add), r=["b" + k, "Mt" + k], w=["em" + k])
        P.op("scalar", lambda e, em_=em_: e.activation(out=em_, in_=em_, func=AF.Exp, scale=-1.0), r=["em" + k], w=["em" + k])
        P.op("vector", lambda e, Mt=Mt: e.tensor_scalar(out=Mt, in0=Mt, scalar1=-1.0, scalar2=None, op0=ALU.mult),
             r=["Mt" + k, "iw" + k, "em" + k], w=["Mt" + k])
        ud.append(u_); wtok.append(w_); iwtok.append(iw_); emt.append(em_); abc.append(ab); Mtok.append(Mt)
    g.arena_off = mark1
    P.barrier()
    qT = g.carve([128, 4, T], BF16)
    kT = g.carve([128, 4, T], BF16)
    ktok = g.carve([128, NCH, 512], BF16)
    v1 = g.carve([128, NCH * 4, 132], BF16)
    mark2 = g.arena_off
    cw = g.carve([128, 8, 4], F32)
    P.dma("sync", cw, g.ml_cw[l], "pc", w=["cw"])
    P.seal()
    W = T + 3
    xb = g.carve([128, W], F32)
    acc = g.carve([128, W], F32)
    P.op("vector", lambda e: e.memset(xb, 0.0), w=["xb"])
    for rc in range(8):
        r0 = 2624 + rc * 128
        P.dma("sync", xb[:, 1:1 + NCTX], PT[r0:r0 + 128, 0:NCTX], "xb", r=["PT%d" % l], w=["xb"])
        P.dma("sync", xb[:, 2 + NCTX:2 + T], PT[r0:r0 + 128, NCTX:T], "xb", r=["PT%d" % l], w=["xb"])
        n = T + 1
        P.op("vector", lambda e, rc=rc: e.tensor_scalar(out=acc[:, 1:1 + n], in0=xb[:, 0:n], scalar1=cw[:, rc, 0:1],
                                                       scalar2=cw[:, rc, 3:4], op0=ALU.mult, op1=ALU.add),
             r=["xb", "cw"], w=["acc"])
        for j in (1, 2):
            P.op("vector", lambda e, rc=rc, j=j: e.scalar_tensor_tensor(
                out=acc[:, 1:1 + n], in0=xb[:, j:j + n], scalar=cw[:, rc, j:j + 1], in1=acc[:, 1:1 + n],
                op0=ALU.mult, op1=ALU.add), r=["xb", "cw", "acc"], w=["acc"])
        dst = qT if rc < 4 else kT
        dk = "qT" if rc < 4 else "kT"
        for (a0, d0, n2) in [(1, 0, NCTX), (2 + NCTX, NCTX, NLAT)]:
            P.op("scalar", lambda e, a0=a0, n2=n2: e.activation(out=acc[:, a0:a0 + n2], in_=acc[:, a0:a0 + n2], func=AF.Silu),
                 r=["acc"], w=["acc"])
            if rc < 4:
                P.op("vector", lambda e, a0=a0, d0=d0, n2=n2, dst=dst, rc=rc: e.tensor_scalar(
                    out=dst[:, rc % 4, d0:d0 + n2], in0=acc[:, a0:a0 + n2], scalar1=128.0 ** -0.5, scalar2=None, op0=ALU.mult),
                    r=["acc"], w=[dk])
            else:
                P.op("vector", lambda e, a0=a0, d0=d0, n2=n2, dst=dst, rc=rc: e.tensor_copy(
                    out=dst[:, rc % 4, d0:d0 + n2], in_=acc[:, a0:a0 + n2]), r=["acc"], w=[dk])
    psb16 = [g.ps[i][:].bitcast(BF16) for i in range(8)]
    for c in range(NCH):
        g.ps_i = (g.ps_i + 1) % 8
        pi = g.ps_i
        pk = "ps%d" % pi
        for hd in range(4):
            P.op("tensor", lambda e, c=c, hd=hd, pi=pi: e.transpose(psb16[pi][:, hd * 128:(hd + 1) * 128],
                                                                   kT[:, hd, c * 128:(c + 1) * 128], identb),
                 r=["kT", "identb"], w=[pk])
        P.op("vector", lambda e, c=c, pi=pi: e.tensor_copy(out=ktok[:, c, :], in_=psb16[pi][:, 0:512]), r=[pk], w=["ktok"])
    v1v = v1.rearrange("p (c h) x -> p c h x", h=4)
    P.op("vector", lambda e: e.memset(v1[:, :, 128:129], 1.0), w=["v1"])
    for hd in range(4):
        P.dma("gpsimd", v1v[:, :, hd, 0:128], PTM[:, hd * 128:(hd + 1) * 128].rearrange("(c p) x -> p c x", p=128), "v1",
              r=["PTM%d" % l], w=["v1"])
    g.arena_off = mark2
    P.barrier()
    chains = [(dr, hd) for dr in range(2) for hd in range(4)]
    st8 = {}
    for (dr, hd) in chains:
        ck = "c%d%d" % (dr, hd)
        st8[ck] = dict(Cf=g.carve([128, 132], F32), Cb=g.carve([128, 132], BF16), dx=g.carve([128, 128], F32),
                       dxm=g.carve([128, 128], F32), sw=g.carve([128, 128], BF16), Asb=g.carve([128, 132], F32),
                       nt=g.carve([128, 132], F32), kw=g.carve([128, 128], BF16), ho=g.carve([128, 128], F32),
                       dd=g.carve([128, 2], F32), dgm=g.carve([128, 128], F32))
        b8 = st8[ck]
        P.op("vector", lambda e, b8=b8: e.memset(b8["Cf"], 0.0), w=[ck + "Cf"])
        P.op("vector", lambda e, b8=b8: e.memset(b8["Cb"], 0.0), w=[ck + "Cb"])
    for step in range(NCH):
        for (dr, hd) in chains:
            ck = "c%d%d" % (dr, hd)
            k = "d%d" % dr
            b8 = st8[ck]
            c = order[dr][step]
            cs = slice(c * 128, (c + 1) * 128)
            ps1, k1 = g.next_ps()
            P.op("tensor", lambda e, ps1=ps1, hd=hd, cs=cs: e.matmul(ps1[:, :128], kT[:, hd, cs], qT[:, hd, cs], start=True, stop=True),
                 r=["kT", "qT"], w=[k1])
            ps2, k2 = g.next_ps()
            P.op("gpsimd", lambda e, b8=b8, dr=dr, c=c, hd=hd: e.tensor_scalar(
                out=b8["dgm"], in0=ident, scalar1=Mtok[dr][:, c, hd:hd + 1], scalar2=None, op0=ALU.mult),
                r=["ident", "Mt" + k], w=[ck + "dgm"])
            P.op("tensor", lambda e, ps2=ps2, b8=b8: e.matmul(ps2[:, :128], onesf, b8["dgm"], start=True, stop=True),
                 r=["onesf", ck + "dgm"], w=[k2])
            P.op("scalar", lambda e, ps2=ps2, b8=b8, dr=dr, c=c, hd=hd: e.activation(
                out=b8["dx"], in_=ps2[:, :128], func=AF.Exp, bias=ud[dr][:, c, hd:hd + 1], scale=1.0), r=[k2, "u" + k], w=[ck + "dx"])
            P.op("gpsimd", lambda e, b8=b8, dr=dr: e.tensor_tensor(out=b8["dxm"], in0=b8["dx"], in1=tri[:, dr, :], op=ALU.mult),
                 r=[ck + "dx", "tri"], w=[ck + "dxm"])
            P.op("vector", lambda e, ps1=ps1, b8=b8: e.tensor_tensor(out=b8["sw"], in0=ps1[:, :128], in1=b8["dxm"], op=ALU.mult),
                 r=[k1, ck + "dxm"], w=[ck + "sw"])
            ps3, k3 = g.next_ps()
            P.op("tensor", lambda e, ps3=ps3, b8=b8, c=c, hd=hd: e.matmul(ps3[:, :129], b8["sw"], v1[:, c * 4 + hd, 0:129], start=True, stop=True),
                 r=[ck + "sw", "v1"], w=[k3])
            ps4, k4 = g.next_ps()
            P.op("tensor", lambda e, ps4=ps4, b8=b8, hd=hd, cs=cs: e.matmul(ps4[:, :129], qT[:, hd, cs], b8["Cb"][:, 0:129], start=True, stop=True),
                 r=["qT", ck + "Cb"], w=[k4])
            P.op("scalar", lambda e, ps3=ps3, b8=b8: e.copy(out=b8["Asb"][:, 0:129], in_=ps3[:, :129]), r=[k3], w=[ck + "Asb"])
            P.op("vector", lambda e, ps4=ps4, b8=b8, dr=dr, c=c, hd=hd: e.scalar_tensor_tensor(
                out=b8["nt"][:, 0:129], in0=ps4[:, :129], scalar=iwtok[dr][:, c, hd:hd + 1], in1=b8["Asb"][:, 0:129],
                op0=ALU.mult, op1=ALU.add), r=[k4, ck + "Asb", "iw" + k], w=[ck + "nt"])
            P.op("vector", lambda e, b8=b8: e.scalar_tensor_tensor(
                out=b8["dd"][:, 0:1], in0=b8["nt"][:, 128:129], scalar=-1.0, in1=b8["nt"][:, 128:129], op0=ALU.mult, op1=ALU.max),
                r=[ck + "nt"], w=[ck + "dd"])
            P.op("vector", lambda e, b8=b8, dr=dr, c=c, hd=hd: e.tensor_tensor(
                out=b8["dd"][:, 0:1], in0=b8["dd"][:, 0:1], in1=emt[dr][:, c, hd:hd + 1], op=ALU.max),
                r=[ck + "dd", "em" + k], w=[ck + "dd"])
            P.op("vector", lambda e, b8=b8: e.reciprocal(out=b8["dd"][:, 1:2], in_=b8["dd"][:, 0:1]), r=[ck + "dd"], w=[ck + "dd"])
            P.op("scalar", lambda e, b8=b8: e.mul(out=b8["ho"], in_=b8["nt"][:, 0:128], mul=b8["dd"][:, 1:2]),
                 r=[ck + "nt", ck + "dd"], w=[ck + "ho"])
            P.dma("sync", g.HD[dr][c * 128:(c + 1) * 128, hd * 128:(hd + 1) * 128], b8["ho"], ck + "ho", r=[ck + "ho"], w=["HD%d" % dr])
            P.op("vector", lambda e, b8=b8, dr=dr, c=c, hd=hd: e.tensor_scalar(
                out=b8["kw"], in0=ktok[:, c, hd * 128:(hd + 1) * 128], scalar1=wtok[dr][:, c, hd:hd + 1], scalar2=None, op0=ALU.mult),
                r=["ktok", "w" + k], w=[ck + "kw"])
            ps5, k5 = g.next_ps()
            P.op("tensor", lambda e, ps5=ps5, b8=b8, c=c, hd=hd: e.matmul(ps5[:, :129], b8["kw"], v1[:, c * 4 + hd, 0:129], start=True, stop=True),
                 r=[ck + "kw", "v1"], w=[k5])
            P.op("vector", lambda e, ps5=ps5, b8=b8, dr=dr, c=c, hd=hd: e.scalar_tensor_tensor(
                out=b8["Cf"][:, 0:129], in0=b8["Cf"][:, 0:129], scalar=abc[dr][:, c, hd:hd + 1], in1=ps5[:, :129],
                op0=ALU.mult, op1=ALU.add), r=[ck + "Cf", k5, "bcd%daRd%d" % (dr, dr)], w=[ck + "Cf"])
            P.op("scalar", lambda e, b8=b8: e.copy(out=b8["Cb"][:, 0:129], in_=b8["Cf"][:, 0:129]), r=[ck + "Cf"], w=[ck + "Cb"])
    g.arena_off = mark2
    P.barrier()
    gbc = g.carve([128, 512], F32)
    P.dma("sync", gbc, g.mix_g[l:l + 1, 1536:2048].to_broadcast([128, 512]), "pc", w=["gbc"])
    P.seal()
    hf = g.carve([128, 512], F32)
    hb = g.carve([128, 512], F32)
    og = g.carve([128, 512], F32)
    junk = g.carve([128, 128], F32)
    ss = g.carve([128, 8], F32)
    outs = [g.carve([128, 512], F32) for i in range(2)]
    for c in range(NCH if l == 0 else NCH):
        P.dma("sync", hf, g.HD[0][c * 128:(c + 1) * 128, :], "hf", r=["HD0"], w=["hf"])
        P.dma("sync", hb, g.HD[1][c * 128:(c + 1) * 128, :], "hb", r=["HD1"], w=["hb"])
        P.dma("sync", og, PTM[c * 128:(c + 1) * 128, 512:1024], "og", r=["PTM%d" % l], w=["og"])
        P.op("vector", lambda e: e.tensor_tensor(out=hf, in0=hf, in1=hb, op=ALU.add), r=["hf", "hb"], w=["hf"])
        for hd in range(4):
            P.op("scalar", lambda e, hd=hd: e.activation(out=junk, in_=hf[:, hd * 128:(hd + 1) * 128], func=AF.Square,
                                                        accum_out=ss[:, hd:hd + 1]), r=["hf"], w=["junk", "ss"])
        P.op("scalar", lambda e: e.activation(out=ss[:, 4:8], in_=ss[:, 0:4], func=AF.Sqrt, bias=g.eps_t[:, 0:1], scale=1.0 / 128),
             r=["ss", "eps_t"], w=["ss"])
        P.op("vector", lambda e: e.reciprocal(out=ss[:, 4:8], in_=ss[:, 4:8]), r=["ss"], w=["ss"])
        P.op("scalar", lambda e: e.activation(out=og, in_=og, func=AF.Sigmoid), r=["og"], w=["og"])
        P.op("vector", lambda e: e.tensor_tensor(out=og, in0=og, in1=gbc, op=ALU.mult), r=["og", "gbc"], w=["og"])
        for hd in range(4):
            P.op("vector", lambda e, hd=hd: e.scalar_tensor_tensor(
                out=hf[:, hd * 128:(hd + 1) * 128], in0=hf[:, hd * 128:(hd + 1) * 128], scalar=ss[:, 4 + hd:5 + hd],
                in1=og[:, hd * 128:(hd + 1) * 128], op0=ALU.mult, op1=ALU.mult), r=["hf", "ss", "og"], w=["hf"])
        ps, psk = g.next_ps()
        for hd in range(4):
            P.op("tensor", lambda e, ps=ps, hd=hd: e.transpose(ps[:, hd * 128:(hd + 1) * 128], hf[:, hd * 128:(hd + 1) * 128], ident),
                 r=["hf", "ident"], w=[psk])
        o_ = outs[c % 2]
        ok = "mlo%d" % (c % 2)
        P.op("vector", lambda e, ps=ps, o_=o_: e.tensor_copy(out=o_, in_=ps[:, :]), r=[psk], w=[ok])
        P.dma("sync", g.CATT[1536:2048, c * 128:(c + 1) * 128].rearrange("(h p) t -> p h t", p=128),
              o_.rearrange("p (h t) -> p h t", t=128), ok, r=[ok], w=["CATT"])


def dft_fwd(g, L, rhsC, rkC, rhsS, rkS, evac, nyq):
    P = g.P
    nch = L // 128
    KG = min(512, L)
    nj = KG // 128
    mats = g.hy_cs[L]
    for kg in range(L // KG):
        banks = [[g.psb(j), g.psb(4 + j)] for j in range(nj)]
        for mi, (rhs, rk) in enumerate([(rhsC, rkC), (rhsS, rkS)]):
            for tc in range(nch):
                bi = g.mt_i
                g.mt_i = (bi + 1) % len(g.mt)
                mt, mk = g.mt[bi], "mt%d" % bi
                P.dma("sync", mt[:, :KG], mats[mi, tc * 128:(tc + 1) * 128, kg * KG:(kg + 1) * KG], mk, w=[mk])
                for j in range(nj):
                    ps, pk = banks[j][mi]
                    P.op("tensor", lambda e, ps=ps, mt=mt, j=j, rhs=rhs, tc=tc: e.matmul(
                        ps[:, :512], mt[:, j * 128:(j + 1) * 128], rhs[:, tc, :], start=(tc == 0), stop=(tc == nch - 1)),
                        r=[mk, rk], w=[pk])
        for j in range(nj):
            (pc, pck), (ps_, psk) = banks[j]
            evac(kg * nj + j, pc, pck, ps_, psk)
    g.ps_i = (g.ps_i + 1) % 8
    pn, pnk = g.psb(g.ps_i)
    for tc in range(nch):
        P.op("tensor", lambda e, tc=tc, pn=pn: e.matmul(pn[:1, :512], g.altc[:, 0:1], rhsC[:, tc, :], start=(tc == 0), stop=(tc == nch - 1)),
             r=["altc", rkC], w=[pnk])
    nyq(pn, pnk)


def hyena_seq(g, l, L, tok0):
    P = g.P
    nch = L // 128
    N2 = 2 * L
    PT = g.PT[l]
    g.new_phase()
    g.ps_rot = list(range(8))
    ident = g.carve([128, 128], F32)
    identb = g.carve([128, 128], BF16)
    P.dma("sync", ident, g.ident[:, :], "pc", w=["ident"])
    P.dma("gpsimd", identb, g.ident[:, :], "pc", w=["identb"])
    g.altc = g.carve([128, 1], BF16)
    altr = g.carve([1, 512], BF16)
    P.dma("gpsimd", g.altc, g.hy_alt[:, 0:1], "pc", w=["altc"], allow_slow_non_contiguous=True)
    P.dma("gpsimd", altr, g.hy_alt[0:1, :], "pc", w=["altr"])
    g.mt = [g.carve([128, 512], BF16) for i in range(4)]
    g.mt_i = 0
    pic = g.carve([128, 2], F32)
    P.op("vector", lambda e: e.memset(pic[:, 0:1], 0.05), w=["pic"])
    mark0 = g.arena_off
    w1 = g.carve([33, 64], F32)
    w2 = g.carve([64, 64], F32)
    w3 = g.carve([64, 2048], F32)
    b12 = g.carve([64, 2], F32)
    dec = g.carve([128, 2048], F32)
    negt = g.carve([128, nch], F32)
    feats = g.carve([33, L], F32)
    h2T = g.carve([64, L], F32)
    P.dma("sync", w1, g.hy_w1[l], "pc", w=["hw1"])
    P.dma("sync", w2, g.hy_w2[l], "pc", w=["hw2"])
    P.dma("sync", w3, g.hy_w3[l], "pc", w=["hw3"])
    P.dma("sync", b12, g.hy_b12[l], "pc", w=["hb12"])
    P.dma("sync", dec, g.hy_decay[l:l + 1, :].to_broadcast([128, 2048]), "pc", w=["hdec"])
    P.dma("sync", negt, g.hy_negt[L][:, :], "pc", w=["negt"])
    P.dma("sync", feats, g.hy_feats[L][:, :], "pc", w=["feats"])
    P.seal()
    zt = g.carve([64, 512], F32)
    ki = g.carve([64, 512], mybir.dt.int32)
    kf = g.carve([64, 512], F32)
    h1 = g.carve([64, 512], F32)
    TW = min(512, L)

    def sin_layer(ps, pk, bcol, dst, dk):
        n = TW
        P.op("vector", lambda e: e.tensor_scalar(out=zt[:, :n], in0=ps[:64, :n], scalar1=b12[:, bcol:bcol + 1], scalar2=None, op0=ALU.add),
             r=[pk, "hb12"], w=["zt"])
        P.op("vector", lambda e: e.tensor_scalar(out=ki[:, :n], in0=zt[:, :n], scalar1=1.0 / (2 * math.pi), scalar2=None, op0=ALU.mult),
             r=["zt"], w=["ki"])
        P.op("vector", lambda e: e.tensor_copy(out=kf[:, :n], in_=ki[:, :n]), r=["ki"], w=["kf"])
        P.op("vector", lambda e: e.scalar_tensor_tensor(out=zt[:, :n], in0=kf[:, :n], scalar=-2 * math.pi, in1=zt[:, :n],
                                                        op0=ALU.mult, op1=ALU.add), r=["kf", "zt"], w=["zt"])
        P.op("vector", lambda e: e.tensor_scalar(out=zt[:, :n], in0=zt[:, :n], scalar1=-3.141592, scalar2=3.141592,
                                                 op0=ALU.max, op1=ALU.min), r=["zt"], w=["zt"])
        P.op("scalar", lambda e: e.activation(out=dst, in_=zt[:, :n], func=AF.Sin), r=["zt"], w=[dk])

    for t0 in range(0, L, TW):
        ps, pk = g.next_ps()
        P.op("tensor", lambda e, ps=ps, t0=t0: e.matmul(ps[:64, :TW], w1, feats[:, t0:t0 + TW], start=True, stop=True),
             r=["hw1", "feats"], w=[pk])
        sin_layer(ps, pk, 0, h1[:, :TW], "h1")
        ps, pk = g.next_ps()
        P.op("tensor", lambda e, ps=ps: e.matmul(ps[:64, :TW], w2, h1[:, :TW], start=True, stop=True), r=["hw2", "h1"], w=[pk])
        sin_layer(ps, pk, 1, h2T[:, t0:t0 + TW], "h2T")
    wt_ = [g.carve([128, 512], F32) for i in range(2)]
    hc = [g.carve([128, 512], F32) for i in range(4)]
    hsb = [g.carve([128, 512], BF16) for i in range(4)]
    q = 0
    for tc in range(nch):
        for cb in range(4):
            ps, pk = g.next_ps()
            P.op("tensor", lambda e, ps=ps, tc=tc, cb=cb: e.matmul(ps[:, :512], h2T[:, tc * 128:(tc + 1) * 128], w3[:, cb * 512:(cb + 1) * 512],
                                                                start=True, stop=True), r=["h2T", "hw3"], w=[pk])
            wt = wt_[cb % 2]
            wk_ = "hwt%d" % (cb % 2)
            P.op("scalar", lambda e, wt=wt, tc=tc, cb=cb: e.activation(out=wt, in_=dec[:, cb * 512:(cb + 1) * 512], func=AF.Exp,
                                                                     scale=negt[:, tc:tc + 1]), r=["hdec", "negt"], w=[wk_])
            P.op("vector", lambda e, wt=wt, ps=ps, cb=cb: e.scalar_tensor_tensor(out=hc[cb], in0=wt, scalar=pic[:, 0:1], in1=ps[:, :512],
                                                                             op0=ALU.add, op1=ALU.mult), r=[wk_, pk, "pic"], w=["hc%d" % cb])
            if tc == 0 and cb % 2 == 1:
                P.op("vector", lambda e, cb=cb: e.memset(hc[cb][0:1, :], 0.0), w=["hc%d" % cb])
        for o in range(2):
            for mi in range(2):
                hb_ = hsb[q % 4]
                hk = "hsb%d" % (q % 4)
                q += 1
                if mi == 0:
                    P.op("gpsimd", lambda e, hb_=hb_, o=o: e.tensor_tensor(out=hb_, in0=hc[2 * o], in1=hc[2 * o + 1], op=ALU.add),
                         r=["hc%d" % (2 * o), "hc%d" % (2 * o + 1)], w=[hk])
                else:
                    P.op("gpsimd", lambda e, hb_=hb_, o=o: e.tensor_tensor(out=hb_, in0=hc[2 * o + 1], in1=hc[2 * o], op=ALU.subtract),
                         r=["hc%d" % (2 * o), "hc%d" % (2 * o + 1)], w=[hk])
                P.dma("sync", g.HS[mi, tc * 128:(tc + 1) * 128, o * 512:(o + 1) * 512], hb_, hk, r=[hk], w=["HS"])
    g.arena_off = mark0
    P.barrier()
    rC = g.carve([128, nch, 512], BF16)
    rS = g.carve([128, nch, 512], BF16)
    stg = [g.carve([128, 512], F32) for i in range(4)]
    sti = [0]

    def nstg():
        sti[0] = (sti[0] + 1) % 4
        return stg[sti[0]], "hstg%d" % sti[0]

    for o in range(2):
        P.dma("sync", rC, g.HS[0, 0:L, o * 512:(o + 1) * 512].rearrange("(c p) x -> p c x", p=128), "rC", r=["HS"], w=["rC"])
        P.dma("sync", rS, g.HS[1, 0:L, o * 512:(o + 1) * 512].rearrange("(c p) x -> p c x", p=128), "rS", r=["HS"], w=["rS"])

        def evac(kc, pc, pck, ps_, psk, o=o):
            for mi, (pp, ppk) in enumerate([(pc, pck), (ps_, psk)]):
                st_, sk_ = nstg()
                if mi == 0:
                    P.op("vector", lambda e, st_=st_, pp=pp: e.tensor_scalar(out=st_, in0=pp[:, :512], scalar1=2.0 / N2, scalar2=None, op0=ALU.mult),
                         r=[ppk], w=[sk_])
                else:
                    P.op("scalar", lambda e, st_=st_, pp=pp: e.mul(out=st_, in_=pp[:, :512], mul=2.0 / N2), r=[ppk], w=[sk_])
                if kc == 0:
                    P.op("vector", lambda e, st_=st_: e.tensor_scalar(out=st_[0:1, :], in0=st_[0:1, :], scalar1=0.5, scalar2=None, op0=ALU.mult),
                         r=[sk_], w=[sk_])
                P.dma("gpsimd", g.KF[mi, kc * 128:(kc + 1) * 128, o * 512:(o + 1) * 512], st_, sk_, r=[sk_], w=["KF"])

        def nyq(pn, pnk, o=o):
            st_, sk_ = nstg()
            P.op("vector", lambda e, st_=st_: e.tensor_scalar(out=st_[0:1, :], in0=pn[0:1, :512], scalar1=1.0 / N2, scalar2=None, op0=ALU.mult),
                 r=[pnk], w=[sk_])
            P.dma("sync", g.KF[0, L:L + 1, o * 512:(o + 1) * 512], st_[0:1, :], sk_, r=[sk_], w=["KF"])

        dft_fwd(g, L, rC, "rC", rS, "rS", evac, nyq)
    g.arena_off = mark0
    P.barrier()
    ztok = g.carve([128, nch, 512], BF16)
    Asb = g.carve([128, nch + 1, 512], BF16)
    Bsb = g.carve([128, nch, 512], BF16)
    skp = g.carve([128, 2, 4], F32)
    P.dma("sync", skp, g.hy_skipT[l], "pc", w=["skp"])
    P.seal()
    mark1 = g.arena_off
    cw = g.carve([128, 12, 4], F32)
    P.dma("sync", cw, g.hy_cw[l], "pc", w=["hcw"])
    P.seal()
    xb = g.carve([128, L + 2], F32)
    acc = g.carve([128, L], F32)
    accb = g.carve([128, L], BF16)
    psb16 = [g.ps[i][:].bitcast(BF16) for i in range(8)]
    P.op("vector", lambda e: e.memset(xb, 0.0), w=["xb"])

    def to_tok(srcb, sk_, cc):
        for t8 in range(0, nch, 8):
            g.ps_i = (g.ps_i + 1) % 8
            pi = g.ps_i
            nn = min(8, nch - t8)
            for j in range(nn):
                P.op("tensor", lambda e, pi=pi, j=j, t8=t8: e.transpose(psb16[pi][:, j * 128:(j + 1) * 128],
                                                                      srcb[:, (t8 + j) * 128:(t8 + j + 1) * 128], identb),
                     r=[sk_, "identb"], w=["ps%d" % pi])
            P.op("vector", lambda e, pi=pi, t8=t8, nn=nn, cc=cc: e.tensor_copy(
                out=ztok[:, t8:t8 + nn, cc * 128:(cc + 1) * 128], in_=psb16[pi][:, 0:nn * 128].rearrange("p (c x) -> p c x", x=128)),
                r=["ps%d" % pi], w=["ztok"])

    for rc in range(12):
        r0 = 1088 + rc * 128
        P.dma("sync", xb[:, 1:1 + L], PT[r0:r0 + 128, tok0:tok0 + L], "xb", r=["PT%d" % l], w=["xb"])
        P.op("vector", lambda e, rc=rc: e.tensor_scalar(out=acc, in0=xb[:, 0:L], scalar1=cw[:, rc, 0:1], scalar2=cw[:, rc, 3:4],
                                                       op0=ALU.mult, op1=ALU.add), r=["xb", "hcw"], w=["acc"])
        for j in (1, 2):
            P.op("vector", lambda e, rc=rc, j=j: e.scalar_tensor_tensor(out=acc, in0=xb[:, j:j + L], scalar=cw[:, rc, j:j + 1], in1=acc,
                                                                     op0=ALU.mult, op1=ALU.add), r=["xb", "hcw", "acc"], w=["acc"])
        P.dma("sync", g.HV[rc * 128:(rc + 1) * 128, 0:L], acc, "acc", r=["acc"], w=["HV"])
        if rc < 4:
            P.op("scalar", lambda e: e.copy(out=accb, in_=acc), r=["acc"], w=["accb"])
            to_tok(accb, "accb", rc)
    g.arena_off = mark1
    P.barrier()
    kr_ = [g.carve([128, 512], F32) for i in range(2)]
    ki_ = [g.carve([128, 512], F32) for i in range(2)]
    t1 = g.carve([128, 512], F32)
    t2 = g.carve([128, 512], F32)
    zin = [g.carve([128, 512], F32) for i in range(2)]
    gat = [g.carve([128, 512], F32) for i in range(2)]
    ot = [g.carve([128, 512], F32) for i in range(2)]
    otb = [g.carve([128, 512], BF16) for i in range(2)]
    stg = [g.carve([128, 512], F32) for i in range(2)]
    for o in range(2):
        def evac(kc, pc, pck, ps_, psk, o=o):
            kr, krk = kr_[kc % 2], "kr%d" % (kc % 2)
            ki2, kik = ki_[kc % 2], "ki%d" % (kc % 2)
            P.dma("sync", kr, g.KF[0, kc * 128:(kc + 1) * 128, o * 512:(o + 1) * 512], krk, r=["KF"], w=[krk])
            P.dma("sync", ki2, g.KF[1, kc * 128:(kc + 1) * 128, o * 512:(o + 1) * 512], kik, r=["KF"], w=[kik])
            P.op("vector", lambda e: e.tensor_tensor(out=t1, in0=pc[:, :512], in1=kr, op=ALU.mult), r=[pck, krk], w=["ht1"])
            P.op("vector", lambda e: e.tensor_tensor(out=t2, in0=ps_[:, :512], in1=ki2, op=ALU.mult), r=[psk, kik], w=["ht2"])
            P.op("gpsimd", lambda e: e.tensor_tensor(out=Asb[:, kc, :], in0=t1, in1=t2, op=ALU.add), r=["ht1", "ht2"], w=["Asb"])
            P.op("vector", lambda e: e.tensor_tensor(out=t1, in0=ps_[:, :512], in1=kr, op=ALU.mult), r=[psk, krk], w=["ht1"])
            P.op("vector", lambda e: e.tensor_tensor(out=t2, in0=pc[:, :512], in1=ki2, op=ALU.mult), r=[pck, kik], w=["ht2"])
            P.op("gpsimd", lambda e: e.tensor_tensor(out=Bsb[:, kc, :], in0=t1, in1=t2, op=ALU.subtract), r=["ht1", "ht2"], w=["Bsb"])

        def nyq(pn, pnk, o=o):
            P.dma("sync", kr_[0][0:1, :], g.KF[0, L:L + 1, o * 512:(o + 1) * 512], "kr0", r=["KF"], w=["kr0"])
            P.op("vector", lambda e: e.tensor_tensor(out=Asb[0:1, nch, :], in0=pn[0:1, :512], in1=kr_[0][0:1, :], op=ALU.mult),
                 r=[pnk, "kr0"], w=["Asb"])

        dft_fwd(g, L, ztok, "ztok", ztok, "ztok", evac, nyq)
        TWI = min(512, L)
        mats = g.hy_cs[L]
        for ti, t0 in enumerate(range(0, L, TWI)):
            banks = [g.psb(4 * (ti % 2) + cc) for cc in range(4)]
            for kc in range(nch):
                tiles = []
                for mi in range(2):
                    bi = g.mt_i
                    g.mt_i = (bi + 1) % len(g.mt)
                    mt, mk = g.mt[bi], "mt%d" % bi
                    P.dma("sync", mt[:, :TWI], mats[mi, kc * 128:(kc + 1) * 128, t0:t0 + TWI], mk, w=[mk])
                    tiles.append((mt, mk))
                for cc in range(4):
                    pb, pbk = banks[cc]
                    P.op("tensor", lambda e, pb=pb, kc=kc, cc=cc, mt=tiles[0][0]: e.matmul(
                        pb[:, :TWI], Asb[:, kc, cc * 128:(cc + 1) * 128], mt[:, :TWI], start=(kc == 0), stop=False),
                        r=["Asb", tiles[0][1]], w=[pbk])
                    P.op("tensor", lambda e, pb=pb, kc=kc, cc=cc, mt=tiles[1][0]: e.matmul(
                        pb[:, :TWI], Bsb[:, kc, cc * 128:(cc + 1) * 128], mt[:, :TWI], start=False, stop=False),
                        r=["Bsb", tiles[1][1]], w=[pbk])
            for cc in range(4):
                pb, pbk = banks[cc]
                P.op("tensor", lambda e, pb=pb, cc=cc: e.matmul(pb[:, :TWI], Asb[0:1, nch, cc * 128:(cc + 1) * 128], altr[0:1, :TWI],
                                                             start=False, stop=True), r=["Asb", "altr"], w=[pbk])
                zi_, zk = zin[cc % 2], "zin%d" % (cc % 2)
                ga_, gk = gat[cc % 2], "gat%d" % (cc % 2)
                o_, ok = ot[cc % 2], "hot%d" % (cc % 2)
                src = g.HV if o == 0 else g.HZ2
                P.dma("sync", zi_[:, :TWI], src[cc * 128:(cc + 1) * 128, t0:t0 + TWI], zk, r=["HV", "HZ2"], w=[zk])
                P.dma("sync", ga_[:, :TWI], g.HV[(o + 1) * 512 + cc * 128:(o + 1) * 512 + (cc + 1) * 128, t0:t0 + TWI], gk, r=["HV"], w=[gk])
                P.op("vector", lambda e, zi_=zi_, pb=pb, cc=cc, o_=o_, o=o: e.scalar_tensor_tensor(
                    out=o_[:, :TWI], in0=zi_[:, :TWI], scalar=skp[:, o, cc:cc + 1], in1=pb[:, :TWI], op0=ALU.mult, op1=ALU.add),
                    r=[zk, "skp", pbk], w=[ok])
                P.op("gpsimd", lambda e, o_=o_, ga_=ga_: e.tensor_tensor(out=o_[:, :TWI], in0=o_[:, :TWI], in1=ga_[:, :TWI], op=ALU.mult),
                     r=[ok, gk], w=[ok])
                if o == 0:
                    P.dma("gpsimd", g.HZ2[cc * 128:(cc + 1) * 128, t0:t0 + TWI], o_[:, :TWI], ok, r=[ok], w=["HZ2"])
                    ob_, obk = otb[cc % 2], "hotb%d" % (cc % 2)
                    P.op("scalar", lambda e, ob_=ob_, o_=o_: e.copy(out=ob_[:, :TWI], in_=o_[:, :TWI]), r=[ok], w=[obk])
                    for j in range(TWI // 128):
                        g.ps_i = (g.ps_i + 1) % 8
                        pi = 4 * (1 - ti % 2) + (j % 4)
                        P.op("tensor", lambda e, pi=pi, j=j, ob_=ob_: e.transpose(psb16[pi][:, 0:128], ob_[:, j * 128:(j + 1) * 128], identb),
                             r=[obk, "identb"], w=["ps%d" % pi])
                        P.op("vector", lambda e, pi=pi, j=j, cc=cc, t0=t0: e.tensor_copy(
                            out=ztok[:, t0 // 128 + j, cc * 128:(cc + 1) * 128], in_=psb16[pi][:, 0:128]), r=["ps%d" % pi], w=["ztok"])
                else:
                    P.dma("gpsimd", g.CATT[1024 + cc * 128:1024 + (cc + 1) * 128, tok0 + t0:tok0 + t0 + TWI], o_[:, :TWI], ok, r=[ok], w=["CATT"])


def phase_c(g, l):
    P = g.P
    g.new_phase()
    g.ps_rot = list(range(8))
    last = (l == DEPTH - 1)
    tiles = TILES[1:] if last else TILES
    jf = lambda t0: 1 if t0 < NCTX else 0
    catn = g.carve([128, KC, T], BF16)
    carve_common(g, 256)
    xt = g.carve([128, 8, 512], F32)
    sq = g.carve([128, 8, 512], BF16)
    rstd = g.carve([128, 512], F32)
    mg = g.carve([128, KC], F32)
    xres = [g.carve([128, 512], F32) for i in range(2)]
    P.dma("sync", mg, g.mix_gT[l], "pc", w=["mg"])
    P.seal()
    for (t0, n) in tiles:
        for (c0, nk, nfeat) in [(0, 8, 1024), (8, 4, 512)]:
            srcv = g.CATT[c0 * 128:(c0 + nk) * 128, :].rearrange("(kc p) t -> p kc t", p=128)
            P.dma("sync", xt[:, :nk, :n], srcv[:, :, t0:t0 + n], "xt0", r=["CATT"], w=["xt0"])
            P.op("scalar", lambda e, n=n, nk=nk: e.activation(out=sq[:, :nk, :n], in_=xt[:, :nk, :n], func=AF.Square), r=["xt0"], w=["sq"])
            ps, psk = g.next_ps()
            for kc in range(nk):
                P.op("tensor", lambda e, kc=kc, n=n, ps=ps, nk=nk: e.matmul(ps[:, :n], g.ones_bf[:], sq[:, kc, :n], start=(kc == 0), stop=(kc == nk - 1)),
                     r=["sq", "ones_bf"], w=[psk])
            rstd_op(g, ps, psk, n, nfeat, rstd, "rstd")
            for kc in range(nk):
                P.op("vector", lambda e, kc=kc, n=n, t0=t0, c0=c0: e.scalar_tensor_tensor(
                    out=catn[:, c0 + kc, t0:t0 + n], in0=xt[:, kc, :n], scalar=mg[:, c0 + kc:c0 + kc + 1], in1=rstd[:, :n],
                    op0=ALU.mult, op1=ALU.mult), r=["xt0", "rstd", "mg"], w=["catn"])
        srcv = g.CATT[1536:2048, :].rearrange("(kc p) t -> p kc t", p=128)
        P.dma("sync", xt[:, :4, :n], srcv[:, :, t0:t0 + n], "xt0", r=["CATT"], w=["xt0"])
        P.op("scalar", lambda e, n=n, t0=t0: e.copy(out=catn[:, 12:16, t0:t0 + n], in_=xt[:, :4, :n]), r=["xt0"], w=["catn"])
    src = g.xT if l == 0 else g.xs
    skey = "xs"
    wv = g.w_out[l].rearrange("(kc p) n -> p kc n", p=128)
    cnt = [0]

    def evac(cb, m, t0, n, ps, psk):
        fc = cb // 128
        j = jf(t0)
        xr = xres[cnt[0] % 2]
        xk = "xres%d" % (cnt[0] % 2)
        cnt[0] += 1
        P.dma("sync", xr[:, :n], src[cb:cb + 128, t0:t0 + n], xk, r=[skey], w=[xk])
        stg, sk = next_stage(g)
        P.op("vector", lambda e: e.scalar_tensor_tensor(out=stg[:, :n], in0=ps[:, :n], scalar=g.mod[l][:, 32 + fc, j:j + 1], in1=xr[:, :n],
                                                        op0=ALU.mult, op1=ALU.add), r=[psk, xk, "mod%d" % l], w=[sk])
        P.dma("sync", g.XM[cb:cb + 128, t0:t0 + n], stg[:, :n], sk, r=[sk], w=["XM"])

    for c0 in range(0, D, 256):
        proj_fm(g, wv, c0, 256, catn, "catn", tiles, evac)
    g.new_phase()
    xn2 = g.carve([128, KC, 1024], BF16)
    hT = g.carve([128, 44, 1024], BF16)
    w1b = [g.carve([128, KC, 256], BF16) for i in range(2)]
    w2b = [g.carve([128, 44, 128], BF16) for i in range(2)]
    g.stage = [g.carve([128, 512], F32) for i in range(2)]
    g.stage_i = 0
    sg = [g.carve([128, 512], F32) for i in range(2)]
    xres = [g.carve([128, 512], F32) for i in range(2)]
    rb = rms_bufs(g)
    w1v = g.ffn_w1[l].rearrange("(kc p) n -> p kc n", p=128)
    w2v = g.ffn_w2[l].rearrange("(kc p) n -> p kc n", p=128)
    groups = [[(256 + 1024 * i, 512), (256 + 1024 * i + 512, 512)] for i in range(4)]
    if not last:
        groups = [[(0, 256)]] + groups
    wi = 0
    w2i = 0
    for grp in groups:
        g0 = grp[0][0]
        gn = sum(n for (_, n) in grp)
        rms_modulate(g, g.XM, l, g.gs2[l], "gs2_%d" % l, 48, xn2, "xn2", [(t0, 128) for t0 in range(g0, g0 + gn, 128)], jf,
                     bufs=rb, srckey="XM", dst_t0=g0)
        for hc in range(44):
            wb, wk = w1b[wi % 2], "w1b%d" % (wi % 2)
            wi += 1
            P.dma("sync", wb.rearrange("p kc n -> p (kc n)"), g.W1B[hc], wk, r=["W1B"], w=[wk])
            for (t0, n) in grp:
                pg, pgk = g.next_ps()
                pu, puk = g.next_ps()
                for (pp, ppk, cofs) in [(pg, pgk, 0), (pu, puk, 128)]:
                    for kc in range(KC):
                        P.op("tensor", lambda e, pp=pp, kc=kc, cofs=cofs, t0=t0, n=n, wb=wb, g0=g0: e.matmul(
                            pp[:, :n], wb[:, kc, cofs:cofs + 128], xn2[:, kc, t0 - g0:t0 - g0 + n], start=(kc == 0), stop=(kc == KC - 1)),
                            r=[wk, "xn2"], w=[ppk])
                s_, sk_ = sg[hc % 2], "sg%d" % (hc % 2)
                P.op("scalar", lambda e, s_=s_, pg=pg, n=n: e.activation(out=s_[:, :n], in_=pg[:, :n], func=AF.Silu), r=[pgk], w=[sk_])
                P.op("vector", lambda e, s_=s_, pu=pu, n=n, hc=hc, t0=t0, g0=g0: e.tensor_tensor(
                    out=hT[:, hc, t0 - g0:t0 - g0 + n], in0=pu[:, :n], in1=s_[:, :n], op=ALU.mult), r=[puk, sk_], w=["hT"])
        for fc in range(KC):
            wb, wk = w2b[w2i % 2], "w2b%d" % (w2i % 2)
            w2i += 1
            P.dma("sync", wb.rearrange("p hc n -> p (hc n)"), g.W2B[fc], wk, r=["W2B"], w=[wk])
            for (t0, n) in grp:
                ps, psk = g.next_ps()
                for hc in range(44):
                    P.op("tensor", lambda e, ps=ps, hc=hc, t0=t0, n=n, wb=wb, g0=g0: e.matmul(
                        ps[:, :n], wb[:, hc, :], hT[:, hc, t0 - g0:t0 - g0 + n], start=(hc == 0), stop=(hc == 43)), r=[wk, "hT"], w=[psk])
                j = jf(t0)
                xr = xres[fc % 2]
                xk = "xres%d" % (fc % 2)
                P.dma("sync", xr[:, :n], g.XM[fc * 128:(fc + 1) * 128, t0:t0 + n], xk, r=["XM"], w=[xk])
                stg, sk = next_stage(g)
                P.op("vector", lambda e, stg=stg, ps=ps, n=n, fc=fc, j=j, xr=xr: e.scalar_tensor_tensor(
                    out=stg[:, :n], in0=ps[:, :n], scalar=g.mod[l][:, 80 + fc, j:j + 1], in1=xr[:, :n], op0=ALU.mult, op1=ALU.add),
                    r=[psk, xk, "mod%d" % l], w=[sk])
                if last:
                    P.dma("gpsimd", g.outT[fc * 128:(fc + 1) * 128, t0 - NCTX:t0 - NCTX + n], stg[:, :n], sk, r=[sk], w=["outT"])
                else:
                    P.dma("gpsimd", g.xs[fc * 128:(fc + 1) * 128, t0:t0 + n], stg[:, :n], sk, r=[sk], w=["xs"])


def host_inputs(inputs, b):
    f = lambda a: np.ascontiguousarray(a, dtype=np.float32)
    x, ctx, c, c_ctx = inputs["x"], inputs["ctx"], inputs["c"], inputs["c_ctx"]
    m = {}
    m["xT"] = f(np.concatenate([ctx[b], x[b]], axis=0).T)
    cc = np.stack([c[b], c_ctx], axis=-1)
    m["cT"] = f(cc.reshape(KC, 128, 2).transpose(1, 0, 2))
    m["ada_w"] = f(inputs["ada_w"])
    m["ada_bT"] = f(inputs["ada_b"].reshape(DEPTH, 96, 128).transpose(0, 2, 1))
    m["norm1_gT"] = f(inputs["norm1_g"].reshape(DEPTH, KC, 128).transpose(0, 2, 1))
    m["norm2_gT"] = f(inputs["norm2_g"].reshape(DEPTH, KC, 128).transpose(0, 2, 1))
    m["w_in"] = f(inputs["w_in"])
    wi = inputs["w_in"]
    m["w_in_rot"] = f(np.concatenate([wi[:, :, 1056:1088], wi[:, :, 1024:1056]], axis=-1))
    qa, kva = inputs["mla_qa_norm"], inputs["mla_kva_norm"]
    m["mla_gT"] = f(np.concatenate([qa.reshape(DEPTH, 4, 128).transpose(0, 2, 1),
                                    kva.reshape(DEPTH, 4, 128).transpose(0, 2, 1)], axis=-1))
    qn_, kn_ = inputs["mla_q_norm"], inputs["mla_k_norm"]
    hg = np.zeros((DEPTH, 128, 6), np.float32)
    hg[:, :, 0] = qn_[:, :128]
    hg[:, :, 1] = kn_[:, :128]
    hg[:, :64, 2] = qn_[:, 128:192]
    hg[:, :64, 3] = np.concatenate([qn_[:, 160:192], qn_[:, 128:160]], -1)
    hg[:, :64, 4] = kn_[:, 128:192]
    hg[:, :64, 5] = np.concatenate([kn_[:, 160:192], kn_[:, 128:160]], -1)
    m["mla_hg"] = hg
    m["rope_cs"] = rope_tables()
    wq = inputs["mla_w_uq"]
    m["w_uq"] = f(wq)
    wq4 = wq.reshape(DEPTH, 512, 8, 192)
    m["w_uq_rot"] = f(np.concatenate([wq4[..., 160:192], wq4[..., 128:160]], -1).reshape(DEPTH, 512, 512))
    m["w_ukv"] = f(inputs["mla_w_ukv"])
    tri = np.zeros((2, 128, 128), np.float32)
    ii = np.arange(128)
    tri[0] = (ii[:, None] <= ii[None, :])
    tri[1] = (ii[:, None] >= ii[None, :])
    m["tri"] = tri
    m["ident"] = np.eye(128, dtype=np.float32)
    sel4 = np.zeros((4, 4, 128), np.float32)
    for r_ in range(4):
        sel4[r_, r_, :] = 1.0
    m["sel4"] = sel4
    cwm = np.concatenate([inputs["ml_conv_w"], inputs["ml_conv_b"][:, None, :]], axis=1)
    m["ml_cw"] = f(cwm.reshape(DEPTH, 4, 8, 128).transpose(0, 3, 2, 1))
    m["ml_gate_b"] = f(inputs["ml_gate_b"])
    m["mix_g"] = f(inputs["mix_norm_g"])
    m["mix_gT"] = f(inputs["mix_norm_g"].reshape(DEPTH, KC, 128).transpose(0, 2, 1))
    m.update(hyena_tables())
    hcw = np.concatenate([inputs["hy_conv_w"], inputs["hy_conv_b"][:, None, :]], axis=1)
    m["hy_cw"] = f(hcw.reshape(DEPTH, 4, 12, 128).transpose(0, 3, 2, 1))
    m["hy_w1"] = f(inputs["hy_w1"])
    m["hy_b12"] = f(np.stack([inputs["hy_b1"], inputs["hy_b2"]], axis=-1))
    m["hy_w2"] = f(inputs["hy_w2"])
    m["hy_w3"] = f(inputs["hy_w3"])
    m["hy_decay"] = f(inputs["hy_decay"])
    m["hy_skipT"] = f(inputs["hy_skip"].reshape(DEPTH, 2, 4, 128).transpose(0, 3, 1, 2))
    m["w_out"] = f(inputs["w_out"])
    m["ffn_w1"] = f(inputs["ffn_w1"])
    m["ffn_w2"] = f(inputs["ffn_w2"])
    return m


def rope_tables():
    rows = NLAT // 64
    row = np.repeat(np.arange(rows, dtype=np.float32), 64)
    col = np.tile(np.arange(64, dtype=np.float32), rows)
    freqs = (np.float32(10000.0) ** (-np.arange(16, dtype=np.float32) / np.float32(16))).astype(np.float32)
    ang = np.concatenate([row[:, None] * freqs, col[:, None] * freqs], axis=-1).astype(np.float32)
    cos, sin = np.cos(ang).astype(np.float32), np.sin(ang).astype(np.float32)
    out = np.zeros((2, 64, T), np.float32)
    out[0, :, :NCTX] = 1.0
    out[0, :32, NCTX:] = cos.T
    out[0, 32:, NCTX:] = cos.T
    out[1, :32, NCTX:] = -sin.T
    out[1, 32:, NCTX:] = sin.T
    return out


_HY_TABLES = {}


def hyena_tables():
    if _HY_TABLES:
        return _HY_TABLES
    import ml_dtypes
    out = {}
    for L in (4096, 256):
        t = (np.arange(L, dtype=np.float32) / np.float32(L)).astype(np.float32)
        bands = np.arange(1, 17, dtype=np.float32)
        ang = (np.float32(2.0 * math.pi) * t[:, None] * bands).astype(np.float32)
        feats = np.concatenate([t[:, None], np.cos(ang), np.sin(ang)], axis=-1).astype(np.float32)
        out["hy_feats%d" % L] = np.ascontiguousarray(feats.T)
        out["hy_negt%d" % L] = np.ascontiguousarray((-t).reshape(L // 128, 128).T)
        ii = np.arange(L, dtype=np.int64)
        ph = (ii[:, None] * ii[None, :]) % (2 * L)
        th = ph.astype(np.float64) * (2.0 * math.pi / (2 * L))
        cs = np.stack([np.cos(th), np.sin(th)]).astype(np.float32).astype(ml_dtypes.bfloat16)
        out["hy_cs%d" % L] = cs
    alt = np.ones((128, 512), np.float32)
    alt[1::2, :] = -1.0
    alt[0, 1::2] = -1.0
    out["hy_alt"] = alt
    _HY_TABLES.update(out)
    return _HY_TABLES


def kernel(**inputs):
    nc = build()
    inputs = {k: np.asarray(v) for k, v in inputs.items()}
    shared = host_inputs(inputs, 0)
    in_maps = []
    for core in range(8):
        b = core % 4
        m = dict(shared)
        if b != 0:
            mb = host_inputs_batch(inputs, b)
            m.update(mb)
        in_maps.append(m)
    res = run_bass_kernel_spmd(nc, in_maps, core_ids=list(range(8)))
    out = np.stack([np.ascontiguousarray(res.results[b]["outT"].T) for b in range(4)], axis=0)
    return out.astype(np.float32)


def host_inputs_batch(inputs, b):
    f = lambda a: np.ascontiguousarray(a, dtype=np.float32)
    x, ctx, c, c_ctx = inputs["x"], inputs["ctx"], inputs["c"], inputs["c_ctx"]
    m = {}
    m["xT"] = f(np.concatenate([ctx[b], x[b]], axis=0).T)
    cc = np.stack([c[b], c_ctx], axis=-1)
    m["cT"] = f(cc.reshape(KC, 128, 2).transpose(1, 0, 2))
    return m
```

```python
import math
from contextlib import ExitStack

import numpy as np
import concourse.bass as bass
import concourse.mybir as mybir
from concourse.bass_utils import run_bass_kernel_spmd

F32 = mybir.dt.float32
BF16 = mybir.dt.bfloat16
AF = mybir.ActivationFunctionType
ALU = mybir.AluOpType
AX = mybir.AxisListType

ENGS = ["sync", "scalar", "vector", "gpsimd", "tensor"]

D = 2048
KC = 16
NCTX = 256
NLAT = 4096
T = NCTX + NLAT
DEPTH = 2
EPS = 1e-6
N_IN = 4688
FFN_H = 5632
ARENA = 204800
TILES = [(0, 256)] + [(256 + 512 * i, 512) for i in range(8)]
NTILES = [(256 * i, 256) for i in range(17)]


class Prog:
    def __init__(self, nc):
        self.nc = nc
        self.ops = {e: [] for e in ENGS}
        self.last_w = {}
        self.readers = {}
        self.dma_cnt = {}
        self.fence = set()
        self.sealed = {}
        self.strict_same = {"scalar": True, "vector": True, "gpsimd": True,
                            "tensor": False, "sync": False}

    def _deps(self, r, w):
        deps = set()
        for k in r:
            lw = self.last_w.get(k)
            if lw is not None:
                deps.add(lw)
        for k in w:
            lw = self.last_w.get(k)
            if lw is not None:
                deps.add(lw)
            for rd in self.readers.get(k, ()):
                deps.add(rd)
        return deps

    def _commit(self, ref, r, w):
        for k in r:
            self.readers.setdefault(k, []).append(ref)
        for k in w:
            self.last_w[k] = ref
            self.readers[k] = []

    def barrier(self):
        f = set()
        for e in ENGS:
            for i in range(len(self.ops[e]) - 1, -1, -1):
                if self.ops[e][i]["dma"] is None:
                    f.add(("c", e, i))
                    break
        for k, c in self.dma_cnt.items():
            f.add(("d", k, c))
        self.fence = f

    def op(self, eng, fn, r=(), w=()):
        deps = self._deps(r, w) | self.fence
        ref = ("c", eng, len(self.ops[eng]))
        self.ops[eng].append(dict(fn=fn, deps=deps, ref=ref, dma=None))
        self._commit(ref, r, w)
        return ref

    def seal(self, key="pc"):
        c = self.dma_cnt.get(key, 0)
        for k, ref in list(self.last_w.items()):
            if ref[0] == "d" and ref[1] == key:
                self.last_w[k] = ("d", key, c)
        self.sealed[key] = c

    def dma(self, eng, out, in_, key, r=(), w=(), **kw):
        deps = self._deps(r, w) | self.fence
        if self.sealed.get(key):
            deps = deps | {("d", key, self.sealed[key])}
        cnt = self.dma_cnt.get(key, 0) + 1
        self.dma_cnt[key] = cnt
        ref = ("d", key, cnt)
        self.ops[eng].append(dict(fn=lambda e: e.dma_start(out=out, in_=in_, **kw),
                                  deps=deps, ref=ref, dma=key))
        self._commit(ref, r, w)
        return ref

    def emit(self):
        nc = self.nc
        signaled = {e: set() for e in ENGS}
        for e in ENGS:
            for o in self.ops[e]:
                for d in o["deps"]:
                    if d[0] == "c":
                        if d[1] == e and not self.strict_same[e]:
                            continue
                        signaled[d[1]].add(d[2])
        sigcount = {}
        for e in ENGS:
            c = 0
            m = {}
            for i, o in enumerate(self.ops[e]):
                if o["dma"] is None and i in signaled[e]:
                    c += 1
                    m[i] = c
            sigcount[e] = m
        with ExitStack() as st:
            csem = {e: st.enter_context(nc.semaphore("s_" + e)) for e in ENGS}
            dsem = {k: st.enter_context(nc.semaphore("d_%d" % i))
                    for i, k in enumerate(self.dma_cnt)}
            block = st.enter_context(nc.Block())

            def make(eng):
                def body(e):
                    waited = {}
                    for i, o in enumerate(self.ops[eng]):
                        waits = {}
                        for d in o["deps"]:
                            if d[0] == "c":
                                if d[1] == eng and not self.strict_same[eng]:
                                    continue
                                s, v, kk = csem[d[1]], sigcount[d[1]][d[2]], ("c", d[1])
                            else:
                                s, v, kk = dsem[d[1]], 16 * d[2], ("d", d[1])
                            if waited.get(kk, 0) >= v:
                                continue
                            if kk not in waits or waits[kk][1] < v:
                                waits[kk] = (s, v)
                        for kk, (s, v) in waits.items():
                            e.wait_ge(s, v)
                            waited[kk] = v
                        ins = o["fn"](e)
                        if o["dma"] is not None:
                            ins.then_inc(dsem[o["dma"]], 16)
                        elif i in sigcount[eng]:
                            ins.then_inc(csem[eng], 1)
                    if eng == "sync":
                        for k, c in self.dma_cnt.items():
                            e.wait_ge(dsem[k], 16 * c)
                return body

            block.sync(make("sync"))
            block.scalar(make("scalar"))
            block.vector(make("vector"))
            block.gpsimd(make("gpsimd"))
            block.tensor(make("tensor"))


class Ctx:
    pass


def build(stop_after="all", dbg=()):
    nc = bass.Bass("TRN2", target_bir_lowering=False)
    P = Prog(nc)
    g = Ctx()
    g.nc, g.P = nc, P
    g.din = lambda name, shape, dt=F32: nc.dram_tensor(name, list(shape), dt, kind="ExternalInput").ap()
    g.dout = lambda name, shape, dt=F32: nc.dram_tensor(name, list(shape), dt, kind="ExternalOutput").ap()
    dbgset = set(dbg)

    def dscr(name, shape, dt=F32):
        kind = "ExternalOutput" if name in dbgset else "Internal"
        return nc.dram_tensor(name, list(shape), dt, kind=kind).ap()
    g.dscr = dscr

    with ExitStack() as st:
        g.st = st
        g.sb = lambda name, shape, dt=F32: st.enter_context(nc.sbuf_tensor(name, list(shape), dt))
        g.ps = [st.enter_context(nc.psum_tensor("ps%d" % i, [128, 512], F32)) for i in range(8)]
        g.ps_i = 0

        g.ps_rot = list(range(8))

        def next_ps():
            g.ps_i = (g.ps_i + 1) % len(g.ps_rot)
            i = g.ps_rot[g.ps_i]
            return g.ps[i], "ps%d" % i
        g.next_ps = next_ps
        g.psb = lambda i: (g.ps[i], "ps%d" % i)

        g.arena = st.enter_context(nc.sbuf_tensor("arena", [128, ARENA // 4], F32))
        g.arena_off = 0

        def carve(shape, dt=F32):
            esz = 2 if dt == BF16 else 4
            free = int(np.prod(shape[1:]))
            nbytes = (free * esz + 63) // 64 * 64
            off = g.arena_off
            assert off + nbytes <= ARENA, ("arena overflow", off, nbytes)
            g.arena_off = off + nbytes
            words = (free * esz + 3) // 4
            v = g.arena[:, off // 4:off // 4 + words]
            if dt == BF16:
                v = v.bitcast(BF16)
            elif dt != F32:
                v = v.bitcast(dt)
            v = v[:shape[0], :free]
            if len(shape) == 3:
                v = v.rearrange("p (a b) -> p a b", b=shape[2])
            return v
        g.carve = carve

        def new_phase():
            g.arena_off = 0
            P.barrier()
        g.new_phase = new_phase

        declare_io(g)
        consts(g)
        for l in range(DEPTH):
            phase_mod(g, l)
        for l in range(DEPTH):
            if stop_after == "mod":
                break
            phase_a(g, l)
            if stop_after == "a%d" % l:
                break
            phase_mla(g, l)
            if stop_after == "mla%d" % l:
                break
            if "noattn" not in dbgset:
                phase_attn(g, l)
            if stop_after == "attn%d" % l:
                break
            if "nomlstm" not in dbgset:
                phase_mlstm(g, l)
            if stop_after == "mlstm%d" % l:
                break
            hyena_seq(g, l, 4096, NCTX)
            if l == 0 and "nohyctx" not in dbgset:
                hyena_seq(g, l, 256, 0)
            if stop_after == "hy%d" % l:
                break
            phase_c(g, l)
            if stop_after == "c%d" % l:
                break
        P.emit()
    return nc


def declare_io(g):
    g.xT = g.din("xT", [D, T])
    g.cT = g.din("cT", [128, KC, 2])
    g.ada_w = g.din("ada_w", [DEPTH, D, 6 * D])
    g.ada_bT = g.din("ada_bT", [DEPTH, 128, 96])
    g.norm1_gT = g.din("norm1_gT", [DEPTH, 128, KC])
    g.norm2_gT = g.din("norm2_gT", [DEPTH, 128, KC])
    g.w_in = g.din("w_in", [DEPTH, D, N_IN])
    g.w_in_rot = g.din("w_in_rot", [DEPTH, D, 64])
    g.PT = [g.dscr("PT%d" % l, [N_IN + 64, T]) for l in range(DEPTH)]
    g.PTM = [g.dscr("PTM%d" % l, [T, 1040]) for l in range(DEPTH)]
    g.modd = g.dscr("modd", [DEPTH, 128, 96 * 2])
    g.mla_gT = g.din("mla_gT", [DEPTH, 128, 8])
    g.mla_hg = g.din("mla_hg", [DEPTH, 128, 6])
    g.rope_cs = g.din("rope_cs", [2, 64, T])
    g.w_uq = g.din("w_uq", [DEPTH, 512, 1536])
    g.w_uq_rot = g.din("w_uq_rot", [DEPTH, 512, 512])
    g.w_ukv = g.din("w_ukv", [DEPTH, 512, 2048])
    g.QT = g.dscr("QT", [8, 192, T], BF16)
    g.tri = g.din("tri", [2, 128, 128])
    g.ident = g.din("ident", [128, 128])
    g.sel4 = g.din("sel4", [4, 4, 128])
    g.ml_cw = g.din("ml_cw", [DEPTH, 128, 8, 4])
    g.ml_gate_b = g.din("ml_gate_b", [DEPTH, 16])
    g.mix_g = g.din("mix_g", [DEPTH, D])
    g.mix_gT = g.din("mix_gT", [DEPTH, 128, KC])
    g.HD = [g.dscr("HD%d" % i, [T, 512]) for i in range(2)]
    g.hy_feats = {4096: g.din("hy_feats4096", [33, 4096]), 256: g.din("hy_feats256", [33, 256])}
    g.hy_negt = {4096: g.din("hy_negt4096", [128, 32]), 256: g.din("hy_negt256", [128, 2])}
    g.hy_cs = {4096: g.din("hy_cs4096", [2, 16, 8, 128, 1024], BF16), 256: g.din("hy_cs256", [2, 1, 1, 128, 512], BF16)}
    g.hy_alt = g.din("hy_alt", [128, 512])
    g.hy_cw = g.din("hy_cw", [DEPTH, 128, 12, 4])
    g.hy_w1 = g.din("hy_w1", [DEPTH, 33, 64])
    g.hy_b12 = g.din("hy_b12", [DEPTH, 64, 2])
    g.hy_w2 = g.din("hy_w2", [DEPTH, 64, 64])
    g.hy_w3 = g.din("hy_w3", [DEPTH, 64, 2048])
    g.hy_decay = g.din("hy_decay", [DEPTH, 2048])
    g.hy_skipT = g.din("hy_skipT", [DEPTH, 128, 2, 4])
    g.HS = g.dscr("HS", [2, 4096, 1024], BF16)
    g.KF = g.dscr("KF", [2, 4097, 1024])
    g.HV = g.dscr("HV", [1536, 4096])
    g.HZ2 = g.dscr("HZ2", [512, 4096])
    g.w_out = g.din("w_out", [DEPTH, D, D])
    g.ffn_w1 = g.din("ffn_w1", [DEPTH, D, 2 * FFN_H])
    g.ffn_w2 = g.din("ffn_w2", [DEPTH, FFN_H, D])
    g.XM = g.dscr("XM", [D, T])
    g.W1B = g.dscr("W1B", [44, 128, 4096], BF16)
    g.W2B = g.dscr("W2B", [16, 128, 5632], BF16)
    g.xs = g.dscr("xs", [D, T])
    g.outT = g.dout("outT", [D, NLAT])
    g.KT = g.dscr("KT", [8, 192, T], BF16)
    g.VTM = g.dscr("VTM", [T, 1024], BF16)
    g.CATT = g.dscr("CATT", [D, T])


def consts(g):
    P, nc = g.P, g.nc
    g.ones_bf = g.sb("ones_bf", [128, 128], BF16)
    P.op("vector", lambda e: e.memset(g.ones_bf[:], 1.0), w=["ones_bf"])
    g.eps_t = g.sb("eps_t", [128, 1], F32)
    P.op("vector", lambda e: e.memset(g.eps_t[:], EPS), w=["eps_t"])
    g.mod = [g.sb("mod%d" % l, [128, 96, 2]) for l in range(DEPTH)]
    g.gs1 = [g.sb("gs1_%d" % l, [128, KC, 2]) for l in range(DEPTH)]
    g.gs2 = [g.sb("gs2_%d" % l, [128, KC, 2]) for l in range(DEPTH)]
    g.n1g = g.sb("n1g", [128, DEPTH, KC])
    g.n2g = g.sb("n2g", [128, DEPTH, KC])
    P.dma("sync", g.n1g[:], g.norm1_gT.rearrange("l p k -> p l k"), "pc", w=["n1g"])
    P.dma("sync", g.n2g[:], g.norm2_gT.rearrange("l p k -> p l k"), "pc", w=["n2g"])
    g.scT = g.sb("scT", [128, KC, 2])
    g.cTs = g.sb("cTs", [128, KC, 2])
    P.dma("sync", g.cTs[:], g.cT[:, :, :], "pc", w=["cTs"])
    P.seal()
    P.op("scalar", lambda e: e.activation(out=g.scT[:], in_=g.cTs[:], func=AF.Silu), r=["cTs"], w=["scT"])
    g.scb = g.sb("scb", [128, KC, 2], BF16)
    P.op("vector", lambda e: e.tensor_copy(out=g.scb[:], in_=g.scT[:]), r=["scT"], w=["scb"])


def next_stage(g):
    i = g.stage_i
    g.stage_i = (i + 1) % len(g.stage)
    return g.stage[i], "stage%d" % i


def carve_common(g, wcols):
    g.wbuf = [g.carve([128, KC * wcols], BF16) for i in range(2)]
    g.w_i = 0
    g.stage = [g.carve([128, 512], F32) for i in range(4)]
    g.stage_i = 0


def phase_mod(g, l):
    P, nc = g.P, g.nc
    g.new_phase()
    carve_common(g, 512)
    adab = g.carve([128, 96], F32)
    ak = "adab"
    P.dma("sync", adab, g.ada_bT[l], "pc", w=[ak])
    P.seal()
    wv = g.ada_w[l].rearrange("(kc p) n -> p kc n", p=128)
    ps, psk = g.next_ps()
    psv = ps[:, 0:192].rearrange("p (c j) -> p c j", j=2)
    for sbk in range(24):
        wb = g.wbuf[sbk % 2]
        wk = "wbuf%d" % (sbk % 2)
        wt = wb.rearrange("p (kc n) -> p kc n", n=512)
        P.dma("gpsimd", wt, wv[:, :, sbk * 512:(sbk + 1) * 512], wk, w=[wk])
        for j in range(4):
            cb = sbk * 4 + j
            for kc in range(KC):
                P.op("tensor", lambda e, wt=wt, j=j, kc=kc, cb=cb: e.matmul(
                    psv[:, cb, :], wt[:, kc, j * 128:(j + 1) * 128], g.scb[:, kc, :],
                    start=(kc == 0), stop=(kc == KC - 1)),
                    r=[wk, "scb"], w=[psk])
    mk = "mod%d" % l
    for j in range(2):
        P.op("vector", lambda e, j=j: e.tensor_tensor(out=g.mod[l][:, :, j], in0=psv[:, :, j], in1=adab,
                                                     op=ALU.add), r=[psk, ak], w=[mk])
    for j in range(2):
        P.op("vector", lambda e, j=j: e.scalar_tensor_tensor(
            out=g.gs1[l][:, :, j], in0=g.mod[l][:, 16:32, j], scalar=1.0, in1=g.n1g[:, l, :],
            op0=ALU.add, op1=ALU.mult), r=[mk, "n1g"], w=["gs1_%d" % l])
        P.op("vector", lambda e, j=j: e.scalar_tensor_tensor(
            out=g.gs2[l][:, :, j], in0=g.mod[l][:, 64:80, j], scalar=1.0, in1=g.n2g[:, l, :],
            op0=ALU.add, op1=ALU.mult), r=[mk, "n2g"], w=["gs2_%d" % l])
    P.dma("sync", g.modd[l], g.mod[l][:].rearrange("p c j -> p (c j)"), mk, r=[mk], w=["modd"])


def rms_bufs(g, NT=128):
    return (g.carve([128, KC, NT], F32), g.carve([128, KC, NT], BF16), g.carve([128, NT], F32),
            [g.carve([128, NT], F32) for i in range(2)])


def rms_modulate(g, src_T, l, gs, gskey, shift_lo, dstT, dkey, tiles, jfn, bufs=None, srckey=None, dst_t0=0):
    P, nc = g.P, g.nc
    xt, sq, rstd, tmps = bufs if bufs is not None else rms_bufs(g)
    rk = [srckey] if srckey else []
    srcv = src_T.rearrange("(kc p) t -> p kc t", p=128)
    for (t0, n) in tiles:
        j = jfn(t0)
        P.dma("sync", xt[:, :, :n], srcv[:, :, t0:t0 + n], "xt0", r=rk, w=["xt0"])
        P.op("scalar", lambda e, n=n: e.activation(out=sq[:, :, :n], in_=xt[:, :, :n], func=AF.Square),
             r=["xt0"], w=["sq"])
        ps, psk = g.next_ps()
        for kc in range(KC):
            P.op("tensor", lambda e, kc=kc, n=n, ps=ps: e.matmul(ps[:, :n], g.ones_bf[:], sq[:, kc, :n],
                                                              start=(kc == 0), stop=(kc == KC - 1)),
                 r=["sq", "ones_bf"], w=[psk])
        P.op("scalar", lambda e, n=n, ps=ps: e.activation(out=rstd[:, :n], in_=ps[:, :n], func=AF.Sqrt,
                                                       bias=g.eps_t[:, 0:1], scale=1.0 / D),
             r=[psk, "eps_t"], w=["rstd"])
        P.op("vector", lambda e, n=n: e.reciprocal(out=rstd[:, :n], in_=rstd[:, :n]), r=["rstd"], w=["rstd"])
        for kc in range(KC):
            tmp = tmps[kc % 2]
            tk = "tmp%d" % (kc % 2)
            P.op("vector", lambda e, kc=kc, n=n, tmp=tmp, j=j: e.scalar_tensor_tensor(
                out=tmp[:, :n], in0=xt[:, kc, :n], scalar=gs[:, kc, j:j + 1], in1=rstd[:, :n],
                op0=ALU.mult, op1=ALU.mult), r=["xt0", "rstd", gskey], w=[tk])
            P.op("scalar", lambda e, kc=kc, n=n, tmp=tmp, j=j, t0=t0: e.activation(
                out=dstT[:, kc, t0 - dst_t0:t0 - dst_t0 + n], in_=tmp[:, :n], func=AF.Identity,
                bias=g.mod[l][:, shift_lo + kc, j:j + 1], scale=1.0), r=[tk, "mod%d" % l], w=[dkey])


def proj_fm(g, w_view, col0, ncols, actT, akey, tiles, evac):
    P = g.P
    nkc = w_view.shape[1]
    wi = g.w_i
    g.w_i = (wi + 1) % 2
    wk = "wbuf%d" % wi
    wt = g.wbuf[wi][:, :nkc * ncols].rearrange("p (kc n) -> p kc n", n=ncols)
    P.dma("gpsimd", wt, w_view[:, :, col0:col0 + ncols], wk, w=[wk])
    for c0 in range(0, ncols, 128):
        m = min(128, ncols - c0)
        for (t0, n) in tiles:
            ps, psk = g.next_ps()
            for kc in range(nkc):
                P.op("tensor", lambda e, kc=kc, c0=c0, m=m, t0=t0, n=n, ps=ps: e.matmul(
                    ps[:m, :n], wt[:, kc, c0:c0 + m], actT[:, kc, t0:t0 + n],
                    start=(kc == 0), stop=(kc == nkc - 1)), r=[wk, akey], w=[psk])
            evac(col0 + c0, m, t0, n, ps, psk)


def phase_a(g, l):
    P, nc = g.P, g.nc
    g.new_phase()
    xnT = g.carve([128, KC, T], BF16)
    carve_common(g, 256)
    src = g.xT if l == 0 else g.xs
    rms_modulate(g, src, l, g.gs1[l], "gs1_%d" % l, 0, xnT, "xnT", [(256 * i, 256) for i in range(17)],
                 lambda t0: 1 if t0 < NCTX else 0, bufs=rms_bufs(g, 256), srckey="xs")
    wv = g.w_in[l].rearrange("(kc p) n -> p kc n", p=128)
    wrv = g.w_in_rot[l].rearrange("(kc p) n -> p kc n", p=128)
    PT = g.PT[l]
    cnt = [0]

    def evac_to(PTrow0):
        def evac(cb, m, t0, n, ps, psk):
            stg, sk = next_stage(g)
            if cnt[0] % 2 == 0:
                P.op("vector", lambda e: e.tensor_copy(out=stg[:m, :n], in_=ps[:m, :n]), r=[psk], w=[sk])
            else:
                P.op("scalar", lambda e: e.copy(out=stg[:m, :n], in_=ps[:m, :n]), r=[psk], w=[sk])
            cnt[0] += 1
            P.dma("sync", PT[PTrow0 + cb:PTrow0 + cb + m, t0:t0 + n], stg[:m, :n], sk, r=[sk], w=["PT%d" % l])
        return evac

    blocks = [(c, 256) for c in range(0, 1024, 256)] + [(1024, 64)] + [(c, 256) for c in range(1088, 3648, 256)]
    for (c0, ncol) in blocks:
        proj_fm(g, wv, c0, ncol, xnT, "xnT", TILES, evac_to(0))
    proj_fm(g, wrv, 0, 64, xnT, "xnT", TILES, evac_to(N_IN))
    for (c0, ncol) in [(3648, 256), (3904, 256), (4160, 256), (4416, 256), (4672, 16)]:
        wi = g.w_i
        g.w_i = (wi + 1) % 2
        wk = "wbuf%d" % wi
        wt = g.wbuf[wi][:, :KC * ncol].rearrange("p (kc n) -> p kc n", n=ncol)
        P.dma("gpsimd", wt, wv[:, :, c0:c0 + ncol], wk, w=[wk])
        for c in range(34):
            ps, psk = g.next_ps()
            for kc in range(KC):
                P.op("tensor", lambda e, kc=kc, c=c, ps=ps, wt=wt, ncol=ncol: e.matmul(
                    ps[:, :ncol], xnT[:, kc, c * 128:(c + 1) * 128], wt[:, kc, :],
                    start=(kc == 0), stop=(kc == KC - 1)), r=[wk, "xnT"], w=[psk])
            stg, sk = next_stage(g)
            if c % 2 == 0:
                P.op("vector", lambda e, ps=ps, stg=stg, ncol=ncol: e.tensor_copy(out=stg[:, :ncol], in_=ps[:, :ncol]),
                     r=[psk], w=[sk])
            else:
                P.op("scalar", lambda e, ps=ps, stg=stg, ncol=ncol: e.copy(out=stg[:, :ncol], in_=ps[:, :ncol]),
                     r=[psk], w=[sk])
            P.dma("sync", g.PTM[l][c * 128:(c + 1) * 128, c0 - 3648:c0 - 3648 + ncol], stg[:, :ncol], sk,
                  r=[sk], w=["PTM%d" % l])


def rstd_op(g, ps, psk, n, nfeat, rstd, rkey, rows=128):
    P = g.P
    P.op("scalar", lambda e: e.activation(out=rstd[:rows, :n], in_=ps[:rows, :n], func=AF.Sqrt,
                                          bias=g.eps_t[:rows, 0:1], scale=1.0 / nfeat), r=[psk, "eps_t"], w=[rkey])
    P.op("vector", lambda e: e.reciprocal(out=rstd[:rows, :n], in_=rstd[:rows, :n]), r=[rkey], w=[rkey])


def phase_mla(g, l):
    P = g.P
    g.new_phase()
    g.ps_rot = list(range(8))
    PT = g.PT[l]
    cqn = g.carve([128, 4, T], BF16)
    ckvn = g.carve([128, 4, T], BF16)
    cos2 = g.carve([64, T], F32)
    sin2 = g.carve([64, T], F32)
    KR = g.carve([64, T], F32)
    sqr = g.carve([64, T], BF16)
    xt = g.carve([128, 4, 512], F32)
    sq = g.carve([128, 4, 512], BF16)
    rstd = g.carve([128, 512], F32)
    gT = g.carve([128, 8], F32)
    hg = g.carve([128, 6], F32)
    wq = g.carve([128, 4, 256], BF16)
    wk = g.carve([128, 4, 128], BF16)
    wvv = g.carve([128, 4, 1024], BF16)
    stb = [g.carve([128, 512], BF16) for i in range(4)]
    tA = g.carve([64, 512], F32)
    tB = g.carve([64, 512], F32)
    P.dma("sync", gT, g.mla_gT[l], "pc", w=["gT"])
    P.dma("sync", hg, g.mla_hg[l], "pc", w=["hg"])
    P.dma("sync", cos2, g.rope_cs[0], "pc", w=["cos2"])
    P.dma("sync", sin2, g.rope_cs[1], "pc", w=["sin2"])
    P.seal()
    sc = 192.0 ** -0.5
    for c in (0, 2, 3):
        P.op("vector", lambda e, c=c: e.tensor_scalar(out=hg[:, c:c + 1], in0=hg[:, c:c + 1], scalar1=sc, scalar2=None,
                                                      op0=ALU.mult), r=["hg"], w=["hg"])
    for (r0, dst, dk, gc) in [(0, cqn, "cqn", 0), (512, ckvn, "ckvn", 4)]:
        srcv = PT[r0:r0 + 512, :].rearrange("(kc p) t -> p kc t", p=128)
        for (t0, n) in TILES:
            P.dma("sync", xt[:, :, :n], srcv[:, :, t0:t0 + n], "xt0", r=["PT%d" % l], w=["xt0"])
            P.op("scalar", lambda e, n=n: e.activation(out=sq[:, :, :n], in_=xt[:, :, :n], func=AF.Square),
                 r=["xt0"], w=["sq"])
            ps, psk = g.next_ps()
            for kc in range(4):
                P.op("tensor", lambda e, kc=kc, n=n, ps=ps: e.matmul(ps[:, :n], g.ones_bf[:], sq[:, kc, :n],
                                                                  start=(kc == 0), stop=(kc == 3)),
                     r=["sq", "ones_bf"], w=[psk])
            rstd_op(g, ps, psk, n, 512, rstd, "rstd")
            for kc in range(4):
                P.op("vector", lambda e, kc=kc, n=n, t0=t0, dst=dst, gc=gc: e.scalar_tensor_tensor(
                    out=dst[:, kc, t0:t0 + n], in0=xt[:, kc, :n], scalar=gT[:, gc + kc:gc + kc + 1], in1=rstd[:, :n],
                    op0=ALU.mult, op1=ALU.mult), r=["xt0", "rstd", "gT"], w=[dk])
    for (t0, n) in TILES:
        P.dma("sync", xt[:64, 0, :n], PT[1024:1088, t0:t0 + n], "xt0", r=["PT%d" % l], w=["xt0"])
        P.dma("sync", xt[:64, 1, :n], PT[N_IN:N_IN + 64, t0:t0 + n], "xt0", r=["PT%d" % l], w=["xt0"])
        P.op("scalar", lambda e, n=n, t0=t0: e.activation(out=sqr[:, t0:t0 + n], in_=xt[:64, 0, :n], func=AF.Square),
             r=["xt0"], w=["sqr"])
        P.op("vector", lambda e, n=n, t0=t0: e.scalar_tensor_tensor(
            out=KR[:, t0:t0 + n], in0=xt[:64, 0, :n], scalar=hg[:64, 4:5], in1=cos2[:, t0:t0 + n],
            op0=ALU.mult, op1=ALU.mult), r=["xt0", "hg", "cos2"], w=["KR"])
        P.op("vector", lambda e, n=n, t0=t0: e.scalar_tensor_tensor(
            out=tA[:, :n], in0=xt[:64, 1, :n], scalar=hg[:64, 5:6], in1=sin2[:, t0:t0 + n],
            op0=ALU.mult, op1=ALU.mult), r=["xt0", "hg", "sin2"], w=["tA"])
        P.op("vector", lambda e, n=n, t0=t0: e.tensor_tensor(out=KR[:, t0:t0 + n], in0=KR[:, t0:t0 + n], in1=tA[:, :n],
                                                           op=ALU.add), r=["tA", "KR"], w=["KR"])
    wuq = g.w_uq[l].rearrange("(kc p) n -> p kc n", p=128)
    wuqr = g.w_uq_rot[l].rearrange("(kc p) n -> p kc n", p=128)
    wukv = g.w_ukv[l].rearrange("(kc p) n -> p kc n", p=128)
    qtiles = TILES if l == 0 else TILES[1:]
    si = [0]

    def nstb():
        si[0] = (si[0] + 1) % 4
        return stb[si[0]], "stb%d" % si[0]

    for h in range(8):
        P.dma("gpsimd", wq[:, :, 0:192], wuq[:, :, h * 192:(h + 1) * 192], "wq", w=["wq"])
        P.dma("gpsimd", wq[:, :, 192:256], wuqr[:, :, h * 64:(h + 1) * 64], "wq", w=["wq"])
        for (t0, n) in qtiles:
            psn, kn_ = g.next_ps()
            psr, kr_ = g.next_ps()
            pst, kt_ = g.next_ps()
            for (ps, pk, c0, m) in [(psn, kn_, 0, 128), (psr, kr_, 128, 64), (pst, kt_, 192, 64)]:
                for kc in range(4):
                    P.op("tensor", lambda e, ps=ps, c0=c0, m=m, kc=kc, t0=t0, n=n: e.matmul(
                        ps[:m, :n], wq[:, kc, c0:c0 + m], cqn[:, kc, t0:t0 + n], start=(kc == 0), stop=(kc == 3)),
                        r=["wq", "cqn"], w=[pk])
            P.op("scalar", lambda e, n=n, psn=psn: e.activation(out=sq[:, 0, :n], in_=psn[:, :n], func=AF.Square),
                 r=[kn_], w=["sq"])
            P.op("scalar", lambda e, n=n, psr=psr: e.activation(out=sq[:64, 1, :n], in_=psr[:64, :n], func=AF.Square),
                 r=[kr_], w=["sq"])
            pss, ks_ = g.next_ps()
            P.op("tensor", lambda e, n=n, pss=pss: e.matmul(pss[:, :n], g.ones_bf[:], sq[:, 0, :n], start=True, stop=False),
                 r=["sq", "ones_bf"], w=[ks_])
            P.op("tensor", lambda e, n=n, pss=pss: e.matmul(pss[:, :n], g.ones_bf[:64, :], sq[:64, 1, :n], start=False, stop=True),
                 r=["sq", "ones_bf"], w=[ks_])
            rstd_op(g, pss, ks_, n, 192, rstd, "rstd")
            o1, o1k = nstb()
            P.op("vector", lambda e, n=n, psn=psn, o1=o1: e.scalar_tensor_tensor(
                out=o1[:, :n], in0=psn[:, :n], scalar=hg[:, 0:1], in1=rstd[:, :n], op0=ALU.mult, op1=ALU.mult),
                r=[kn_, "hg", "rstd"], w=[o1k])
            P.dma("sync", g.QT[h, 0:128, t0:t0 + n], o1[:, :n], o1k, r=[o1k], w=["QT"])
            P.op("vector", lambda e, n=n, psr=psr, t0=t0: e.scalar_tensor_tensor(
                out=tA[:, :n], in0=psr[:64, :n], scalar=hg[:64, 2:3], in1=cos2[:, t0:t0 + n], op0=ALU.mult, op1=ALU.mult),
                r=[kr_, "hg", "cos2"], w=["tA"])
            P.op("vector", lambda e, n=n, pst=pst, t0=t0: e.scalar_tensor_tensor(
                out=tB[:, :n], in0=pst[:64, :n], scalar=hg[:64, 3:4], in1=sin2[:, t0:t0 + n], op0=ALU.mult, op1=ALU.mult),
                r=[kt_, "hg", "sin2"], w=["tB"])
            P.op("vector", lambda e, n=n: e.tensor_tensor(out=tA[:, :n], in0=tA[:, :n], in1=tB[:, :n], op=ALU.add),
                 r=["tA", "tB"], w=["tA"])
            o2, o2k = nstb()
            P.op("vector", lambda e, n=n, o2=o2: e.tensor_tensor(out=o2[:64, :n], in0=tA[:, :n], in1=rstd[:64, :n],
                                                              op=ALU.mult), r=["tA", "rstd"], w=[o2k])
            P.dma("sync", g.QT[h, 128:192, t0:t0 + n], o2[:64, :n], o2k, r=[o2k], w=["QT"])
    for h in range(8):
        P.dma("gpsimd", wk[:, :, :], wukv[:, :, h * 256:h * 256 + 128], "wk", w=["wk"])
        for (t0, n) in TILES:
            psn, kn_ = g.next_ps()
            for kc in range(4):
                P.op("tensor", lambda e, psn=psn, kc=kc, t0=t0, n=n: e.matmul(
                    psn[:, :n], wk[:, kc, :], ckvn[:, kc, t0:t0 + n], start=(kc == 0), stop=(kc == 3)),
                    r=["wk", "ckvn"], w=[kn_])
            P.op("scalar", lambda e, n=n, psn=psn: e.activation(out=sq[:, 0, :n], in_=psn[:, :n], func=AF.Square),
                 r=[kn_], w=["sq"])
            pss, ks_ = g.next_ps()
            P.op("tensor", lambda e, n=n, pss=pss: e.matmul(pss[:, :n], g.ones_bf[:], sq[:, 0, :n], start=True, stop=False),
                 r=["sq", "ones_bf"], w=[ks_])
            P.op("tensor", lambda e, n=n, pss=pss, t0=t0: e.matmul(pss[:, :n], g.ones_bf[:64, :], sqr[:, t0:t0 + n],
                                                                 start=False, stop=True),
                 r=["sqr", "ones_bf"], w=[ks_])
            rstd_op(g, pss, ks_, n, 192, rstd, "rstd")
            o1, o1k = nstb()
            P.op("vector", lambda e, n=n, psn=psn, o1=o1: e.scalar_tensor_tensor(
                out=o1[:, :n], in0=psn[:, :n], scalar=hg[:, 1:2], in1=rstd[:, :n], op0=ALU.mult, op1=ALU.mult),
                r=[kn_, "hg", "rstd"], w=[o1k])
            P.dma("sync", g.KT[h, 0:128, t0:t0 + n], o1[:, :n], o1k, r=[o1k], w=["KT"])
            o2, o2k = nstb()
            P.op("vector", lambda e, n=n, o2=o2, t0=t0: e.tensor_tensor(out=o2[:64, :n], in0=KR[:, t0:t0 + n],
                                                                     in1=rstd[:64, :n], op=ALU.mult),
                 r=["KR", "rstd"], w=[o2k])
            P.dma("sync", g.KT[h, 128:192, t0:t0 + n], o2[:64, :n], o2k, r=[o2k], w=["KT"])
    for kc in range(4):
        P.dma("gpsimd", wvv[:, kc, :].rearrange("p (h x) -> p h x", x=128),
              g.w_ukv[l][kc * 128:(kc + 1) * 128, :].rearrange("p (h x) -> p h x", x=256)[:, :, 128:256], "wvv", w=["wvv"])
    for c in range(34):
        for half in range(2):
            ps, psk = g.next_ps()
            for kc in range(4):
                P.op("tensor", lambda e, ps=ps, kc=kc, c=c, half=half: e.matmul(
                    ps[:, :], ckvn[:, kc, c * 128:(c + 1) * 128], wvv[:, kc, half * 512:(half + 1) * 512],
                    start=(kc == 0), stop=(kc == 3)), r=["wvv", "ckvn"], w=[psk])
            o1, o1k = nstb()
            if half == 0:
                P.op("vector", lambda e, ps=ps, o1=o1: e.tensor_copy(out=o1[:, :], in_=ps[:, :]), r=[psk], w=[o1k])
            else:
                P.op("scalar", lambda e, ps=ps, o1=o1: e.copy(out=o1[:, :], in_=ps[:, :]), r=[psk], w=[o1k])
            P.dma("sync", g.VTM[c * 128:(c + 1) * 128, half * 512:(half + 1) * 512], o1[:, :], o1k, r=[o1k], w=["VTM"])


def phase_attn(g, l):
    P = g.P
    g.new_phase()
    g.ps_rot = [0, 1, 2, 3]
    kn = g.carve([128, T], BF16)
    kr = g.carve([64, T], BF16)
    qn = g.carve([128, T], BF16)
    qr = g.carve([64, T], BF16)
    vh = g.carve([128, 34, 128], BF16)
    ebuf = [g.carve([128, 512], BF16) for i in range(4)]
    rden = g.carve([128, 512], F32)
    ost = [g.carve([128, 512], F32) for i in range(2)]
    qtiles = [(t0, n, 34) for (t0, n) in TILES[1:]]
    if l == 0:
        qtiles = [(0, 256, 2)] + qtiles
    ti = 0
    cv = [g.carve([128, 5632], BF16) for i in range(2)]
    w1v = g.ffn_w1[l].rearrange("(kc p) n -> p kc n", p=128)
    w2v = g.ffn_w2[l].rearrange("(kc p) n -> p kc n", p=128)
    units = [("w1", hc) for hc in range(44)] + [("w2", fc) for fc in range(16)]
    pend = [None]
    ui = [0]

    def conv_unit():
        i = ui[0]
        if i < len(units):
            kind, j = units[i]
            buf, bk = cv[i % 2], "cv%d" % (i % 2)
            if kind == "w1":
                v = buf[:, 0:4096].rearrange("p (kc n) -> p kc n", n=256)
                P.dma("gpsimd", v[:, :, 0:128], w1v[:, :, j * 128:(j + 1) * 128], bk, w=[bk])
                P.dma("gpsimd", v[:, :, 128:256], w1v[:, :, FFN_H + j * 128:FFN_H + (j + 1) * 128], bk, w=[bk])
                st_ = (g.W1B[j], buf[:, 0:4096], bk, "W1B")
            else:
                v = buf[:, 0:5632].rearrange("p (hc n) -> p hc n", n=128)
                P.dma("gpsimd", v, w2v[:, :, j * 128:(j + 1) * 128], bk, w=[bk])
                st_ = (g.W2B[j], buf[:, 0:5632], bk, "W2B")
        else:
            st_ = None
        if pend[0] is not None:
            dst, src_, bk_, wkey = pend[0]
            P.dma("gpsimd", dst, src_, bk_ + "s", r=[bk_], w=[wkey])
        pend[0] = st_
        ui[0] = i + 1

    for h in range(8):
        P.dma("sync", kn, g.KT[h, 0:128, :], "kn", r=["KT"], w=["kn"])
        P.dma("sync", kr, g.KT[h, 128:192, :], "kr", r=["KT"], w=["kr"])
        P.dma("sync", qn, g.QT[h, 0:128, :], "qn", r=["QT"], w=["qn"])
        P.dma("sync", qr, g.QT[h, 128:192, :], "qr", r=["QT"], w=["qr"])
        P.dma("sync", vh, g.VTM[:, h * 128:(h + 1) * 128].rearrange("(c p) x -> p c x", p=128), "vh", r=["VTM"], w=["vh"])
        for (t0, n, nch) in qtiles:
            po, pok = g.psb(4 + 2 * (ti % 2))
            pd, pdk = g.psb(5 + 2 * (ti % 2))
            ti += 1
            conv_unit()

            def s_mm(c, t0=t0, n=n):
                ps, psk = g.next_ps()
                P.op("tensor", lambda e: e.matmul(ps[:, :n], kn[:, c * 128:(c + 1) * 128], qn[:, t0:t0 + n],
                                                  start=True, stop=False), r=["kn", "qn"], w=[psk])
                P.op("tensor", lambda e: e.matmul(ps[:, :n], kr[:, c * 128:(c + 1) * 128], qr[:, t0:t0 + n],
                                                  start=False, stop=True), r=["kr", "qr"], w=[psk])
                return ps, psk

            cur = s_mm(0)
            for c in range(nch):
                nxt = s_mm(c + 1) if c + 1 < nch else None
                ps, psk = cur
                eb = ebuf[c % 4]
                ek = "ebuf%d" % (c % 4)
                P.op("scalar", lambda e, ps=ps, eb=eb, n=n: e.activation(out=eb[:, :n], in_=ps[:, :n], func=AF.Exp),
                     r=[psk], w=[ek])
                P.op("tensor", lambda e, eb=eb, c=c, n=n, po=po, nch=nch: e.matmul(
                    po[:, :n], vh[:, c, :], eb[:, :n], start=(c == 0), stop=(c == nch - 1)), r=["vh", ek], w=[pok])
                P.op("tensor", lambda e, eb=eb, c=c, n=n, pd=pd, nch=nch: e.matmul(
                    pd[:, :n], g.ones_bf[:], eb[:, :n], start=(c == 0), stop=(c == nch - 1)), r=["ones_bf", ek], w=[pdk])
                cur = nxt
            P.op("vector", lambda e, n=n, pd=pd: e.reciprocal(out=rden[:, :n], in_=pd[:, :n]), r=[pdk], w=["rden"])
            os_ = ost[ti % 2]
            ok = "ost%d" % (ti % 2)
            P.op("vector", lambda e, n=n, po=po, os_=os_: e.tensor_tensor(out=os_[:, :n], in0=po[:, :n], in1=rden[:, :n],
                                                                       op=ALU.mult), r=[pok, "rden"], w=[ok])
            P.dma("sync", g.CATT[h * 128:(h + 1) * 128, t0:t0 + n], os_[:, :n], ok, r=[ok], w=["CATT"])
    while ui[0] <= len(units):
        conv_unit()


def phase_mlstm(g, l):
    P = g.P
    g.new_phase()
    g.ps_rot = list(range(8))
    PT, PTM = g.PT[l], g.PTM[l]
    NCH = 34
    order = [list(range(NCH)), [1, 0] + list(range(NCH - 1, 1, -1))]
    tri = g.carve([128, 2, 128], F32)
    ident = g.carve([128, 128], F32)
    identb = g.carve([128, 128], BF16)
    sel4 = g.carve([4, 4, 128], F32)
    ones4 = g.carve([4, 128], F32)
    one_t = g.carve([128, 1], F32)
    onesf = g.carve([128, 128], F32)
    P.op("vector", lambda e: e.memset(onesf, 1.0), w=["onesf"])
    pers = [[g.carve([128, NCH, 4], F32) for i in range(6)] for dr in range(2)]
    mark1 = g.arena_off
    P.dma("sync", tri, g.tri.rearrange("d s t -> s d t"), "pc", w=["tri"])
    P.dma("sync", ident, g.ident[:, :], "pc", w=["ident"])
    P.dma("gpsimd", identb, g.ident[:, :], "pc", w=["identb"])
    P.dma("sync", sel4, g.sel4[:, :, :], "pc", w=["sel4"])
    P.seal()
    P.op("vector", lambda e: e.memset(ones4, 1.0), w=["ones4"])
    P.op("vector", lambda e: e.memset(one_t, 1.0), w=["one_t"])
    Gt = g.carve([128, NCH, 16], F32)
    gb = g.carve([128, 16], F32)
    P.dma("sync", Gt, PTM[:, 1024:1040].rearrange("(c p) x -> p c x", p=128), "pc", r=["PTM%d" % l], w=["Gt"])
    P.dma("sync", gb, g.ml_gate_b[l:l + 1, :].to_broadcast([128, 16]), "pc", w=["gb"])
    P.seal()
    P.op("vector", lambda e: e.tensor_tensor(out=Gt, in0=Gt, in1=gb.unsqueeze(1).to_broadcast([128, NCH, 16]), op=ALU.add),
         r=["Gt", "gb"], w=["Gt"])
    t1 = g.carve([128, NCH, 8], F32)
    t2 = g.carve([128, NCH, 8], F32)
    lf = g.carve([128, NCH, 8], F32)
    xf = Gt[:, :, 8:16]
    P.op("vector", lambda e: e.scalar_tensor_tensor(out=t1, in0=xf, scalar=-1.0, in1=xf, op0=ALU.mult, op1=ALU.max),
         r=["Gt"], w=["t1"])
    P.op("scalar", lambda e: e.activation(out=t1, in_=t1, func=AF.Exp, scale=-1.0), r=["t1"], w=["t1"])
    P.op("scalar", lambda e: e.activation(out=t1, in_=t1, func=AF.Ln, bias=one_t[:, 0:1], scale=1.0), r=["t1", "one_t"], w=["t1"])
    P.op("vector", lambda e: e.tensor_scalar(out=t2, in0=xf, scalar1=-1.0, scalar2=0.0, op0=ALU.mult, op1=ALU.max),
         r=["Gt"], w=["t2"])
    P.op("vector", lambda e: e.scalar_tensor_tensor(out=lf, in0=t2, scalar=-1.0, in1=t1, op0=ALU.mult, op1=ALU.subtract),
         r=["t1", "t2"], w=["lf"])
    lfd, lid, bd, ud, wtok, iwtok, emt, abc, Mtok = [], [], [], [], [], [], [], [], []
    negMR = []
    for dr in range(2):
        k = "d%d" % dr
        lf_ = g.carve([128, NCH, 4], F32)
        li_ = g.carve([128, NCH, 4], F32)
        b_ = g.carve([128, NCH, 4], F32)
        u_ = pers[dr][0]
        P.op("vector", lambda e, lf_=lf_, dr=dr: e.tensor_copy(out=lf_, in_=lf[:, :, dr * 4:(dr + 1) * 4]), r=["lf"], w=["lf" + k])
        P.op("vector", lambda e, li_=li_, dr=dr: e.tensor_copy(out=li_, in_=Gt[:, :, dr * 4:(dr + 1) * 4]), r=["Gt"], w=["li" + k])
        ps, psk = g.next_ps()
        P.op("tensor", lambda e, ps=ps, dr=dr, lf_=lf_: e.matmul(ps[:, :NCH * 4], tri[:, dr, :], lf_.rearrange("p c h -> p (c h)"),
                                                              start=True, stop=True), r=["tri", "lf" + k], w=[psk])
        P.op("vector", lambda e, ps=ps, b_=b_: e.tensor_copy(out=b_.rearrange("p c h -> p (c h)"), in_=ps[:, :NCH * 4]),
             r=[psk], w=["b" + k])
        P.op("vector", lambda e, u_=u_, li_=li_, b_=b_: e.tensor_tensor(out=u_, in0=li_, in1=b_, op=ALU.subtract),
             r=["li" + k, "b" + k], w=["u" + k])
        uR = g.carve([4, NCH, 128], F32)
        lfR = g.carve([4, NCH, 128], F32)
        for (src, sk_, dstR, dk_) in [(u_, "u" + k, uR, "uR" + k), (lf_, "lf" + k, lfR, "lfR" + k)]:
            for c4 in range(0, NCH, 4):
                ps, psk = g.next_ps()
                nn = min(4, NCH - c4)
                for j in range(nn):
                    P.op("tensor", lambda e, ps=ps, src=src, c4=c4, j=j: e.transpose(ps[:4, j * 128:(j + 1) * 128],
                                                                                  src[:, c4 + j, :], ident),
                         r=[sk_, "ident"], w=[psk])
                P.op("vector", lambda e, ps=ps, dstR=dstR, c4=c4, nn=nn: e.tensor_copy(
                    out=dstR[:, c4:c4 + nn, :].rearrange("p c t -> p (c t)"), in_=ps[:4, :nn * 128]), r=[psk], w=[dk_])
        blast = g.carve([4, NCH], F32)
        cml = g.carve([4, NCH], F32)
        P.op("vector", lambda e, blast=blast, lfR=lfR: e.reduce_sum(out=blast, in_=lfR, axis=AX.X), r=["lfR" + k], w=["blast" + k])
        P.op("vector", lambda e, cml=cml, uR=uR: e.reduce_max(out=cml, in_=uR, axis=AX.X), r=["uR" + k], w=["cml" + k])
        cmB = g.carve([4, NCH, 128], F32)
        bufs = [(uR, "uR" + k), (cmB, "cmB" + k)]
        cur = 0
        for sh in (1, 2, 4, 8, 16, 32, 64):
            (sa, sak), (da, dak) = bufs[cur], bufs[1 - cur]
            if dr == 0:
                P.op("vector", lambda e, sa=sa, da=da, sh=sh: e.tensor_tensor(out=da[:, :, sh:], in0=sa[:, :, sh:], in1=sa[:, :, :128 - sh],
                                                                            op=ALU.max), r=[sak], w=[dak])
                P.op("gpsimd", lambda e, sa=sa, da=da, sh=sh: e.tensor_copy(out=da[:, :, :sh], in_=sa[:, :, :sh]), r=[sak], w=[dak])
            else:
                P.op("vector", lambda e, sa=sa, da=da, sh=sh: e.tensor_tensor(out=da[:, :, :128 - sh], in0=sa[:, :, :128 - sh], in1=sa[:, :, sh:],
                                                                            op=ALU.max), r=[sak], w=[dak])
                P.op("gpsimd", lambda e, sa=sa, da=da, sh=sh: e.tensor_copy(out=da[:, :, 128 - sh:], in_=sa[:, :, 128 - sh:]), r=[sak], w=[dak])
            cur = 1 - cur
        cm, cmk = bufs[cur]
        mR = g.carve([4, NCH], F32)
        P.op("vector", lambda e, mR=mR: e.memset(mR, 0.0), w=["mR" + k])
        od = order[dr]
        for i in range(NCH - 1):
            c, c2 = od[i], od[i + 1]
            P.op("vector", lambda e, mR=mR, c=c, c2=c2, cml=cml, blast=blast: e.scalar_tensor_tensor(
                out=mR[:, c2:c2 + 1], in0=mR[:, c:c + 1], scalar=cml[:, c:c + 1], in1=blast[:, c:c + 1],
                op0=ALU.max, op1=ALU.add), r=["mR" + k, "cml" + k, "blast" + k], w=["mR" + k])
        MR = bufs[1 - cur][0]
        MRk = bufs[1 - cur][1]
        P.op("vector", lambda e, MR=MR, cm=cm, mR=mR: e.tensor_tensor(out=MR, in0=cm, in1=mR.unsqueeze(2).to_broadcast([4, NCH, 128]),
                                                                    op=ALU.max), r=[cmk, "mR" + k], w=[MRk])
        MlR = g.carve([4, NCH], F32)
        aR = g.carve([4, NCH], F32)
        P.op("vector", lambda e, MlR=MlR, mR=mR, cml=cml: e.tensor_tensor(out=MlR, in0=mR, in1=cml, op=ALU.max),
             r=["mR" + k, "cml" + k], w=["MlR" + k])
        P.op("vector", lambda e, aR=aR, mR=mR, MlR=MlR: e.tensor_tensor(out=aR, in0=mR, in1=MlR, op=ALU.subtract),
             r=["mR" + k, "MlR" + k], w=["aR" + k])
        P.op("scalar", lambda e, aR=aR: e.activation(out=aR, in_=aR, func=AF.Exp), r=["aR" + k], w=["aR" + k])
        Mt = pers[dr][1]
        for c4 in range(0, NCH, 32):
            ps, psk = g.next_ps()
            nn = min(32, NCH - c4)
            for j in range(nn):
                P.op("tensor", lambda e, ps=ps, MR=MR, c4=c4, j=j: e.transpose(ps[:, j * 4:(j + 1) * 4], MR[:, c4 + j, :], ident[:4, :4]),
                     r=[MRk, "ident"], w=[psk])
            P.op("vector", lambda e, ps=ps, Mt=Mt, c4=c4, nn=nn: e.tensor_copy(
                out=Mt[:, c4:c4 + nn, :].rearrange("p c h -> p (c h)"), in_=ps[:, :nn * 4]), r=[psk], w=["Mt" + k])
        bc = []
        dg = g.carve([4, NCH, 4], F32)
        for (src, sk_) in [(MlR, "MlR" + k), (aR, "aR" + k), (mR, "mR" + k)]:
            P.op("vector", lambda e, src=src: e.tensor_tensor(out=dg, in0=src.unsqueeze(2).to_broadcast([4, NCH, 4]),
                                                             in1=ident[:4, :4].unsqueeze(1).to_broadcast([4, NCH, 4]), op=ALU.mult),
                 r=[sk_, "ident"], w=["dg" + k])
            ps, psk = g.next_ps()
            P.op("tensor", lambda e, ps=ps: e.matmul(ps[:, :NCH * 4], ones4, dg.rearrange("p c h -> p (c h)"), start=True, stop=True),
                 r=["dg" + k, "ones4"], w=[psk])
            o_ = pers[dr][5] if src is aR else g.carve([128, NCH, 4], F32)
            P.op("vector", lambda e, ps=ps, o_=o_: e.tensor_copy(out=o_.rearrange("p c h -> p (c h)"), in_=ps[:, :NCH * 4]),
                 r=[psk], w=["bc" + k + sk_])
            bc.append((o_, "bc" + k + sk_))
        (Mlb, Mlbk), (ab, abk), (mb, mbk) = bc
        w_, iw_, em_ = pers[dr][2], pers[dr][3], pers[dr][4]
        P.op("vector", lambda e, w_=w_, u_=u_, Mlb=Mlb: e.tensor_tensor(out=w_, in0=u_, in1=Mlb, op=ALU.subtract), r=["u" + k, Mlbk], w=["w" + k])
        P.op("scalar", lambda e, w_=w_: e.activation(out=w_, in_=w_, func=AF.Exp), r=["w" + k], w=["w" + k])
        P.op("vector", lambda e, iw_=iw_, mb=mb, Mt=Mt: e.tensor_tensor(out=iw_, in0=mb, in1=Mt, op=ALU.subtract), r=[mbk, "Mt" + k], w=["iw" + k])
        P.op("scalar", lambda e, iw_=iw_: e.activation(out=iw_, in_=iw_, func=AF.Exp), r=["iw" + k], w=["iw" + k])
        P.op("vector", lambda e, em_=em_, b_=b_, Mt=Mt: e.tensor_tensor(out=em_, in0=b_, in1=Mt, op=ALU.add), r=["b" + k, "Mt" + k], w=["em" + k])
        P.op("scalar", lambda e, em_=em_: e.activation(out=em_, in_=em_, func=AF.Exp, scale=-1.0), r=["em" + k], w=["em" + k])
        P.op("vector", lambda e, Mt=Mt: e.tensor_scalar(out=Mt, in0=Mt, scalar1=-1.0, scalar2=None, op0=ALU.mult),
             r=["Mt" + k, "iw" + k, "em" + k], w=["Mt" + k])
        ud.append(u_); wtok.append(w_); iwtok.append(iw_); emt.append(em_); abc.append(ab); Mtok.append(Mt)
    g.arena_off = mark1
    P.barrier()
    qT = g.carve([128, 4, T], BF16)
    kT = g.carve([128, 4, T], BF16)
    ktok = g.carve([128, NCH, 512], BF16)
    v1 = g.carve([128, NCH * 4, 132], BF16)
    mark2 = g.arena_off
    cw = g.carve([128, 8, 4], F32)
    P.dma("sync", cw, g.ml_cw[l], "pc", w=["cw"])
    P.seal()
    W = T + 3
    xb = g.carve([128, W], F32)
    acc = g.carve([128, W], F32)
    P.op("vector", lambda e: e.memset(xb, 0.0), w=["xb"])
    for rc in range(8):
        r0 = 2624 + rc * 128
        P.dma("sync", xb[:, 1:1 + NCTX], PT[r0:r0 + 128, 0:NCTX], "xb", r=["PT%d" % l], w=["xb"])
        P.dma("sync", xb[:, 2 + NCTX:2 + T], PT[r0:r0 + 128, NCTX:T], "xb", r=["PT%d" % l], w=["xb"])
        n = T + 1
        P.op("vector", lambda e, rc=rc: e.tensor_scalar(out=acc[:, 1:1 + n], in0=xb[:, 0:n], scalar1=cw[:, rc, 0:1],
                                                       scalar2=cw[:, rc, 3:4], op0=ALU.mult, op1=ALU.add),
             r=["xb", "cw"], w=["acc"])
        for j in (1, 2):
            P.op("vector", lambda e, rc=rc, j=j: e.scalar_tensor_tensor(
                out=acc[:, 1:1 + n], in0=xb[:, j:j + n], scalar=cw[:, rc, j:j + 1], in1=acc[:, 1:1 + n],
                op0=ALU.mult, op1=ALU.add), r=["xb", "cw", "acc"], w=["acc"])
        dst = qT if rc < 4 else kT
        dk = "qT" if rc < 4 else "kT"
        for (a0, d0, n2) in [(1, 0, NCTX), (2 + NCTX, NCTX, NLAT)]:
            P.op("scalar", lambda e, a0=a0, n2=n2: e.activation(out=acc[:, a0:a0 + n2], in_=acc[:, a0:a0 + n2], func=AF.Silu),
                 r=["acc"], w=["acc"])
            if rc < 4:
                P.op("vector", lambda e, a0=a0, d0=d0, n2=n2, dst=dst, rc=rc: e.tensor_scalar(
                    out=dst[:, rc % 4, d0:d0 + n2], in0=acc[:, a0:a0 + n2], scalar1=128.0 ** -0.5, scalar2=None, op0=ALU.mult),
                    r=["acc"], w=[dk])
            else:
                P.op("vector", lambda e, a0=a0, d0=d0, n2=n2, dst=dst, rc=rc: e.tensor_copy(
                    out=dst[:, rc % 4, d0:d0 + n2], in_=acc[:, a0:a0 + n2]), r=["acc"], w=[dk])
    psb16 = [g.ps[i][:].bitcast(BF16) for i in range(8)]
    for c in range(NCH):
        g.ps_i = (g.ps_i + 1) % 8
        pi = g.ps_i
        pk = "ps%d" % pi
        for hd in range(4):
            P.op("tensor", lambda e, c=c, hd=hd, pi=pi: e.transpose(psb16[pi][:, hd * 128:(hd + 1) * 128],
                                                                   kT[:, hd, c * 128:(c + 1) * 128], identb),
                 r=["kT", "identb"], w=[pk])
        P.op("vector", lambda e, c=c, pi=pi: e.tensor_copy(out=ktok[:, c, :], in_=psb16[pi][:, 0:512]), r=[pk], w=["ktok"])
    v1v = v1.rearrange("p (c h) x -> p c h x", h=4)
    P.op("vector", lambda e: e.memset(v1[:, :, 128:129], 1.0), w=["v1"])
    for hd in range(4):
        P.dma("gpsimd", v1v[:, :, hd, 0:128], PTM[:, hd * 128:(hd + 1) * 128].rearrange("(c p) x -> p c x", p=128), "v1",
              r=["PTM%d" % l], w=["v1"])
    g.arena_off = mark2
    P.barrier()
    chains = [(dr, hd) for dr in range(2) for hd in range(4)]
    st8 = {}
    for ci, (dr, hd) in enumerate(chains):
        ck = "c%d%d" % (dr, hd)
        st8[ck] = dict(Cf=g.carve([128, 132], F32), Cb=g.carve([128, 132], BF16), dx=g.carve([128, 128], F32),
                       dxm=g.carve([128, 128], F32), sw=g.carve([128, 128], BF16), Asb=g.carve([128, 132], F32),
                       nt=g.carve([128, 132], F32), kw=g.carve([128, 128], BF16), ho=g.carve([128, 128], F32),
                       dd=g.carve([128, 2], F32), dgm=g.carve([128, 128], F32), bank=g.ps[ci], ci=ci)
        b8 = st8[ck]
        P.op("vector", lambda e, b8=b8: e.memset(b8["Cf"], 0.0), w=[ck + "Cf"])
        P.op("vector", lambda e, b8=b8: e.memset(b8["Cb"], 0.0), w=[ck + "Cb"])
    for step in range(NCH):
        info = []
        for (dr, hd) in chains:
            ck = "c%d%d" % (dr, hd)
            b8 = st8[ck]
            c = order[dr][step]
            info.append((dr, hd, ck, "d%d" % dr, b8, c, slice(c * 128, (c + 1) * 128),
                         "mp%d" % b8["ci"], "mp%d" % b8["ci"], "mp%d" % b8["ci"]))
        for (dr, hd, ck, k, b8, c, cs, k0, k1, k2) in info:
            bk = b8["bank"]
            P.op("tensor", lambda e, bk=bk, hd=hd, cs=cs: e.matmul(bk[:, 0:128], kT[:, hd, cs], qT[:, hd, cs], start=True, stop=True),
                 r=["kT", "qT"], w=[k0])
            P.op("gpsimd", lambda e, b8=b8, dr=dr, c=c, hd=hd: e.tensor_scalar(
                out=b8["dgm"], in0=ident, scalar1=Mtok[dr][:, c, hd:hd + 1], scalar2=None, op0=ALU.mult),
                r=["ident", "Mt" + k], w=[ck + "dgm"])
            P.op("tensor", lambda e, bk=bk, b8=b8: e.matmul(bk[:, 129:257], onesf, b8["dgm"], start=True, stop=True),
                 r=["onesf", ck + "dgm"], w=[k1])
        for (dr, hd, ck, k, b8, c, cs, k0, k1, k2) in info:
            bk = b8["bank"]
            P.op("scalar", lambda e, bk=bk, b8=b8, dr=dr, c=c, hd=hd: e.activation(
                out=b8["dx"], in_=bk[:, 129:257], func=AF.Exp, bias=ud[dr][:, c, hd:hd + 1], scale=1.0), r=[k1, "u" + k], w=[ck + "dx"])
            P.op("gpsimd", lambda e, b8=b8, dr=dr: e.tensor_tensor(out=b8["dxm"], in0=b8["dx"], in1=tri[:, dr, :], op=ALU.mult),
                 r=[ck + "dx", "tri"], w=[ck + "dxm"])
            P.op("vector", lambda e, bk=bk, b8=b8: e.tensor_tensor(out=b8["sw"], in0=bk[:, 0:128], in1=b8["dxm"], op=ALU.mult),
                 r=[k0, ck + "dxm"], w=[ck + "sw"])
        for (dr, hd, ck, k, b8, c, cs, k0, k1, k2) in info:
            bk = b8["bank"]
            P.op("tensor", lambda e, bk=bk, b8=b8, c=c, hd=hd: e.matmul(bk[:, 258:387], b8["sw"], v1[:, c * 4 + hd, 0:129], start=True, stop=True),
                 r=[ck + "sw", "v1"], w=[k2])
            P.op("tensor", lambda e, bk=bk, b8=b8, hd=hd, cs=cs: e.matmul(bk[:, 129:258], qT[:, hd, cs], b8["Cb"][:, 0:129], start=True, stop=True),
                 r=["qT", ck + "Cb"], w=[k1])
        for (dr, hd, ck, k, b8, c, cs, k0, k1, k2) in info:
            bk = b8["bank"]
            P.op("scalar", lambda e, bk=bk, b8=b8: e.copy(out=b8["Asb"][:, 0:129], in_=bk[:, 258:387]), r=[k2], w=[ck + "Asb"])
            P.op("vector", lambda e, bk=bk, b8=b8, dr=dr, c=c, hd=hd: e.scalar_tensor_tensor(
                out=b8["nt"][:, 0:129], in0=bk[:, 129:258], scalar=iwtok[dr][:, c, hd:hd + 1], in1=b8["Asb"][:, 0:129],
                op0=ALU.mult, op1=ALU.add), r=[k1, ck + "Asb", "iw" + k], w=[ck + "nt"])
            P.op("vector", lambda e, b8=b8: e.scalar_tensor_tensor(
                out=b8["dd"][:, 0:1], in0=b8["nt"][:, 128:129], scalar=-1.0, in1=b8["nt"][:, 128:129], op0=ALU.mult, op1=ALU.max),
                r=[ck + "nt"], w=[ck + "dd"])
            P.op("vector", lambda e, b8=b8, dr=dr, c=c, hd=hd: e.tensor_tensor(
                out=b8["dd"][:, 0:1], in0=b8["dd"][:, 0:1], in1=emt[dr][:, c, hd:hd + 1], op=ALU.max),
                r=[ck + "dd", "em" + k], w=[ck + "dd"])
            P.op("vector", lambda e, b8=b8: e.reciprocal(out=b8["dd"][:, 1:2], in_=b8["dd"][:, 0:1]), r=[ck + "dd"], w=[ck + "dd"])
            P.op("scalar", lambda e, b8=b8: e.mul(out=b8["ho"], in_=b8["nt"][:, 0:128], mul=b8["dd"][:, 1:2]),
                 r=[ck + "nt", ck + "dd"], w=[ck + "ho"])
            P.dma("sync", g.HD[dr][c * 128:(c + 1) * 128, hd * 128:(hd + 1) * 128], b8["ho"], ck + "ho", r=[ck + "ho"], w=["HD%d" % dr])
        for (dr, hd, ck, k, b8, c, cs, k0, k1, k2) in info:
            bk = b8["bank"]
            P.op("vector", lambda e, b8=b8, dr=dr, c=c, hd=hd: e.tensor_scalar(
                out=b8["kw"], in0=ktok[:, c, hd * 128:(hd + 1) * 128], scalar1=wtok[dr][:, c, hd:hd + 1], scalar2=None, op0=ALU.mult),
                r=["ktok", "w" + k], w=[ck + "kw"])
            P.op("tensor", lambda e, bk=bk, b8=b8, c=c, hd=hd: e.matmul(bk[:, 0:129], b8["kw"], v1[:, c * 4 + hd, 0:129], start=True, stop=True),
                 r=[ck + "kw", "v1"], w=[k0])
            P.op("vector", lambda e, bk=bk, b8=b8, dr=dr, c=c, hd=hd: e.scalar_tensor_tensor(
                out=b8["Cf"][:, 0:129], in0=b8["Cf"][:, 0:129], scalar=abc[dr][:, c, hd:hd + 1], in1=bk[:, 0:129],
                op0=ALU.mult, op1=ALU.add), r=[ck + "Cf", k0, "bcd%daRd%d" % (dr, dr)], w=[ck + "Cf"])
            P.op("scalar", lambda e, b8=b8: e.copy(out=b8["Cb"][:, 0:129], in_=b8["Cf"][:, 0:129]), r=[ck + "Cf"], w=[ck + "Cb"])
    g.arena_off = mark2
    P.barrier()
    gbc = g.carve([128, 512], F32)
    P.dma("sync", gbc, g.mix_g[l:l + 1, 1536:2048].to_broadcast([128, 512]), "pc", w=["gbc"])
    P.seal()
    hf2 = [g.carve([128, 512], F32) for i in range(2)]
    hb2 = [g.carve([128, 512], F32) for i in range(2)]
    og2 = [g.carve([128, 512], F32) for i in range(2)]
    junk = g.carve([128, 128], F32)
    ss2 = [g.carve([128, 8], F32) for i in range(2)]
    outs = [g.carve([128, 512], F32) for i in range(2)]
    for c in range(NCH):
        q_ = c % 2
        hf, hb, og, ss = hf2[q_], hb2[q_], og2[q_], ss2[q_]
        hfk, hbk, ogk, ssk = "hf%d" % q_, "hb%d" % q_, "og%d" % q_, "ss%d" % q_
        P.dma("sync", hf, g.HD[0][c * 128:(c + 1) * 128, :], hfk, r=["HD0"], w=[hfk])
        P.dma("sync", hb, g.HD[1][c * 128:(c + 1) * 128, :], hbk, r=["HD1"], w=[hbk])
        P.dma("sync", og, PTM[c * 128:(c + 1) * 128, 512:1024], ogk, r=["PTM%d" % l], w=[ogk])
        P.op("vector", lambda e, hf=hf, hb=hb: e.tensor_tensor(out=hf, in0=hf, in1=hb, op=ALU.add), r=[hfk, hbk], w=[hfk])
        for hd in range(4):
            P.op("scalar", lambda e, hd=hd, hf=hf, ss=ss: e.activation(out=junk, in_=hf[:, hd * 128:(hd + 1) * 128], func=AF.Square,
                                                                      accum_out=ss[:, hd:hd + 1]), r=[hfk], w=["junk", ssk])
        P.op("scalar", lambda e, ss=ss: e.activation(out=ss[:, 4:8], in_=ss[:, 0:4], func=AF.Sqrt, bias=g.eps_t[:, 0:1], scale=1.0 / 128),
             r=[ssk, "eps_t"], w=[ssk])
        P.op("vector", lambda e, ss=ss: e.reciprocal(out=ss[:, 4:8], in_=ss[:, 4:8]), r=[ssk], w=[ssk])
        P.op("scalar", lambda e, og=og: e.activation(out=og, in_=og, func=AF.Sigmoid), r=[ogk], w=[ogk])
        P.op("gpsimd", lambda e, og=og: e.tensor_tensor(out=og, in0=og, in1=gbc, op=ALU.mult), r=[ogk, "gbc"], w=[ogk])
        for hd in range(4):
            P.op("vector", lambda e, hd=hd, hf=hf, ss=ss, og=og: e.scalar_tensor_tensor(
                out=hf[:, hd * 128:(hd + 1) * 128], in0=hf[:, hd * 128:(hd + 1) * 128], scalar=ss[:, 4 + hd:5 + hd],
                in1=og[:, hd * 128:(hd + 1) * 128], op0=ALU.mult, op1=ALU.mult), r=[hfk, ssk, ogk], w=[hfk])
        ps, psk = g.next_ps()
        for hd in range(4):
            P.op("tensor", lambda e, ps=ps, hd=hd, hf=hf: e.transpose(ps[:, hd * 128:(hd + 1) * 128], hf[:, hd * 128:(hd + 1) * 128], ident),
                 r=[hfk, "ident"], w=[psk])
        o_ = outs[c % 2]
        ok = "mlo%d" % (c % 2)
        P.op("scalar", lambda e, ps=ps, o_=o_: e.copy(out=o_, in_=ps[:, :]), r=[psk], w=[ok])
        P.dma("sync", g.CATT[1536:2048, c * 128:(c + 1) * 128].rearrange("(h p) t -> p h t", p=128),
              o_.rearrange("p (h t) -> p h t", t=128), ok, r=[ok], w=["CATT"])


def dft_fwd(g, L, rhsC, rkC, rhsS, rkS, evac, nyq):
    P = g.P
    nch = L // 128
    KG = min(512, L)
    nj = KG // 128
    mats = g.hy_cs[L]
    for kg in range(L // KG):
        banks = [[g.psb(j), g.psb(4 + j)] for j in range(nj)]
        for mi, (rhs, rk) in enumerate([(rhsC, rkC), (rhsS, rkS)]):
            for tc in range(nch):
                if tc % 2 == 0:
                    bi = g.mt_i
                    g.mt_i = (bi + 1) % len(g.mt)
                    mt2, mk = g.mt[bi], "mt%d" % bi
                    P.dma("sync", mt2[:, :, :KG], mats[mi, tc // 2, kg].rearrange("p (a k) -> p a k", a=2),
                          mk, w=[mk])
                mt = mt2[:, tc % 2, :]
                for j in range(nj):
                    ps, pk = banks[j][mi]
                    P.op("tensor", lambda e, ps=ps, mt=mt, j=j, rhs=rhs, tc=tc: e.matmul(
                        ps[:, :512], mt[:, j * 128:(j + 1) * 128], rhs[:, tc, :], start=(tc == 0), stop=(tc == nch - 1)),
                        r=[mk, rk], w=[pk])
        for j in range(nj):
            (pc, pck), (ps_, psk) = banks[j]
            evac(kg * nj + j, pc, pck, ps_, psk)
    g.ps_i = (g.ps_i + 1) % 8
    pn, pnk = g.psb(g.ps_i)
    for tc in range(nch):
        P.op("tensor", lambda e, tc=tc, pn=pn: e.matmul(pn[:1, :512], g.altc[:, 0:1], rhsC[:, tc, :], start=(tc == 0), stop=(tc == nch - 1)),
             r=["altc", rkC], w=[pnk])
    nyq(pn, pnk)


def hyena_seq(g, l, L, tok0):
    P = g.P
    nch = L // 128
    N2 = 2 * L
    PT = g.PT[l]
    g.new_phase()
    g.ps_rot = list(range(8))
    ident = g.carve([128, 128], F32)
    identb = g.carve([128, 128], BF16)
    P.dma("sync", ident, g.ident[:, :], "pc", w=["ident"])
    P.dma("gpsimd", identb, g.ident[:, :], "pc", w=["identb"])
    g.altc = g.carve([128, 1], BF16)
    altr = g.carve([1, 512], BF16)
    P.dma("gpsimd", g.altc, g.hy_alt[:, 0:1], "pc", w=["altc"], allow_slow_non_contiguous=True)
    P.dma("gpsimd", altr, g.hy_alt[0:1, :], "pc", w=["altr"])
    g.mt = [g.carve([128, 2, 512], BF16) for i in range(8)]
    g.mt_i = 0
    pic = g.carve([128, 2], F32)
    P.op("vector", lambda e: e.memset(pic[:, 0:1], 0.05), w=["pic"])
    mark0 = g.arena_off
    w1 = g.carve([33, 64], F32)
    w2 = g.carve([64, 64], F32)
    w3 = g.carve([64, 2048], F32)
    b12 = g.carve([64, 2], F32)
    dec = g.carve([128, 2048], F32)
    negt = g.carve([128, nch], F32)
    feats = g.carve([33, L], F32)
    h2T = g.carve([64, L], F32)
    P.dma("sync", w1, g.hy_w1[l], "pc", w=["hw1"])
    P.dma("sync", w2, g.hy_w2[l], "pc", w=["hw2"])
    P.dma("sync", w3, g.hy_w3[l], "pc", w=["hw3"])
    P.dma("sync", b12, g.hy_b12[l], "pc", w=["hb12"])
    P.dma("sync", dec, g.hy_decay[l:l + 1, :].to_broadcast([128, 2048]), "pc", w=["hdec"])
    P.dma("sync", negt, g.hy_negt[L][:, :], "pc", w=["negt"])
    P.dma("sync", feats, g.hy_feats[L][:, :], "pc", w=["feats"])
    P.seal()
    zt = g.carve([64, 512], F32)
    ki = g.carve([64, 512], mybir.dt.int32)
    kf = g.carve([64, 512], F32)
    h1 = g.carve([64, 512], F32)
    TW = min(512, L)

    def sin_layer(ps, pk, bcol, dst, dk):
        n = TW
        P.op("vector", lambda e: e.tensor_scalar(out=zt[:, :n], in0=ps[:64, :n], scalar1=b12[:, bcol:bcol + 1], scalar2=None, op0=ALU.add),
             r=[pk, "hb12"], w=["zt"])
        P.op("vector", lambda e: e.tensor_scalar(out=ki[:, :n], in0=zt[:, :n], scalar1=1.0 / (2 * math.pi), scalar2=None, op0=ALU.mult),
             r=["zt"], w=["ki"])
        P.op("vector", lambda e: e.tensor_copy(out=kf[:, :n], in_=ki[:, :n]), r=["ki"], w=["kf"])
        P.op("vector", lambda e: e.scalar_tensor_tensor(out=zt[:, :n], in0=kf[:, :n], scalar=-2 * math.pi, in1=zt[:, :n],
                                                        op0=ALU.mult, op1=ALU.add), r=["kf", "zt"], w=["zt"])
        P.op("vector", lambda e: e.tensor_scalar(out=zt[:, :n], in0=zt[:, :n], scalar1=-3.141592, scalar2=3.141592,
                                                 op0=ALU.max, op1=ALU.min), r=["zt"], w=["zt"])
        P.op("scalar", lambda e: e.activation(out=dst, in_=zt[:, :n], func=AF.Sin), r=["zt"], w=[dk])

    for t0 in range(0, L, TW):
        ps, pk = g.next_ps()
        P.op("tensor", lambda e, ps=ps, t0=t0: e.matmul(ps[:64, :TW], w1, feats[:, t0:t0 + TW], start=True, stop=True),
             r=["hw1", "feats"], w=[pk])
        sin_layer(ps, pk, 0, h1[:, :TW], "h1")
        ps, pk = g.next_ps()
        P.op("tensor", lambda e, ps=ps: e.matmul(ps[:64, :TW], w2, h1[:, :TW], start=True, stop=True), r=["hw2", "h1"], w=[pk])
        sin_layer(ps, pk, 1, h2T[:, t0:t0 + TW], "h2T")
    wt_ = [g.carve([128, 512], F32) for i in range(2)]
    hc = [g.carve([128, 512], F32) for i in range(4)]
    hsb = [g.carve([128, 512], BF16) for i in range(4)]
    q = 0
    for tc in range(nch):
        for cb in range(4):
            ps, pk = g.next_ps()
            P.op("tensor", lambda e, ps=ps, tc=tc, cb=cb: e.matmul(ps[:, :512], h2T[:, tc * 128:(tc + 1) * 128], w3[:, cb * 512:(cb + 1) * 512],
                                                                start=True, stop=True), r=["h2T", "hw3"], w=[pk])
            wt = wt_[cb % 2]
            wk_ = "hwt%d" % (cb % 2)
            P.op("scalar", lambda e, wt=wt, tc=tc, cb=cb: e.activation(out=wt, in_=dec[:, cb * 512:(cb + 1) * 512], func=AF.Exp,
                                                                     scale=negt[:, tc:tc + 1]), r=["hdec", "negt"], w=[wk_])
            P.op("vector", lambda e, wt=wt, ps=ps, cb=cb: e.scalar_tensor_tensor(out=hc[cb], in0=wt, scalar=pic[:, 0:1], in1=ps[:, :512],
                                                                             op0=ALU.add, op1=ALU.mult), r=[wk_, pk, "pic"], w=["hc%d" % cb])
            if tc == 0 and cb % 2 == 1:
                P.op("vector", lambda e, cb=cb: e.memset(hc[cb][0:1, :], 0.0), w=["hc%d" % cb])
        for o in range(2):
            for mi in range(2):
                hb_ = hsb[q % 4]
                hk = "hsb%d" % (q % 4)
                q += 1
                if mi == 0:
                    P.op("gpsimd", lambda e, hb_=hb_, o=o: e.tensor_tensor(out=hb_, in0=hc[2 * o], in1=hc[2 * o + 1], op=ALU.add),
                         r=["hc%d" % (2 * o), "hc%d" % (2 * o + 1)], w=[hk])
                else:
                    P.op("gpsimd", lambda e, hb_=hb_, o=o: e.tensor_tensor(out=hb_, in0=hc[2 * o + 1], in1=hc[2 * o], op=ALU.subtract),
                         r=["hc%d" % (2 * o), "hc%d" % (2 * o + 1)], w=[hk])
                P.dma("sync", g.HS[mi, tc * 128:(tc + 1) * 128, o * 512:(o + 1) * 512], hb_, hk, r=[hk], w=["HS"])
    g.arena_off = mark0
    P.barrier()
    R = [[g.carve([128, nch, 512], BF16) for o in range(2)] for mi in range(2)]
    stg = [g.carve([128, 512], F32) for i in range(4)]
    sti = [0]

    def nstg():
        sti[0] = (sti[0] + 1) % 4
        return stg[sti[0]], "hstg%d" % sti[0]

    for mi in range(2):
        for o in range(2):
            P.dma("sync", R[mi][o], g.HS[mi, 0:L, o * 512:(o + 1) * 512].rearrange("(c p) x -> p c x", p=128), "R%d%d" % (mi, o),
                  r=["HS"], w=["R%d%d" % (mi, o)])
    KG = min(512, L)
    nj = KG // 128
    mats = g.hy_cs[L]
    ec = [0]
    for kg in range(L // KG):
        for mi in range(2):
            for tc in range(nch):
                if tc % 2 == 0:
                    bi = g.mt_i
                    g.mt_i = (bi + 1) % len(g.mt)
                    mt2, mk = g.mt[bi], "mt%d" % bi
                    P.dma("sync", mt2[:, :, :KG], mats[mi, tc // 2, kg].rearrange("p (a k) -> p a k", a=2),
                          mk, w=[mk])
                mt = mt2[:, tc % 2, :]
                for o in range(2):
                    for j in range(nj):
                        pb, pbk = g.psb(o * nj + j)
                        P.op("tensor", lambda e, pb=pb, mt=mt, j=j, tc=tc, mi=mi, o=o: e.matmul(
                            pb[:, :512], mt[:, j * 128:(j + 1) * 128], R[mi][o][:, tc, :], start=(tc == 0), stop=(tc == nch - 1)),
                            r=[mk, "R%d%d" % (mi, o)], w=[pbk])
            for o in range(2):
                for j in range(nj):
                    pb, pbk = g.psb(o * nj + j)
                    kc = kg * nj + j
                    st_, sk_ = nstg()
                    if ec[0] % 2 == 0:
                        P.op("vector", lambda e, st_=st_, pb=pb: e.tensor_scalar(out=st_, in0=pb[:, :512], scalar1=2.0 / N2, scalar2=None, op0=ALU.mult),
                             r=[pbk], w=[sk_])
                    else:
                        P.op("scalar", lambda e, st_=st_, pb=pb: e.mul(out=st_, in_=pb[:, :512], mul=2.0 / N2), r=[pbk], w=[sk_])
                    ec[0] += 1
                    if kc == 0:
                        P.op("vector", lambda e, st_=st_: e.tensor_scalar(out=st_[0:1, :], in0=st_[0:1, :], scalar1=0.5, scalar2=None, op0=ALU.mult),
                             r=[sk_], w=[sk_])
                    P.dma("gpsimd", g.KF[mi, kc * 128:(kc + 1) * 128, o * 512:(o + 1) * 512], st_, sk_, r=[sk_], w=["KF"])
    for o in range(2):
        pn, pnk = g.psb(o)
        for tc in range(nch):
            P.op("tensor", lambda e, tc=tc, pn=pn, o=o: e.matmul(pn[:1, :512], g.altc[:, 0:1], R[0][o][:, tc, :], start=(tc == 0), stop=(tc == nch - 1)),
                 r=["altc", "R0%d" % o], w=[pnk])
        st_, sk_ = nstg()
        P.op("vector", lambda e, st_=st_, pn=pn: e.tensor_scalar(out=st_[0:1, :], in0=pn[0:1, :512], scalar1=1.0 / N2, scalar2=None, op0=ALU.mult),
             r=[pnk], w=[sk_])
        P.dma("gpsimd", g.KF[0, L:L + 1, o * 512:(o + 1) * 512], st_[0:1, :], sk_, r=[sk_], w=["KF"])
    g.arena_off = mark0
    P.barrier()
    ztok = g.carve([128, nch, 512], BF16)
    Asb = g.carve([128, nch + 1, 512], BF16)
    Bsb = g.carve([128, nch, 512], BF16)
    skp = g.carve([128, 2, 4], F32)
    P.dma("sync", skp, g.hy_skipT[l], "pc", w=["skp"])
    P.seal()
    mark1 = g.arena_off
    cw = g.carve([128, 12, 4], F32)
    P.dma("sync", cw, g.hy_cw[l], "pc", w=["hcw"])
    P.seal()
    xb2 = [g.carve([128, L + 2], F32) for i in range(2)]
    acc2 = [g.carve([128, L], F32) for i in range(2)]
    accb = g.carve([128, L], BF16)
    psb16 = [g.ps[i][:].bitcast(BF16) for i in range(8)]
    for i in range(2):
        P.op("vector", lambda e, i=i: e.memset(xb2[i], 0.0), w=["xb%d" % i])

    def to_tok(srcb, sk_, cc):
        for t8 in range(0, nch, 8):
            g.ps_i = (g.ps_i + 1) % 8
            pi = g.ps_i
            nn = min(8, nch - t8)
            for j in range(nn):
                P.op("tensor", lambda e, pi=pi, j=j, t8=t8: e.transpose(psb16[pi][:, j * 128:(j + 1) * 128],
                                                                      srcb[:, (t8 + j) * 128:(t8 + j + 1) * 128], identb),
                     r=[sk_, "identb"], w=["ps%d" % pi])
            P.op("vector", lambda e, pi=pi, t8=t8, nn=nn, cc=cc: e.tensor_copy(
                out=ztok[:, t8:t8 + nn, cc * 128:(cc + 1) * 128], in_=psb16[pi][:, 0:nn * 128].rearrange("p (c x) -> p c x", x=128)),
                r=["ps%d" % pi], w=["ztok"])

    for rc in range(12):
        r0 = 1088 + rc * 128
        xb, xk = xb2[rc % 2], "xb%d" % (rc % 2)
        acc, ak = acc2[rc % 2], "acc%d" % (rc % 2)
        P.dma("sync", xb[:, 1:1 + L], PT[r0:r0 + 128, tok0:tok0 + L], xk, r=["PT%d" % l], w=[xk])
        eng = "vector"
        P.op(eng, lambda e, rc=rc, xb=xb, acc=acc: e.tensor_scalar(out=acc, in0=xb[:, 0:L], scalar1=cw[:, rc, 0:1], scalar2=cw[:, rc, 3:4],
                                                                  op0=ALU.mult, op1=ALU.add), r=[xk, "hcw"], w=[ak])
        for j in (1, 2):
            P.op(eng, lambda e, rc=rc, j=j, xb=xb, acc=acc: e.scalar_tensor_tensor(out=acc, in0=xb[:, j:j + L], scalar=cw[:, rc, j:j + 1], in1=acc,
                                                                                op0=ALU.mult, op1=ALU.add), r=[xk, "hcw", ak], w=[ak])
        P.dma("sync", g.HV[rc * 128:(rc + 1) * 128, 0:L], acc, ak, r=[ak], w=["HV"])
        if rc < 4:
            P.op("scalar", lambda e, acc=acc: e.copy(out=accb, in_=acc), r=[ak], w=["accb"])
            to_tok(accb, "accb", rc)
    g.arena_off = mark1
    P.barrier()
    kr_ = [g.carve([128, 512], F32) for i in range(2)]
    ki_ = [g.carve([128, 512], F32) for i in range(2)]
    t1 = g.carve([128, 512], F32)
    t2 = g.carve([128, 512], F32)
    zin = [g.carve([128, 512], F32) for i in range(2)]
    gat = [g.carve([128, 512], F32) for i in range(2)]
    ot = [g.carve([128, 512], F32) for i in range(2)]
    otb = [g.carve([128, 512], BF16) for i in range(2)]
    stg = [g.carve([128, 512], F32) for i in range(2)]
    for o in range(2):
        def evac(kc, pc, pck, ps_, psk, o=o):
            kr, krk = kr_[kc % 2], "kr%d" % (kc % 2)
            ki2, kik = ki_[kc % 2], "ki%d" % (kc % 2)
            P.dma("sync", kr, g.KF[0, kc * 128:(kc + 1) * 128, o * 512:(o + 1) * 512], krk, r=["KF"], w=[krk])
            P.dma("sync", ki2, g.KF[1, kc * 128:(kc + 1) * 128, o * 512:(o + 1) * 512], kik, r=["KF"], w=[kik])
            P.op("vector", lambda e: e.tensor_tensor(out=t1, in0=pc[:, :512], in1=kr, op=ALU.mult), r=[pck, krk], w=["ht1"])
            P.op("vector", lambda e: e.tensor_tensor(out=t2, in0=ps_[:, :512], in1=ki2, op=ALU.mult), r=[psk, kik], w=["ht2"])
            P.op("gpsimd", lambda e: e.tensor_tensor(out=Asb[:, kc, :], in0=t1, in1=t2, op=ALU.add), r=["ht1", "ht2"], w=["Asb"])
            P.op("vector", lambda e: e.tensor_tensor(out=t1, in0=ps_[:, :512], in1=kr, op=ALU.mult), r=[psk, krk], w=["ht1"])
            P.op("vector", lambda e: e.tensor_tensor(out=t2, in0=pc[:, :512], in1=ki2, op=ALU.mult), r=[pck, kik], w=["ht2"])
            P.op("gpsimd", lambda e: e.tensor_tensor(out=Bsb[:, kc, :], in0=t1, in1=t2, op=ALU.subtract), r=["ht1", "ht2"], w=["Bsb"])

        def nyq(pn, pnk, o=o):
            P.dma("sync", kr_[0][0:1, :], g.KF[0, L:L + 1, o * 512:(o + 1) * 512], "kr0", r=["KF"], w=["kr0"])
            P.op("vector", lambda e: e.tensor_tensor(out=Asb[0:1, nch, :], in0=pn[0:1, :512], in1=kr_[0][0:1, :], op=ALU.mult),
                 r=[pnk, "kr0"], w=["Asb"])

        dft_fwd(g, L, ztok, "ztok", ztok, "ztok", evac, nyq)
        TWI = min(512, L)
        mats = g.hy_cs[L]
        for ti, t0 in enumerate(range(0, L, TWI)):
            banks = [g.psb(4 * (ti % 2) + cc) for cc in range(4)]
            pair = [None, None]
            for kc in range(nch):
                tiles = []
                for mi in range(2):
                    if kc % 2 == 0:
                        bi = g.mt_i
                        g.mt_i = (bi + 1) % len(g.mt)
                        mt2, mk = g.mt[bi], "mt%d" % bi
                        P.dma("sync", mt2[:, :, :TWI], mats[mi, kc // 2, t0 // TWI].rearrange("p (a k) -> p a k", a=2),
                              mk, w=[mk])
                        pair[mi] = (mt2, mk)
                    tiles.append((pair[mi][0][:, kc % 2, :], pair[mi][1]))
                for cc in range(4):
                    pb, pbk = banks[cc]
                    P.op("tensor", lambda e, pb=pb, kc=kc, cc=cc, mt=tiles[0][0]: e.matmul(
                        pb[:, :TWI], Asb[:, kc, cc * 128:(cc + 1) * 128], mt[:, :TWI], start=(kc == 0), stop=False),
                        r=["Asb", tiles[0][1]], w=[pbk])
                    P.op("tensor", lambda e, pb=pb, kc=kc, cc=cc, mt=tiles[1][0]: e.matmul(
                        pb[:, :TWI], Bsb[:, kc, cc * 128:(cc + 1) * 128], mt[:, :TWI], start=False, stop=False),
                        r=["Bsb", tiles[1][1]], w=[pbk])
            for cc in range(4):
                pb, pbk = banks[cc]
                P.op("tensor", lambda e, pb=pb, cc=cc: e.matmul(pb[:, :TWI], Asb[0:1, nch, cc * 128:(cc + 1) * 128], altr[0:1, :TWI],
                                                             start=False, stop=True), r=["Asb", "altr"], w=[pbk])
                zi_, zk = zin[cc % 2], "zin%d" % (cc % 2)
                ga_, gk = gat[cc % 2], "gat%d" % (cc % 2)
                o_, ok = ot[cc % 2], "hot%d" % (cc % 2)
                src = g.HV if o == 0 else g.HZ2
                P.dma("sync", zi_[:, :TWI], src[cc * 128:(cc + 1) * 128, t0:t0 + TWI], zk, r=["HV", "HZ2"], w=[zk])
                P.dma("sync", ga_[:, :TWI], g.HV[(o + 1) * 512 + cc * 128:(o + 1) * 512 + (cc + 1) * 128, t0:t0 + TWI], gk, r=["HV"], w=[gk])
                P.op("vector", lambda e, zi_=zi_, pb=pb, cc=cc, o_=o_, o=o: e.scalar_tensor_tensor(
                    out=o_[:, :TWI], in0=zi_[:, :TWI], scalar=skp[:, o, cc:cc + 1], in1=pb[:, :TWI], op0=ALU.mult, op1=ALU.add),
                    r=[zk, "skp", pbk], w=[ok])
                P.op("gpsimd", lambda e, o_=o_, ga_=ga_: e.tensor_tensor(out=o_[:, :TWI], in0=o_[:, :TWI], in1=ga_[:, :TWI], op=ALU.mult),
                     r=[ok, gk], w=[ok])
                if o == 0:
                    P.dma("gpsimd", g.HZ2[cc * 128:(cc + 1) * 128, t0:t0 + TWI], o_[:, :TWI], ok, r=[ok], w=["HZ2"])
                    ob_, obk = otb[cc % 2], "hotb%d" % (cc % 2)
                    P.op("scalar", lambda e, ob_=ob_, o_=o_: e.copy(out=ob_[:, :TWI], in_=o_[:, :TWI]), r=[ok], w=[obk])
                    for j in range(TWI // 128):
                        g.ps_i = (g.ps_i + 1) % 8
                        pi = 4 * (1 - ti % 2) + (j % 4)
                        P.op("tensor", lambda e, pi=pi, j=j, ob_=ob_: e.transpose(psb16[pi][:, 0:128], ob_[:, j * 128:(j + 1) * 128], identb),
                             r=[obk, "identb"], w=["ps%d" % pi])
                        P.op("vector", lambda e, pi=pi, j=j, cc=cc, t0=t0: e.tensor_copy(
                            out=ztok[:, t0 // 128 + j, cc * 128:(cc + 1) * 128], in_=psb16[pi][:, 0:128]), r=["ps%d" % pi], w=["ztok"])
                else:
                    P.dma("gpsimd", g.CATT[1024 + cc * 128:1024 + (cc + 1) * 128, tok0 + t0:tok0 + t0 + TWI], o_[:, :TWI], ok, r=[ok], w=["CATT"])


def phase_c(g, l):
    P = g.P
    g.new_phase()
    g.ps_rot = list(range(8))
    last = (l == DEPTH - 1)
    tiles = TILES[1:] if last else TILES
    jf = lambda t0: 1 if t0 < NCTX else 0
    catn = g.carve([128, KC, T], BF16)
    carve_common(g, 256)
    xt = g.carve([128, 8, 512], F32)
    sq = g.carve([128, 8, 512], BF16)
    rstd = g.carve([128, 512], F32)
    mg = g.carve([128, KC], F32)
    xres = [g.carve([128, 512], F32) for i in range(2)]
    P.dma("sync", mg, g.mix_gT[l], "pc", w=["mg"])
    P.seal()
    for (t0, n) in tiles:
        for (c0, nk, nfeat) in [(0, 8, 1024), (8, 4, 512)]:
            srcv = g.CATT[c0 * 128:(c0 + nk) * 128, :].rearrange("(kc p) t -> p kc t", p=128)
            P.dma("sync", xt[:, :nk, :n], srcv[:, :, t0:t0 + n], "xt0", r=["CATT"], w=["xt0"])
            P.op("scalar", lambda e, n=n, nk=nk: e.activation(out=sq[:, :nk, :n], in_=xt[:, :nk, :n], func=AF.Square), r=["xt0"], w=["sq"])
            ps, psk = g.next_ps()
            for kc in range(nk):
                P.op("tensor", lambda e, kc=kc, n=n, ps=ps, nk=nk: e.matmul(ps[:, :n], g.ones_bf[:], sq[:, kc, :n], start=(kc == 0), stop=(kc == nk - 1)),
                     r=["sq", "ones_bf"], w=[psk])
            rstd_op(g, ps, psk, n, nfeat, rstd, "rstd")
            for kc in range(nk):
                P.op("vector", lambda e, kc=kc, n=n, t0=t0, c0=c0: e.scalar_tensor_tensor(
                    out=catn[:, c0 + kc, t0:t0 + n], in0=xt[:, kc, :n], scalar=mg[:, c0 + kc:c0 + kc + 1], in1=rstd[:, :n],
                    op0=ALU.mult, op1=ALU.mult), r=["xt0", "rstd", "mg"], w=["catn"])
        srcv = g.CATT[1536:2048, :].rearrange("(kc p) t -> p kc t", p=128)
        P.dma("sync", xt[:, :4, :n], srcv[:, :, t0:t0 + n], "xt0", r=["CATT"], w=["xt0"])
        P.op("scalar", lambda e, n=n, t0=t0: e.copy(out=catn[:, 12:16, t0:t0 + n], in_=xt[:, :4, :n]), r=["xt0"], w=["catn"])
    src = g.xT if l == 0 else g.xs
    skey = "xs"
    wv = g.w_out[l].rearrange("(kc p) n -> p kc n", p=128)
    cnt = [0]

    def evac(cb, m, t0, n, ps, psk):
        fc = cb // 128
        j = jf(t0)
        xr = xres[cnt[0] % 2]
        xk = "xres%d" % (cnt[0] % 2)
        cnt[0] += 1
        P.dma("sync", xr[:, :n], src[cb:cb + 128, t0:t0 + n], xk, r=[skey], w=[xk])
        stg, sk = next_stage(g)
        P.op("vector", lambda e: e.scalar_tensor_tensor(out=stg[:, :n], in0=ps[:, :n], scalar=g.mod[l][:, 32 + fc, j:j + 1], in1=xr[:, :n],
                                                        op0=ALU.mult, op1=ALU.add), r=[psk, xk, "mod%d" % l], w=[sk])
        P.dma("sync", g.XM[cb:cb + 128, t0:t0 + n], stg[:, :n], sk, r=[sk], w=["XM"])

    for c0 in range(0, D, 256):
        proj_fm(g, wv, c0, 256, catn, "catn", tiles, evac)
    g.new_phase()
    xn2 = g.carve([128, KC, 1024], BF16)
    hT = g.carve([128, 44, 1024], BF16)
    w1b = [g.carve([128, KC, 256], BF16) for i in range(2)]
    w2b = [g.carve([128, 44, 128], BF16) for i in range(2)]
    g.stage = [g.carve([128, 512], F32) for i in range(2)]
    g.stage_i = 0
    sg = [g.carve([128, 512], F32) for i in range(2)]
    xres = [g.carve([128, 512], F32) for i in range(2)]
    rb = rms_bufs(g)
    w1v = g.ffn_w1[l].rearrange("(kc p) n -> p kc n", p=128)
    w2v = g.ffn_w2[l].rearrange("(kc p) n -> p kc n", p=128)
    groups = [[(256 + 1024 * i, 512), (256 + 1024 * i + 512, 512)] for i in range(4)]
    if not last:
        groups = [[(0, 256)]] + groups
    wi = 0
    w2i = 0
    for grp in groups:
        g0 = grp[0][0]
        gn = sum(n for (_, n) in grp)
        rms_modulate(g, g.XM, l, g.gs2[l], "gs2_%d" % l, 48, xn2, "xn2", [(t0, 128) for t0 in range(g0, g0 + gn, 128)], jf,
                     bufs=rb, srckey="XM", dst_t0=g0)
        for hc in range(44):
            wb, wk = w1b[wi % 2], "w1b%d" % (wi % 2)
            wi += 1
            P.dma("sync", wb.rearrange("p kc n -> p (kc n)"), g.W1B[hc], wk, r=["W1B"], w=[wk])
            for (t0, n) in grp:
                pg, pgk = g.next_ps()
                pu, puk = g.next_ps()
                for (pp, ppk, cofs) in [(pg, pgk, 0), (pu, puk, 128)]:
                    for kc in range(KC):
                        P.op("tensor", lambda e, pp=pp, kc=kc, cofs=cofs, t0=t0, n=n, wb=wb, g0=g0: e.matmul(
                            pp[:, :n], wb[:, kc, cofs:cofs + 128], xn2[:, kc, t0 - g0:t0 - g0 + n], start=(kc == 0), stop=(kc == KC - 1)),
                            r=[wk, "xn2"], w=[ppk])
                s_, sk_ = sg[hc % 2], "sg%d" % (hc % 2)
                P.op("scalar", lambda e, s_=s_, pg=pg, n=n: e.activation(out=s_[:, :n], in_=pg[:, :n], func=AF.Silu), r=[pgk], w=[sk_])
                P.op("vector", lambda e, s_=s_, pu=pu, n=n, hc=hc, t0=t0, g0=g0: e.tensor_tensor(
                    out=hT[:, hc, t0 - g0:t0 - g0 + n], in0=pu[:, :n], in1=s_[:, :n], op=ALU.mult), r=[puk, sk_], w=["hT"])
        for fc in range(KC):
            wb, wk = w2b[w2i % 2], "w2b%d" % (w2i % 2)
            w2i += 1
            P.dma("sync", wb.rearrange("p hc n -> p (hc n)"), g.W2B[fc], wk, r=["W2B"], w=[wk])
            for (t0, n) in grp:
                ps, psk = g.next_ps()
                for hc in range(44):
                    P.op("tensor", lambda e, ps=ps, hc=hc, t0=t0, n=n, wb=wb, g0=g0: e.matmul(
                        ps[:, :n], wb[:, hc, :], hT[:, hc, t0 - g0:t0 - g0 + n], start=(hc == 0), stop=(hc == 43)), r=[wk, "hT"], w=[psk])
                j = jf(t0)
                xr = xres[fc % 2]
                xk = "xres%d" % (fc % 2)
                P.dma("sync", xr[:, :n], g.XM[fc * 128:(fc + 1) * 128, t0:t0 + n], xk, r=["XM"], w=[xk])
                stg, sk = next_stage(g)
                P.op("vector", lambda e, stg=stg, ps=ps, n=n, fc=fc, j=j, xr=xr: e.scalar_tensor_tensor(
                    out=stg[:, :n], in0=ps[:, :n], scalar=g.mod[l][:, 80 + fc, j:j + 1], in1=xr[:, :n], op0=ALU.mult, op1=ALU.add),
                    r=[psk, xk, "mod%d" % l], w=[sk])
                if last:
                    P.dma("gpsimd", g.outT[fc * 128:(fc + 1) * 128, t0 - NCTX:t0 - NCTX + n], stg[:, :n], sk, r=[sk], w=["outT"])
                else:
                    P.dma("gpsimd", g.xs[fc * 128:(fc + 1) * 128, t0:t0 + n], stg[:, :n], sk, r=[sk], w=["xs"])


def host_inputs(inputs, b):
    f = lambda a: np.ascontiguousarray(a, dtype=np.float32)
    x, ctx, c, c_ctx = inputs["x"], inputs["ctx"], inputs["c"], inputs["c_ctx"]
    m = {}
    m["xT"] = f(np.concatenate([ctx[b], x[b]], axis=0).T)
    cc = np.stack([c[b], c_ctx], axis=-1)
    m["cT"] = f(cc.reshape(KC, 128, 2).transpose(1, 0, 2))
    m["ada_w"] = f(inputs["ada_w"])
    m["ada_bT"] = f(inputs["ada_b"].reshape(DEPTH, 96, 128).transpose(0, 2, 1))
    m["norm1_gT"] = f(inputs["norm1_g"].reshape(DEPTH, KC, 128).transpose(0, 2, 1))
    m["norm2_gT"] = f(inputs["norm2_g"].reshape(DEPTH, KC, 128).transpose(0, 2, 1))
    m["w_in"] = f(inputs["w_in"])
    wi = inputs["w_in"]
    m["w_in_rot"] = f(np.concatenate([wi[:, :, 1056:1088], wi[:, :, 1024:1056]], axis=-1))
    qa, kva = inputs["mla_qa_norm"], inputs["mla_kva_norm"]
    m["mla_gT"] = f(np.concatenate([qa.reshape(DEPTH, 4, 128).transpose(0, 2, 1),
                                    kva.reshape(DEPTH, 4, 128).transpose(0, 2, 1)], axis=-1))
    qn_, kn_ = inputs["mla_q_norm"], inputs["mla_k_norm"]
    hg = np.zeros((DEPTH, 128, 6), np.float32)
    hg[:, :, 0] = qn_[:, :128]
    hg[:, :, 1] = kn_[:, :128]
    hg[:, :64, 2] = qn_[:, 128:192]
    hg[:, :64, 3] = np.concatenate([qn_[:, 160:192], qn_[:, 128:160]], -1)
    hg[:, :64, 4] = kn_[:, 128:192]
    hg[:, :64, 5] = np.concatenate([kn_[:, 160:192], kn_[:, 128:160]], -1)
    m["mla_hg"] = hg
    m["rope_cs"] = rope_tables()
    wq = inputs["mla_w_uq"]
    m["w_uq"] = f(wq)
    wq4 = wq.reshape(DEPTH, 512, 8, 192)
    m["w_uq_rot"] = f(np.concatenate([wq4[..., 160:192], wq4[..., 128:160]], -1).reshape(DEPTH, 512, 512))
    m["w_ukv"] = f(inputs["mla_w_ukv"])
    tri = np.zeros((2, 128, 128), np.float32)
    ii = np.arange(128)
    tri[0] = (ii[:, None] <= ii[None, :])
    tri[1] = (ii[:, None] >= ii[None, :])
    m["tri"] = tri
    m["ident"] = np.eye(128, dtype=np.float32)
    sel4 = np.zeros((4, 4, 128), np.float32)
    for r_ in range(4):
        sel4[r_, r_, :] = 1.0
    m["sel4"] = sel4
    cwm = np.concatenate([inputs["ml_conv_w"], inputs["ml_conv_b"][:, None, :]], axis=1)
    m["ml_cw"] = f(cwm.reshape(DEPTH, 4, 8, 128).transpose(0, 3, 2, 1))
    m["ml_gate_b"] = f(inputs["ml_gate_b"])
    m["mix_g"] = f(inputs["mix_norm_g"])
    m["mix_gT"] = f(inputs["mix_norm_g"].reshape(DEPTH, KC, 128).transpose(0, 2, 1))
    m.update(hyena_tables())
    hcw = np.concatenate([inputs["hy_conv_w"], inputs["hy_conv_b"][:, None, :]], axis=1)
    m["hy_cw"] = f(hcw.reshape(DEPTH, 4, 12, 128).transpose(0, 3, 2, 1))
    m["hy_w1"] = f(inputs["hy_w1"])
    m["hy_b12"] = f(np.stack([inputs["hy_b1"], inputs["hy_b2"]], axis=-1))
    m["hy_w2"] = f(inputs["hy_w2"])
    m["hy_w3"] = f(inputs["hy_w3"])
    m["hy_decay"] = f(inputs["hy_decay"])
    m["hy_skipT"] = f(inputs["hy_skip"].reshape(DEPTH, 2, 4, 128).transpose(0, 3, 1, 2))
    m["w_out"] = f(inputs["w_out"])
    m["ffn_w1"] = f(inputs["ffn_w1"])
    m["ffn_w2"] = f(inputs["ffn_w2"])
    return m


def rope_tables():
    rows = NLAT // 64
    row = np.repeat(np.arange(rows, dtype=np.float32), 64)
    col = np.tile(np.arange(64, dtype=np.float32), rows)
    freqs = (np.float32(10000.0) ** (-np.arange(16, dtype=np.float32) / np.float32(16))).astype(np.float32)
    ang = np.concatenate([row[:, None] * freqs, col[:, None] * freqs], axis=-1).astype(np.float32)
    cos, sin = np.cos(ang).astype(np.float32), np.sin(ang).astype(np.float32)
    out = np.zeros((2, 64, T), np.float32)
    out[0, :, :NCTX] = 1.0
    out[0, :32, NCTX:] = cos.T
    out[0, 32:, NCTX:] = cos.T
    out[1, :32, NCTX:] = -sin.T
    out[1, 32:, NCTX:] = sin.T
    return out


_HY_TABLES = {}


def hyena_tables():
    if _HY_TABLES:
        return _HY_TABLES
    import ml_dtypes
    out = {}
    for L in (4096, 256):
        t = (np.arange(L, dtype=np.float32) / np.float32(L)).astype(np.float32)
        bands = np.arange(1, 17, dtype=np.float32)
        ang = (np.float32(2.0 * math.pi) * t[:, None] * bands).astype(np.float32)
        feats = np.concatenate([t[:, None], np.cos(ang), np.sin(ang)], axis=-1).astype(np.float32)
        out["hy_feats%d" % L] = np.ascontiguousarray(feats.T)
        out["hy_negt%d" % L] = np.ascontiguousarray((-t).reshape(L // 128, 128).T)
        ii = np.arange(L, dtype=np.int64)
        ph = (ii[:, None] * ii[None, :]) % (2 * L)
        th = ph.astype(np.float64) * (2.0 * math.pi / (2 * L))
        cs = np.stack([np.cos(th), np.sin(th)]).astype(np.float32).astype(ml_dtypes.bfloat16)
        KG = min(512, L)
        nch = L // 128
        cs = cs.reshape(2, nch // 2, 2, 128, L // KG, KG).transpose(0, 1, 4, 3, 2, 5)
        out["hy_cs%d" % L] = np.ascontiguousarray(cs).reshape(2, nch // 2, L // KG, 128, 2 * KG)
    alt = np.ones((128, 512), np.float32)
    alt[1::2, :] = -1.0
    alt[0, 1::2] = -1.0
    out["hy_alt"] = alt
    _HY_TABLES.update(out)
    return _HY_TABLES


def kernel(**inputs):
    nc = build()
    inputs = {k: np.asarray(v) for k, v in inputs.items()}
    shared = host_inputs(inputs, 0)
    in_maps = []
    for core in range(8):
        b = core % 4
        m = dict(shared)
        if b != 0:
            mb = host_inputs_batch(inputs, b)
            m.update(mb)
        in_maps.append(m)
    res = run_bass_kernel_spmd(nc, in_maps, core_ids=list(range(8)))
    out = np.stack([np.ascontiguousarray(res.results[b]["outT"].T) for b in range(4)], axis=0)
    return out.astype(np.float32)


def host_inputs_batch(inputs, b):
    f = lambda a: np.ascontiguousarray(a, dtype=np.float32)
    x, ctx, c, c_ctx = inputs["x"], inputs["ctx"], inputs["c"], inputs["c_ctx"]
    m = {}
    m["xT"] = f(np.concatenate([ctx[b], x[b]], axis=0).T)
    cc = np.stack([c[b], c_ctx], axis=-1)
    m["cT"] = f(cc.reshape(KC, 128, 2).transpose(1, 0, 2))
    return m
```

```python
import math
from contextlib import ExitStack

import numpy as np
import concourse.bass as bass
import concourse.mybir as mybir
from concourse.bass_utils import run_bass_kernel_spmd

F32 = mybir.dt.float32
BF16 = mybir.dt.bfloat16
AF = mybir.ActivationFunctionType
ALU = mybir.AluOpType
AX = mybir.AxisListType

ENGS = ["sync", "scalar", "vector", "gpsimd", "tensor"]

D = 2048
KC = 16
NCTX = 256
NLAT = 4096
T = NCTX + NLAT
DEPTH = 2
EPS = 1e-6
N_IN = 4688
FFN_H = 5632
ARENA = 204800
TILES = [(0, 256)] + [(256 + 512 * i, 512) for i in range(8)]
NTILES = [(256 * i, 256) for i in range(17)]


class Prog:
    def __init__(self, nc):
        self.nc = nc
        self.ops = {e: [] for e in ENGS}
        self.last_w = {}
        self.readers = {}
        self.dma_cnt = {}
        self.fence = set()
        self.sealed = {}
        self.strict_same = {"scalar": True, "vector": True, "gpsimd": True,
                            "tensor": False, "sync": False}

    def _deps(self, r, w):
        deps = set()
        for k in r:
            lw = self.last_w.get(k)
            if lw is not None:
                deps.add(lw)
        for k in w:
            lw = self.last_w.get(k)
            if lw is not None:
                deps.add(lw)
            for rd in self.readers.get(k, ()):
                deps.add(rd)
        return deps

    def _commit(self, ref, r, w):
        for k in r:
            self.readers.setdefault(k, []).append(ref)
        for k in w:
            self.last_w[k] = ref
            self.readers[k] = []

    def barrier(self):
        f = set()
        for e in ENGS:
            for i in range(len(self.ops[e]) - 1, -1, -1):
                if self.ops[e][i]["dma"] is None:
                    f.add(("c", e, i))
                    break
        for k, c in self.dma_cnt.items():
            f.add(("d", k, c))
        self.fence = f

    def op(self, eng, fn, r=(), w=()):
        deps = self._deps(r, w) | self.fence
        ref = ("c", eng, len(self.ops[eng]))
        self.ops[eng].append(dict(fn=fn, deps=deps, ref=ref, dma=None))
        self._commit(ref, r, w)
        return ref

    def seal(self, key="pc"):
        c = self.dma_cnt.get(key, 0)
        for k, ref in list(self.last_w.items()):
            if ref[0] == "d" and ref[1] == key:
                self.last_w[k] = ("d", key, c)
        self.sealed[key] = c

    def dma(self, eng, out, in_, key, r=(), w=(), **kw):
        deps = self._deps(r, w) | self.fence
        if self.sealed.get(key):
            deps = deps | {("d", key, self.sealed[key])}
        cnt = self.dma_cnt.get(key, 0) + 1
        self.dma_cnt[key] = cnt
        ref = ("d", key, cnt)
        self.ops[eng].append(dict(fn=lambda e: e.dma_start(out=out, in_=in_, **kw),
                                  deps=deps, ref=ref, dma=key))
        self._commit(ref, r, w)
        return ref

    def emit(self):
        nc = self.nc
        signaled = {e: set() for e in ENGS}
        for e in ENGS:
            for o in self.ops[e]:
                for d in o["deps"]:
                    if d[0] == "c":
                        if d[1] == e and not self.strict_same[e]:
                            continue
                        signaled[d[1]].add(d[2])
        sigcount = {}
        for e in ENGS:
            c = 0
            m = {}
            for i, o in enumerate(self.ops[e]):
                if o["dma"] is None and i in signaled[e]:
                    c += 1
                    m[i] = c
            sigcount[e] = m
        with ExitStack() as st:
            csem = {e: st.enter_context(nc.semaphore("s_" + e)) for e in ENGS}
            dsem = {k: st.enter_context(nc.semaphore("d_%d" % i))
                    for i, k in enumerate(self.dma_cnt)}
            block = st.enter_context(nc.Block())

            def make(eng):
                def body(e):
                    waited = {}
                    for i, o in enumerate(self.ops[eng]):
                        waits = {}
                        for d in o["deps"]:
                            if d[0] == "c":
                                if d[1] == eng and not self.strict_same[eng]:
                                    continue
                                s, v, kk = csem[d[1]], sigcount[d[1]][d[2]], ("c", d[1])
                            else:
                                s, v, kk = dsem[d[1]], 16 * d[2], ("d", d[1])
                            if waited.get(kk, 0) >= v:
                                continue
                            if kk not in waits or waits[kk][1] < v:
                                waits[kk] = (s, v)
                        for kk, (s, v) in waits.items():
                            e.wait_ge(s, v)
                            waited[kk] = v
                        ins = o["fn"](e)
                        if o["dma"] is not None:
                            ins.then_inc(dsem[o["dma"]], 16)
                        elif i in sigcount[eng]:
                            ins.then_inc(csem[eng], 1)
                    if eng == "sync":
                        for k, c in self.dma_cnt.items():
                            e.wait_ge(dsem[k], 16 * c)
                return body

            block.sync(make("sync"))
            block.scalar(make("scalar"))
            block.vector(make("vector"))
            block.gpsimd(make("gpsimd"))
            block.tensor(make("tensor"))


class Ctx:
    pass


def build(stop_after="all", dbg=()):
    nc = bass.Bass("TRN2", target_bir_lowering=False)
    P = Prog(nc)
    g = Ctx()
    g.nc, g.P = nc, P
    g.din = lambda name, shape, dt=F32: nc.dram_tensor(name, list(shape), dt, kind="ExternalInput").ap()
    g.dout = lambda name, shape, dt=F32: nc.dram_tensor(name, list(shape), dt, kind="ExternalOutput").ap()
    dbgset = set(dbg)

    def dscr(name, shape, dt=F32):
        kind = "ExternalOutput" if name in dbgset else "Internal"
        return nc.dram_tensor(name, list(shape), dt, kind=kind).ap()
    g.dscr = dscr

    with ExitStack() as st:
        g.st = st
        g.sb = lambda name, shape, dt=F32: st.enter_context(nc.sbuf_tensor(name, list(shape), dt))
        g.ps = [st.enter_context(nc.psum_tensor("ps%d" % i, [128, 512], F32)) for i in range(8)]
        g.ps_i = 0

        g.ps_rot = list(range(8))

        def next_ps():
            g.ps_i = (g.ps_i + 1) % len(g.ps_rot)
            i = g.ps_rot[g.ps_i]
            return g.ps[i], "ps%d" % i
        g.next_ps = next_ps
        g.psb = lambda i: (g.ps[i], "ps%d" % i)

        g.arena = st.enter_context(nc.sbuf_tensor("arena", [128, ARENA // 4], F32))
        g.arena_off = 0

        def carve(shape, dt=F32):
            esz = 2 if dt == BF16 else 4
            free = int(np.prod(shape[1:]))
            nbytes = (free * esz + 63) // 64 * 64
            off = g.arena_off
            assert off + nbytes <= ARENA, ("arena overflow", off, nbytes)
            g.arena_off = off + nbytes
            words = (free * esz + 3) // 4
            v = g.arena[:, off // 4:off // 4 + words]
            if dt == BF16:
                v = v.bitcast(BF16)
            elif dt != F32:
                v = v.bitcast(dt)
            v = v[:shape[0], :free]
            if len(shape) == 3:
                v = v.rearrange("p (a b) -> p a b", b=shape[2])
            return v
        g.carve = carve

        def new_phase():
            g.arena_off = 0
            P.barrier()
        g.new_phase = new_phase

        declare_io(g)
        consts(g)
        for l in range(DEPTH):
            phase_mod(g, l)
        for l in range(DEPTH):
            if stop_after == "mod":
                break
            phase_a(g, l)
            if stop_after == "a%d" % l:
                break
            phase_mla(g, l)
            if stop_after == "mla%d" % l:
                break
            if "noattn" not in dbgset:
                phase_attn(g, l)
            if stop_after == "attn%d" % l:
                break
            if "nomlstm" not in dbgset:
                phase_mlstm(g, l)
            if stop_after == "mlstm%d" % l:
                break
            hyena_seq(g, l, 4096, NCTX)
            if l == 0 and "nohyctx" not in dbgset:
                hyena_seq(g, l, 256, 0)
            if stop_after == "hy%d" % l:
                break
            phase_c(g, l)
            if stop_after == "c%d" % l:
                break
        P.emit()
    return nc


def declare_io(g):
    g.xT = g.din("xT", [D, T])
    g.cT = g.din("cT", [128, KC, 2])
    g.ada_w = g.din("ada_w", [DEPTH, D, 6 * D])
    g.ada_bT = g.din("ada_bT", [DEPTH, 128, 96])
    g.norm1_gT = g.din("norm1_gT", [DEPTH, 128, KC])
    g.norm2_gT = g.din("norm2_gT", [DEPTH, 128, KC])
    g.w_in = g.din("w_in", [DEPTH, D, N_IN])
    g.w_in_rot = g.din("w_in_rot", [DEPTH, D, 64])
    g.PT = [g.dscr("PT%d" % l, [N_IN + 64, T]) for l in range(DEPTH)]
    g.PTM = [g.dscr("PTM%d" % l, [T, 1040]) for l in range(DEPTH)]
    g.modd = g.dscr("modd", [DEPTH, 128, 96 * 2])
    g.mla_gT = g.din("mla_gT", [DEPTH, 128, 8])
    g.mla_hg = g.din("mla_hg", [DEPTH, 128, 6])
    g.rope_cs = g.din("rope_cs", [2, 64, T])
    g.w_uq = g.din("w_uq", [DEPTH, 512, 1536])
    g.w_uq_rot = g.din("w_uq_rot", [DEPTH, 512, 512])
    g.w_ukv = g.din("w_ukv", [DEPTH, 512, 2048])
    g.QT = g.dscr("QT", [8, 192, T], BF16)
    g.tri = g.din("tri", [2, 128, 128])
    g.ident = g.din("ident", [128, 128])
    g.sel4 = g.din("sel4", [4, 4, 128])
    g.ml_cw = g.din("ml_cw", [DEPTH, 128, 8, 4])
    g.ml_gate_b = g.din("ml_gate_b", [DEPTH, 16])
    g.mix_g = g.din("mix_g", [DEPTH, D])
    g.mix_gT = g.din("mix_gT", [DEPTH, 128, KC])
    g.HD = [g.dscr("HD%d" % i, [T, 512]) for i in range(2)]
    g.hy_feats = {4096: g.din("hy_feats4096", [33, 4096]), 256: g.din("hy_feats256", [33, 256])}
    g.hy_negt = {4096: g.din("hy_negt4096", [128, 32]), 256: g.din("hy_negt256", [128, 2])}
    g.hy_cs = {4096: g.din("hy_cs4096", [2, 16, 8, 128, 1024], BF16), 256: g.din("hy_cs256", [2, 1, 1, 128, 512], BF16)}
    g.hy_alt = g.din("hy_alt", [128, 512])
    g.hy_cw = g.din("hy_cw", [DEPTH, 128, 12, 4])
    g.hy_w1 = g.din("hy_w1", [DEPTH, 33, 64])
    g.hy_b12 = g.din("hy_b12", [DEPTH, 64, 2])
    g.hy_w2 = g.din("hy_w2", [DEPTH, 64, 64])
    g.hy_w3 = g.din("hy_w3", [DEPTH, 64, 2048])
    g.hy_decay = g.din("hy_decay", [DEPTH, 2048])
    g.hy_skipT = g.din("hy_skipT", [DEPTH, 128, 2, 4])
    g.HS = g.dscr("HS", [2, 4096, 1024], BF16)
    g.KF = g.dscr("KF", [2, 4097, 1024])
    g.HV = g.dscr("HV", [1536, 4096])
    g.HZ2 = g.dscr("HZ2", [512, 4096])
    g.w_out = g.din("w_out", [DEPTH, D, D])
    g.ffn_w1 = g.din("ffn_w1", [DEPTH, D, 2 * FFN_H])
    g.ffn_w2 = g.din("ffn_w2", [DEPTH, FFN_H, D])
    g.XM = g.dscr("XM", [D, T])
    g.W1B = g.dscr("W1B", [44, 128, 4096], BF16)
    g.W2B = g.dscr("W2B", [16, 128, 5632], BF16)
    g.xs = g.dscr("xs", [D, T])
    g.outT = g.dout("outT", [D, NLAT])
    g.KT = g.dscr("KT", [8, 192, T], BF16)
    g.VTM = g.dscr("VTM", [T, 1024], BF16)
    g.CATT = g.dscr("CATT", [D, T])


def consts(g):
    P, nc = g.P, g.nc
    g.ones_bf = g.sb("ones_bf", [128, 128], BF16)
    P.op("vector", lambda e: e.memset(g.ones_bf[:], 1.0), w=["ones_bf"])
    g.eps_t = g.sb("eps_t", [128, 1], F32)
    P.op("vector", lambda e: e.memset(g.eps_t[:], EPS), w=["eps_t"])
    g.mod = [g.sb("mod%d" % l, [128, 96, 2]) for l in range(DEPTH)]
    g.gs1 = [g.sb("gs1_%d" % l, [128, KC, 2]) for l in range(DEPTH)]
    g.gs2 = [g.sb("gs2_%d" % l, [128, KC, 2]) for l in range(DEPTH)]
    g.n1g = g.sb("n1g", [128, DEPTH, KC])
    g.n2g = g.sb("n2g", [128, DEPTH, KC])
    P.dma("sync", g.n1g[:], g.norm1_gT.rearrange("l p k -> p l k"), "pc", w=["n1g"])
    P.dma("sync", g.n2g[:], g.norm2_gT.rearrange("l p k -> p l k"), "pc", w=["n2g"])
    g.scT = g.sb("scT", [128, KC, 2])
    g.cTs = g.sb("cTs", [128, KC, 2])
    P.dma("sync", g.cTs[:], g.cT[:, :, :], "pc", w=["cTs"])
    P.seal()
    P.op("scalar", lambda e: e.activation(out=g.scT[:], in_=g.cTs[:], func=AF.Silu), r=["cTs"], w=["scT"])
    g.scb = g.sb("scb", [128, KC, 2], BF16)
    P.op("vector", lambda e: e.tensor_copy(out=g.scb[:], in_=g.scT[:]), r=["scT"], w=["scb"])


def next_stage(g):
    i = g.stage_i
    g.stage_i = (i + 1) % len(g.stage)
    return g.stage[i], "stage%d" % i


def carve_common(g, wcols):
    g.wbuf = [g.carve([128, KC * wcols], BF16) for i in range(2)]
    g.w_i = 0
    g.stage = [g.carve([128, 512], F32) for i in range(4)]
    g.stage_i = 0


def phase_mod(g, l):
    P, nc = g.P, g.nc
    g.new_phase()
    carve_common(g, 512)
    adab = g.carve([128, 96], F32)
    ak = "adab"
    P.dma("sync", adab, g.ada_bT[l], "pc", w=[ak])
    P.seal()
    wv = g.ada_w[l].rearrange("(kc p) n -> p kc n", p=128)
    ps, psk = g.next_ps()
    psv = ps[:, 0:192].rearrange("p (c j) -> p c j", j=2)
    for sbk in range(24):
        wb = g.wbuf[sbk % 2]
        wk = "wbuf%d" % (sbk % 2)
        wt = wb.rearrange("p (kc n) -> p kc n", n=512)
        P.dma("gpsimd", wt, wv[:, :, sbk * 512:(sbk + 1) * 512], wk, w=[wk])
        for j in range(4):
            cb = sbk * 4 + j
            for kc in range(KC):
                P.op("tensor", lambda e, wt=wt, j=j, kc=kc, cb=cb: e.matmul(
                    psv[:, cb, :], wt[:, kc, j * 128:(j + 1) * 128], g.scb[:, kc, :],
                    start=(kc == 0), stop=(kc == KC - 1)),
                    r=[wk, "scb"], w=[psk])
    mk = "mod%d" % l
    for j in range(2):
        P.op("vector", lambda e, j=j: e.tensor_tensor(out=g.mod[l][:, :, j], in0=psv[:, :, j], in1=adab,
                                                     op=ALU.add), r=[psk, ak], w=[mk])
    for j in range(2):
        P.op("vector", lambda e, j=j: e.scalar_tensor_tensor(
            out=g.gs1[l][:, :, j], in0=g.mod[l][:, 16:32, j], scalar=1.0, in1=g.n1g[:, l, :],
            op0=ALU.add, op1=ALU.mult), r=[mk, "n1g"], w=["gs1_%d" % l])
        P.op("vector", lambda e, j=j: e.scalar_tensor_tensor(
            out=g.gs2[l][:, :, j], in0=g.mod[l][:, 64:80, j], scalar=1.0, in1=g.n2g[:, l, :],
            op0=ALU.add, op1=ALU.mult), r=[mk, "n2g"], w=["gs2_%d" % l])
    P.dma("sync", g.modd[l], g.mod[l][:].rearrange("p c j -> p (c j)"), mk, r=[mk], w=["modd"])


def rms_bufs(g, NT=128):
    return (g.carve([128, KC, NT], F32), g.carve([128, KC, NT], BF16), g.carve([128, NT], F32),
            [g.carve([128, NT], F32) for i in range(2)])


def rms_modulate(g, src_T, l, gs, gskey, shift_lo, dstT, dkey, tiles, jfn, bufs=None, srckey=None, dst_t0=0):
    P, nc = g.P, g.nc
    xt, sq, rstd, tmps = bufs if bufs is not None else rms_bufs(g)
    rk = [srckey] if srckey else []
    srcv = src_T.rearrange("(kc p) t -> p kc t", p=128)
    for (t0, n) in tiles:
        j = jfn(t0)
        P.dma("sync", xt[:, :, :n], srcv[:, :, t0:t0 + n], "xt0", r=rk, w=["xt0"])
        P.op("scalar", lambda e, n=n: e.activation(out=sq[:, :, :n], in_=xt[:, :, :n], func=AF.Square),
             r=["xt0"], w=["sq"])
        ps, psk = g.next_ps()
        for kc in range(KC):
            P.op("tensor", lambda e, kc=kc, n=n, ps=ps: e.matmul(ps[:, :n], g.ones_bf[:], sq[:, kc, :n],
                                                              start=(kc == 0), stop=(kc == KC - 1)),
                 r=["sq", "ones_bf"], w=[psk])
        P.op("scalar", lambda e, n=n, ps=ps: e.activation(out=rstd[:, :n], in_=ps[:, :n], func=AF.Sqrt,
                                                       bias=g.eps_t[:, 0:1], scale=1.0 / D),
             r=[psk, "eps_t"], w=["rstd"])
        P.op("vector", lambda e, n=n: e.reciprocal(out=rstd[:, :n], in_=rstd[:, :n]), r=["rstd"], w=["rstd"])
        for kc in range(KC):
            tmp = tmps[kc % 2]
            tk = "tmp%d" % (kc % 2)
            P.op("vector", lambda e, kc=kc, n=n, tmp=tmp, j=j: e.scalar_tensor_tensor(
                out=tmp[:, :n], in0=xt[:, kc, :n], scalar=gs[:, kc, j:j + 1], in1=rstd[:, :n],
                op0=ALU.mult, op1=ALU.mult), r=["xt0", "rstd", gskey], w=[tk])
            P.op("scalar", lambda e, kc=kc, n=n, tmp=tmp, j=j, t0=t0: e.activation(
                out=dstT[:, kc, t0 - dst_t0:t0 - dst_t0 + n], in_=tmp[:, :n], func=AF.Identity,
                bias=g.mod[l][:, shift_lo + kc, j:j + 1], scale=1.0), r=[tk, "mod%d" % l], w=[dkey])


def proj_fm(g, w_view, col0, ncols, actT, akey, tiles, evac):
    P = g.P
    nkc = w_view.shape[1]
    wi = g.w_i
    g.w_i = (wi + 1) % 2
    wk = "wbuf%d" % wi
    wt = g.wbuf[wi][:, :nkc * ncols].rearrange("p (kc n) -> p kc n", n=ncols)
    P.dma("gpsimd", wt, w_view[:, :, col0:col0 + ncols], wk, w=[wk])
    for c0 in range(0, ncols, 128):
        m = min(128, ncols - c0)
        for (t0, n) in tiles:
            ps, psk = g.next_ps()
            for kc in range(nkc):
                P.op("tensor", lambda e, kc=kc, c0=c0, m=m, t0=t0, n=n, ps=ps: e.matmul(
                    ps[:m, :n], wt[:, kc, c0:c0 + m], actT[:, kc, t0:t0 + n],
                    start=(kc == 0), stop=(kc == nkc - 1)), r=[wk, akey], w=[psk])
            evac(col0 + c0, m, t0, n, ps, psk)


def phase_a(g, l):
    P, nc = g.P, g.nc
    g.new_phase()
    xnT = g.carve([128, KC, T], BF16)
    carve_common(g, 256)
    src = g.xT if l == 0 else g.xs
    rms_modulate(g, src, l, g.gs1[l], "gs1_%d" % l, 0, xnT, "xnT", [(256 * i, 256) for i in range(17)],
                 lambda t0: 1 if t0 < NCTX else 0, bufs=rms_bufs(g, 256), srckey="xs")
    wv = g.w_in[l].rearrange("(kc p) n -> p kc n", p=128)
    wrv = g.w_in_rot[l].rearrange("(kc p) n -> p kc n", p=128)
    PT = g.PT[l]
    cnt = [0]

    def evac_to(PTrow0):
        def evac(cb, m, t0, n, ps, psk):
            stg, sk = next_stage(g)
            if cnt[0] % 2 == 0:
                P.op("vector", lambda e: e.tensor_copy(out=stg[:m, :n], in_=ps[:m, :n]), r=[psk], w=[sk])
            else:
                P.op("scalar", lambda e: e.copy(out=stg[:m, :n], in_=ps[:m, :n]), r=[psk], w=[sk])
            cnt[0] += 1
            P.dma("sync", PT[PTrow0 + cb:PTrow0 + cb + m, t0:t0 + n], stg[:m, :n], sk, r=[sk], w=["PT%d" % l])
        return evac

    blocks = [(c, 256) for c in range(0, 1024, 256)] + [(1024, 64)] + [(c, 256) for c in range(1088, 3648, 256)]
    for (c0, ncol) in blocks:
        proj_fm(g, wv, c0, ncol, xnT, "xnT", TILES, evac_to(0))
    proj_fm(g, wrv, 0, 64, xnT, "xnT", TILES, evac_to(N_IN))
    for (c0, ncol) in [(3648, 256), (3904, 256), (4160, 256), (4416, 256), (4672, 16)]:
        wi = g.w_i
        g.w_i = (wi + 1) % 2
        wk = "wbuf%d" % wi
        wt = g.wbuf[wi][:, :KC * ncol].rearrange("p (kc n) -> p kc n", n=ncol)
        P.dma("gpsimd", wt, wv[:, :, c0:c0 + ncol], wk, w=[wk])
        for c in range(34):
            ps, psk = g.next_ps()
            for kc in range(KC):
                P.op("tensor", lambda e, kc=kc, c=c, ps=ps, wt=wt, ncol=ncol: e.matmul(
                    ps[:, :ncol], xnT[:, kc, c * 128:(c + 1) * 128], wt[:, kc, :],
                    start=(kc == 0), stop=(kc == KC - 1)), r=[wk, "xnT"], w=[psk])
            stg, sk = next_stage(g)
            if c % 2 == 0:
                P.op("vector", lambda e, ps=ps, stg=stg, ncol=ncol: e.tensor_copy(out=stg[:, :ncol], in_=ps[:, :ncol]),
                     r=[psk], w=[sk])
            else:
                P.op("scalar", lambda e, ps=ps, stg=stg, ncol=ncol: e.copy(out=stg[:, :ncol], in_=ps[:, :ncol]),
                     r=[psk], w=[sk])
            P.dma("sync", g.PTM[l][c * 128:(c + 1) * 128, c0 - 3648:c0 - 3648 + ncol], stg[:, :ncol], sk,
                  r=[sk], w=["PTM%d" % l])


def rstd_op(g, ps, psk, n, nfeat, rstd, rkey, rows=128):
    P = g.P
    P.op("scalar", lambda e: e.activation(out=rstd[:rows, :n], in_=ps[:rows, :n], func=AF.Sqrt,
                                          bias=g.eps_t[:rows, 0:1], scale=1.0 / nfeat), r=[psk, "eps_t"], w=[rkey])
    P.op("vector", lambda e: e.reciprocal(out=rstd[:rows, :n], in_=rstd[:rows, :n]), r=[rkey], w=[rkey])


def phase_mla(g, l):
    P = g.P
    g.new_phase()
    g.ps_rot = list(range(8))
    PT = g.PT[l]
    cqn = g.carve([128, 4, T], BF16)
    ckvn = g.carve([128, 4, T], BF16)
    cos2 = g.carve([64, T], F32)
    sin2 = g.carve([64, T], F32)
    KR = g.carve([64, T], F32)
    sqr = g.carve([64, T], BF16)
    xt = g.carve([128, 4, 512], F32)
    sq = g.carve([128, 4, 512], BF16)
    rstd = g.carve([128, 512], F32)
    gT = g.carve([128, 8], F32)
    hg = g.carve([128, 6], F32)
    wq = g.carve([128, 4, 256], BF16)
    wk = g.carve([128, 4, 128], BF16)
    wvv = g.carve([128, 4, 1024], BF16)
    stb = [g.carve([128, 512], BF16) for i in range(4)]
    tA = g.carve([64, 512], F32)
    tB = g.carve([64, 512], F32)
    P.dma("sync", gT, g.mla_gT[l], "pc", w=["gT"])
    P.dma("sync", hg, g.mla_hg[l], "pc", w=["hg"])
    P.dma("sync", cos2, g.rope_cs[0], "pc", w=["cos2"])
    P.dma("sync", sin2, g.rope_cs[1], "pc", w=["sin2"])
    P.seal()
    sc = 192.0 ** -0.5
    for c in (0, 2, 3):
        P.op("vector", lambda e, c=c: e.tensor_scalar(out=hg[:, c:c + 1], in0=hg[:, c:c + 1], scalar1=sc, scalar2=None,
                                                      op0=ALU.mult), r=["hg"], w=["hg"])
    for (r0, dst, dk, gc) in [(0, cqn, "cqn", 0), (512, ckvn, "ckvn", 4)]:
        srcv = PT[r0:r0 + 512, :].rearrange("(kc p) t -> p kc t", p=128)
        for (t0, n) in TILES:
            P.dma("sync", xt[:, :, :n], srcv[:, :, t0:t0 + n], "xt0", r=["PT%d" % l], w=["xt0"])
            P.op("scalar", lambda e, n=n: e.activation(out=sq[:, :, :n], in_=xt[:, :, :n], func=AF.Square),
                 r=["xt0"], w=["sq"])
            ps, psk = g.next_ps()
            for kc in range(4):
                P.op("tensor", lambda e, kc=kc, n=n, ps=ps: e.matmul(ps[:, :n], g.ones_bf[:], sq[:, kc, :n],
                                                                  start=(kc == 0), stop=(kc == 3)),
                     r=["sq", "ones_bf"], w=[psk])
            rstd_op(g, ps, psk, n, 512, rstd, "rstd")
            for kc in range(4):
                P.op("vector", lambda e, kc=kc, n=n, t0=t0, dst=dst, gc=gc: e.scalar_tensor_tensor(
                    out=dst[:, kc, t0:t0 + n], in0=xt[:, kc, :n], scalar=gT[:, gc + kc:gc + kc + 1], in1=rstd[:, :n],
                    op0=ALU.mult, op1=ALU.mult), r=["xt0", "rstd", "gT"], w=[dk])
    for (t0, n) in TILES:
        P.dma("sync", xt[:64, 0, :n], PT[1024:1088, t0:t0 + n], "xt0", r=["PT%d" % l], w=["xt0"])
        P.dma("sync", xt[:64, 1, :n], PT[N_IN:N_IN + 64, t0:t0 + n], "xt0", r=["PT%d" % l], w=["xt0"])
        P.op("scalar", lambda e, n=n, t0=t0: e.activation(out=sqr[:, t0:t0 + n], in_=xt[:64, 0, :n], func=AF.Square),
             r=["xt0"], w=["sqr"])
        P.op("vector", lambda e, n=n, t0=t0: e.scalar_tensor_tensor(
            out=KR[:, t0:t0 + n], in0=xt[:64, 0, :n], scalar=hg[:64, 4:5], in1=cos2[:, t0:t0 + n],
            op0=ALU.mult, op1=ALU.mult), r=["xt0", "hg", "cos2"], w=["KR"])
        P.op("vector", lambda e, n=n, t0=t0: e.scalar_tensor_tensor(
            out=tA[:, :n], in0=xt[:64, 1, :n], scalar=hg[:64, 5:6], in1=sin2[:, t0:t0 + n],
            op0=ALU.mult, op1=ALU.mult), r=["xt0", "hg", "sin2"], w=["tA"])
        P.op("vector", lambda e, n=n, t0=t0: e.tensor_tensor(out=KR[:, t0:t0 + n], in0=KR[:, t0:t0 + n], in1=tA[:, :n],
                                                           op=ALU.add), r=["tA", "KR"], w=["KR"])
    wuq = g.w_uq[l].rearrange("(kc p) n -> p kc n", p=128)
    wuqr = g.w_uq_rot[l].rearrange("(kc p) n -> p kc n", p=128)
    wukv = g.w_ukv[l].rearrange("(kc p) n -> p kc n", p=128)
    qtiles = TILES if l == 0 else TILES[1:]
    si = [0]

    def nstb():
        si[0] = (si[0] + 1) % 4
        return stb[si[0]], "stb%d" % si[0]

    for h in range(8):
        P.dma("gpsimd", wq[:, :, 0:192], wuq[:, :, h * 192:(h + 1) * 192], "wq", w=["wq"])
        P.dma("gpsimd", wq[:, :, 192:256], wuqr[:, :, h * 64:(h + 1) * 64], "wq", w=["wq"])
        for (t0, n) in qtiles:
            psn, kn_ = g.next_ps()
            psr, kr_ = g.next_ps()
            pst, kt_ = g.next_ps()
            for (ps, pk, c0, m) in [(psn, kn_, 0, 128), (psr, kr_, 128, 64), (pst, kt_, 192, 64)]:
                for kc in range(4):
                    P.op("tensor", lambda e, ps=ps, c0=c0, m=m, kc=kc, t0=t0, n=n: e.matmul(
                        ps[:m, :n], wq[:, kc, c0:c0 + m], cqn[:, kc, t0:t0 + n], start=(kc == 0), stop=(kc == 3)),
                        r=["wq", "cqn"], w=[pk])
            P.op("scalar", lambda e, n=n, psn=psn: e.activation(out=sq[:, 0, :n], in_=psn[:, :n], func=AF.Square),
                 r=[kn_], w=["sq"])
            P.op("scalar", lambda e, n=n, psr=psr: e.activation(out=sq[:64, 1, :n], in_=psr[:64, :n], func=AF.Square),
                 r=[kr_], w=["sq"])
            pss, ks_ = g.next_ps()
            P.op("tensor", lambda e, n=n, pss=pss: e.matmul(pss[:, :n], g.ones_bf[:], sq[:, 0, :n], start=True, stop=False),
                 r=["sq", "ones_bf"], w=[ks_])
            P.op("tensor", lambda e, n=n, pss=pss: e.matmul(pss[:, :n], g.ones_bf[:64, :], sq[:64, 1, :n], start=False, stop=True),
                 r=["sq", "ones_bf"], w=[ks_])
            rstd_op(g, pss, ks_, n, 192, rstd, "rstd")
            o1, o1k = nstb()
            P.op("vector", lambda e, n=n, psn=psn, o1=o1: e.scalar_tensor_tensor(
                out=o1[:, :n], in0=psn[:, :n], scalar=hg[:, 0:1], in1=rstd[:, :n], op0=ALU.mult, op1=ALU.mult),
                r=[kn_, "hg", "rstd"], w=[o1k])
            P.dma("sync", g.QT[h, 0:128, t0:t0 + n], o1[:, :n], o1k, r=[o1k], w=["QT"])
            P.op("vector", lambda e, n=n, psr=psr, t0=t0: e.scalar_tensor_tensor(
                out=tA[:, :n], in0=psr[:64, :n], scalar=hg[:64, 2:3], in1=cos2[:, t0:t0 + n], op0=ALU.mult, op1=ALU.mult),
                r=[kr_, "hg", "cos2"], w=["tA"])
            P.op("vector", lambda e, n=n, pst=pst, t0=t0: e.scalar_tensor_tensor(
                out=tB[:, :n], in0=pst[:64, :n], scalar=hg[:64, 3:4], in1=sin2[:, t0:t0 + n], op0=ALU.mult, op1=ALU.mult),
                r=[kt_, "hg", "sin2"], w=["tB"])
            P.op("vector", lambda e, n=n: e.tensor_tensor(out=tA[:, :n], in0=tA[:, :n], in1=tB[:, :n], op=ALU.add),
                 r=["tA", "tB"], w=["tA"])
            o2, o2k = nstb()
            P.op("vector", lambda e, n=n, o2=o2: e.tensor_tensor(out=o2[:64, :n], in0=tA[:, :n], in1=rstd[:64, :n],
                                                              op=ALU.mult), r=["tA", "rstd"], w=[o2k])
            P.dma("sync", g.QT[h, 128:192, t0:t0 + n], o2[:64, :n], o2k, r=[o2k], w=["QT"])
    for h in range(8):
        P.dma("gpsimd", wk[:, :, :], wukv[:, :, h * 256:h * 256 + 128], "wk", w=["wk"])
        for (t0, n) in TILES:
            psn, kn_ = g.next_ps()
            for kc in range(4):
                P.op("tensor", lambda e, psn=psn, kc=kc, t0=t0, n=n: e.matmul(
                    psn[:, :n], wk[:, kc, :], ckvn[:, kc, t0:t0 + n], start=(kc == 0), stop=(kc == 3)),
                    r=["wk", "ckvn"], w=[kn_])
            P.op("scalar", lambda e, n=n, psn=psn: e.activation(out=sq[:, 0, :n], in_=psn[:, :n], func=AF.Square),
                 r=[kn_], w=["sq"])
            pss, ks_ = g.next_ps()
            P.op("tensor", lambda e, n=n, pss=pss: e.matmul(pss[:, :n], g.ones_bf[:], sq[:, 0, :n], start=True, stop=False),
                 r=["sq", "ones_bf"], w=[ks_])
            P.op("tensor", lambda e, n=n, pss=pss, t0=t0: e.matmul(pss[:, :n], g.ones_bf[:64, :], sqr[:, t0:t0 + n],
                                                                 start=False, stop=True),
                 r=["sqr", "ones_bf"], w=[ks_])
            rstd_op(g, pss, ks_, n, 192, rstd, "rstd")
            o1, o1k = nstb()
            P.op("vector", lambda e, n=n, psn=psn, o1=o1: e.scalar_tensor_tensor(
                out=o1[:, :n], in0=psn[:, :n], scalar=hg[:, 1:2], in1=rstd[:, :n], op0=ALU.mult, op1=ALU.mult),
                r=[kn_, "hg", "rstd"], w=[o1k])
            P.dma("sync", g.KT[h, 0:128, t0:t0 + n], o1[:, :n], o1k, r=[o1k], w=["KT"])
            o2, o2k = nstb()
            P.op("vector", lambda e, n=n, o2=o2, t0=t0: e.tensor_tensor(out=o2[:64, :n], in0=KR[:, t0:t0 + n],
                                                                     in1=rstd[:64, :n], op=ALU.mult),
                 r=["KR", "rstd"], w=[o2k])
            P.dma("sync", g.KT[h, 128:192, t0:t0 + n], o2[:64, :n], o2k, r=[o2k], w=["KT"])
    for kc in range(4):
        P.dma("gpsimd", wvv[:, kc, :].rearrange("p (h x) -> p h x", x=128),
              g.w_ukv[l][kc * 128:(kc + 1) * 128, :].rearrange("p (h x) -> p h x", x=256)[:, :, 128:256], "wvv", w=["wvv"])
    for c in range(34):
        for half in range(2):
            ps, psk = g.next_ps()
            for kc in range(4):
                P.op("tensor", lambda e, ps=ps, kc=kc, c=c, half=half: e.matmul(
                    ps[:, :], ckvn[:, kc, c * 128:(c + 1) * 128], wvv[:, kc, half * 512:(half + 1) * 512],
                    start=(kc == 0), stop=(kc == 3)), r=["wvv", "ckvn"], w=[psk])
            o1, o1k = nstb()
            if half == 0:
                P.op("vector", lambda e, ps=ps, o1=o1: e.tensor_copy(out=o1[:, :], in_=ps[:, :]), r=[psk], w=[o1k])
            else:
                P.op("scalar", lambda e, ps=ps, o1=o1: e.copy(out=o1[:, :], in_=ps[:, :]), r=[psk], w=[o1k])
            P.dma("sync", g.VTM[c * 128:(c + 1) * 128, half * 512:(half + 1) * 512], o1[:, :], o1k, r=[o1k], w=["VTM"])


def phase_attn(g, l):
    P = g.P
    g.new_phase()
    g.ps_rot = [0, 1, 2, 3]
    kn = g.carve([128, T], BF16)
    kr = g.carve([64, T], BF16)
    qn = g.carve([128, T], BF16)
    qr = g.carve([64, T], BF16)
    vh = g.carve([128, 34, 128], BF16)
    ebuf = [g.carve([128, 512], BF16) for i in range(4)]
    rden = g.carve([128, 512], F32)
    ost = [g.carve([128, 512], F32) for i in range(2)]
    qtiles = [(t0, n, 34) for (t0, n) in TILES[1:]]
    if l == 0:
        qtiles = [(0, 256, 2)] + qtiles
    ti = 0
    cv = [g.carve([128, 5632], BF16) for i in range(2)]
    w1v = g.ffn_w1[l].rearrange("(kc p) n -> p kc n", p=128)
    w2v = g.ffn_w2[l].rearrange("(kc p) n -> p kc n", p=128)
    units = [("w1", hc) for hc in range(44)] + [("w2", fc) for fc in range(16)]
    pend = [None]
    ui = [0]

    def conv_unit():
        i = ui[0]
        if i < len(units):
            kind, j = units[i]
            buf, bk = cv[i % 2], "cv%d" % (i % 2)
            if kind == "w1":
                v = buf[:, 0:4096].rearrange("p (kc n) -> p kc n", n=256)
                P.dma("gpsimd", v[:, :, 0:128], w1v[:, :, j * 128:(j + 1) * 128], bk, w=[bk])
                P.dma("gpsimd", v[:, :, 128:256], w1v[:, :, FFN_H + j * 128:FFN_H + (j + 1) * 128], bk, w=[bk])
                st_ = (g.W1B[j], buf[:, 0:4096], bk, "W1B")
            else:
                v = buf[:, 0:5632].rearrange("p (hc n) -> p hc n", n=128)
                P.dma("gpsimd", v, w2v[:, :, j * 128:(j + 1) * 128], bk, w=[bk])
                st_ = (g.W2B[j], buf[:, 0:5632], bk, "W2B")
        else:
            st_ = None
        if pend[0] is not None:
            dst, src_, bk_, wkey = pend[0]
            P.dma("gpsimd", dst, src_, bk_ + "s", r=[bk_], w=[wkey])
        pend[0] = st_
        ui[0] = i + 1

    for h in range(8):
        P.dma("sync", kn, g.KT[h, 0:128, :], "kn", r=["KT"], w=["kn"])
        P.dma("sync", kr, g.KT[h, 128:192, :], "kr", r=["KT"], w=["kr"])
        P.dma("sync", qn, g.QT[h, 0:128, :], "qn", r=["QT"], w=["qn"])
        P.dma("sync", qr, g.QT[h, 128:192, :], "qr", r=["QT"], w=["qr"])
        P.dma("sync", vh, g.VTM[:, h * 128:(h + 1) * 128].rearrange("(c p) x -> p c x", p=128), "vh", r=["VTM"], w=["vh"])
        for (t0, n, nch) in qtiles:
            po, pok = g.psb(4 + 2 * (ti % 2))
            pd, pdk = g.psb(5 + 2 * (ti % 2))
            ti += 1
            conv_unit()

            def s_mm(c, t0=t0, n=n):
                ps, psk = g.next_ps()
                P.op("tensor", lambda e: e.matmul(ps[:, :n], kn[:, c * 128:(c + 1) * 128], qn[:, t0:t0 + n],
                                                  start=True, stop=False), r=["kn", "qn"], w=[psk])
                P.op("tensor", lambda e: e.matmul(ps[:, :n], kr[:, c * 128:(c + 1) * 128], qr[:, t0:t0 + n],
                                                  start=False, stop=True), r=["kr", "qr"], w=[psk])
                return ps, psk

            pend_s = [s_mm(0)]
            if nch > 1:
                pend_s.append(s_mm(1))
            for c in range(nch):
                if c + 2 < nch:
                    pend_s.append(s_mm(c + 2))
                ps, psk = pend_s.pop(0)
                eb = ebuf[c % 4]
                ek = "ebuf%d" % (c % 4)
                P.op("scalar", lambda e, ps=ps, eb=eb, n=n: e.activation(out=eb[:, :n], in_=ps[:, :n], func=AF.Exp),
                     r=[psk], w=[ek])
                P.op("tensor", lambda e, eb=eb, c=c, n=n, po=po, nch=nch: e.matmul(
                    po[:, :n], vh[:, c, :], eb[:, :n], start=(c == 0), stop=(c == nch - 1)), r=["vh", ek], w=[pok])
                P.op("tensor", lambda e, eb=eb, c=c, n=n, pd=pd, nch=nch: e.matmul(
                    pd[:, :n], g.ones_bf[:], eb[:, :n], start=(c == 0), stop=(c == nch - 1)), r=["ones_bf", ek], w=[pdk])
            P.op("vector", lambda e, n=n, pd=pd: e.reciprocal(out=rden[:, :n], in_=pd[:, :n]), r=[pdk], w=["rden"])
            os_ = ost[ti % 2]
            ok = "ost%d" % (ti % 2)
            P.op("vector", lambda e, n=n, po=po, os_=os_: e.tensor_tensor(out=os_[:, :n], in0=po[:, :n], in1=rden[:, :n],
                                                                       op=ALU.mult), r=[pok, "rden"], w=[ok])
            P.dma("sync", g.CATT[h * 128:(h + 1) * 128, t0:t0 + n], os_[:, :n], ok, r=[ok], w=["CATT"])
    while ui[0] <= len(units):
        conv_unit()


def phase_mlstm(g, l):
    P = g.P
    g.new_phase()
    g.ps_rot = list(range(8))
    PT, PTM = g.PT[l], g.PTM[l]
    NCH = 34
    order = [list(range(NCH)), [1, 0] + list(range(NCH - 1, 1, -1))]
    tri = g.carve([128, 2, 128], F32)
    ident = g.carve([128, 128], F32)
    identb = g.carve([128, 128], BF16)
    sel4 = g.carve([4, 4, 128], F32)
    ones4 = g.carve([4, 128], F32)
    one_t = g.carve([128, 1], F32)
    onesf = g.carve([128, 128], F32)
    P.op("vector", lambda e: e.memset(onesf, 1.0), w=["onesf"])
    pers = [[g.carve([128, NCH, 4], F32) for i in range(6)] for dr in range(2)]
    mark1 = g.arena_off
    P.dma("sync", tri, g.tri.rearrange("d s t -> s d t"), "pc", w=["tri"])
    P.dma("sync", ident, g.ident[:, :], "pc", w=["ident"])
    P.dma("gpsimd", identb, g.ident[:, :], "pc", w=["identb"])
    P.dma("sync", sel4, g.sel4[:, :, :], "pc", w=["sel4"])
    P.seal()
    P.op("vector", lambda e: e.memset(ones4, 1.0), w=["ones4"])
    P.op("vector", lambda e: e.memset(one_t, 1.0), w=["one_t"])
    Gt = g.carve([128, NCH, 16], F32)
    gb = g.carve([128, 16], F32)
    P.dma("sync", Gt, PTM[:, 1024:1040].rearrange("(c p) x -> p c x", p=128), "pc", r=["PTM%d" % l], w=["Gt"])
    P.dma("sync", gb, g.ml_gate_b[l:l + 1, :].to_broadcast([128, 16]), "pc", w=["gb"])
    P.seal()
    P.op("vector", lambda e: e.tensor_tensor(out=Gt, in0=Gt, in1=gb.unsqueeze(1).to_broadcast([128, NCH, 16]), op=ALU.add),
         r=["Gt", "gb"], w=["Gt"])
    t1 = g.carve([128, NCH, 8], F32)
    t2 = g.carve([128, NCH, 8], F32)
    lf = g.carve([128, NCH, 8], F32)
    xf = Gt[:, :, 8:16]
    P.op("vector", lambda e: e.scalar_tensor_tensor(out=t1, in0=xf, scalar=-1.0, in1=xf, op0=ALU.mult, op1=ALU.max),
         r=["Gt"], w=["t1"])
    P.op("scalar", lambda e: e.activation(out=t1, in_=t1, func=AF.Exp, scale=-1.0), r=["t1"], w=["t1"])
    P.op("scalar", lambda e: e.activation(out=t1, in_=t1, func=AF.Ln, bias=one_t[:, 0:1], scale=1.0), r=["t1", "one_t"], w=["t1"])
    P.op("vector", lambda e: e.tensor_scalar(out=t2, in0=xf, scalar1=-1.0, scalar2=0.0, op0=ALU.mult, op1=ALU.max),
         r=["Gt"], w=["t2"])
    P.op("vector", lambda e: e.scalar_tensor_tensor(out=lf, in0=t2, scalar=-1.0, in1=t1, op0=ALU.mult, op1=ALU.subtract),
         r=["t1", "t2"], w=["lf"])
    lfd, lid, bd, ud, wtok, iwtok, emt, abc, Mtok = [], [], [], [], [], [], [], [], []
    negMR = []
    for dr in range(2):
        k = "d%d" % dr
        lf_ = g.carve([128, NCH, 4], F32)
        li_ = g.carve([128, NCH, 4], F32)
        b_ = g.carve([128, NCH, 4], F32)
        u_ = pers[dr][0]
        P.op("vector", lambda e, lf_=lf_, dr=dr: e.tensor_copy(out=lf_, in_=lf[:, :, dr * 4:(dr + 1) * 4]), r=["lf"], w=["lf" + k])
        P.op("vector", lambda e, li_=li_, dr=dr: e.tensor_copy(out=li_, in_=Gt[:, :, dr * 4:(dr + 1) * 4]), r=["Gt"], w=["li" + k])
        ps, psk = g.next_ps()
        P.op("tensor", lambda e, ps=ps, dr=dr, lf_=lf_: e.matmul(ps[:, :NCH * 4], tri[:, dr, :], lf_.rearrange("p c h -> p (c h)"),
                                                              start=True, stop=True), r=["tri", "lf" + k], w=[psk])
        P.op("vector", lambda e, ps=ps, b_=b_: e.tensor_copy(out=b_.rearrange("p c h -> p (c h)"), in_=ps[:, :NCH * 4]),
             r=[psk], w=["b" + k])
        P.op("vector", lambda e, u_=u_, li_=li_, b_=b_: e.tensor_tensor(out=u_, in0=li_, in1=b_, op=ALU.subtract),
             r=["li" + k, "b" + k], w=["u" + k])
        uR = g.carve([4, NCH, 128], F32)
        lfR = g.carve([4, NCH, 128], F32)
        for (src, sk_, dstR, dk_) in [(u_, "u" + k, uR, "uR" + k), (lf_, "lf" + k, lfR, "lfR" + k)]:
            for c4 in range(0, NCH, 4):
                ps, psk = g.next_ps()
                nn = min(4, NCH - c4)
                for j in range(nn):
                    P.op("tensor", lambda e, ps=ps, src=src, c4=c4, j=j: e.transpose(ps[:4, j * 128:(j + 1) * 128],
                                                                                  src[:, c4 + j, :], ident),
                         r=[sk_, "ident"], w=[psk])
                P.op("vector", lambda e, ps=ps, dstR=dstR, c4=c4, nn=nn: e.tensor_copy(
                    out=dstR[:, c4:c4 + nn, :].rearrange("p c t -> p (c t)"), in_=ps[:4, :nn * 128]), r=[psk], w=[dk_])
        blast = g.carve([4, NCH], F32)
        cml = g.carve([4, NCH], F32)
        P.op("vector", lambda e, blast=blast, lfR=lfR: e.reduce_sum(out=blast, in_=lfR, axis=AX.X), r=["lfR" + k], w=["blast" + k])
        P.op("vector", lambda e, cml=cml, uR=uR: e.reduce_max(out=cml, in_=uR, axis=AX.X), r=["uR" + k], w=["cml" + k])
        cmB = g.carve([4, NCH, 128], F32)
        bufs = [(uR, "uR" + k), (cmB, "cmB" + k)]
        cur = 0
        for sh in (1, 2, 4, 8, 16, 32, 64):
            (sa, sak), (da, dak) = bufs[cur], bufs[1 - cur]
            if dr == 0:
                P.op("vector", lambda e, sa=sa, da=da, sh=sh: e.tensor_tensor(out=da[:, :, sh:], in0=sa[:, :, sh:], in1=sa[:, :, :128 - sh],
                                                                            op=ALU.max), r=[sak], w=[dak])
                P.op("gpsimd", lambda e, sa=sa, da=da, sh=sh: e.tensor_copy(out=da[:, :, :sh], in_=sa[:, :, :sh]), r=[sak], w=[dak])
            else:
                P.op("vector", lambda e, sa=sa, da=da, sh=sh: e.tensor_tensor(out=da[:, :, :128 - sh], in0=sa[:, :, :128 - sh], in1=sa[:, :, sh:],
                                                                            op=ALU.max), r=[sak], w=[dak])
                P.op("gpsimd", lambda e, sa=sa, da=da, sh=sh: e.tensor_copy(out=da[:, :, 128 - sh:], in_=sa[:, :, 128 - sh:]), r=[sak], w=[dak])
            cur = 1 - cur
        cm, cmk = bufs[cur]
        mR = g.carve([4, NCH], F32)
        P.op("vector", lambda e, mR=mR: e.memset(mR, 0.0), w=["mR" + k])
        od = order[dr]
        for i in range(NCH - 1):
            c, c2 = od[i], od[i + 1]
            P.op("vector", lambda e, mR=mR, c=c, c2=c2, cml=cml, blast=blast: e.scalar_tensor_tensor(
                out=mR[:, c2:c2 + 1], in0=mR[:, c:c + 1], scalar=cml[:, c:c + 1], in1=blast[:, c:c + 1],
                op0=ALU.max, op1=ALU.add), r=["mR" + k, "cml" + k, "blast" + k], w=["mR" + k])
        MR = bufs[1 - cur][0]
        MRk = bufs[1 - cur][1]
        P.op("vector", lambda e, MR=MR, cm=cm, mR=mR: e.tensor_tensor(out=MR, in0=cm, in1=mR.unsqueeze(2).to_broadcast([4, NCH, 128]),
                                                                    op=ALU.max), r=[cmk, "mR" + k], w=[MRk])
        MlR = g.carve([4, NCH], F32)
        aR = g.carve([4, NCH], F32)
        P.op("vector", lambda e, MlR=MlR, mR=mR, cml=cml: e.tensor_tensor(out=MlR, in0=mR, in1=cml, op=ALU.max),
             r=["mR" + k, "cml" + k], w=["MlR" + k])
        P.op("vector", lambda e, aR=aR, mR=mR, MlR=MlR: e.tensor_tensor(out=aR, in0=mR, in1=MlR, op=ALU.subtract),
             r=["mR" + k, "MlR" + k], w=["aR" + k])
        P.op("scalar", lambda e, aR=aR: e.activation(out=aR, in_=aR, func=AF.Exp), r=["aR" + k], w=["aR" + k])
        Mt = pers[dr][1]
        for c4 in range(0, NCH, 32):
            ps, psk = g.next_ps()
            nn = min(32, NCH - c4)
            for j in range(nn):
                P.op("tensor", lambda e, ps=ps, MR=MR, c4=c4, j=j: e.transpose(ps[:, j * 4:(j + 1) * 4], MR[:, c4 + j, :], ident[:4, :4]),
                     r=[MRk, "ident"], w=[psk])
            P.op("vector", lambda e, ps=ps, Mt=Mt, c4=c4, nn=nn: e.tensor_copy(
                out=Mt[:, c4:c4 + nn, :].rearrange("p c h -> p (c h)"), in_=ps[:, :nn * 4]), r=[psk], w=["Mt" + k])
        bc = []
        dg = g.carve([4, NCH, 4], F32)
        for (src, sk_) in [(MlR, "MlR" + k), (aR, "aR" + k), (mR, "mR" + k)]:
            P.op("vector", lambda e, src=src: e.tensor_tensor(out=dg, in0=src.unsqueeze(2).to_broadcast([4, NCH, 4]),
                                                             in1=ident[:4, :4].unsqueeze(1).to_broadcast([4, NCH, 4]), op=ALU.mult),
                 r=[sk_, "ident"], w=["dg" + k])
            ps, psk = g.next_ps()
            P.op("tensor", lambda e, ps=ps: e.matmul(ps[:, :NCH * 4], ones4, dg.rearrange("p c h -> p (c h)"), start=True, stop=True),
                 r=["dg" + k, "ones4"], w=[psk])
            o_ = pers[dr][5] if src is aR else g.carve([128, NCH, 4], F32)
            P.op("vector", lambda e, ps=ps, o_=o_: e.tensor_copy(out=o_.rearrange("p c h -> p (c h)"), in_=ps[:, :NCH * 4]),
                 r=[psk], w=["bc" + k + sk_])
            bc.append((o_, "bc" + k + sk_))
        (Mlb, Mlbk), (ab, abk), (mb, mbk) = bc
        w_, iw_, em_ = pers[dr][2], pers[dr][3], pers[dr][4]
        P.op("vector", lambda e, w_=w_, u_=u_, Mlb=Mlb: e.tensor_tensor(out=w_, in0=u_, in1=Mlb, op=ALU.subtract), r=["u" + k, Mlbk], w=["w" + k])
        P.op("scalar", lambda e, w_=w_: e.activation(out=w_, in_=w_, func=AF.Exp), r=["w" + k], w=["w" + k])
        P.op("vector", lambda e, iw_=iw_, mb=mb, Mt=Mt: e.tensor_tensor(out=iw_, in0=mb, in1=Mt, op=ALU.subtract), r=[mbk, "Mt" + k], w=["iw" + k])
        P.op("scalar", lambda e, iw_=iw_: e.activation(out=iw_, in_=iw_, func=AF.Exp), r=["iw" + k], w=["iw" + k])
        P.op("vector", lambda e, em_=em_, b_=b_, Mt=Mt: e.tensor_tensor(out=em_, in0=b_, in1=Mt, op=ALU.add), r=["b" + k, "Mt" + k], w=["em" + k])
        P.op("scalar", lambda e, em_=em_: e.activation(out=em_, in_=em_, func=AF.Exp, scale=-1.0), r=["em" + k], w=["em" + k])
        P.op("vector", lambda e, Mt=Mt: e.tensor_scalar(out=Mt, in0=Mt, scalar1=-1.0, scalar2=None, op0=ALU.mult),
             r=["Mt" + k, "iw" + k, "em" + k], w=["Mt" + k])
        ud.append(u_); wtok.append(w_); iwtok.append(iw_); emt.append(em_); abc.append(ab); Mtok.append(Mt)
    g.arena_off = mark1
    P.barrier()
    qT = g.carve([128, 4, T], BF16)
    kT = g.carve([128, 4, T], BF16)
    ktok = g.carve([128, NCH, 512], BF16)
    v1 = g.carve([128, NCH * 4, 132], BF16)
    mark2 = g.arena_off
    cw = g.carve([128, 8, 4], F32)
    P.dma("sync", cw, g.ml_cw[l], "pc", w=["cw"])
    P.seal()
    W = T + 3
    xb = g.carve([128, W], F32)
    acc = g.carve([128, W], F32)
    P.op("vector", lambda e: e.memset(xb, 0.0), w=["xb"])
    for rc in range(8):
        r0 = 2624 + rc * 128
        P.dma("sync", xb[:, 1:1 + NCTX], PT[r0:r0 + 128, 0:NCTX], "xb", r=["PT%d" % l], w=["xb"])
        P.dma("sync", xb[:, 2 + NCTX:2 + T], PT[r0:r0 + 128, NCTX:T], "xb", r=["PT%d" % l], w=["xb"])
        n = T + 1
        P.op("vector", lambda e, rc=rc: e.tensor_scalar(out=acc[:, 1:1 + n], in0=xb[:, 0:n], scalar1=cw[:, rc, 0:1],
                                                       scalar2=cw[:, rc, 3:4], op0=ALU.mult, op1=ALU.add),
             r=["xb", "cw"], w=["acc"])
        for j in (1, 2):
            P.op("vector", lambda e, rc=rc, j=j: e.scalar_tensor_tensor(
                out=acc[:, 1:1 + n], in0=xb[:, j:j + n], scalar=cw[:, rc, j:j + 1], in1=acc[:, 1:1 + n],
                op0=ALU.mult, op1=ALU.add), r=["xb", "cw", "acc"], w=["acc"])
        dst = qT if rc < 4 else kT
        dk = "qT" if rc < 4 else "kT"
        for (a0, d0, n2) in [(1, 0, NCTX), (2 + NCTX, NCTX, NLAT)]:
            P.op("scalar", lambda e, a0=a0, n2=n2: e.activation(out=acc[:, a0:a0 + n2], in_=acc[:, a0:a0 + n2], func=AF.Silu),
                 r=["acc"], w=["acc"])
            if rc < 4:
                P.op("vector", lambda e, a0=a0, d0=d0, n2=n2, dst=dst, rc=rc: e.tensor_scalar(
                    out=dst[:, rc % 4, d0:d0 + n2], in0=acc[:, a0:a0 + n2], scalar1=128.0 ** -0.5, scalar2=None, op0=ALU.mult),
                    r=["acc"], w=[dk])
            else:
                P.op("vector", lambda e, a0=a0, d0=d0, n2=n2, dst=dst, rc=rc: e.tensor_copy(
                    out=dst[:, rc % 4, d0:d0 + n2], in_=acc[:, a0:a0 + n2]), r=["acc"], w=[dk])
    psb16 = [g.ps[i][:].bitcast(BF16) for i in range(8)]
    for c in range(NCH):
        g.ps_i = (g.ps_i + 1) % 8
        pi = g.ps_i
        pk = "ps%d" % pi
        for hd in range(4):
            P.op("tensor", lambda e, c=c, hd=hd, pi=pi: e.transpose(psb16[pi][:, hd * 128:(hd + 1) * 128],
                                                                   kT[:, hd, c * 128:(c + 1) * 128], identb),
                 r=["kT", "identb"], w=[pk])
        P.op("vector", lambda e, c=c, pi=pi: e.tensor_copy(out=ktok[:, c, :], in_=psb16[pi][:, 0:512]), r=[pk], w=["ktok"])
    v1v = v1.rearrange("p (c h) x -> p c h x", h=4)
    P.op("vector", lambda e: e.memset(v1[:, :, 128:129], 1.0), w=["v1"])
    for hd in range(4):
        P.dma("gpsimd", v1v[:, :, hd, 0:128], PTM[:, hd * 128:(hd + 1) * 128].rearrange("(c p) x -> p c x", p=128), "v1",
              r=["PTM%d" % l], w=["v1"])
    g.arena_off = mark2
    P.barrier()
    chains = [(dr, hd) for dr in range(2) for hd in range(4)]
    st8 = {}
    for ci, (dr, hd) in enumerate(chains):
        ck = "c%d%d" % (dr, hd)
        st8[ck] = dict(Cf=g.carve([128, 132], F32), Cb=g.carve([128, 132], BF16), dx=g.carve([128, 128], F32),
                       dxm=g.carve([128, 128], F32), sw=g.carve([128, 128], BF16), Asb=g.carve([128, 132], F32),
                       nt=g.carve([128, 132], F32), kw=g.carve([128, 128], BF16), ho=g.carve([128, 128], F32),
                       dd=g.carve([128, 2], F32), dgm=g.carve([128, 128], F32), bank=g.ps[ci], ci=ci)
        b8 = st8[ck]
        P.op("vector", lambda e, b8=b8: e.memset(b8["Cf"], 0.0), w=[ck + "Cf"])
        P.op("vector", lambda e, b8=b8: e.memset(b8["Cb"], 0.0), w=[ck + "Cb"])
    for step in range(NCH):
        info = []
        for (dr, hd) in chains:
            ck = "c%d%d" % (dr, hd)
            b8 = st8[ck]
            c = order[dr][step]
            info.append((dr, hd, ck, "d%d" % dr, b8, c, slice(c * 128, (c + 1) * 128),
                         "mp%d" % b8["ci"], "mp%d" % b8["ci"], "mp%d" % b8["ci"]))
        for (dr, hd, ck, k, b8, c, cs, k0, k1, k2) in info:
            bk = b8["bank"]
            P.op("tensor", lambda e, bk=bk, hd=hd, cs=cs: e.matmul(bk[:, 0:128], kT[:, hd, cs], qT[:, hd, cs], start=True, stop=True),
                 r=["kT", "qT"], w=[k0])
            P.op("gpsimd", lambda e, b8=b8, dr=dr, c=c, hd=hd: e.tensor_scalar(
                out=b8["dgm"], in0=ident, scalar1=Mtok[dr][:, c, hd:hd + 1], scalar2=None, op0=ALU.mult),
                r=["ident", "Mt" + k], w=[ck + "dgm"])
            P.op("tensor", lambda e, bk=bk, b8=b8: e.matmul(bk[:, 129:257], onesf, b8["dgm"], start=True, stop=True),
                 r=["onesf", ck + "dgm"], w=[k1])
        for (dr, hd, ck, k, b8, c, cs, k0, k1, k2) in info:
            bk = b8["bank"]
            P.op("scalar", lambda e, bk=bk, b8=b8, dr=dr, c=c, hd=hd: e.activation(
                out=b8["dx"], in_=bk[:, 129:257], func=AF.Exp, bias=ud[dr][:, c, hd:hd + 1], scale=1.0), r=[k1, "u" + k], w=[ck + "dx"])
            P.op("gpsimd", lambda e, b8=b8, dr=dr: e.tensor_tensor(out=b8["dxm"], in0=b8["dx"], in1=tri[:, dr, :], op=ALU.mult),
                 r=[ck + "dx", "tri"], w=[ck + "dxm"])
            P.op("vector", lambda e, bk=bk, b8=b8: e.tensor_tensor(out=b8["sw"], in0=bk[:, 0:128], in1=b8["dxm"], op=ALU.mult),
                 r=[k0, ck + "dxm"], w=[ck + "sw"])
        for (dr, hd, ck, k, b8, c, cs, k0, k1, k2) in info:
            bk = b8["bank"]
            P.op("tensor", lambda e, bk=bk, b8=b8, c=c, hd=hd: e.matmul(bk[:, 258:387], b8["sw"], v1[:, c * 4 + hd, 0:129], start=True, stop=True),
                 r=[ck + "sw", "v1"], w=[k2])
            P.op("tensor", lambda e, bk=bk, b8=b8, hd=hd, cs=cs: e.matmul(bk[:, 129:258], qT[:, hd, cs], b8["Cb"][:, 0:129], start=True, stop=True),
                 r=["qT", ck + "Cb"], w=[k1])
        for (dr, hd, ck, k, b8, c, cs, k0, k1, k2) in info:
            bk = b8["bank"]
            P.op("scalar", lambda e, bk=bk, b8=b8: e.copy(out=b8["Asb"][:, 0:129], in_=bk[:, 258:387]), r=[k2], w=[ck + "Asb"])
            P.op("vector", lambda e, bk=bk, b8=b8, dr=dr, c=c, hd=hd: e.scalar_tensor_tensor(
                out=b8["nt"][:, 0:129], in0=bk[:, 129:258], scalar=iwtok[dr][:, c, hd:hd + 1], in1=b8["Asb"][:, 0:129],
                op0=ALU.mult, op1=ALU.add), r=[k1, ck + "Asb", "iw" + k], w=[ck + "nt"])
            P.op("vector", lambda e, b8=b8: e.scalar_tensor_tensor(
                out=b8["dd"][:, 0:1], in0=b8["nt"][:, 128:129], scalar=-1.0, in1=b8["nt"][:, 128:129], op0=ALU.mult, op1=ALU.max),
                r=[ck + "nt"], w=[ck + "dd"])
            P.op("vector", lambda e, b8=b8, dr=dr, c=c, hd=hd: e.tensor_tensor(
                out=b8["dd"][:, 0:1], in0=b8["dd"][:, 0:1], in1=emt[dr][:, c, hd:hd + 1], op=ALU.max),
                r=[ck + "dd", "em" + k], w=[ck + "dd"])
            P.op("vector", lambda e, b8=b8: e.reciprocal(out=b8["dd"][:, 1:2], in_=b8["dd"][:, 0:1]), r=[ck + "dd"], w=[ck + "dd"])
            P.op("scalar", lambda e, b8=b8: e.mul(out=b8["ho"], in_=b8["nt"][:, 0:128], mul=b8["dd"][:, 1:2]),
                 r=[ck + "nt", ck + "dd"], w=[ck + "ho"])
            P.dma("sync", g.HD[dr][c * 128:(c + 1) * 128, hd * 128:(hd + 1) * 128], b8["ho"], ck + "ho", r=[ck + "ho"], w=["HD%d" % dr])
        for (dr, hd, ck, k, b8, c, cs, k0, k1, k2) in info:
            bk = b8["bank"]
            P.op("vector", lambda e, b8=b8, dr=dr, c=c, hd=hd: e.tensor_scalar(
                out=b8["kw"], in0=ktok[:, c, hd * 128:(hd + 1) * 128], scalar1=wtok[dr][:, c, hd:hd + 1], scalar2=None, op0=ALU.mult),
                r=["ktok", "w" + k], w=[ck + "kw"])
            P.op("tensor", lambda e, bk=bk, b8=b8, c=c, hd=hd: e.matmul(bk[:, 0:129], b8["kw"], v1[:, c * 4 + hd, 0:129], start=True, stop=True),
                 r=[ck + "kw", "v1"], w=[k0])
            P.op("vector", lambda e, bk=bk, b8=b8, dr=dr, c=c, hd=hd: e.scalar_tensor_tensor(
                out=b8["Cf"][:, 0:129], in0=b8["Cf"][:, 0:129], scalar=abc[dr][:, c, hd:hd + 1], in1=bk[:, 0:129],
                op0=ALU.mult, op1=ALU.add), r=[ck + "Cf", k0, "bcd%daRd%d" % (dr, dr)], w=[ck + "Cf"])
            P.op("scalar", lambda e, b8=b8: e.copy(out=b8["Cb"][:, 0:129], in_=b8["Cf"][:, 0:129]), r=[ck + "Cf"], w=[ck + "Cb"])
    g.arena_off = mark2
    P.barrier()
    gbc = g.carve([128, 512], F32)
    P.dma("sync", gbc, g.mix_g[l:l + 1, 1536:2048].to_broadcast([128, 512]), "pc", w=["gbc"])
    P.seal()
    hf2 = [g.carve([128, 512], F32) for i in range(2)]
    hb2 = [g.carve([128, 512], F32) for i in range(2)]
    og2 = [g.carve([128, 512], F32) for i in range(2)]
    junk = g.carve([128, 128], F32)
    ss2 = [g.carve([128, 8], F32) for i in range(2)]
    outs = [g.carve([128, 512], F32) for i in range(2)]
    for c in range(NCH):
        q_ = c % 2
        hf, hb, og, ss = hf2[q_], hb2[q_], og2[q_], ss2[q_]
        hfk, hbk, ogk, ssk = "hf%d" % q_, "hb%d" % q_, "og%d" % q_, "ss%d" % q_
        P.dma("sync", hf, g.HD[0][c * 128:(c + 1) * 128, :], hfk, r=["HD0"], w=[hfk])
        P.dma("sync", hb, g.HD[1][c * 128:(c + 1) * 128, :], hbk, r=["HD1"], w=[hbk])
        P.dma("sync", og, PTM[c * 128:(c + 1) * 128, 512:1024], ogk, r=["PTM%d" % l], w=[ogk])
        P.op("vector", lambda e, hf=hf, hb=hb: e.tensor_tensor(out=hf, in0=hf, in1=hb, op=ALU.add), r=[hfk, hbk], w=[hfk])
        for hd in range(4):
            P.op("scalar", lambda e, hd=hd, hf=hf, ss=ss: e.activation(out=junk, in_=hf[:, hd * 128:(hd + 1) * 128], func=AF.Square,
                                                                      accum_out=ss[:, hd:hd + 1]), r=[hfk], w=["junk", ssk])
        P.op("scalar", lambda e, ss=ss: e.activation(out=ss[:, 4:8], in_=ss[:, 0:4], func=AF.Sqrt, bias=g.eps_t[:, 0:1], scale=1.0 / 128),
             r=[ssk, "eps_t"], w=[ssk])
        P.op("vector", lambda e, ss=ss: e.reciprocal(out=ss[:, 4:8], in_=ss[:, 4:8]), r=[ssk], w=[ssk])
        P.op("scalar", lambda e, og=og: e.activation(out=og, in_=og, func=AF.Sigmoid), r=[ogk], w=[ogk])
        P.op("gpsimd", lambda e, og=og: e.tensor_tensor(out=og, in0=og, in1=gbc, op=ALU.mult), r=[ogk, "gbc"], w=[ogk])
        for hd in range(4):
            P.op("vector", lambda e, hd=hd, hf=hf, ss=ss, og=og: e.scalar_tensor_tensor(
                out=hf[:, hd * 128:(hd + 1) * 128], in0=hf[:, hd * 128:(hd + 1) * 128], scalar=ss[:, 4 + hd:5 + hd],
                in1=og[:, hd * 128:(hd + 1) * 128], op0=ALU.mult, op1=ALU.mult), r=[hfk, ssk, ogk], w=[hfk])
        ps, psk = g.next_ps()
        for hd in range(4):
            P.op("tensor", lambda e, ps=ps, hd=hd, hf=hf: e.transpose(ps[:, hd * 128:(hd + 1) * 128], hf[:, hd * 128:(hd + 1) * 128], ident),
                 r=[hfk, "ident"], w=[psk])
        o_ = outs[c % 2]
        ok = "mlo%d" % (c % 2)
        P.op("scalar", lambda e, ps=ps, o_=o_: e.copy(out=o_, in_=ps[:, :]), r=[psk], w=[ok])
        P.dma("sync", g.CATT[1536:2048, c * 128:(c + 1) * 128].rearrange("(h p) t -> p h t", p=128),
              o_.rearrange("p (h t) -> p h t", t=128), ok, r=[ok], w=["CATT"])


def dft_fwd(g, L, rhsC, rkC, rhsS, rkS, evac, nyq):
    P = g.P
    nch = L // 128
    KG = min(512, L)
    nj = KG // 128
    mats = g.hy_cs[L]
    for kg in range(L // KG):
        banks = [[g.psb(j), g.psb(4 + j)] for j in range(nj)]
        for mi, (rhs, rk) in enumerate([(rhsC, rkC), (rhsS, rkS)]):
            for tc in range(nch):
                if tc % 2 == 0:
                    bi = g.mt_i
                    g.mt_i = (bi + 1) % len(g.mt)
                    mt2, mk = g.mt[bi], "mt%d" % bi
                    P.dma("sync", mt2[:, :, :KG], mats[mi, tc // 2, kg].rearrange("p (a k) -> p a k", a=2),
                          mk, w=[mk])
                mt = mt2[:, tc % 2, :]
                for j in range(nj):
                    ps, pk = banks[j][mi]
                    P.op("tensor", lambda e, ps=ps, mt=mt, j=j, rhs=rhs, tc=tc: e.matmul(
                        ps[:, :512], mt[:, j * 128:(j + 1) * 128], rhs[:, tc, :], start=(tc == 0), stop=(tc == nch - 1)),
                        r=[mk, rk], w=[pk])
        for j in range(nj):
            (pc, pck), (ps_, psk) = banks[j]
            evac(kg * nj + j, pc, pck, ps_, psk)
    g.ps_i = (g.ps_i + 1) % 8
    pn, pnk = g.psb(g.ps_i)
    for tc in range(nch):
        P.op("tensor", lambda e, tc=tc, pn=pn: e.matmul(pn[:1, :512], g.altc[:, 0:1], rhsC[:, tc, :], start=(tc == 0), stop=(tc == nch - 1)),
             r=["altc", rkC], w=[pnk])
    nyq(pn, pnk)


def hyena_seq(g, l, L, tok0):
    P = g.P
    nch = L // 128
    N2 = 2 * L
    PT = g.PT[l]
    g.new_phase()
    g.ps_rot = list(range(8))
    ident = g.carve([128, 128], F32)
    identb = g.carve([128, 128], BF16)
    P.dma("sync", ident, g.ident[:, :], "pc", w=["ident"])
    P.dma("gpsimd", identb, g.ident[:, :], "pc", w=["identb"])
    g.altc = g.carve([128, 1], BF16)
    altr = g.carve([1, 512], BF16)
    P.dma("gpsimd", g.altc, g.hy_alt[:, 0:1], "pc", w=["altc"], allow_slow_non_contiguous=True)
    P.dma("gpsimd", altr, g.hy_alt[0:1, :], "pc", w=["altr"])
    g.mt = [g.carve([128, 2, 512], BF16) for i in range(8)]
    g.mt_i = 0
    pic = g.carve([128, 2], F32)
    P.op("vector", lambda e: e.memset(pic[:, 0:1], 0.05), w=["pic"])
    mark0 = g.arena_off
    w1 = g.carve([33, 64], F32)
    w2 = g.carve([64, 64], F32)
    w3 = g.carve([64, 2048], F32)
    b12 = g.carve([64, 2], F32)
    dec = g.carve([128, 2048], F32)
    negt = g.carve([128, nch], F32)
    feats = g.carve([33, L], F32)
    h2T = g.carve([64, L], F32)
    P.dma("sync", w1, g.hy_w1[l], "pc", w=["hw1"])
    P.dma("sync", w2, g.hy_w2[l], "pc", w=["hw2"])
    P.dma("sync", w3, g.hy_w3[l], "pc", w=["hw3"])
    P.dma("sync", b12, g.hy_b12[l], "pc", w=["hb12"])
    P.dma("sync", dec, g.hy_decay[l:l + 1, :].to_broadcast([128, 2048]), "pc", w=["hdec"])
    P.dma("sync", negt, g.hy_negt[L][:, :], "pc", w=["negt"])
    P.dma("sync", feats, g.hy_feats[L][:, :], "pc", w=["feats"])
    P.seal()
    zt = g.carve([64, 512], F32)
    ki = g.carve([64, 512], mybir.dt.int32)
    kf = g.carve([64, 512], F32)
    h1 = g.carve([64, 512], F32)
    TW = min(512, L)

    def sin_layer(ps, pk, bcol, dst, dk):
        n = TW
        P.op("vector", lambda e: e.tensor_scalar(out=zt[:, :n], in0=ps[:64, :n], scalar1=b12[:, bcol:bcol + 1], scalar2=None, op0=ALU.add),
             r=[pk, "hb12"], w=["zt"])
        P.op("vector", lambda e: e.tensor_scalar(out=ki[:, :n], in0=zt[:, :n], scalar1=1.0 / (2 * math.pi), scalar2=None, op0=ALU.mult),
             r=["zt"], w=["ki"])
        P.op("vector", lambda e: e.tensor_copy(out=kf[:, :n], in_=ki[:, :n]), r=["ki"], w=["kf"])
        P.op("vector", lambda e: e.scalar_tensor_tensor(out=zt[:, :n], in0=kf[:, :n], scalar=-2 * math.pi, in1=zt[:, :n],
                                                        op0=ALU.mult, op1=ALU.add), r=["kf", "zt"], w=["zt"])
        P.op("vector", lambda e: e.tensor_scalar(out=zt[:, :n], in0=zt[:, :n], scalar1=-3.141592, scalar2=3.141592,
                                                 op0=ALU.max, op1=ALU.min), r=["zt"], w=["zt"])
        P.op("scalar", lambda e: e.activation(out=dst, in_=zt[:, :n], func=AF.Sin), r=["zt"], w=[dk])

    for t0 in range(0, L, TW):
        ps, pk = g.next_ps()
        P.op("tensor", lambda e, ps=ps, t0=t0: e.matmul(ps[:64, :TW], w1, feats[:, t0:t0 + TW], start=True, stop=True),
             r=["hw1", "feats"], w=[pk])
        sin_layer(ps, pk, 0, h1[:, :TW], "h1")
        ps, pk = g.next_ps()
        P.op("tensor", lambda e, ps=ps: e.matmul(ps[:64, :TW], w2, h1[:, :TW], start=True, stop=True), r=["hw2", "h1"], w=[pk])
        sin_layer(ps, pk, 1, h2T[:, t0:t0 + TW], "h2T")
    wt_ = [g.carve([128, 512], F32) for i in range(2)]
    hc = [g.carve([128, 512], F32) for i in range(4)]
    hsb = [g.carve([128, 512], BF16) for i in range(4)]
    q = 0
    for tc in range(nch):
        for cb in range(4):
            ps, pk = g.next_ps()
            P.op("tensor", lambda e, ps=ps, tc=tc, cb=cb: e.matmul(ps[:, :512], h2T[:, tc * 128:(tc + 1) * 128], w3[:, cb * 512:(cb + 1) * 512],
                                                                start=True, stop=True), r=["h2T", "hw3"], w=[pk])
            wt = wt_[cb % 2]
            wk_ = "hwt%d" % (cb % 2)
            P.op("scalar", lambda e, wt=wt, tc=tc, cb=cb: e.activation(out=wt, in_=dec[:, cb * 512:(cb + 1) * 512], func=AF.Exp,
                                                                     scale=negt[:, tc:tc + 1]), r=["hdec", "negt"], w=[wk_])
            P.op("vector", lambda e, wt=wt, ps=ps, cb=cb: e.scalar_tensor_tensor(out=hc[cb], in0=wt, scalar=pic[:, 0:1], in1=ps[:, :512],
                                                                             op0=ALU.add, op1=ALU.mult), r=[wk_, pk, "pic"], w=["hc%d" % cb])
            if tc == 0 and cb % 2 == 1:
                P.op("vector", lambda e, cb=cb: e.memset(hc[cb][0:1, :], 0.0), w=["hc%d" % cb])
        for o in range(2):
            for mi in range(2):
                hb_ = hsb[q % 4]
                hk = "hsb%d" % (q % 4)
                q += 1
                if mi == 0:
                    P.op("gpsimd", lambda e, hb_=hb_, o=o: e.tensor_tensor(out=hb_, in0=hc[2 * o], in1=hc[2 * o + 1], op=ALU.add),
                         r=["hc%d" % (2 * o), "hc%d" % (2 * o + 1)], w=[hk])
                else:
                    P.op("gpsimd", lambda e, hb_=hb_, o=o: e.tensor_tensor(out=hb_, in0=hc[2 * o + 1], in1=hc[2 * o], op=ALU.subtract),
                         r=["hc%d" % (2 * o), "hc%d" % (2 * o + 1)], w=[hk])
                P.dma("sync", g.HS[mi, tc * 128:(tc + 1) * 128, o * 512:(o + 1) * 512], hb_, hk, r=[hk], w=["HS"])
    g.arena_off = mark0
    P.barrier()
    R = [[g.carve([128, nch, 512], BF16) for o in range(2)] for mi in range(2)]
    stg = [g.carve([128, 512], F32) for i in range(4)]
    sti = [0]

    def nstg():
        sti[0] = (sti[0] + 1) % 4
        return stg[sti[0]], "hstg%d" % sti[0]

    for mi in range(2):
        for o in range(2):
            P.dma("sync", R[mi][o], g.HS[mi, 0:L, o * 512:(o + 1) * 512].rearrange("(c p) x -> p c x", p=128), "R%d%d" % (mi, o),
                  r=["HS"], w=["R%d%d" % (mi, o)])
    KG = min(512, L)
    nj = KG // 128
    mats = g.hy_cs[L]
    ec = [0]
    for kg in range(L // KG):
        for mi in range(2):
            for tc in range(nch):
                if tc % 2 == 0:
                    bi = g.mt_i
                    g.mt_i = (bi + 1) % len(g.mt)
                    mt2, mk = g.mt[bi], "mt%d" % bi
                    P.dma("sync", mt2[:, :, :KG], mats[mi, tc // 2, kg].rearrange("p (a k) -> p a k", a=2),
                          mk, w=[mk])
                mt = mt2[:, tc % 2, :]
                for o in range(2):
                    for j in range(nj):
                        pb, pbk = g.psb(o * nj + j)
                        P.op("tensor", lambda e, pb=pb, mt=mt, j=j, tc=tc, mi=mi, o=o: e.matmul(
                            pb[:, :512], mt[:, j * 128:(j + 1) * 128], R[mi][o][:, tc, :], start=(tc == 0), stop=(tc == nch - 1)),
                            r=[mk, "R%d%d" % (mi, o)], w=[pbk])
            for o in range(2):
                for j in range(nj):
                    pb, pbk = g.psb(o * nj + j)
                    kc = kg * nj + j
                    st_, sk_ = nstg()
                    if ec[0] % 2 == 0:
                        P.op("vector", lambda e, st_=st_, pb=pb: e.tensor_scalar(out=st_, in0=pb[:, :512], scalar1=2.0 / N2, scalar2=None, op0=ALU.mult),
                             r=[pbk], w=[sk_])
                    else:
                        P.op("scalar", lambda e, st_=st_, pb=pb: e.mul(out=st_, in_=pb[:, :512], mul=2.0 / N2), r=[pbk], w=[sk_])
                    ec[0] += 1
                    if kc == 0:
                        P.op("vector", lambda e, st_=st_: e.tensor_scalar(out=st_[0:1, :], in0=st_[0:1, :], scalar1=0.5, scalar2=None, op0=ALU.mult),
                             r=[sk_], w=[sk_])
                    P.dma("gpsimd", g.KF[mi, kc * 128:(kc + 1) * 128, o * 512:(o + 1) * 512], st_, sk_, r=[sk_], w=["KF"])
    for o in range(2):
        pn, pnk = g.psb(o)
        for tc in range(nch):
            P.op("tensor", lambda e, tc=tc, pn=pn, o=o: e.matmul(pn[:1, :512], g.altc[:, 0:1], R[0][o][:, tc, :], start=(tc == 0), stop=(tc == nch - 1)),
                 r=["altc", "R0%d" % o], w=[pnk])
        st_, sk_ = nstg()
        P.op("vector", lambda e, st_=st_, pn=pn: e.tensor_scalar(out=st_[0:1, :], in0=pn[0:1, :512], scalar1=1.0 / N2, scalar2=None, op0=ALU.mult),
             r=[pnk], w=[sk_])
        P.dma("gpsimd", g.KF[0, L:L + 1, o * 512:(o + 1) * 512], st_[0:1, :], sk_, r=[sk_], w=["KF"])
    g.arena_off = mark0
    P.barrier()
    ztok = g.carve([128, nch, 512], BF16)
    Asb = g.carve([128, nch + 1, 512], BF16)
    Bsb = g.carve([128, nch, 512], BF16)
    skp = g.carve([128, 2, 4], F32)
    P.dma("sync", skp, g.hy_skipT[l], "pc", w=["skp"])
    P.seal()
    mark1 = g.arena_off
    cw = g.carve([128, 12, 4], F32)
    P.dma("sync", cw, g.hy_cw[l], "pc", w=["hcw"])
    P.seal()
    xb2 = [g.carve([128, L + 2], F32) for i in range(2)]
    acc2 = [g.carve([128, L], F32) for i in range(2)]
    accb = g.carve([128, L], BF16)
    psb16 = [g.ps[i][:].bitcast(BF16) for i in range(8)]
    for i in range(2):
        P.op("vector", lambda e, i=i: e.memset(xb2[i], 0.0), w=["xb%d" % i])

    def to_tok(srcb, sk_, cc):
        for t8 in range(0, nch, 8):
            g.ps_i = (g.ps_i + 1) % 8
            pi = g.ps_i
            nn = min(8, nch - t8)
            for j in range(nn):
                P.op("tensor", lambda e, pi=pi, j=j, t8=t8: e.transpose(psb16[pi][:, j * 128:(j + 1) * 128],
                                                                      srcb[:, (t8 + j) * 128:(t8 + j + 1) * 128], identb),
                     r=[sk_, "identb"], w=["ps%d" % pi])
            P.op("vector", lambda e, pi=pi, t8=t8, nn=nn, cc=cc: e.tensor_copy(
                out=ztok[:, t8:t8 + nn, cc * 128:(cc + 1) * 128], in_=psb16[pi][:, 0:nn * 128].rearrange("p (c x) -> p c x", x=128)),
                r=["ps%d" % pi], w=["ztok"])

    for rc in range(12):
        r0 = 1088 + rc * 128
        xb, xk = xb2[rc % 2], "xb%d" % (rc % 2)
        acc, ak = acc2[rc % 2], "acc%d" % (rc % 2)
        P.dma("sync", xb[:, 1:1 + L], PT[r0:r0 + 128, tok0:tok0 + L], xk, r=["PT%d" % l], w=[xk])
        eng = "vector"
        P.op(eng, lambda e, rc=rc, xb=xb, acc=acc: e.tensor_scalar(out=acc, in0=xb[:, 0:L], scalar1=cw[:, rc, 0:1], scalar2=cw[:, rc, 3:4],
                                                                  op0=ALU.mult, op1=ALU.add), r=[xk, "hcw"], w=[ak])
        for j in (1, 2):
            P.op(eng, lambda e, rc=rc, j=j, xb=xb, acc=acc: e.scalar_tensor_tensor(out=acc, in0=xb[:, j:j + L], scalar=cw[:, rc, j:j + 1], in1=acc,
                                                                                op0=ALU.mult, op1=ALU.add), r=[xk, "hcw", ak], w=[ak])
        P.dma("sync", g.HV[rc * 128:(rc + 1) * 128, 0:L], acc, ak, r=[ak], w=["HV"])
        if rc < 4:
            P.op("scalar", lambda e, acc=acc: e.copy(out=accb, in_=acc), r=[ak], w=["accb"])
            to_tok(accb, "accb", rc)
    g.arena_off = mark1
    P.barrier()
    kr_ = [g.carve([128, 512], F32) for i in range(2)]
    ki_ = [g.carve([128, 512], F32) for i in range(2)]
    t1 = g.carve([128, 512], F32)
    t2 = g.carve([128, 512], F32)
    zin = [g.carve([128, 512], F32) for i in range(2)]
    gat = [g.carve([128, 512], F32) for i in range(2)]
    ot = [g.carve([128, 512], F32) for i in range(2)]
    otb = [g.carve([128, 512], BF16) for i in range(2)]
    stg = [g.carve([128, 512], F32) for i in range(2)]
    for o in range(2):
        def evac(kc, pc, pck, ps_, psk, o=o):
            kr, krk = kr_[kc % 2], "kr%d" % (kc % 2)
            ki2, kik = ki_[kc % 2], "ki%d" % (kc % 2)
            P.dma("sync", kr, g.KF[0, kc * 128:(kc + 1) * 128, o * 512:(o + 1) * 512], krk, r=["KF"], w=[krk])
            P.dma("sync", ki2, g.KF[1, kc * 128:(kc + 1) * 128, o * 512:(o + 1) * 512], kik, r=["KF"], w=[kik])
            P.op("vector", lambda e: e.tensor_tensor(out=t1, in0=pc[:, :512], in1=kr, op=ALU.mult), r=[pck, krk], w=["ht1"])
            P.op("vector", lambda e: e.tensor_tensor(out=t2, in0=ps_[:, :512], in1=ki2, op=ALU.mult), r=[psk, kik], w=["ht2"])
            P.op("gpsimd", lambda e: e.tensor_tensor(out=Asb[:, kc, :], in0=t1, in1=t2, op=ALU.add), r=["ht1", "ht2"], w=["Asb"])
            P.op("vector", lambda e: e.tensor_tensor(out=t1, in0=ps_[:, :512], in1=kr, op=ALU.mult), r=[psk, krk], w=["ht1"])
            P.op("vector", lambda e: e.tensor_tensor(out=t2, in0=pc[:, :512], in1=ki2, op=ALU.mult), r=[pck, kik], w=["ht2"])
            P.op("gpsimd", lambda e: e.tensor_tensor(out=Bsb[:, kc, :], in0=t1, in1=t2, op=ALU.subtract), r=["ht1", "ht2"], w=["Bsb"])

        def nyq(pn, pnk, o=o):
            P.dma("sync", kr_[0][0:1, :], g.KF[0, L:L + 1, o * 512:(o + 1) * 512], "kr0", r=["KF"], w=["kr0"])
            P.op("vector", lambda e: e.tensor_tensor(out=Asb[0:1, nch, :], in0=pn[0:1, :512], in1=kr_[0][0:1, :], op=ALU.mult),
                 r=[pnk, "kr0"], w=["Asb"])

        dft_fwd(g, L, ztok, "ztok", ztok, "ztok", evac, nyq)
        TWI = min(512, L)
        mats = g.hy_cs[L]
        for ti, t0 in enumerate(range(0, L, TWI)):
            banks = [g.psb(4 * (ti % 2) + cc) for cc in range(4)]
            pair = [None, None]
            for kc in range(nch):
                tiles = []
                for mi in range(2):
                    if kc % 2 == 0:
                        bi = g.mt_i
                        g.mt_i = (bi + 1) % len(g.mt)
                        mt2, mk = g.mt[bi], "mt%d" % bi
                        P.dma("sync", mt2[:, :, :TWI], mats[mi, kc // 2, t0 // TWI].rearrange("p (a k) -> p a k", a=2),
                              mk, w=[mk])
                        pair[mi] = (mt2, mk)
                    tiles.append((pair[mi][0][:, kc % 2, :], pair[mi][1]))
                for cc in range(4):
                    pb, pbk = banks[cc]
                    P.op("tensor", lambda e, pb=pb, kc=kc, cc=cc, mt=tiles[0][0]: e.matmul(
                        pb[:, :TWI], Asb[:, kc, cc * 128:(cc + 1) * 128], mt[:, :TWI], start=(kc == 0), stop=False),
                        r=["Asb", tiles[0][1]], w=[pbk])
                    P.op("tensor", lambda e, pb=pb, kc=kc, cc=cc, mt=tiles[1][0]: e.matmul(
                        pb[:, :TWI], Bsb[:, kc, cc * 128:(cc + 1) * 128], mt[:, :TWI], start=False, stop=False),
                        r=["Bsb", tiles[1][1]], w=[pbk])
            for cc in range(4):
                pb, pbk = banks[cc]
                P.op("tensor", lambda e, pb=pb, cc=cc: e.matmul(pb[:, :TWI], Asb[0:1, nch, cc * 128:(cc + 1) * 128], altr[0:1, :TWI],
                                                             start=False, stop=True), r=["Asb", "altr"], w=[pbk])
                zi_, zk = zin[cc % 2], "zin%d" % (cc % 2)
                ga_, gk = gat[cc % 2], "gat%d" % (cc % 2)
                o_, ok = ot[cc % 2], "hot%d" % (cc % 2)
                src = g.HV if o == 0 else g.HZ2
                P.dma("sync", zi_[:, :TWI], src[cc * 128:(cc + 1) * 128, t0:t0 + TWI], zk, r=["HV", "HZ2"], w=[zk])
                P.dma("sync", ga_[:, :TWI], g.HV[(o + 1) * 512 + cc * 128:(o + 1) * 512 + (cc + 1) * 128, t0:t0 + TWI], gk, r=["HV"], w=[gk])
                P.op("vector", lambda e, zi_=zi_, pb=pb, cc=cc, o_=o_, o=o: e.scalar_tensor_tensor(
                    out=o_[:, :TWI], in0=zi_[:, :TWI], scalar=skp[:, o, cc:cc + 1], in1=pb[:, :TWI], op0=ALU.mult, op1=ALU.add),
                    r=[zk, "skp", pbk], w=[ok])
                P.op("gpsimd", lambda e, o_=o_, ga_=ga_: e.tensor_tensor(out=o_[:, :TWI], in0=o_[:, :TWI], in1=ga_[:, :TWI], op=ALU.mult),
                     r=[ok, gk], w=[ok])
                if o == 0:
                    P.dma("gpsimd", g.HZ2[cc * 128:(cc + 1) * 128, t0:t0 + TWI], o_[:, :TWI], ok, r=[ok], w=["HZ2"])
                    ob_, obk = otb[cc % 2], "hotb%d" % (cc % 2)
                    P.op("scalar", lambda e, ob_=ob_, o_=o_: e.copy(out=ob_[:, :TWI], in_=o_[:, :TWI]), r=[ok], w=[obk])
                    for j in range(TWI // 128):
                        g.ps_i = (g.ps_i + 1) % 8
                        pi = 4 * (1 - ti % 2) + (j % 4)
                        P.op("tensor", lambda e, pi=pi, j=j, ob_=ob_: e.transpose(psb16[pi][:, 0:128], ob_[:, j * 128:(j + 1) * 128], identb),
                             r=[obk, "identb"], w=["ps%d" % pi])
                        P.op("vector", lambda e, pi=pi, j=j, cc=cc, t0=t0: e.tensor_copy(
                            out=ztok[:, t0 // 128 + j, cc * 128:(cc + 1) * 128], in_=psb16[pi][:, 0:128]), r=["ps%d" % pi], w=["ztok"])
                else:
                    P.dma("gpsimd", g.CATT[1024 + cc * 128:1024 + (cc + 1) * 128, tok0 + t0:tok0 + t0 + TWI], o_[:, :TWI], ok, r=[ok], w=["CATT"])


def phase_c(g, l):
    P = g.P
    g.new_phase()
    g.ps_rot = list(range(8))
    last = (l == DEPTH - 1)
    tiles = TILES[1:] if last else TILES
    jf = lambda t0: 1 if t0 < NCTX else 0
    catn = g.carve([128, KC, T], BF16)
    carve_common(g, 256)
    xt = g.carve([128, 8, 512], F32)
    sq = g.carve([128, 8, 512], BF16)
    rstd = g.carve([128, 512], F32)
    mg = g.carve([128, KC], F32)
    xres = [g.carve([128, 512], F32) for i in range(2)]
    P.dma("sync", mg, g.mix_gT[l], "pc", w=["mg"])
    P.seal()
    for (t0, n) in tiles:
        for (c0, nk, nfeat) in [(0, 8, 1024), (8, 4, 512)]:
            srcv = g.CATT[c0 * 128:(c0 + nk) * 128, :].rearrange("(kc p) t -> p kc t", p=128)
            P.dma("sync", xt[:, :nk, :n], srcv[:, :, t0:t0 + n], "xt0", r=["CATT"], w=["xt0"])
            P.op("scalar", lambda e, n=n, nk=nk: e.activation(out=sq[:, :nk, :n], in_=xt[:, :nk, :n], func=AF.Square), r=["xt0"], w=["sq"])
            ps, psk = g.next_ps()
            for kc in range(nk):
                P.op("tensor", lambda e, kc=kc, n=n, ps=ps, nk=nk: e.matmul(ps[:, :n], g.ones_bf[:], sq[:, kc, :n], start=(kc == 0), stop=(kc == nk - 1)),
                     r=["sq", "ones_bf"], w=[psk])
            rstd_op(g, ps, psk, n, nfeat, rstd, "rstd")
            for kc in range(nk):
                P.op("vector", lambda e, kc=kc, n=n, t0=t0, c0=c0: e.scalar_tensor_tensor(
                    out=catn[:, c0 + kc, t0:t0 + n], in0=xt[:, kc, :n], scalar=mg[:, c0 + kc:c0 + kc + 1], in1=rstd[:, :n],
                    op0=ALU.mult, op1=ALU.mult), r=["xt0", "rstd", "mg"], w=["catn"])
        srcv = g.CATT[1536:2048, :].rearrange("(kc p) t -> p kc t", p=128)
        P.dma("sync", xt[:, :4, :n], srcv[:, :, t0:t0 + n], "xt0", r=["CATT"], w=["xt0"])
        P.op("scalar", lambda e, n=n, t0=t0: e.copy(out=catn[:, 12:16, t0:t0 + n], in_=xt[:, :4, :n]), r=["xt0"], w=["catn"])
    src = g.xT if l == 0 else g.xs
    skey = "xs"
    wv = g.w_out[l].rearrange("(kc p) n -> p kc n", p=128)
    cnt = [0]

    def evac(cb, m, t0, n, ps, psk):
        fc = cb // 128
        j = jf(t0)
        xr = xres[cnt[0] % 2]
        xk = "xres%d" % (cnt[0] % 2)
        cnt[0] += 1
        P.dma("sync", xr[:, :n], src[cb:cb + 128, t0:t0 + n], xk, r=[skey], w=[xk])
        stg, sk = next_stage(g)
        P.op("vector", lambda e: e.scalar_tensor_tensor(out=stg[:, :n], in0=ps[:, :n], scalar=g.mod[l][:, 32 + fc, j:j + 1], in1=xr[:, :n],
                                                        op0=ALU.mult, op1=ALU.add), r=[psk, xk, "mod%d" % l], w=[sk])
        P.dma("sync", g.XM[cb:cb + 128, t0:t0 + n], stg[:, :n], sk, r=[sk], w=["XM"])

    for c0 in range(0, D, 256):
        proj_fm(g, wv, c0, 256, catn, "catn", tiles, evac)
    g.new_phase()
    xn2 = g.carve([128, KC, 1024], BF16)
    hT = g.carve([128, 44, 1024], BF16)
    w1b = [g.carve([128, KC, 256], BF16) for i in range(2)]
    w2b = [g.carve([128, 44, 128], BF16) for i in range(2)]
    g.stage = [g.carve([128, 512], F32) for i in range(2)]
    g.stage_i = 0
    sg = [g.carve([128, 512], F32) for i in range(2)]
    xres = [g.carve([128, 512], F32) for i in range(2)]
    rb = rms_bufs(g, 256)
    w1v = g.ffn_w1[l].rearrange("(kc p) n -> p kc n", p=128)
    w2v = g.ffn_w2[l].rearrange("(kc p) n -> p kc n", p=128)
    groups = [[(256 + 1024 * i, 512), (256 + 1024 * i + 512, 512)] for i in range(4)]
    if not last:
        groups = [[(0, 256)]] + groups
    wi = 0
    w2i = 0
    for grp in groups:
        g0 = grp[0][0]
        gn = sum(n for (_, n) in grp)
        rms_modulate(g, g.XM, l, g.gs2[l], "gs2_%d" % l, 48, xn2, "xn2", [(t0, 256) for t0 in range(g0, g0 + gn, 256)], jf,
                     bufs=rb, srckey="XM", dst_t0=g0)
        for hc in range(44):
            wb, wk = w1b[wi % 2], "w1b%d" % (wi % 2)
            wi += 1
            P.dma("sync", wb.rearrange("p kc n -> p (kc n)"), g.W1B[hc], wk, r=["W1B"], w=[wk])
            for (t0, n) in grp:
                pg, pgk = g.next_ps()
                pu, puk = g.next_ps()
                for (pp, ppk, cofs) in [(pg, pgk, 0), (pu, puk, 128)]:
                    for kc in range(KC):
                        P.op("tensor", lambda e, pp=pp, kc=kc, cofs=cofs, t0=t0, n=n, wb=wb, g0=g0: e.matmul(
                            pp[:, :n], wb[:, kc, cofs:cofs + 128], xn2[:, kc, t0 - g0:t0 - g0 + n], start=(kc == 0), stop=(kc == KC - 1)),
                            r=[wk, "xn2"], w=[ppk])
                s_, sk_ = sg[hc % 2], "sg%d" % (hc % 2)
                P.op("scalar", lambda e, s_=s_, pg=pg, n=n: e.activation(out=s_[:, :n], in_=pg[:, :n], func=AF.Silu), r=[pgk], w=[sk_])
                P.op("vector", lambda e, s_=s_, pu=pu, n=n, hc=hc, t0=t0, g0=g0: e.tensor_tensor(
                    out=hT[:, hc, t0 - g0:t0 - g0 + n], in0=pu[:, :n], in1=s_[:, :n], op=ALU.mult), r=[puk, sk_], w=["hT"])
        for fc in range(KC):
            wb, wk = w2b[w2i % 2], "w2b%d" % (w2i % 2)
            w2i += 1
            P.dma("sync", wb.rearrange("p hc n -> p (hc n)"), g.W2B[fc], wk, r=["W2B"], w=[wk])
            for (t0, n) in grp:
                ps, psk = g.next_ps()
                for hc in range(44):
                    P.op("tensor", lambda e, ps=ps, hc=hc, t0=t0, n=n, wb=wb, g0=g0: e.matmul(
                        ps[:, :n], wb[:, hc, :], hT[:, hc, t0 - g0:t0 - g0 + n], start=(hc == 0), stop=(hc == 43)), r=[wk, "hT"], w=[psk])
                j = jf(t0)
                xr = xres[fc % 2]
                xk = "xres%d" % (fc % 2)
                P.dma("sync", xr[:, :n], g.XM[fc * 128:(fc + 1) * 128, t0:t0 + n], xk, r=["XM"], w=[xk])
                stg, sk = next_stage(g)
                P.op("vector", lambda e, stg=stg, ps=ps, n=n, fc=fc, j=j, xr=xr: e.scalar_tensor_tensor(
                    out=stg[:, :n], in0=ps[:, :n], scalar=g.mod[l][:, 80 + fc, j:j + 1], in1=xr[:, :n], op0=ALU.mult, op1=ALU.add),
                    r=[psk, xk, "mod%d" % l], w=[sk])
                if last:
                    P.dma("gpsimd", g.outT[fc * 128:(fc + 1) * 128, t0 - NCTX:t0 - NCTX + n], stg[:, :n], sk, r=[sk], w=["outT"])
                else:
                    P.dma("gpsimd", g.xs[fc * 128:(fc + 1) * 128, t0:t0 + n], stg[:, :n], sk, r=[sk], w=["xs"])


def host_inputs(inputs, b):
    f = lambda a: np.ascontiguousarray(a, dtype=np.float32)
    x, ctx, c, c_ctx = inputs["x"], inputs["ctx"], inputs["c"], inputs["c_ctx"]
    m = {}
    m["xT"] = f(np.concatenate([ctx[b], x[b]], axis=0).T)
    cc = np.stack([c[b], c_ctx], axis=-1)
    m["cT"] = f(cc.reshape(KC, 128, 2).transpose(1, 0, 2))
    m["ada_w"] = f(inputs["ada_w"])
    m["ada_bT"] = f(inputs["ada_b"].reshape(DEPTH, 96, 128).transpose(0, 2, 1))
    m["norm1_gT"] = f(inputs["norm1_g"].reshape(DEPTH, KC, 128).transpose(0, 2, 1))
    m["norm2_gT"] = f(inputs["norm2_g"].reshape(DEPTH, KC, 128).transpose(0, 2, 1))
    m["w_in"] = f(inputs["w_in"])
    wi = inputs["w_in"]
    m["w_in_rot"] = f(np.concatenate([wi[:, :, 1056:1088], wi[:, :, 1024:1056]], axis=-1))
    qa, kva = inputs["mla_qa_norm"], inputs["mla_kva_norm"]
    m["mla_gT"] = f(np.concatenate([qa.reshape(DEPTH, 4, 128).transpose(0, 2, 1),
                                    kva.reshape(DEPTH, 4, 128).transpose(0, 2, 1)], axis=-1))
    qn_, kn_ = inputs["mla_q_norm"], inputs["mla_k_norm"]
    hg = np.zeros((DEPTH, 128, 6), np.float32)
    hg[:, :, 0] = qn_[:, :128]
    hg[:, :, 1] = kn_[:, :128]
    hg[:, :64, 2] = qn_[:, 128:192]
    hg[:, :64, 3] = np.concatenate([qn_[:, 160:192], qn_[:, 128:160]], -1)
    hg[:, :64, 4] = kn_[:, 128:192]
    hg[:, :64, 5] = np.concatenate([kn_[:, 160:192], kn_[:, 128:160]], -1)
    m["mla_hg"] = hg
    m["rope_cs"] = rope_tables()
    wq = inputs["mla_w_uq"]
    m["w_uq"] = f(wq)
    wq4 = wq.reshape(DEPTH, 512, 8, 192)
    m["w_uq_rot"] = f(np.concatenate([wq4[..., 160:192], wq4[..., 128:160]], -1).reshape(DEPTH, 512, 512))
    m["w_ukv"] = f(inputs["mla_w_ukv"])
    tri = np.zeros((2, 128, 128), np.float32)
    ii = np.arange(128)
    tri[0] = (ii[:, None] <= ii[None, :])
    tri[1] = (ii[:, None] >= ii[None, :])
    m["tri"] = tri
    m["ident"] = np.eye(128, dtype=np.float32)
    sel4 = np.zeros((4, 4, 128), np.float32)
    for r_ in range(4):
        sel4[r_, r_, :] = 1.0
    m["sel4"] = sel4
    cwm = np.concatenate([inputs["ml_conv_w"], inputs["ml_conv_b"][:, None, :]], axis=1)
    m["ml_cw"] = f(cwm.reshape(DEPTH, 4, 8, 128).transpose(0, 3, 2, 1))
    m["ml_gate_b"] = f(inputs["ml_gate_b"])
    m["mix_g"] = f(inputs["mix_norm_g"])
    m["mix_gT"] = f(inputs["mix_norm_g"].reshape(DEPTH, KC, 128).transpose(0, 2, 1))
    m.update(hyena_tables())
    hcw = np.concatenate([inputs["hy_conv_w"], inputs["hy_conv_b"][:, None, :]], axis=1)
    m["hy_cw"] = f(hcw.reshape(DEPTH, 4, 12, 128).transpose(0, 3, 2, 1))
    m["hy_w1"] = f(inputs["hy_w1"])
    m["hy_b12"] = f(np.stack([inputs["hy_b1"], inputs["hy_b2"]], axis=-1))
    m["hy_w2"] = f(inputs["hy_w2"])
    m["hy_w3"] = f(inputs["hy_w3"])
    m["hy_decay"] = f(inputs["hy_decay"])
    m["hy_skipT"] = f(inputs["hy_skip"].reshape(DEPTH, 2, 4, 128).transpose(0, 3, 1, 2))
    m["w_out"] = f(inputs["w_out"])
    m["ffn_w1"] = f(inputs["ffn_w1"])
    m["ffn_w2"] = f(inputs["ffn_w2"])
    return m


def rope_tables():
    rows = NLAT // 64
    row = np.repeat(np.arange(rows, dtype=np.float32), 64)
    col = np.tile(np.arange(64, dtype=np.float32), rows)
    freqs = (np.float32(10000.0) ** (-np.arange(16, dtype=np.float32) / np.float32(16))).astype(np.float32)
    ang = np.concatenate([row[:, None] * freqs, col[:, None] * freqs], axis=-1).astype(np.float32)
    cos, sin = np.cos(ang).astype(np.float32), np.sin(ang).astype(np.float32)
    out = np.zeros((2, 64, T), np.float32)
    out[0, :, :NCTX] = 1.0
    out[0, :32, NCTX:] = cos.T
    out[0, 32:, NCTX:] = cos.T
    out[1, :32, NCTX:] = -sin.T
    out[1, 32:, NCTX:] = sin.T
    return out


_HY_TABLES = {}


def hyena_tables():
    if _HY_TABLES:
        return _HY_TABLES
    import ml_dtypes
    out = {}
    for L in (4096, 256):
        t = (np.arange(L, dtype=np.float32) / np.float32(L)).astype(np.float32)
        bands = np.arange(1, 17, dtype=np.float32)
        ang = (np.float32(2.0 * math.pi) * t[:, None] * bands).astype(np.float32)
        feats = np.concatenate([t[:, None], np.cos(ang), np.sin(ang)], axis=-1).astype(np.float32)
        out["hy_feats%d" % L] = np.ascontiguousarray(feats.T)
        out["hy_negt%d" % L] = np.ascontiguousarray((-t).reshape(L // 128, 128).T)
        ii = np.arange(L, dtype=np.int64)
        ph = (ii[:, None] * ii[None, :]) % (2 * L)
        th = ph.astype(np.float64) * (2.0 * math.pi / (2 * L))
        cs = np.stack([np.cos(th), np.sin(th)]).astype(np.float32).astype(ml_dtypes.bfloat16)
        KG = min(512, L)
        nch = L // 128
        cs = cs.reshape(2, nch // 2, 2, 128, L // KG, KG).transpose(0, 1, 4, 3, 2, 5)
        out["hy_cs%d" % L] = np.ascontiguousarray(cs).reshape(2, nch // 2, L // KG, 128, 2 * KG)
    alt = np.ones((128, 512), np.float32)
    alt[1::2, :] = -1.0
    alt[0, 1::2] = -1.0
    out["hy_alt"] = alt
    _HY_TABLES.update(out)
    return _HY_TABLES


def kernel(**inputs):
    nc = build()
    inputs = {k: np.asarray(v) for k, v in inputs.items()}
    shared = host_inputs(inputs, 0)
    in_maps = []
    for core in range(8):
        b = core % 4
        m = dict(shared)
        if b != 0:
            mb = host_inputs_batch(inputs, b)
            m.update(mb)
        in_maps.append(m)
    res = run_bass_kernel_spmd(nc, in_maps, core_ids=list(range(8)))
    out = np.stack([np.ascontiguousarray(res.results[b]["outT"].T) for b in range(4)], axis=0)
    return out.astype(np.float32)


def host_inputs_batch(inputs, b):
    f = lambda a: np.ascontiguousarray(a, dtype=np.float32)
    x, ctx, c, c_ctx = inputs["x"], inputs["ctx"], inputs["c"], inputs["c_ctx"]
    m = {}
    m["xT"] = f(np.concatenate([ctx[b], x[b]], axis=0).T)
    cc = np.stack([c[b], c_ctx], axis=-1)
    m["cT"] = f(cc.reshape(KC, 128, 2).transpose(1, 0, 2))
    return m
```
